# Optimizing a Trainium2 kernel written in Bass

```python
import math
import jax, jax.numpy as jnp
from jax import lax
import numpy as np

D_MODEL = 1024
BATCH = 16
SEQ = 4096
DEPTH = 4

GRID_W = 64
CTX_LEN = 256
HEAD_DIM = 64
N_Q_HEADS = D_MODEL // (2 * HEAD_DIM)
N_KV_HEADS = max(1, N_Q_HEADS // 4)
GQA_GROUP = N_Q_HEADS // N_KV_HEADS
Q_W = N_Q_HEADS * HEAD_DIM
KV_W = N_KV_HEADS * HEAD_DIM
RNN_W = D_MODEL // 2
RNN_BLOCKS = RNN_W // HEAD_DIM
RNN_BW = RNN_W // RNN_BLOCKS
IN_W = Q_W + 2 * KV_W + 2 * RNN_W
MIX_W = Q_W + RNN_W
CONV_K = 4
CONV_LEFT = 2
RG_C = 8.0
ROPE_THETA = 10000.0
Q_BLOCK = 128
RWKV_HEADS = D_MODEL // HEAD_DIM
RWKV_LORA = 64
RWKV_GATE_LORA = 128
DECAY_SCALE = math.exp(-0.5)
GN_EPS = 64e-5
D_FF = 4 * D_MODEL
N_EVEN = (DEPTH + 1) // 2
N_ODD = DEPTH // 2
EPS = 1e-6
F32 = jnp.float32

kernel_name = "hybrid_rglru_gqa_rwkv7_dit_prefix"


def rmsnorm(x, g):
    xf = x.astype(F32)
    y = xf * lax.rsqrt(jnp.mean(xf * xf, axis=-1, keepdims=True) + EPS)
    return (y * g.astype(F32)).astype(x.dtype)


def ada_chunks(cvec, w_mod, b_mod):
    m = jnp.dot(jax.nn.silu(cvec), w_mod) + b_mod
    return jnp.split(m, 6, axis=-1)


def axial_rope(n):
    rows = n // GRID_W
    row = jnp.repeat(jnp.arange(rows, dtype=F32), GRID_W)
    col = jnp.tile(jnp.arange(GRID_W, dtype=F32), rows)
    half = HEAD_DIM // 2
    inv = ROPE_THETA ** (-jnp.arange(0, half, 2, dtype=F32) / half)
    ang = jnp.concatenate([row[:, None] * inv, col[:, None] * inv], axis=-1)
    return jnp.cos(ang), jnp.sin(ang)


def apply_rope(x, cos, sin):
    x1 = x[..., 0::2].astype(F32)
    x2 = x[..., 1::2].astype(F32)
    y = jnp.stack([x1 * cos - x2 * sin, x1 * sin + x2 * cos], axis=-1)
    return y.reshape(x.shape).astype(x.dtype)


def attend(q, k, v):
    s = jnp.einsum('bqhgd,bkhd->bhgqk', q, k).astype(F32) * HEAD_DIM ** -0.5
    p = jax.nn.softmax(s, axis=-1).astype(v.dtype)
    return jnp.einsum('bhgqk,bkhd->bqhgd', p, v)


def blocked_attend(q, k, v):
    b, n = q.shape[0], q.shape[1]
    nb = n // Q_BLOCK
    qb = jnp.moveaxis(q.reshape(b, nb, Q_BLOCK, N_KV_HEADS, GQA_GROUP, HEAD_DIM), 1, 0)
    o = lax.map(lambda qi: attend(qi, k, v), qb)
    return jnp.moveaxis(o, 0, 1).reshape(b, n, Q_W)


def dwconv(x, w, bias):
    t = x.shape[1]
    xp = jnp.pad(x, ((0, 0), (CONV_LEFT, CONV_K - 1 - CONV_LEFT), (0, 0)))
    out = bias
    for j in range(CONV_K):
        out = out + xp[:, j:j + t] * w[j]
    return out


def rglru_coeffs(x, w, bias, lam):
    xf = x.astype(F32)
    xb = xf.reshape(xf.shape[0], xf.shape[1], RNN_BLOCKS, RNN_BW)
    gates = jnp.einsum('btnd,gnde->gbtne', xb, w.astype(F32)).reshape((2,) + xf.shape)
    gates = gates + bias.astype(F32)[:, None, None, :]
    r = jax.nn.sigmoid(gates[0])
    i = jax.nn.sigmoid(gates[1])
    log_a = -RG_C * r * jax.nn.softplus(-lam.astype(F32))
    a = jnp.exp(log_a)
    u = jnp.sqrt(-jnp.expm1(2.0 * log_a)) * (i * xf)
    return a, u


def linear_scan(a, u, h0, reverse):
    if h0 is not None:
        idx = -1 if reverse else 0
        u = u.at[:, idx].add(a[:, idx] * h0)

    def comb(l, r):
        return l[0] * r[0], r[0] * l[1] + r[1]

    _, h = lax.associative_scan(comb, (a, u), reverse=reverse, axis=1)
    return h


def hybrid_mixer(hl, hc, w_in, w_out, q_norm, k_norm, conv_w, conv_b, gate_w, gate_b, lam, cos, sin, need_ctx):
    b, n, _ = hl.shape
    cl = hc.shape[1]
    cuts = [Q_W, Q_W + KV_W, Q_W + 2 * KV_W, Q_W + 2 * KV_W + RNN_W]
    ql, kl, vl, xl, gl = jnp.split(hl @ w_in, cuts, axis=-1)
    qc, kc, vc, xc, gc = jnp.split(hc @ w_in, cuts, axis=-1)
    qlh = apply_rope(rmsnorm(ql.reshape(b, n, N_KV_HEADS, GQA_GROUP, HEAD_DIM), q_norm),
                     cos[None, :, None, None, :], sin[None, :, None, None, :])
    klh = apply_rope(rmsnorm(kl.reshape(b, n, N_KV_HEADS, HEAD_DIM), k_norm),
                     cos[None, :, None, :], sin[None, :, None, :])
    kch = rmsnorm(kc.reshape(b, cl, N_KV_HEADS, HEAD_DIM), k_norm)
    vch = vc.reshape(b, cl, N_KV_HEADS, HEAD_DIM)
    k_all = jnp.concatenate([kch, klh], axis=1)
    v_all = jnp.concatenate([vch, vl.reshape(b, n, N_KV_HEADS, HEAD_DIM)], axis=1)
    att_l = blocked_attend(qlh, k_all, v_all)
    xl = dwconv(xl, conv_w, conv_b)
    xc = dwconv(xc, conv_w, conv_b)
    h_lat, h_ctx = [], []
    for d, rev in enumerate((False, True)):
        a_c, u_c = rglru_coeffs(xc, gate_w[d], gate_b[d], lam[d])
        hcd = linear_scan(a_c, u_c, None, rev)
        h0 = hcd[:, 0] if rev else hcd[:, -1]
        a_l, u_l = rglru_coeffs(xl, gate_w[d], gate_b[d], lam[d])
        h_lat.append(linear_scan(a_l, u_l, h0, rev))
        h_ctx.append(hcd)
    rec_l = (jax.nn.gelu(gl.astype(F32)) * (h_lat[0] + h_lat[1])).astype(hl.dtype)
    out_l = jnp.concatenate([att_l, rec_l], axis=-1) @ w_out
    out_c = None
    if need_ctx:
        qch = rmsnorm(qc.reshape(b, cl, N_KV_HEADS, GQA_GROUP, HEAD_DIM), q_norm)
        att_c = attend(qch, kch, vch).reshape(b, cl, Q_W)
        rec_c = (jax.nn.gelu(gc.astype(F32)) * (h_ctx[0] + h_ctx[1])).astype(hc.dtype)
        out_c = jnp.concatenate([att_c, rec_c], axis=-1) @ w_out
    return out_l, out_c


def rwkv7_project(h, mu, w_rkv, lora_down, lora_up, lora_bias, gate_down, gate_up, k_k, k_a):
    b, t, _ = h.shape
    hp = jnp.pad(h, ((0, 0), (1, 1), (0, 0)))
    xx = 0.5 * (hp[:, :-2] + hp[:, 2:]) - h
    lerp = lambda j: h + xx * mu[j]
    heads = lambda z: z.astype(F32).reshape(b, t, RWKV_HEADS, HEAD_DIM)
    r = heads(lerp(0) @ w_rkv[0])
    k = heads(lerp(2) @ w_rkv[1])
    v = heads(lerp(3) @ w_rkv[2])
    g = jax.nn.sigmoid(lerp(5) @ gate_down) @ gate_up
    kk = k * k_k.astype(F32).reshape(RWKV_HEADS, HEAD_DIM)
    kk = kk / jnp.maximum(jnp.sqrt(jnp.sum(kk * kk, axis=-1, keepdims=True)), 1e-12)
    k_a_h = k_a.astype(F32).reshape(RWKV_HEADS, HEAD_DIM)
    xw, xa = lerp(1), lerp(4)
    dirs = []
    for d in range(2):
        dec = heads(lora_bias[d, 0] + jnp.tanh(xw @ lora_down[d, 0]) @ lora_up[d, 0])
        w = jnp.exp(-DECAY_SCALE * jax.nn.sigmoid(dec))
        a = jax.nn.sigmoid(heads(lora_bias[d, 1] + (xa @ lora_down[d, 1]) @ lora_up[d, 1]))
        kd = k * (1.0 + (a - 1.0) * k_a_h)
        dirs.append((w, kk * a, kd))
    return r, kk, v, g, dirs


def wkv_scan(r, w, kk, bvec, k, v, s0, reverse):
    def step(s, inp):
        r_t, w_t, kk_t, b_t, k_t, v_t = inp
        sk = jnp.einsum('bhvk,bhk->bhv', s, kk_t)
        s = s * w_t[:, :, None, :] - sk[..., None] * b_t[:, :, None, :] + v_t[..., None] * k_t[:, :, None, :]
        return s, jnp.einsum('bhvk,bhk->bhv', s, r_t)

    xs = tuple(jnp.moveaxis(z, 1, 0) for z in (r, w, kk, bvec, k, v))
    s_fin, y = lax.scan(step, s0, xs, reverse=reverse)
    return s_fin, jnp.moveaxis(y, 0, 1)


def rwkv7_out(y, r, v, kds, g, r_k, gn_g, gn_b, w_o):
    b, t = y.shape[0], y.shape[1]
    mean = jnp.mean(y, axis=-1, keepdims=True)
    var = jnp.mean(jnp.square(y - mean), axis=-1, keepdims=True)
    yn = ((y - mean) * lax.rsqrt(var + GN_EPS)).reshape(b, t, D_MODEL)
    rk = r_k.astype(F32)
    bonus = sum(jnp.sum(r * kd * rk, axis=-1, keepdims=True) for kd in kds) * v
    o = (yn * gn_g.astype(F32) + gn_b.astype(F32) + bonus.reshape(b, t, D_MODEL)) * g
    return o.astype(g.dtype) @ w_o


def rwkv7_mixer(hl, hc, mu, w_rkv, w_o, lora_down, lora_up, lora_bias, gate_down, gate_up,
                k_k, k_a, r_k, gn_g, gn_b, need_ctx):
    proj = lambda h: rwkv7_project(h, mu, w_rkv, lora_down, lora_up, lora_bias, gate_down, gate_up, k_k, k_a)
    rl, kkl, vl, gl, dl = proj(hl)
    rc, kkc, vc, gc, dc = proj(hc)
    s0 = jnp.zeros((hl.shape[0], RWKV_HEADS, HEAD_DIM, HEAD_DIM), F32)
    ys_l, ys_c = [], []
    for d, rev in enumerate((False, True)):
        s_c, y_c = wkv_scan(rc, dc[d][0], kkc, dc[d][1], dc[d][2], vc, s0, rev)
        _, y_l = wkv_scan(rl, dl[d][0], kkl, dl[d][1], dl[d][2], vl, s_c, rev)
        ys_l.append(y_l)
        ys_c.append(y_c)
    out_l = rwkv7_out(ys_l[0] + ys_l[1], rl, vl, [dl[0][2], dl[1][2]], gl, r_k, gn_g, gn_b, w_o)
    out_c = None
    if need_ctx:
        out_c = rwkv7_out(ys_c[0] + ys_c[1], rc, vc, [dc[0][2], dc[1][2]], gc, r_k, gn_g, gn_b, w_o)
    return out_l, out_c


def sqrelu_mlp(h, w1, w2):
    return jnp.square(jax.nn.relu(h @ w1)) @ w2


def setup_inputs(seed: int = 0) -> dict:
    key = jax.random.key(seed)
    ks = iter(jax.random.split(key, 40))
    nrm = lambda shape, scale: scale * jax.random.normal(next(ks), shape, F32)
    uni = lambda shape, lo, hi: jax.random.uniform(next(ks), shape, F32, lo, hi)
    D = D_MODEL
    x = nrm((BATCH, SEQ, D), 1.0)
    c = nrm((BATCH, D), 1.0)
    ctx = nrm((BATCH, CTX_LEN, D), 1.0)
    c_ctx = nrm((D,), 1.0)
    ada_w = nrm((DEPTH, D, 6 * D), 0.5 * D ** -0.5)
    ada_b = nrm((DEPTH, 6 * D), 0.02)
    norm_g = 1.0 + nrm((DEPTH, 2, D), 0.05)
    mlp_w1 = nrm((DEPTH, D, D_FF), D ** -0.5)
    mlp_w2 = nrm((DEPTH, D_FF, D), D_FF ** -0.5)
    hy_w_in = nrm((N_EVEN, D, IN_W), D ** -0.5)
    hy_w_out = nrm((N_EVEN, MIX_W, D), MIX_W ** -0.5)
    hy_q_norm = 1.0 + nrm((N_EVEN, HEAD_DIM), 0.05)
    hy_k_norm = 1.0 + nrm((N_EVEN, HEAD_DIM), 0.05)
    hy_conv_w = nrm((N_EVEN, CONV_K, RNN_W), CONV_K ** -0.5)
    hy_conv_b = nrm((N_EVEN, RNN_W), 0.02)
    hy_gate_w = nrm((N_EVEN, 2, 2, RNN_BLOCKS, RNN_BW, RNN_BW), RNN_BW ** -0.5)
    hy_gate_b = nrm((N_EVEN, 2, 2, RNN_W), 0.1)
    s = uni((N_EVEN, 2, RNN_W), 0.9, 0.999) ** (1.0 / RG_C)
    hy_lam = jnp.log(s) - jnp.log1p(-s)
    rw_mu = uni((N_ODD, 6, D), 0.0, 1.0)
    rw_w_rkv = nrm((N_ODD, 3, D, D), D ** -0.5)
    rw_w_o = nrm((N_ODD, D, D), D ** -0.5)
    rw_lora_down = nrm((N_ODD, 2, 2, D, RWKV_LORA), D ** -0.5)
    rw_lora_up = nrm((N_ODD, 2, 2, RWKV_LORA, D), 0.1 * RWKV_LORA ** -0.5)
    w0 = uni((N_ODD, 2, D), -6.0, -0.5)
    a0 = nrm((N_ODD, 2, D), 0.1)
    rw_lora_bias = jnp.stack([w0, a0], axis=2)
    rw_gate_down = nrm((N_ODD, D, RWKV_GATE_LORA), D ** -0.5)
    rw_gate_up = nrm((N_ODD, RWKV_GATE_LORA, D), RWKV_GATE_LORA ** -0.5)
    rw_k_k = 0.85 + nrm((N_ODD, D), 0.05)
    rw_k_a = 1.0 + nrm((N_ODD, D), 0.05)
    rw_r_k = nrm((N_ODD, RWKV_HEADS, HEAD_DIM), 0.1)
    rw_gn_g = 1.0 + nrm((N_ODD, D), 0.05)
    rw_gn_b = nrm((N_ODD, D), 0.02)
    return {"x": x, "c": c, "ctx": ctx, "c_ctx": c_ctx,
            "ada_w": ada_w, "ada_b": ada_b, "norm_g": norm_g, "mlp_w1": mlp_w1, "mlp_w2": mlp_w2,
            "hy_w_in": hy_w_in, "hy_w_out": hy_w_out, "hy_q_norm": hy_q_norm, "hy_k_norm": hy_k_norm,
            "hy_conv_w": hy_conv_w, "hy_conv_b": hy_conv_b, "hy_gate_w": hy_gate_w, "hy_gate_b": hy_gate_b,
            "hy_lam": hy_lam,
            "rw_mu": rw_mu, "rw_w_rkv": rw_w_rkv, "rw_w_o": rw_w_o, "rw_lora_down": rw_lora_down,
            "rw_lora_up": rw_lora_up, "rw_lora_bias": rw_lora_bias, "rw_gate_down": rw_gate_down,
            "rw_gate_up": rw_gate_up, "rw_k_k": rw_k_k, "rw_k_a": rw_k_a, "rw_r_k": rw_r_k,
            "rw_gn_g": rw_gn_g, "rw_gn_b": rw_gn_b}


def reference(x, c, ctx, c_ctx, ada_w, ada_b, norm_g, mlp_w1, mlp_w2,
              hy_w_in, hy_w_out, hy_q_norm, hy_k_norm, hy_conv_w, hy_conv_b, hy_gate_w, hy_gate_b, hy_lam,
              rw_mu, rw_w_rkv, rw_w_o, rw_lora_down, rw_lora_up, rw_lora_bias, rw_gate_down, rw_gate_up,
              rw_k_k, rw_k_a, rw_r_k, rw_gn_g, rw_gn_b):
    cos, sin = axial_rope(x.shape[1])
    for l in range(DEPTH):
        i = l // 2
        need_ctx = l < DEPTH - 1
        ml = [m[:, None, :] for m in ada_chunks(c, ada_w[l], ada_b[l])]
        mc = ada_chunks(c_ctx, ada_w[l], ada_b[l])
        hl = rmsnorm(x, norm_g[l, 0]) * (1.0 + ml[1]) + ml[0]
        hc = rmsnorm(ctx, norm_g[l, 0]) * (1.0 + mc[1]) + mc[0]
        if l % 2 == 0:
            ol, oc = hybrid_mixer(hl, hc, hy_w_in[i], hy_w_out[i], hy_q_norm[i], hy_k_norm[i],
                                  hy_conv_w[i], hy_conv_b[i], hy_gate_w[i], hy_gate_b[i], hy_lam[i],
                                  cos, sin, need_ctx)
        else:
            ol, oc = rwkv7_mixer(hl, hc, rw_mu[i], rw_w_rkv[i], rw_w_o[i], rw_lora_down[i], rw_lora_up[i],
                                 rw_lora_bias[i], rw_gate_down[i], rw_gate_up[i], rw_k_k[i], rw_k_a[i],
                                 rw_r_k[i], rw_gn_g[i], rw_gn_b[i], need_ctx)
        x = x + ml[2] * ol
        x = x + ml[5] * sqrelu_mlp(rmsnorm(x, norm_g[l, 1]) * (1.0 + ml[4]) + ml[3], mlp_w1[l], mlp_w2[l])
        if need_ctx:
            ctx = ctx + mc[2] * oc
            ctx = ctx + mc[5] * sqrelu_mlp(rmsnorm(ctx, norm_g[l, 1]) * (1.0 + mc[4]) + mc[3], mlp_w1[l], mlp_w2[l])
    return x
```

```python
import math
import numpy as np
import concourse.bass as bass
import concourse.mybir as mybir
from concourse.bass_utils import run_bass_kernel_spmd

F32 = mybir.dt.float32
BF16 = mybir.dt.bfloat16
AF = mybir.ActivationFunctionType
ALU = mybir.AluOpType

D = 1024
SEQ = 4096
CTX = 256
T = SEQ + CTX
NB = 2
DEPTH = 4
EPS = 1e-6
DFF = 4096
TILES = [(0, CTX)] + [(CTX + 512 * i, 512) for i in range(8)]
WTILES = [(0, CTX)] + [(CTX + 256 * i, 256) for i in range(16)]
GELU_C = 2.0 * math.sqrt(2.0 / math.pi)
DECAY_SCALE = math.exp(-0.5)
GN_EPS = 64e-5


class Buf:
    __slots__ = ("w", "r")

    def __init__(self):
        self.w = None
        self.r = {}


class KB:
    NDMA = 40

    def __init__(self, nc):
        self.nc = nc
        self.eng = dict(pe=nc.tensor, act=nc.scalar, dve=nc.vector, pool=nc.gpsimd, sp=nc.sync)
        self.sems = {}
        self.cnt = {}
        for e in ("pe", "act", "dve", "pool"):
            self.sems[e] = nc.alloc_semaphore("s_" + e)
            self.cnt[e] = 0
        self.dsem = [nc.alloc_semaphore("d%d" % i) for i in range(self.NDMA)]
        self.dval = [0] * self.NDMA
        self.drr = 0
        self.waited = {e: {} for e in self.eng}
        self.sb_off = 16512
        self.sb_base = 16512
        self.nalloc = 0
        self.ninst = 0
        self.pending = []
        self.rec = None

    def sb(self, shape, dt=F32, name=None):
        self.nalloc += 1
        nm = "%s_%d" % (name or "t", self.nalloc)
        n = 1
        for s_ in shape[1:]:
            n *= s_
        nbytes = n * (4 if dt == F32 else 2)
        nbytes = (nbytes + 63) // 64 * 64
        off = self.sb_off
        self.sb_off += nbytes
        assert self.sb_off <= 229376, ("SBUF overflow", nm, self.sb_off)
        return self.nc.alloc_sbuf_tensor_at(nm, list(shape), dt, offset=off).ap()

    def phase_reset(self):
        self.barrier()
        self.sb_off = self.sb_base

    def persist_mark(self):
        self.sb_base = self.sb_off

    def _semh(self, key):
        return self.sems[key] if isinstance(key, str) else self.dsem[key[1]]

    def _wait(self, e, key, val, raw=False):
        if key == e and (not raw or e == "pe"):
            return
        w = self.waited[e]
        if w.get(key, 0) >= val:
            return
        w[key] = val
        self.pending.append((key, val))

    def _take(self):
        p = self.pending
        self.pending = []
        d = {}
        for k_, v_ in p:
            if d.get(k_, 0) < v_:
                d[k_] = v_
        return list(d.items())

    def _deps(self, e, reads, writes):
        for b in reads:
            if b.w is not None:
                self._wait(e, b.w[0], b.w[1], raw=True)
        for b in writes:
            if b.w is not None:
                self._wait(e, b.w[0], b.w[1])
            for k_, v_ in b.r.items():
                self._wait(e, k_, v_)

    def _mark(self, tok, reads, writes):
        k_, v_ = tok
        for b in reads:
            if b.r.get(k_, 0) < v_:
                b.r[k_] = v_
        for b in writes:
            b.w = tok
            b.r = {}

    def op(self, e, ins_fn, reads=(), writes=(), sig=True):
        self._deps(e, reads, writes)
        items = self._take()
        if self.rec is not None:
            self.rec.append((e, list(items), e if sig else None, 1))
        last = items.pop() if items else None
        for k_, v_ in items:
            self.eng[e].wait_ge(self._semh(k_), v_)
            self.ninst += 1
        ins = ins_fn(self.eng[e])
        if last is not None:
            ins._wait_ge(self._semh(last[0]), last[1])
        self.ninst += 1
        if sig:
            self.cnt[e] += 1
            ins.then_inc(self.sems[e], 1)
            tok = (e, self.cnt[e])
        else:
            tok = (e, self.cnt[e] + 1)
        self._mark(tok, reads, writes)
        return tok

    def dma(self, q, out, in_, reads=(), writes=(), **kw):
        i = self.drr
        self.drr = (self.drr + 1) % self.NDMA
        key = ("d", i)
        if self.dval[i] > 0:
            self._wait(q, key, self.dval[i])
        self._deps(q, reads, writes)
        its_ = self._take()
        if self.rec is not None:
            self.rec.append((q, list(its_), key, 16))
        for k_, v_ in its_:
            self.eng[q].wait_ge(self._semh(k_), v_)
            self.ninst += 1
        self.eng[q].dma_start(out=out, in_=in_, **kw).then_inc(self.dsem[i], 16)
        self.ninst += 1
        self.dval[i] += 16
        tok = (key, self.dval[i])
        self._mark(tok, reads, writes)
        return tok

    def barrier(self):
        for e in self.eng:
            for o in ("pe", "act", "dve", "pool"):
                if self.cnt[o] > 0:
                    self._wait(e, o, self.cnt[o])
            for i in range(self.NDMA):
                if self.dval[i] > 0:
                    self._wait(e, ("d", i), self.dval[i])
            its_ = self._take()
            if self.rec is not None:
                self.rec.append((e, list(its_), None, 0))
            for k_, v_ in its_:
                self.eng[e].wait_ge(self._semh(k_), v_)
                self.ninst += 1

    def finish(self):
        self.barrier()


def vb_layout():
    m = {}
    off = 0

    def add(name, n):
        nonlocal off
        m[name] = off
        off += n
    for l in range(DEPTH):
        add("ng0_%d" % l, 8)
        add("ng1_%d" % l, 8)
        add("adab_%d" % l, 48)
    for i in range(2):
        add("gq_%d" % i, 1)
        add("gk_%d" % i, 1)
        add("convw_%d" % i, 16)
        add("convb_%d" % i, 4)
        add("gateb_%d" % i, 16)
        add("lam_%d" % i, 8)
    for i in range(2):
        add("mu_%d" % i, 48)
        add("kk_%d" % i, 8)
        add("ka_%d" % i, 8)
        add("rk_%d" % i, 8)
        add("gng_%d" % i, 8)
        add("gnb_%d" % i, 8)
        add("lb_%d" % i, 32)
    return m, off


VBM, NVB = vb_layout()
PERM = np.concatenate([np.arange(0, 64, 2), np.arange(1, 64, 2)])


class Prog:
    def __init__(self, debug=(), nlayers=DEPTH, ext_in=()):
        self.debug = set(debug)
        self.ext_in = set(ext_in)
        self.nlayers = nlayers
        self.layers = list(range(nlayers))
        self.rw_stop = 99
        self.wkv_stage = 99
        self.wkv_ntiles = 99
        nc = self.nc = bass.Bass("TRN2", target_bir_lowering=False)
        k = self.k = KB(nc)
        self.dbuf = {}
        di = self.din
        self.xin = di("xT_in", [NB, D, SEQ])
        self.cin = di("ctxT_in", [NB, D, CTX])
        self.cT = di("cT", [128, 8, 3])
        self.vbd = di("vb", [128, NVB])
        self.ada_w = di("ada_w", [DEPTH, D, 6 * D])
        self.w1 = di("mlp_w1", [DEPTH, D, DFF])
        self.w2 = di("mlp_w2", [DEPTH, DFF, D])
        self.hy_win = di("hy_win", [2, D, 1920])
        self.hy_wout = di("hy_wout", [2, D, D])
        self.hy_gw = di("hy_gw", [2, 2, 2, 4, 128, 128])
        self.cst = di("consts", [128, 512])
        self.cosd = di("cosT", [128, SEQ])
        self.sind = di("sinT", [128, SEQ])
        self.rw_wrkv = di("rw_wrkv", [2, 3, D, D])
        self.rw_wvst = di("rw_wvst", [2, D, D])
        self.rw_wo = di("rw_wo", [2, D, D])
        self.rw_ld = di("rw_ld", [2, D, 256])
        self.rw_lu = di("rw_lu", [2, 4, 64, D])
        self.rw_gd = di("rw_gd", [2, D, 128])
        self.rw_gu = di("rw_gu", [2, 128, D])
        self.wmask = di("wmask", [128, 2, 512])
        self.rmask = di("rmask", [128, 2, 2048])
        self.out = nc.dram_tensor("outT", [NB, D, SEQ], F32, kind="ExternalOutput").ap()
        self.hnT = self.dscr("hnT", [D, T])
        self.rT = self.dscr("rT", [D, T])
        self.kkT = self.dscr("kkT", [D, T])
        self.kdT = self.dscr("kdT", [2, D, T])
        self.bT = self.dscr("bT", [2, D, T])
        self.lwT = self.dscr("lwT", [2, D, T])
        self.gT = self.dscr("gT", [D, T])
        self.bonT = self.dscr("bonT", [D, T])
        self.Vst = self.dscr("Vst", [T // 64, 128, 512])
        self.yT = self.dscr("yT", [2, D, T])
        self.xT = self.dscr("xT", [NB, D, T])
        self.qT = self.dscr("qT", [512, T], BF16)
        self.kT2 = self.dscr("kT2", [256, T], BF16)
        self.vtok = self.dscr("vtok", [T // 128, 128, 384], BF16)
        self.xr = self.dscr("xr", [512, T])
        self.gg = self.dscr("gg", [512, T], BF16)
        self.rec = self.dscr("rec", [512, T], BF16)
        self.ps = [nc.alloc_psum_tensor("ps%d" % i, [128, 512], F32).ap() for i in range(8)]
        self.PS = [Buf() for _ in range(8)]
        self.prr = 0
        self.vb = k.sb([128, NVB], F32, "vb")
        self.VB = Buf()
        self.cs = k.sb([128, 512], F32, "cs")
        self.CS = Buf()
        self.ident = self.cs[:, 0:128]
        self.ones_bf = k.sb([128, 128], BF16, "ones")
        self.bones_bf = k.sb([128, 128], BF16, "bones")
        self.pswap_bf = k.sb([128, 128], BF16, "pswap")
        self.scT = k.sb([128, 8, 3], BF16, "scT")
        self.mod = k.sb([128, 48, 3], F32, "mod")
        self.A1 = k.sb([128, 8, 3], F32, "A1")
        self.A2 = k.sb([128, 8, 3], F32, "A2")
        self.MOD = Buf()
        self.CONST = Buf()
        k.persist_mark()

    def din(self, name, shape, dt=F32):
        return self.nc.dram_tensor(name, list(shape), dt, kind="ExternalInput").ap()

    def dscr(self, name, shape, dt=F32):
        kind = "ExternalOutput" if name in self.debug else "Internal"
        if name in getattr(self, "ext_in", ()):
            kind = "ExternalInput"
        return self.nc.dram_tensor(name, list(shape), dt, kind=kind).ap()

    def DB(self, *key):
        b = self.dbuf.get(key)
        if b is None:
            b = self.dbuf[key] = Buf()
        return b

    def pb(self):
        i = self.prr
        self.prr = (self.prr + 1) % 8
        return self.ps[i], self.PS[i]

    @staticmethod
    def fm(ap2d, t0, n):
        return ap2d.rearrange("(fc p) t -> p fc t", p=128)[:, :, t0:t0 + n]

    def V(self, name, j=0, n=1):
        o = VBM[name] + j
        return self.vb[:, o:o + n]

    def setup(self):
        k = self.k
        k.dma("sp", self.vb, self.vbd, writes=[self.VB])
        k.dma("sp", self.cs, self.cst, writes=[self.CS])
        k.op("dve", lambda e: e.memset(self.ones_bf, 1.0), writes=[self.CONST])
        k.op("act", lambda e: e.activation(out=self.bones_bf, in_=self.cs[:, 128:256], func=AF.Copy),
             reads=[self.CS], writes=[self.CONST])
        k.op("act", lambda e: e.activation(out=self.pswap_bf, in_=self.cs[:, 256:384], func=AF.Copy),
             reads=[self.CS], writes=[self.CONST])
        ct = k.sb([128, 8, 3], F32)
        sg = k.sb([128, 8, 3], F32)
        CTB = Buf()
        k.dma("sp", ct, self.cT, writes=[CTB])
        k.op("act", lambda e: e.activation(out=sg, in_=ct, func=AF.Sigmoid), reads=[CTB], writes=[CTB])
        k.op("dve", lambda e: e.tensor_tensor(out=self.scT, in0=ct, in1=sg, op=ALU.mult), reads=[CTB],
             writes=[self.CONST])
        for b in range(NB):
            k.dma("sp", self.xT[b, :, 0:CTX], self.cin[b], writes=[self.DB("xT", b, 0)])
            for ti in range(1, 9):
                t0, n = TILES[ti]
                k.dma("sp", self.xT[b, :, t0:t0 + n], self.xin[b, :, t0 - CTX:t0 - CTX + n],
                      writes=[self.DB("xT", b, ti)])
        k.phase_reset()

    def phase_mod(self, l):
        k = self.k
        wa = k.sb([128, 8, 3072], BF16)
        WA = Buf()
        pm, PM = self.pb()
        for half in range(2):
            src = self.ada_w[l].rearrange("(kc p) n -> p kc n", p=128)[:, :, half * 3072:(half + 1) * 3072]
            for kc in range(8):
                k.dma("pool", wa[:, kc, :], src[:, kc, :], writes=[WA])
            for j in range(24):
                jj = half * 24 + j
                for kc in range(8):
                    k.op("pe", lambda e: e.matmul(pm[:, jj * 4:jj * 4 + 3], lhsT=wa[:, kc, j * 128:(j + 1) * 128],
                                                  rhs=self.scT[:, kc, :], start=(kc == 0), stop=(kc == 7)),
                         reads=[WA, self.CONST], writes=[PM], sig=(kc == 7))
        pmv = pm[:, 0:192].rearrange("p (j f) -> p j f", f=4)[:, :, 0:3]
        bias = self.V("adab_%d" % l, 0, 48).unsqueeze(2).broadcast_to([128, 48, 3])
        k.op("dve", lambda e: e.tensor_tensor(out=self.mod, in0=pmv, in1=bias, op=ALU.add),
             reads=[PM, self.VB], writes=[self.MOD])
        for (A, gname, sc0) in ((self.A1, "ng0_%d" % l, 8), (self.A2, "ng1_%d" % l, 32)):
            g = self.V(gname, 0, 8).unsqueeze(2).broadcast_to([128, 8, 3])
            k.op("dve", lambda e: e.scalar_tensor_tensor(out=A, in0=self.mod[:, sc0:sc0 + 8, :], scalar=1.0, in1=g,
                                                         op0=ALU.add, op1=ALU.mult),
                 reads=[self.MOD, self.VB], writes=[self.MOD])
        k.phase_reset()

    def norm_tile(self, xt, XTB, out, OUTB, A, Bsh, mi, n, W):
        k = self.k
        sq, rstd, tmp = W["sq"], W["rstd"], W["tmp"]
        k.op("act", lambda e: e.activation(out=sq[:, :, 0:n], in_=xt[:, :, 0:n], func=AF.Square),
             reads=[XTB], writes=[W["SQ"]])
        pn, PN = self.pb()
        for fc in range(8):
            k.op("pe", lambda e: e.matmul(pn[:, 0:n], lhsT=self.ones_bf, rhs=sq[:, fc, 0:n], start=(fc == 0),
                                          stop=(fc == 7)), reads=[W["SQ"], self.CONST], writes=[PN], sig=(fc == 7))
        k.op("act", lambda e: e.activation(out=rstd[:, 0:n], in_=pn[:, 0:n], func=AF.Sqrt, bias=EPS, scale=1.0 / D),
             reads=[PN], writes=[W["RSTD"]])
        k.op("dve", lambda e: e.reciprocal(out=rstd[:, 0:n], in_=rstd[:, 0:n]), reads=[W["RSTD"]], writes=[W["RSTD"]])
        for fc in range(8):
            j = fc % 2
            k.op("dve", lambda e: e.tensor_tensor(out=tmp[:, j, 0:n], in0=xt[:, fc, 0:n], in1=rstd[:, 0:n],
                                                  op=ALU.mult), reads=[XTB, W["RSTD"]], writes=[W["TMP"][j]])
            k.op("act", lambda e: e.activation(out=out[:, fc, 0:n], in_=tmp[:, j, 0:n], func=AF.Identity,
                                               bias=Bsh[:, fc, mi:mi + 1], scale=A[:, fc, mi:mi + 1]),
                 reads=[W["TMP"][j], self.MOD], writes=[OUTB])

    def norm_work(self):
        k = self.k
        return dict(sq=k.sb([128, 8, 512], BF16), rstd=k.sb([128, 512], F32), tmp=k.sb([128, 2, 512], F32),
                    SQ=Buf(), RSTD=Buf(), TMP=[Buf(), Buf()])

    def load_w(self, dst, DSTB, src2d, nkc, split=1):
        v = src2d.rearrange("(kc p) n -> p kc n", p=128)
        for kc in range(nkc):
            self.k.dma("pool", dst[:, kc, :], v[:, kc, :], writes=[DSTB])

    def phase_mlp(self, l, b, last):
        k = self.k
        w1 = k.sb([128, 8, DFF], BF16)
        w2 = k.sb([128, 32, D], BF16)
        W1B, W2B = Buf(), Buf()
        self.load_w(w1, W1B, self.w1[l], 8)
        self.load_w(w2, W2B, self.w2[l], 32)
        W = self.norm_work()
        xt = k.sb([128, 8, 512], F32)
        XTB = Buf()
        hn = k.sb([128, 8, 512], BF16)
        HNB = Buf()
        h1 = k.sb([128, 32, 512], BF16)
        H1B = [Buf() for _ in range(32)]
        rl = k.sb([128, 2, 512], BF16)
        RLB = [Buf(), Buf()]
        for ti, (t0, n) in enumerate(TILES):
            if last and ti == 0:
                continue
            mi = 2 if ti == 0 else b
            k.dma("sp", xt[:, :, 0:n], self.fm(self.xT[b], t0, n), reads=[self.DB("xT", b, ti)], writes=[XTB])
            self.norm_tile(xt, XTB, hn, HNB, self.A2, self.mod[:, 24:32, :], mi, n, W)
            for oc in range(32):
                p, P = self.pb()
                for kc in range(8):
                    k.op("pe", lambda e: e.matmul(p[:, 0:n], lhsT=w1[:, kc, oc * 128:(oc + 1) * 128], rhs=hn[:, kc, 0:n],
                                                  start=(kc == 0), stop=(kc == 7)),
                         reads=[W1B, HNB], writes=[P], sig=(kc == 7))
                j = oc % 2
                k.op("act", lambda e: e.activation(out=rl[:, j, 0:n], in_=p[:, 0:n], func=AF.Relu),
                     reads=[P], writes=[RLB[j]])
                k.op("pool", lambda e: e.tensor_tensor(out=h1[:, oc, 0:n], in0=rl[:, j, 0:n], in1=rl[:, j, 0:n],
                                                       op=ALU.mult), reads=[RLB[j]], writes=[H1B[oc]])
            for oc in range(8):
                p, P = self.pb()
                for kc in range(32):
                    k.op("pe", lambda e: e.matmul(p[:, 0:n], lhsT=w2[:, kc, oc * 128:(oc + 1) * 128], rhs=h1[:, kc, 0:n],
                                                  start=(kc == 0), stop=(kc == 31)),
                         reads=[W2B, H1B[kc]], writes=[P], sig=(kc == 31))
                k.op("dve", lambda e: e.scalar_tensor_tensor(out=xt[:, oc, 0:n], in0=p[:, 0:n],
                                                             scalar=self.mod[:, 40 + oc, mi:mi + 1],
                                                             in1=xt[:, oc, 0:n], op0=ALU.mult, op1=ALU.add),
                     reads=[P, self.MOD, XTB], writes=[XTB])
            if last:
                k.dma("sp", self.out[b].rearrange("(fc p) t -> p fc t", p=128)[:, :, t0 - CTX:t0 - CTX + n],
                      xt[:, :, 0:n], reads=[XTB], writes=[self.DB("out", b, ti)])
            else:
                k.dma("sp", self.fm(self.xT[b], t0, n), xt[:, :, 0:n], reads=[XTB], writes=[self.DB("xT", b, ti)])
        k.phase_reset()

    def phase_hy_inproj(self, l, b):
        k = self.k
        i = l // 2
        win = k.sb([128, 8, 1920], BF16)
        WB = Buf()
        self.load_w(win, WB, self.hy_win[i], 8)
        W = self.norm_work()
        xt = [k.sb([128, 8, 512], F32) for _ in range(2)]
        XTB = [Buf(), Buf()]
        hn = k.sb([128, 8, 512], BF16)
        HNB = Buf()
        cs_t = k.sb([128, 2, 512], F32)
        CSB = Buf()
        sq = k.sb([128, 512], BF16)
        SQB = Buf()
        rs = k.sb([128, 512], F32)
        RSB = Buf()
        xn = k.sb([128, 512], BF16)
        XNB = Buf()
        t1 = k.sb([128, 512], F32)
        t2 = k.sb([128, 512], F32)
        T1B, T2B = Buf(), Buf()
        qk = k.sb([128, 6, 512], BF16)
        QKB = Buf()
        vt = k.sb([128, 4, 384], BF16)
        VTB = Buf()
        xro = k.sb([128, 4, 512], F32)
        XRB = Buf()
        ggo = k.sb([128, 4, 512], BF16)
        GGB = Buf()
        z2 = k.sb([128, 512], F32)
        Z2B = Buf()
        k.op("dve", lambda e: e.memset(vt, 1.0), writes=[VTB])
        for ti, (t0, n) in enumerate(TILES):
            mi = 2 if ti == 0 else b
            x_ = xt[ti % 2]
            XB = XTB[ti % 2]
            k.dma("sp", x_[:, :, 0:n], self.fm(self.xT[b], t0, n), reads=[self.DB("xT", b, ti)], writes=[XB])
            if ti > 0:
                k.dma("sp", cs_t[:, 0, 0:n], self.cosd[:, t0 - CTX:t0 - CTX + n], writes=[CSB])
                k.dma("sp", cs_t[:, 1, 0:n], self.sind[:, t0 - CTX:t0 - CTX + n], writes=[CSB])
            self.norm_tile(x_, XB, hn, HNB, self.A1, self.mod[:, 0:8, :], mi, n, W)
            for oc in range(6):
                p, P = self.pb()
                for kc in range(8):
                    k.op("pe", lambda e: e.matmul(p[:, 0:n], lhsT=win[:, kc, oc * 128:(oc + 1) * 128], rhs=hn[:, kc, 0:n],
                                                  start=(kc == 0), stop=(kc == 7)), reads=[WB, HNB], writes=[P],
                         sig=(kc == 7))
                k.op("act", lambda e: e.activation(out=sq[:, 0:n], in_=p[:, 0:n], func=AF.Square), reads=[P], writes=[SQB])
                p2, P2 = self.pb()
                k.op("pe", lambda e: e.matmul(p2[:, 0:n], lhsT=self.bones_bf, rhs=sq[:, 0:n], start=True, stop=True),
                     reads=[SQB, self.CONST], writes=[P2])
                k.op("act", lambda e: e.activation(out=rs[:, 0:n], in_=p2[:, 0:n], func=AF.Sqrt, bias=EPS,
                                                   scale=1.0 / 64), reads=[P2], writes=[RSB])
                k.op("dve", lambda e: e.reciprocal(out=rs[:, 0:n], in_=rs[:, 0:n]), reads=[RSB], writes=[RSB])
                g = self.V("gq_%d" % i) if oc < 4 else self.V("gk_%d" % i)
                dst = qk[:, oc, 0:n] if ti == 0 else xn[:, 0:n]
                DSTB = QKB if ti == 0 else XNB
                k.op("dve", lambda e: e.scalar_tensor_tensor(out=dst, in0=p[:, 0:n], scalar=g, in1=rs[:, 0:n],
                                                             op0=ALU.mult, op1=ALU.mult),
                     reads=[P, RSB, self.VB], writes=[DSTB])
                if ti > 0:
                    p3, P3 = self.pb()
                    k.op("pe", lambda e: e.matmul(p3[:, 0:n], lhsT=self.pswap_bf, rhs=xn[:, 0:n], start=True, stop=True),
                         reads=[XNB, self.CONST], writes=[P3])
                    k.op("pool", lambda e: e.tensor_tensor(out=t1[:, 0:n], in0=xn[:, 0:n], in1=cs_t[:, 0, 0:n],
                                                           op=ALU.mult), reads=[XNB, CSB], writes=[T1B])
                    k.op("dve", lambda e: e.tensor_tensor(out=t2[:, 0:n], in0=p3[:, 0:n], in1=cs_t[:, 1, 0:n],
                                                          op=ALU.mult), reads=[P3, CSB], writes=[T2B])
                    k.op("dve", lambda e: e.tensor_tensor(out=qk[:, oc, 0:n], in0=t1[:, 0:n], in1=t2[:, 0:n],
                                                          op=ALU.add), reads=[T1B, T2B], writes=[QKB])
            k.dma("sp", self.fm(self.qT, t0, n), qk[:, 0:4, 0:n], reads=[QKB], writes=[self.DB("qT", ti)])
            k.dma("sp", self.fm(self.kT2, t0, n), qk[:, 4:6, 0:n], reads=[QKB], writes=[self.DB("kT2", ti)])
            nst = n // 128
            for st in range(nst):
                p, P = self.pb()
                for kc in range(8):
                    k.op("pe", lambda e: e.matmul(p[:, 0:128], lhsT=hn[:, kc, st * 128:(st + 1) * 128],
                                                  rhs=win[:, kc, 768:896], start=(kc == 0), stop=(kc == 7)),
                         reads=[WB, HNB], writes=[P], sig=(kc == 7))
                vv = vt[:, st, :].rearrange("p (h c) -> p h c", c=192)[:, :, 64:128]
                k.op("act", lambda e: e.activation(out=vv, in_=p[:, 0:128].rearrange("p (h c) -> p h c", c=64),
                                                   func=AF.Copy), reads=[P], writes=[VTB])
            c0 = t0 // 128
            k.dma("sp", self.vtok[c0:c0 + nst].rearrange("c p f -> p c f"), vt[:, 0:nst, :], reads=[VTB],
                  writes=[self.DB("vtok", ti)])
            for oc in range(4):
                p, P = self.pb()
                for kc in range(8):
                    k.op("pe", lambda e: e.matmul(p[:, 0:n], lhsT=win[:, kc, 896 + oc * 128:896 + (oc + 1) * 128],
                                                  rhs=hn[:, kc, 0:n], start=(kc == 0), stop=(kc == 7)),
                         reads=[WB, HNB], writes=[P], sig=(kc == 7))
                k.op("act", lambda e: e.activation(out=xro[:, oc, 0:n], in_=p[:, 0:n], func=AF.Copy), reads=[P],
                     writes=[XRB])
            k.dma("sp", self.fm(self.xr, t0, n), xro[:, :, 0:n], reads=[XRB], writes=[self.DB("xr", ti)])
            for oc in range(4):
                p, P = self.pb()
                for kc in range(8):
                    k.op("pe", lambda e: e.matmul(p[:, 0:n], lhsT=win[:, kc, 1408 + oc * 128:1408 + (oc + 1) * 128],
                                                  rhs=hn[:, kc, 0:n], start=(kc == 0), stop=(kc == 7)),
                         reads=[WB, HNB], writes=[P], sig=(kc == 7))
                self.gelu_from_psum(p, P, ggo[:, oc, 0:n], GGB, n, z2, Z2B, t1, T1B)
            k.dma("sp", self.fm(self.gg, t0, n), ggo[:, :, 0:n], reads=[GGB], writes=[self.DB("gg", ti)])
        k.phase_reset()

    def gelu_from_psum(self, p, P, dst, DSTB, n, z2, Z2B, t1, T1B):
        k = self.k
        k.op("act", lambda e: e.activation(out=z2[:, 0:n], in_=p[:, 0:n], func=AF.Square), reads=[P], writes=[Z2B])
        k.op("dve", lambda e: e.tensor_scalar(out=z2[:, 0:n], in0=z2[:, 0:n], scalar1=0.044715, scalar2=1.0,
                                              op0=ALU.mult, op1=ALU.add), reads=[Z2B], writes=[Z2B])
        k.op("dve", lambda e: e.tensor_tensor(out=z2[:, 0:n], in0=z2[:, 0:n], in1=p[:, 0:n], op=ALU.mult),
             reads=[Z2B, P], writes=[Z2B])
        k.op("act", lambda e: e.activation(out=t1[:, 0:n], in_=z2[:, 0:n], func=AF.Sigmoid, scale=GELU_C),
             reads=[Z2B], writes=[T1B])
        k.op("dve", lambda e: e.tensor_tensor(out=dst, in0=t1[:, 0:n], in1=p[:, 0:n], op=ALU.mult),
             reads=[T1B, P], writes=[DSTB])

    def phase_rglru(self, l, b):
        k = self.k
        i = l // 2
        gw = k.sb([128, 16, 128], BF16)
        GWB = Buf()
        k.dma("pool", gw, self.hy_gw[i].rearrange("d g c p m -> p (d g c) m"), writes=[GWB])
        c1 = k.sb([128, 8], F32)
        c2 = k.sb([128, 8], F32)
        C1B = Buf()
        k.op("act", lambda e: e.activation(out=c1, in_=self.V("lam_%d" % i, 0, 8), func=AF.Exp, scale=-1.0),
             reads=[self.VB], writes=[C1B])
        k.op("act", lambda e: e.activation(out=c1, in_=c1, func=AF.Ln, bias=1.0), reads=[C1B], writes=[C1B])
        k.op("dve", lambda e: e.tensor_scalar(out=c2, in0=c1, scalar1=-16.0, scalar2=None, op0=ALU.mult),
             reads=[C1B], writes=[C1B])
        k.op("dve", lambda e: e.tensor_scalar(out=c1, in0=c1, scalar1=-8.0, scalar2=None, op0=ALU.mult),
             reads=[C1B], writes=[C1B])
        x = k.sb([128, T], F32)
        xc = k.sb([128, T], F32)
        xcb = k.sb([128, T], BF16)
        r = k.sb([128, T], F32)
        ig = k.sb([128, T], F32)
        a = k.sb([128, T], F32)
        u = k.sb([128, T], F32)
        h = [k.sb([128, T], F32) for _ in range(2)]
        ggt = k.sb([128, T], BF16)
        rec = k.sb([128, T], BF16)
        XB, XCB, XCBB, RB, IB, AB, UB, GB, RECB = [Buf() for _ in range(9)]
        HB = [Buf(), Buf()]
        segs = [(0, CTX), (CTX, T)]
        for c in range(4):
            k.dma("sp", x, self.xr[c * 128:(c + 1) * 128, :], reads=[self.DB("xr", ti) for ti in range(9)], writes=[XB])
            k.dma("sp", ggt, self.gg[c * 128:(c + 1) * 128, :], reads=[self.DB("gg", ti) for ti in range(9)],
                  writes=[GB])
            k.op("act", lambda e: e.activation(out=xc, in_=x, func=AF.Identity, bias=self.V("convb_%d" % i, c),
                                               scale=self.V("convw_%d" % i, 2 * 4 + c)),
                 reads=[XB, self.VB], writes=[XCB])
            for (s0, s1) in segs:
                for j in (0, 1, 3):
                    sh = j - 2
                    a0 = max(s0, s0 - sh)
                    a1 = min(s1, s1 - sh)
                    k.op("dve", lambda e: e.scalar_tensor_tensor(out=xc[:, a0:a1], in0=x[:, a0 + sh:a1 + sh],
                                                                 scalar=self.V("convw_%d" % i, j * 4 + c),
                                                                 in1=xc[:, a0:a1], op0=ALU.mult, op1=ALU.add),
                         reads=[XB, XCB, self.VB], writes=[XCB])
            k.op("act", lambda e: e.activation(out=xcb, in_=xc, func=AF.Copy), reads=[XCB], writes=[XCBB])
            for d in range(2):
                for (t0, n) in TILES:
                    for g, (dst, DB_) in enumerate(((r, RB), (ig, IB))):
                        p, P = self.pb()
                        k.op("pe", lambda e: e.matmul(p[:, 0:n], lhsT=gw[:, (d * 2 + g) * 4 + c, :], rhs=xcb[:, t0:t0 + n],
                                                      start=True, stop=True), reads=[GWB, XCBB], writes=[P])
                        k.op("act", lambda e: e.activation(out=dst[:, t0:t0 + n], in_=p[:, 0:n], func=AF.Sigmoid,
                                                           bias=self.V("gateb_%d" % i, (d * 2 + g) * 4 + c)),
                             reads=[P, self.VB], writes=[DB_])
                k.op("act", lambda e: e.activation(out=a, in_=r, func=AF.Exp, scale=c1[:, d * 4 + c:d * 4 + c + 1]),
                     reads=[RB, C1B], writes=[AB])
                k.op("act", lambda e: e.activation(out=u, in_=r, func=AF.Exp, scale=c2[:, d * 4 + c:d * 4 + c + 1]),
                     reads=[RB, C1B], writes=[UB])
                k.op("act", lambda e: e.activation(out=u, in_=u, func=AF.Sqrt, bias=1.0, scale=-1.0),
                     reads=[UB], writes=[UB])
                k.op("dve", lambda e: e.tensor_tensor(out=ig, in0=ig, in1=xc, op=ALU.mult), reads=[IB, XCB], writes=[IB])
                k.op("dve", lambda e: e.tensor_tensor(out=u, in0=u, in1=ig, op=ALU.mult), reads=[UB, IB], writes=[UB])
                hd = h[d]
                if d == 0:
                    k.op("dve", lambda e: e.tensor_tensor_scan(out=hd, data0=a, data1=u, initial=0.0, op0=ALU.mult,
                                                               op1=ALU.add), reads=[AB, UB], writes=[HB[d]])
                else:
                    k.op("dve", lambda e: e.tensor_tensor_scan(out=hd[:, 0:CTX][:, ::-1], data0=a[:, 0:CTX][:, ::-1],
                                                               data1=u[:, 0:CTX][:, ::-1], initial=0.0, op0=ALU.mult,
                                                               op1=ALU.add), reads=[AB, UB], writes=[HB[d]])
                    k.op("dve", lambda e: e.tensor_tensor_scan(out=hd[:, CTX:T][:, ::-1], data0=a[:, CTX:T][:, ::-1],
                                                               data1=u[:, CTX:T][:, ::-1], initial=hd[:, 0:1],
                                                               op0=ALU.mult, op1=ALU.add), reads=[AB, UB, HB[d]],
                         writes=[HB[d]])
            if "rgd" in self.debug and c == 0 and b == 0:
                rgd = self.nc.dram_tensor("rgd", [8, 128, T], F32, kind="ExternalOutput").ap()
                for j_, (t_, B_) in enumerate(((x, XB), (xc, XCB), (r, RB), (ig, IB), (a, AB), (u, UB), (h[0], HB[0]),
                                               (h[1], HB[1]))):
                    k.dma("sp", rgd[j_], t_, reads=[B_], writes=[Buf()])
                c1d = self.nc.dram_tensor("c1d", [2, 128, 8], F32, kind="ExternalOutput").ap()
                k.dma("sp", c1d[0], c1, reads=[C1B], writes=[Buf()])
                k.dma("sp", c1d[1], c2, reads=[C1B], writes=[Buf()])
            k.op("dve", lambda e: e.tensor_tensor(out=h[0], in0=h[0], in1=h[1], op=ALU.add), reads=[HB[0], HB[1]],
                 writes=[HB[0]])
            k.op("dve", lambda e: e.tensor_tensor(out=rec, in0=h[0], in1=ggt, op=ALU.mult), reads=[HB[0], GB],
                 writes=[RECB])
            k.dma("sp", self.rec[c * 128:(c + 1) * 128, :], rec, reads=[RECB], writes=[self.DB("rec", c)])
        k.phase_reset()

    def phase_attn(self, l, b):
        k = self.k
        i = l // 2
        kt = k.sb([128, 2, T], BF16)
        KTB = Buf()
        k.dma("sp", kt, self.kT2.rearrange("(c p) t -> p c t", p=128), reads=[self.DB("kT2", ti) for ti in range(9)],
              writes=[KTB])
        vt = k.sb([128, T // 128, 384], BF16)
        VTB = Buf()
        k.dma("sp", vt, self.vtok.rearrange("c p f -> p c f"), reads=[self.DB("vtok", ti) for ti in range(9)],
              writes=[VTB])
        wo = k.sb([128, 8, D], BF16)
        WOB = Buf()
        self.load_w(wo, WOB, self.hy_wout[i], 8)
        xt = k.sb([128, 8, 512], F32)
        XB = Buf()
        q = k.sb([128, 4, 512], BF16)
        QB = Buf()
        rc = k.sb([128, 4, 512], BF16)
        RCB = Buf()
        att = k.sb([128, 4, 512], BF16)
        ATB = Buf()
        NPT = 6
        pt = [k.sb([128, 512], BF16) for _ in range(NPT)]
        PTB = [Buf() for _ in range(NPT)]
        den = k.sb([128, 512], F32)
        DNB = Buf()
        ptr = 0
        RECALL = [self.DB("rec", c) for c in range(4)]
        for ti, (t0, n) in enumerate(TILES):
            mi = 2 if ti == 0 else b
            nkc = 2 if ti == 0 else T // 128
            k.dma("sp", xt[:, :, 0:n], self.fm(self.xT[b], t0, n), reads=[self.DB("xT", b, ti)], writes=[XB])
            k.dma("sp", q[:, :, 0:n], self.fm(self.qT, t0, n), reads=[self.DB("qT", ti)], writes=[QB])
            k.dma("sp", rc[:, :, 0:n], self.fm(self.rec, t0, n), reads=RECALL, writes=[RCB])
            for hh in range(8):
                kv, hp, fc = hh // 4, hh % 2, hh // 2
                lo, hi = hp * 64, hp * 64 + 64
                olo, ohi = (1 - hp) * 64, (1 - hp) * 64 + 64
                po, PO = self.ps[6 + hh % 2], self.PS[6 + hh % 2]
                voff = kv * 192 + (64 if hp == 0 else 0)
                LA = 3
                slots = []

                def pv(kc, pj):
                    k.op("pe", lambda e: e.matmul(po[:, 0:n], lhsT=vt[:, kc, voff:voff + 128], rhs=pt[pj][:, 0:n],
                                                  start=(kc == 0), stop=(kc == nkc - 1)),
                         reads=[VTB, PTB[pj]], writes=[PO], sig=(kc == nkc - 1))
                for kc in range(nkc):
                    sbi = ptr % 6
                    ps_, PSB = self.ps[sbi], self.PS[sbi]
                    k.op("pe", lambda e: e.matmul(ps_[:, 0:n], lhsT=kt[lo:hi, kv, kc * 128:(kc + 1) * 128],
                                                  rhs=q[lo:hi, fc, 0:n], start=True, stop=True),
                         reads=[KTB, QB], writes=[PSB])
                    pj = ptr % NPT
                    ptr += 1
                    k.op("act", lambda e: e.activation(out=pt[pj][:, 0:n], in_=ps_[:, 0:n], func=AF.Exp, scale=0.125),
                         reads=[PSB], writes=[PTB[pj]])
                    slots.append((kc, pj))
                    if len(slots) > LA:
                        pv(*slots.pop(0))
                while slots:
                    pv(*slots.pop(0))
                k.op("act", lambda e: e.activation(out=den[lo:hi, 0:n], in_=po[olo:ohi, 0:n], func=AF.Copy),
                     reads=[PO], writes=[DNB])
                k.op("dve", lambda e: e.reciprocal(out=den[lo:hi, 0:n], in_=den[lo:hi, 0:n]), reads=[DNB], writes=[DNB])
                k.op("dve", lambda e: e.tensor_tensor(out=att[lo:hi, fc, 0:n], in0=po[lo:hi, 0:n], in1=den[lo:hi, 0:n],
                                                      op=ALU.mult), reads=[PO, DNB], writes=[ATB])
            for oc in range(8):
                p, P = self.pb()
                for kc in range(8):
                    src = att[:, kc, 0:n] if kc < 4 else rc[:, kc - 4, 0:n]
                    k.op("pe", lambda e: e.matmul(p[:, 0:n], lhsT=wo[:, kc, oc * 128:(oc + 1) * 128], rhs=src,
                                                  start=(kc == 0), stop=(kc == 7)),
                         reads=[WOB, ATB, RCB], writes=[P], sig=(kc == 7))
                k.op("dve", lambda e: e.scalar_tensor_tensor(out=xt[:, oc, 0:n], in0=p[:, 0:n],
                                                             scalar=self.mod[:, 16 + oc, mi:mi + 1], in1=xt[:, oc, 0:n],
                                                             op0=ALU.mult, op1=ALU.add),
                     reads=[P, self.MOD, XB], writes=[XB])
            k.dma("sp", self.fm(self.xT[b], t0, n), xt[:, :, 0:n], reads=[XB], writes=[self.DB("xT", b, ti)])
        k.phase_reset()

    def phase_rwkv(self, l, b):
        last = (l == DEPTH - 1)
        self.rw_norm(l, b)
        if self.rw_stop >= 1:
            self.rw_proj(l, b)
        for d in range(2):
            if self.rw_stop >= 2 + d:
                self.rw_wkv(l, b, d)
        if self.rw_stop >= 4:
            self.rw_out(l, b, last)

    def rw_norm(self, l, b):
        k = self.k
        W = self.norm_work()
        xt = [k.sb([128, 8, 512], F32) for _ in range(2)]
        XB = [Buf(), Buf()]
        hn = [k.sb([128, 8, 512], F32) for _ in range(2)]
        HB = [Buf(), Buf()]
        for ti, (t0, n) in enumerate(TILES):
            mi = 2 if ti == 0 else b
            j = ti % 2
            k.dma("sp", xt[j][:, :, 0:n], self.fm(self.xT[b], t0, n), reads=[self.DB("xT", b, ti)], writes=[XB[j]])
            self.norm_tile(xt[j], XB[j], hn[j], HB[j], self.A1, self.mod[:, 0:8, :], mi, n, W)
            k.dma("sp", self.fm(self.hnT, t0, n), hn[j][:, :, 0:n], reads=[HB[j]], writes=[self.DB("hnT", ti)])
        k.phase_reset()

    def rw_proj(self, l, b):
        k = self.k
        i = l // 2
        wr = k.sb([128, 8, D], BF16)
        wk = k.sb([128, 8, D], BF16)
        wvs = k.sb([128, 8, D], BF16)
        ld = k.sb([128, 8, 256], BF16)
        lu = k.sb([64, 4, D], BF16)
        gd = k.sb([128, 8, 128], BF16)
        gu = k.sb([128, D], BF16)
        WB = Buf()
        self.load_w(wr, WB, self.rw_wrkv[i, 0], 8)
        self.load_w(wk, WB, self.rw_wrkv[i, 1], 8)
        self.load_w(wvs, WB, self.rw_wvst[i], 8)
        self.load_w(ld, WB, self.rw_ld[i], 8)
        self.load_w(gd, WB, self.rw_gd[i], 8)
        k.dma("pool", lu, self.rw_lu[i].rearrange("q p n -> p q n"), writes=[WB])
        k.dma("pool", gu, self.rw_gu[i], writes=[WB])
        omka = k.sb([128, 8], F32)
        OMB = Buf()
        k.op("dve", lambda e: e.tensor_scalar(out=omka, in0=self.V("ka_%d" % i, 0, 8), scalar1=-1.0, scalar2=1.0,
                                              op0=ALU.mult, op1=ALU.add), reads=[self.VB], writes=[OMB])
        hh = k.sb([128, 8, 258], F32)
        HHB = Buf()
        xx = k.sb([128, 8, 256], F32)
        XXB = Buf()
        L = [k.sb([128, 8, 256], BF16) for _ in range(6)]
        LB = [Buf() for _ in range(6)]
        rt = k.sb([128, 8, 256], F32)
        kt = k.sb([128, 8, 256], F32)
        kkt = k.sb([128, 8, 256], F32)
        kd = [k.sb([128, 8, 256], F32) for _ in range(2)]
        o1 = k.sb([128, 8, 256], F32)
        RTB, KTB, KKB, O1B = Buf(), Buf(), Buf(), Buf()
        KDB = [Buf(), Buf()]
        vst = k.sb([128, 2, 512], F32)
        VSB = [Buf(), Buf()]
        sm = k.sb([128, 512], BF16)
        SMB = Buf()
        at = k.sb([128, 2, 512], F32)
        ATB = [Buf(), Buf()]
        w1 = k.sb([128, 2, 512], F32)
        W1B = [Buf(), Buf()]
        sqb = k.sb([128, 512], BF16)
        SQB = Buf()
        bones_f = self.cs[:, 128:256]
        for ti, (t0, n) in enumerate(WTILES):
            seg0, seg1 = (0, CTX) if ti == 0 else (CTX, T)
            lo = max(t0 - 1, seg0)
            hi = min(t0 + n + 1, seg1)
            k.dma("sp", hh[:, :, lo - (t0 - 1):hi - (t0 - 1)], self.fm(self.hnT, lo, hi - lo),
                  reads=[self.DB("hnT", j) for j in range(9)], writes=[HHB])
            if lo != t0 - 1:
                k.op("dve", lambda e: e.memset(hh[:, :, 0:1], 0.0), writes=[HHB])
            if hi != t0 + n + 1:
                k.op("dve", lambda e: e.memset(hh[:, :, n + 1:n + 2], 0.0), writes=[HHB])
            h = hh[:, :, 1:n + 1]
            k.op("dve", lambda e: e.tensor_tensor(out=xx[:, :, 0:n], in0=hh[:, :, 0:n], in1=hh[:, :, 2:n + 2], op=ALU.add),
                 reads=[HHB], writes=[XXB])
            k.op("dve", lambda e: e.scalar_tensor_tensor(out=xx[:, :, 0:n], in0=xx[:, :, 0:n], scalar=0.5, in1=h,
                                                         op0=ALU.mult, op1=ALU.subtract), reads=[XXB, HHB], writes=[XXB])
            for j in range(6):
                for fc in range(8):
                    eng = "dve"
                    k.op(eng, lambda e: e.scalar_tensor_tensor(out=L[j][:, fc, 0:n], in0=xx[:, fc, 0:n],
                                                               scalar=self.V("mu_%d" % i, j * 8 + fc), in1=hh[:, fc, 1:n + 1],
                                                               op0=ALU.mult, op1=ALU.add),
                         reads=[XXB, HHB, self.VB], writes=[LB[j]])

            def proj(w, Lj, LjB, dst, DSTB):
                for oc in range(8):
                    p, P = self.pb()
                    for kc in range(8):
                        k.op("pe", lambda e: e.matmul(p[:, 0:n], lhsT=w[:, kc, oc * 128:(oc + 1) * 128], rhs=Lj[:, kc, 0:n],
                                                      start=(kc == 0), stop=(kc == 7)), reads=[WB, LjB], writes=[P],
                             sig=(kc == 7))
                    k.op("act", lambda e: e.activation(out=dst[:, oc, 0:n], in_=p[:, 0:n], func=AF.Copy), reads=[P],
                         writes=[DSTB])
            proj(wr, L[0], LB[0], rt, RTB)
            k.dma("sp", self.fm(self.rT, t0, n), rt[:, :, 0:n], reads=[RTB], writes=[self.DB("rT", ti)])
            proj(wk, L[2], LB[2], kt, KTB)
            for ch in range(n // 64):
                p, P = self.pb()
                for hp in range(2):
                    for kc in range(8):
                        k.op("pe", lambda e: e.matmul(p[hp * 64:(hp + 1) * 64, :], lhsT=L[3][:, kc, ch * 64:(ch + 1) * 64],
                                                      rhs=wvs[:, kc, hp * 512:(hp + 1) * 512], start=(kc == 0),
                                                      stop=(kc == 7)), reads=[WB, LB[3]], writes=[P],
                             sig=(kc == 7 and hp == 1))
                j = ch % 2
                k.op("act", lambda e: e.activation(out=vst[:, j, :], in_=p, func=AF.Copy), reads=[P], writes=[VSB[j]])
                k.dma("sp", self.Vst[t0 // 64 + ch], vst[:, j, :], reads=[VSB[j]], writes=[self.DB("Vst", ti)])
            for oc in range(8):
                k.op("dve", lambda e: e.tensor_scalar(out=kkt[:, oc, 0:n], in0=kt[:, oc, 0:n],
                                                      scalar1=self.V("kk_%d" % i, oc), scalar2=None, op0=ALU.mult),
                     reads=[KTB, self.VB], writes=[KKB])
                k.op("act", lambda e: e.activation(out=sqb[:, 0:n], in_=kkt[:, oc, 0:n], func=AF.Square), reads=[KKB],
                     writes=[SQB])
                p, P = self.pb()
                k.op("pe", lambda e: e.matmul(p[:, 0:n], lhsT=self.bones_bf, rhs=sqb[:, 0:n], start=True, stop=True),
                     reads=[SQB, self.CONST], writes=[P])
                j = oc % 2
                k.op("act", lambda e: e.activation(out=w1[:, j, 0:n], in_=p[:, 0:n], func=AF.Sqrt), reads=[P],
                     writes=[W1B[j]])
                k.op("dve", lambda e: e.tensor_scalar(out=w1[:, j, 0:n], in0=w1[:, j, 0:n], scalar1=1e-12, scalar2=None,
                                                      op0=ALU.max), reads=[W1B[j]], writes=[W1B[j]])
                k.op("dve", lambda e: e.reciprocal(out=w1[:, j, 0:n], in_=w1[:, j, 0:n]), reads=[W1B[j]], writes=[W1B[j]])
                k.op("dve", lambda e: e.tensor_tensor(out=kkt[:, oc, 0:n], in0=kkt[:, oc, 0:n], in1=w1[:, j, 0:n],
                                                      op=ALU.mult), reads=[KKB, W1B[j]], writes=[KKB])
            k.dma("sp", self.fm(self.kkT, t0, n), kkt[:, :, 0:n], reads=[KKB], writes=[self.DB("kkT", ti)])
            p, P = self.pb()
            for kc in range(8):
                k.op("pe", lambda e: e.matmul(p[:, 0:n], lhsT=gd[:, kc, :], rhs=L[5][:, kc, 0:n], start=(kc == 0),
                                              stop=(kc == 7)), reads=[WB, LB[5]], writes=[P], sig=(kc == 7))
            k.op("act", lambda e: e.activation(out=sm[:, 0:n], in_=p[:, 0:n], func=AF.Sigmoid), reads=[P], writes=[SMB])
            for oc in range(8):
                p, P = self.pb()
                k.op("pe", lambda e: e.matmul(p[:, 0:n], lhsT=gu[:, oc * 128:(oc + 1) * 128], rhs=sm[:, 0:n], start=True,
                                              stop=True), reads=[WB, SMB], writes=[P])
                k.op("act", lambda e: e.activation(out=o1[:, oc, 0:n], in_=p[:, 0:n], func=AF.Copy), reads=[P],
                     writes=[O1B])
            k.dma("sp", self.fm(self.gT, t0, n), o1[:, :, 0:n], reads=[O1B], writes=[self.DB("gT", ti)])
            for d in range(2):
                p, P = self.pb()
                for kc in range(8):
                    k.op("pe", lambda e: e.matmul(p[0:64, 0:n], lhsT=ld[:, kc, (d * 2) * 64:(d * 2 + 1) * 64],
                                                  rhs=L[1][:, kc, 0:n], start=(kc == 0), stop=(kc == 7)),
                         reads=[WB, LB[1]], writes=[P], sig=(kc == 7))
                k.op("act", lambda e: e.activation(out=sm[0:64, 0:n], in_=p[0:64, 0:n], func=AF.Tanh), reads=[P],
                     writes=[SMB])
                for oc in range(8):
                    p, P = self.pb()
                    k.op("pe", lambda e: e.matmul(p[:, 0:n], lhsT=lu[:, d * 2, oc * 128:(oc + 1) * 128], rhs=sm[0:64, 0:n],
                                                  start=True, stop=True), reads=[WB, SMB], writes=[P])
                    k.op("act", lambda e: e.activation(out=o1[:, oc, 0:n], in_=p[:, 0:n], func=AF.Sigmoid,
                                                       bias=self.V("lb_%d" % i, (d * 2) * 8 + oc)),
                         reads=[P, self.VB], writes=[O1B])
                    k.op("dve", lambda e: e.tensor_scalar(out=o1[:, oc, 0:n], in0=o1[:, oc, 0:n], scalar1=-DECAY_SCALE,
                                                           scalar2=None, op0=ALU.mult), reads=[O1B], writes=[O1B])
                k.dma("sp", self.fm(self.lwT[d], t0, n), o1[:, :, 0:n], reads=[O1B], writes=[self.DB("lwT", d, ti)])
                p, P = self.pb()
                for kc in range(8):
                    k.op("pe", lambda e: e.matmul(p[0:64, 0:n], lhsT=ld[:, kc, (d * 2 + 1) * 64:(d * 2 + 2) * 64],
                                                  rhs=L[4][:, kc, 0:n], start=(kc == 0), stop=(kc == 7)),
                         reads=[WB, LB[4]], writes=[P], sig=(kc == 7))
                k.op("act", lambda e: e.activation(out=sm[0:64, 0:n], in_=p[0:64, 0:n], func=AF.Copy), reads=[P],
                     writes=[SMB])
                for oc in range(8):
                    p, P = self.pb()
                    k.op("pe", lambda e: e.matmul(p[:, 0:n], lhsT=lu[:, d * 2 + 1, oc * 128:(oc + 1) * 128],
                                                  rhs=sm[0:64, 0:n], start=True, stop=True), reads=[WB, SMB], writes=[P])
                    j = oc % 2
                    k.op("act", lambda e: e.activation(out=at[:, j, 0:n], in_=p[:, 0:n], func=AF.Sigmoid,
                                                       bias=self.V("lb_%d" % i, (d * 2 + 1) * 8 + oc)),
                         reads=[P, self.VB], writes=[ATB[j]])
                    k.op("dve", lambda e: e.tensor_tensor(out=o1[:, oc, 0:n], in0=kkt[:, oc, 0:n], in1=at[:, j, 0:n],
                                                          op=ALU.mult), reads=[KKB, ATB[j]], writes=[O1B])
                    k.op("dve", lambda e: e.tensor_scalar(out=at[:, j, 0:n], in0=at[:, j, 0:n],
                                                           scalar1=self.V("ka_%d" % i, oc), scalar2=omka[:, oc:oc + 1],
                                                           op0=ALU.mult, op1=ALU.add),
                         reads=[ATB[j], self.VB, OMB], writes=[ATB[j]])
                    k.op("dve", lambda e: e.tensor_tensor(out=kd[d][:, oc, 0:n], in0=at[:, j, 0:n], in1=kt[:, oc, 0:n],
                                                          op=ALU.mult), reads=[ATB[j], KTB], writes=[KDB[d]])
                k.dma("sp", self.fm(self.bT[d], t0, n), o1[:, :, 0:n], reads=[O1B], writes=[self.DB("bT", d, ti)])
                k.dma("sp", self.fm(self.kdT[d], t0, n), kd[d][:, :, 0:n], reads=[KDB[d]], writes=[self.DB("kdT", d, ti)])
            for oc in range(8):
                j = oc % 2
                k.op("dve", lambda e: e.tensor_tensor(out=at[:, j, 0:n], in0=kd[0][:, oc, 0:n], in1=kd[1][:, oc, 0:n],
                                                      op=ALU.add), reads=[KDB[0], KDB[1]], writes=[ATB[j]])
                k.op("dve", lambda e: e.scalar_tensor_tensor(out=at[:, j, 0:n], in0=rt[:, oc, 0:n],
                                                             scalar=self.V("rk_%d" % i, oc), in1=at[:, j, 0:n],
                                                             op0=ALU.mult, op1=ALU.mult),
                     reads=[RTB, ATB[j], self.VB], writes=[ATB[j]])
                p, P = self.pb()
                k.op("pe", lambda e: e.matmul(p[:, 0:n], lhsT=bones_f, rhs=at[:, j, 0:n], start=True, stop=True),
                     reads=[ATB[j], self.CS], writes=[P])
                k.op("act", lambda e: e.activation(out=w1[:, j, 0:n], in_=p[:, 0:n], func=AF.Copy), reads=[P],
                     writes=[W1B[j]])
                hpv = [(2 * oc) % 2, (2 * oc + 1) % 2]
                p2, P2 = self.pb()
                for half in range(2):
                    hd_ = 2 * oc + half
                    c0 = (hd_ % 2) * 512 + (hd_ // 2) * 64
                    for kc in range(8):
                        k.op("pe", lambda e: e.matmul(p2[half * 64:(half + 1) * 64, 0:n], lhsT=wvs[:, kc, c0:c0 + 64],
                                                      rhs=L[3][:, kc, 0:n], start=(kc == 0), stop=(kc == 7)),
                             reads=[WB, LB[3]], writes=[P2], sig=(kc == 7 and half == 1))
                k.op("dve", lambda e: e.tensor_tensor(out=o1[:, oc, 0:n], in0=p2[:, 0:n], in1=w1[:, j, 0:n], op=ALU.mult),
                     reads=[P2, W1B[j]], writes=[O1B])
            k.dma("sp", self.fm(self.bonT, t0, n), o1[:, :, 0:n], reads=[O1B], writes=[self.DB("bonT", ti)])
        k.phase_reset()

    def rw_wkv(self, l, b, d):
        k = self.k
        NW = 256
        rev = (d == 1)
        msk = k.sb([128, 512], F32)
        rmk = k.sb([128, 2048], F32)
        MB = Buf()
        k.dma("sp", msk, self.wmask[:, d, :], writes=[MB])
        k.dma("sp", rmk, self.rmask[:, d, :], writes=[MB])
        bones_f = self.cs[:, 128:256]
        def t8():
            return k.sb([128, 8, NW], F32)
        rt, kk, bb, kdt, lw, cum, ec = t8(), t8(), t8(), t8(), t8(), t8(), t8()
        INB = Buf()
        ECB = Buf()
        vst = k.sb([128, 4, 512], F32)
        VB_ = Buf()
        yt = k.sb([128, 8, NW], F32)
        YB = Buf()
        XR = [k.sb([128, 8, 192], F32) for _ in range(2)]
        BE = [k.sb([128, 8, 128], F32) for _ in range(2)]
        KT = [k.sb([128, 8, 128], F32) for _ in range(2)]
        CHB = [Buf(), Buf()]
        AM = k.sb([128, 8, 512], F32)
        AMB = [Buf() for _ in range(8)]
        X = k.sb([128, 8, 128], F32)
        XB = Buf()
        Pn = k.sb([128, 8, 128], F32)
        PTn = k.sb([128, 8, 128], F32)
        PNB = [Buf(), Buf()]
        PTB = [Buf(), Buf()]
        BEt = k.sb([128, 8, 128], F32)
        KTt = k.sb([128, 8, 128], F32)
        BTB, KTTB = Buf(), Buf()
        Wsb = k.sb([128, 8, 64], F32)
        Ust = k.sb([128, 8, 64], F32)
        Ubd = k.sb([128, 8, 128], F32)
        Vbd = k.sb([128, 8, 128], F32)
        Sst = k.sb([128, 8, 64], F32)
        Sbd = k.sb([128, 8, 128], F32)
        WSB, USB, UBB, VBB, SSB, SBB = [Buf() for _ in range(6)]
        for t_, B_ in ((XR[0], CHB[0]), (XR[1], CHB[1]), (BE[0], CHB[0]), (BE[1], CHB[1]), (KT[0], CHB[0]),
                       (KT[1], CHB[1]), (Ubd, UBB), (Vbd, VBB), (Sbd, SBB), (Sst, SSB)):
            k.op("pool", lambda e: e.memset(t_, 0.0), writes=[B_])
        wt = [(0, 0)] + [(1 + w, CTX + NW * w) for w in range(16)]
        order = wt if not rev else [wt[0]] + wt[:0:-1]
        nchunk = 0
        for (wi, t0) in order[:self.wkv_ntiles]:
            ti = wi
            for (dst, src, key) in ((rt, self.rT, ("rT", ti)), (kk, self.kkT, ("kkT", ti)), (bb, self.bT[d], ("bT", d, ti)),
                                    (kdt, self.kdT[d], ("kdT", d, ti)), (lw, self.lwT[d], ("lwT", d, ti))):
                k.dma("sp", dst, self.fm(src, t0, NW), reads=[self.DB(*key)], writes=[INB])
            k.dma("sp", vst, self.Vst[t0 // 64:t0 // 64 + 4].rearrange("c p f -> p c f"), reads=[self.DB("Vst", ti)],
                  writes=[VB_])
            fl = lambda a: a.rearrange("p f t -> p (f t)")
            rv = (lambda a: a[:, ::-1]) if rev else (lambda a: a)
            k.op("dve", lambda e: e.tensor_tensor_scan(out=rv(fl(cum)), data0=rv(rmk), data1=rv(fl(lw)), initial=0.0,
                                                       op0=ALU.mult, op1=ALU.add), reads=[INB, MB, ECB], writes=[ECB])
            k.op("dve", lambda e: e.tensor_tensor(out=lw, in0=cum, in1=lw, op=ALU.subtract), reads=[ECB, INB], writes=[INB])
            k.op("act", lambda e: e.activation(out=lw, in_=lw, func=AF.Exp), reads=[INB], writes=[INB])
            k.op("act", lambda e: e.activation(out=ec, in_=cum, func=AF.Exp), reads=[ECB], writes=[ECB])
            k.op("act", lambda e: e.activation(out=cum, in_=cum, func=AF.Exp, scale=-1.0), reads=[ECB], writes=[ECB])
            chs = range(4) if not rev else range(3, -1, -1)
            if self.wkv_stage < 2:
                continue
            for ch in chs:
                c0 = ch * 64
                cs_ = slice(c0, c0 + 64)
                jb = nchunk % 2
                nchunk += 1
                xr_, be_, kt_, CB = XR[jb], BE[jb], KT[jb], CHB[jb]
                for hp in range(2):
                    ps_ = slice(hp * 64, hp * 64 + 64)
                    k.op("dve", lambda e: e.scalar_tensor_tensor(out=xr_[ps_, :, hp * 64:hp * 64 + 64], in0=kk[ps_, :, cs_],
                                                                 scalar=-1.0, in1=lw[ps_, :, cs_], op0=ALU.mult,
                                                                 op1=ALU.mult), reads=[INB], writes=[CB])
                    k.op("pool", lambda e: e.tensor_tensor(out=be_[ps_, :, hp * 64:hp * 64 + 64], in0=bb[ps_, :, cs_],
                                                           in1=cum[ps_, :, cs_], op=ALU.mult), reads=[INB, ECB],
                         writes=[CB])
                    k.op("pool", lambda e: e.tensor_tensor(out=kt_[ps_, :, hp * 64:hp * 64 + 64], in0=kdt[ps_, :, cs_],
                                                           in1=cum[ps_, :, cs_], op=ALU.mult), reads=[INB, ECB],
                         writes=[CB])
                k.op("dve", lambda e: e.tensor_tensor(out=xr_[:, :, 128:192], in0=rt[:, :, cs_], in1=ec[:, :, cs_],
                                                      op=ALU.mult), reads=[INB, ECB], writes=[CB])
                vs_ = vst[:, ch, :].rearrange("p (q v) -> p q v", v=64)
                for hp in range(2):
                    ps_ = slice(hp * 64, hp * 64 + 64)
                    k.op("act", lambda e: e.activation(out=Vbd[ps_, :, hp * 64:hp * 64 + 64], in_=vs_[ps_, :, :],
                                                       func=AF.Copy), reads=[VB_], writes=[VBB])
                if self.wkv_stage < 3:
                    continue
                for p_ in range(8):
                    pa, PA = self.pb()
                    k.op("pe", lambda e: e.matmul(pa[:, 0:192], lhsT=be_[:, p_, :], rhs=xr_[:, p_, :], start=True,
                                                  stop=True), reads=[CB], writes=[PA], sig=False)
                    k.op("pe", lambda e: e.matmul(pa[:, 192:384], lhsT=kt_[:, p_, :], rhs=xr_[:, p_, :], start=True,
                                                  stop=True), reads=[CB], writes=[PA], sig=False)
                    k.op("pe", lambda e: e.matmul(pa[:, 384:512], lhsT=xr_[:, p_, 0:128], rhs=be_[:, p_, :], start=True,
                                                  stop=True), reads=[CB], writes=[PA])
                    k.op("dve", lambda e: e.tensor_tensor(out=AM[:, p_, :], in0=pa, in1=msk, op=ALU.mult),
                         reads=[PA, MB], writes=[AMB[p_]])
                if self.wkv_stage < 4:
                    continue
                for (src_, dst_, DB_) in ((be_, BEt, BTB), (kt_, KTt, KTTB)):
                    for q_ in range(2):
                        pt_, PT_ = self.pb()
                        for pi in range(4):
                            p_ = q_ * 4 + pi
                            k.op("pe", lambda e: e.transpose(pt_[:, pi * 128:(pi + 1) * 128], src_[:, p_, :], self.ident),
                                 reads=[CB, self.CS], writes=[PT_], sig=(pi == 3))
                        k.op("act", lambda e: e.activation(out=dst_[:, q_ * 4:q_ * 4 + 4, :],
                                                           in_=pt_.rearrange("p (q c) -> p q c", c=128), func=AF.Copy),
                             reads=[PT_], writes=[DB_])
                if self.wkv_stage < 5:
                    continue
                k.op("dve", lambda e: e.tensor_tensor(out=X, in0=AM[:, :, 0:128],
                                                      in1=self.ident.unsqueeze(1).broadcast_to([128, 8, 128]), op=ALU.add),
                     reads=AMB + [self.CS], writes=[XB])
                for kk_ in range(1, 6):
                    Pq = (lambda p_: AM[:, p_, 0:128]) if kk_ == 1 else (lambda p_: Pn[:, p_, :])
                    PTq = (lambda p_: AM[:, p_, 384:512]) if kk_ == 1 else (lambda p_: PTn[:, p_, :])
                    bk = {}
                    for q_ in range(2):
                        RD = (AMB[q_ * 4:q_ * 4 + 4]) if kk_ == 1 else [PNB[q_], PTB[q_]]
                        pb_, PB_ = self.pb()
                        for pi in range(4):
                            p_ = q_ * 4 + pi
                            k.op("pe", lambda e: e.matmul(pb_[:, pi * 128:(pi + 1) * 128], lhsT=Pq(p_), rhs=PTq(p_),
                                                          start=True, stop=True), reads=RD, writes=[PB_], sig=(pi == 3))
                        bk[("b", q_)] = (pb_, PB_)
                        if kk_ < 5:
                            pa_, PA_ = self.pb()
                            for pi in range(4):
                                p_ = q_ * 4 + pi
                                k.op("pe", lambda e: e.matmul(pa_[:, pi * 128:(pi + 1) * 128], lhsT=PTq(p_), rhs=Pq(p_),
                                                              start=True, stop=True), reads=RD, writes=[PA_],
                                     sig=(pi == 3))
                            bk[("a", q_)] = (pa_, PA_)
                    for q_ in range(2):
                        pb_, PB_ = bk[("b", q_)]
                        k.op("act", lambda e: e.activation(out=PTn[:, q_ * 4:q_ * 4 + 4, :],
                                                           in_=pb_.rearrange("p (q c) -> p q c", c=128), func=AF.Copy),
                             reads=[PB_], writes=[PTB[q_]])
                        if kk_ < 5:
                            pa_, PA_ = bk[("a", q_)]
                            k.op("dve", lambda e: e.tensor_copy(out=Pn[:, q_ * 4:q_ * 4 + 4, :],
                                                                in_=pa_.rearrange("p (q c) -> p q c", c=128)),
                                 reads=[PA_], writes=[PNB[q_]])
                    for q_ in range(2):
                        pc_, PC_ = self.pb()
                        for pi in range(4):
                            p_ = q_ * 4 + pi
                            k.op("pe", lambda e: e.matmul(pc_[:, pi * 128:(pi + 1) * 128], lhsT=PTn[:, p_, :], rhs=X[:, p_, :],
                                                          start=True, stop=True), reads=[PTB[q_], XB], writes=[PC_],
                                 sig=(pi == 3))
                        bk[("c", q_)] = (pc_, PC_)
                    for q_ in range(2):
                        pc_, PC_ = bk[("c", q_)]
                        k.op("dve", lambda e: e.tensor_tensor(out=X[:, q_ * 4:q_ * 4 + 4, :], in0=X[:, q_ * 4:q_ * 4 + 4, :],
                                                              in1=pc_.rearrange("p (q c) -> p q c", c=128), op=ALU.add),
                             reads=[PC_, XB], writes=[XB])
                if self.wkv_stage < 6:
                    continue
                r3 = lambda t_: t_.rearrange("p (q c) -> p q c", c=64)
                pw, PW = self.pb()
                pw2, PW2 = self.pb()
                for p_ in range(8):
                    k.op("pe", lambda e: e.matmul(pw[:, p_ * 64:(p_ + 1) * 64], lhsT=xr_[:, p_, 0:128], rhs=Sst[:, p_, :],
                                                  start=True, stop=True), reads=[CB, SSB], writes=[PW], sig=(p_ == 7))
                for p_ in range(8):
                    k.op("pe", lambda e: e.matmul(pw2[:, p_ * 64:(p_ + 1) * 64], lhsT=AM[:, p_, 192:320], rhs=vs_[:, p_, :],
                                                  start=True, stop=True), reads=[AMB[p_], VB_], writes=[PW2], sig=(p_ == 7))
                k.op("act", lambda e: e.activation(out=Wsb, in_=r3(pw), func=AF.Copy), reads=[PW], writes=[WSB])
                k.op("dve", lambda e: e.tensor_tensor(out=Wsb, in0=Wsb, in1=r3(pw2), op=ALU.add), reads=[WSB, PW2],
                     writes=[WSB])
                if self.wkv_stage < 7:
                    continue
                pu, PU = self.pb()
                for p_ in range(8):
                    k.op("pe", lambda e: e.matmul(pu[:, p_ * 64:(p_ + 1) * 64], lhsT=X[:, p_, :], rhs=Wsb[:, p_, :], start=True,
                                                  stop=True), reads=[XB, WSB], writes=[PU], sig=(p_ == 7))
                pu3 = r3(pu)
                k.op("act", lambda e: e.activation(out=Ust, in_=pu3, func=AF.Copy), reads=[PU], writes=[USB])
                for hp in range(2):
                    ps_ = slice(hp * 64, hp * 64 + 64)
                    k.op("act", lambda e: e.activation(out=Ubd[ps_, :, hp * 64:hp * 64 + 64], in_=pu3[ps_, :, :],
                                                       func=AF.Copy), reads=[PU], writes=[UBB])
                if self.wkv_stage < 8:
                    continue
                py1, PY1 = self.pb()
                py2, PY2 = self.pb()
                py3, PY3 = self.pb()
                for p_ in range(8):
                    k.op("pe", lambda e: e.matmul(py1[:, p_ * 64:(p_ + 1) * 64], lhsT=Sbd[:, p_, :], rhs=xr_[:, p_, 128:192],
                                                  start=True, stop=True), reads=[SBB, CB], writes=[PY1], sig=(p_ == 7))
                for p_ in range(8):
                    k.op("pe", lambda e: e.matmul(py2[:, p_ * 64:(p_ + 1) * 64], lhsT=Ubd[:, p_, :], rhs=AM[:, p_, 128:192],
                                                  start=True, stop=True), reads=[UBB, AMB[p_]], writes=[PY2], sig=(p_ == 7))
                for p_ in range(8):
                    k.op("pe", lambda e: e.matmul(py3[:, p_ * 64:(p_ + 1) * 64], lhsT=Vbd[:, p_, :], rhs=AM[:, p_, 320:384],
                                                  start=True, stop=True), reads=[VBB, AMB[p_]], writes=[PY3], sig=(p_ == 7))
                k.op("act", lambda e: e.activation(out=yt[:, :, cs_], in_=r3(py1), func=AF.Copy), reads=[PY1], writes=[YB])
                k.op("dve", lambda e: e.tensor_tensor(out=yt[:, :, cs_], in0=yt[:, :, cs_], in1=r3(py2), op=ALU.add),
                     reads=[YB, PY2], writes=[YB])
                k.op("dve", lambda e: e.tensor_tensor(out=yt[:, :, cs_], in0=yt[:, :, cs_], in1=r3(py3), op=ALU.add),
                     reads=[YB, PY3], writes=[YB])
                if self.wkv_stage < 9:
                    continue
                pS1, PS1 = self.pb()
                pS2, PS2 = self.pb()
                for p_ in range(8):
                    k.op("pe", lambda e: e.matmul(pS1[:, p_ * 64:(p_ + 1) * 64], lhsT=BEt[:, p_, :], rhs=Ust[:, p_, :],
                                                  start=True, stop=True), reads=[BTB, USB], writes=[PS1], sig=(p_ == 7))
                for p_ in range(8):
                    k.op("pe", lambda e: e.matmul(pS2[:, p_ * 64:(p_ + 1) * 64], lhsT=KTt[:, p_, :], rhs=vs_[:, p_, :],
                                                  start=True, stop=True), reads=[KTTB, VB_], writes=[PS2], sig=(p_ == 7))
                k.op("act", lambda e: e.activation(out=Wsb, in_=r3(pS1), func=AF.Copy), reads=[PS1], writes=[WSB])
                k.op("dve", lambda e: e.tensor_tensor(out=Wsb, in0=Wsb, in1=r3(pS2), op=ALU.add), reads=[WSB, PS2],
                     writes=[WSB])
                k.op("dve", lambda e: e.tensor_tensor(out=Wsb, in0=Wsb, in1=Sst, op=ALU.add), reads=[WSB, SSB],
                     writes=[WSB])
                gcol = (c0 + 63) if not rev else c0
                k.op("dve", lambda e: e.tensor_tensor(out=Sst, in0=Wsb, in1=ec[:, :, gcol:gcol + 1].broadcast_to([128, 8, 64]),
                                                      op=ALU.mult), reads=[WSB, ECB], writes=[SSB])
                for hp in range(2):
                    ps_ = slice(hp * 64, hp * 64 + 64)
                    k.op("act", lambda e: e.activation(out=Sbd[ps_, :, hp * 64:hp * 64 + 64], in_=Sst[ps_, :, :],
                                                       func=AF.Copy), reads=[SSB], writes=[SBB])
            k.dma("sp", self.fm(self.yT[d], t0, NW), yt, reads=[YB], writes=[self.DB("yT", d, wi)])
        k.phase_reset()

    def rw_out(self, l, b, last):
        k = self.k
        i = l // 2
        wo = k.sb([128, 8, D], BF16)
        WOB = Buf()
        self.load_w(wo, WOB, self.rw_wo[i], 8)
        bones_f = self.cs[:, 128:256]
        y0 = k.sb([128, 8, 512], F32)
        y1 = k.sb([128, 8, 512], F32)
        bon = k.sb([128, 8, 512], F32)
        gt = k.sb([128, 8, 512], F32)
        xt = k.sb([128, 8, 512], F32)
        ob = k.sb([128, 8, 512], BF16)
        Y0B, Y1B, BNB, GTB, XB, OBB = [Buf() for _ in range(6)]
        yc = k.sb([128, 2, 512], F32)
        sq = k.sb([128, 2, 512], F32)
        rs = k.sb([128, 2, 512], F32)
        YCB, SQB, RSB = [Buf(), Buf()], [Buf(), Buf()], [Buf(), Buf()]
        for ti, (t0, n) in enumerate(TILES):
            if last and ti == 0:
                continue
            mi = 2 if ti == 0 else b
            wis = [0] if ti == 0 else [2 * ti - 1, 2 * ti]
            k.dma("sp", y0[:, :, 0:n], self.fm(self.yT[0], t0, n), reads=[self.DB("yT", 0, w) for w in wis], writes=[Y0B])
            k.dma("sp", y1[:, :, 0:n], self.fm(self.yT[1], t0, n), reads=[self.DB("yT", 1, w) for w in wis], writes=[Y1B])
            k.dma("sp", bon[:, :, 0:n], self.fm(self.bonT, t0, n), reads=[self.DB("bonT", w) for w in wis], writes=[BNB])
            k.dma("sp", gt[:, :, 0:n], self.fm(self.gT, t0, n), reads=[self.DB("gT", w) for w in wis], writes=[GTB])
            k.dma("sp", xt[:, :, 0:n], self.fm(self.xT[b], t0, n), reads=[self.DB("xT", b, ti)], writes=[XB])
            k.op("pool", lambda e: e.tensor_tensor(out=y0[:, :, 0:n], in0=y0[:, :, 0:n], in1=y1[:, :, 0:n], op=ALU.add),
                 reads=[Y0B, Y1B], writes=[Y0B])
            for fc in range(8):
                j = fc % 2
                p, P = self.pb()
                k.op("pe", lambda e: e.matmul(p[:, 0:n], lhsT=bones_f, rhs=y0[:, fc, 0:n], start=True, stop=True),
                     reads=[Y0B, self.CS], writes=[P])
                k.op("dve", lambda e: e.scalar_tensor_tensor(out=yc[:, j, 0:n], in0=p[:, 0:n], scalar=-1.0 / 64,
                                                             in1=y0[:, fc, 0:n], op0=ALU.mult, op1=ALU.add),
                     reads=[P, Y0B], writes=[YCB[j]])
                k.op("act", lambda e: e.activation(out=sq[:, j, 0:n], in_=yc[:, j, 0:n], func=AF.Square), reads=[YCB[j]],
                     writes=[SQB[j]])
                p2, P2 = self.pb()
                k.op("pe", lambda e: e.matmul(p2[:, 0:n], lhsT=bones_f, rhs=sq[:, j, 0:n], start=True, stop=True),
                     reads=[SQB[j], self.CS], writes=[P2])
                k.op("act", lambda e: e.activation(out=rs[:, j, 0:n], in_=p2[:, 0:n], func=AF.Sqrt, bias=GN_EPS,
                                                   scale=1.0 / 64), reads=[P2], writes=[RSB[j]])
                k.op("dve", lambda e: e.reciprocal(out=rs[:, j, 0:n], in_=rs[:, j, 0:n]), reads=[RSB[j]], writes=[RSB[j]])
                k.op("dve", lambda e: e.tensor_tensor(out=yc[:, j, 0:n], in0=yc[:, j, 0:n], in1=rs[:, j, 0:n], op=ALU.mult),
                     reads=[YCB[j], RSB[j]], writes=[YCB[j]])
                k.op("act", lambda e: e.activation(out=yc[:, j, 0:n], in_=yc[:, j, 0:n], func=AF.Identity,
                                                   bias=self.V("gnb_%d" % i, fc), scale=self.V("gng_%d" % i, fc)),
                     reads=[YCB[j], self.VB], writes=[YCB[j]])
                k.op("pool", lambda e: e.tensor_tensor(out=yc[:, j, 0:n], in0=yc[:, j, 0:n], in1=bon[:, fc, 0:n], op=ALU.add),
                     reads=[YCB[j], BNB], writes=[YCB[j]])
                k.op("dve", lambda e: e.tensor_tensor(out=ob[:, fc, 0:n], in0=yc[:, j, 0:n], in1=gt[:, fc, 0:n], op=ALU.mult),
                     reads=[YCB[j], GTB], writes=[OBB])
            for oc in range(8):
                p, P = self.pb()
                for kc in range(8):
                    k.op("pe", lambda e: e.matmul(p[:, 0:n], lhsT=wo[:, kc, oc * 128:(oc + 1) * 128], rhs=ob[:, kc, 0:n],
                                                  start=(kc == 0), stop=(kc == 7)), reads=[WOB, OBB], writes=[P],
                         sig=(kc == 7))
                k.op("dve", lambda e: e.scalar_tensor_tensor(out=xt[:, oc, 0:n], in0=p[:, 0:n],
                                                             scalar=self.mod[:, 16 + oc, mi:mi + 1], in1=xt[:, oc, 0:n],
                                                             op0=ALU.mult, op1=ALU.add), reads=[P, self.MOD, XB], writes=[XB])
            k.dma("sp", self.fm(self.xT[b], t0, n), xt[:, :, 0:n], reads=[XB], writes=[self.DB("xT", b, ti)])
        k.phase_reset()

    def build(self, nphases=None):
        ph = [lambda: self.setup()]
        for l in self.layers:
            last = (l == DEPTH - 1)
            ph.append(lambda l=l: self.phase_mod(l))
            for b in range(NB):
                if l % 2 == 0:
                    ph.append(lambda l=l, b=b: self.phase_hy_inproj(l, b))
                    ph.append(lambda l=l, b=b: self.phase_rglru(l, b))
                    ph.append(lambda l=l, b=b: self.phase_attn(l, b))
                else:
                    ph.append(lambda l=l, b=b: self.phase_rwkv(l, b))
                ph.append(lambda l=l, b=b, last=last: self.phase_mlp(l, b, last))
        for f in (ph if nphases is None else ph[:nphases]):
            f()
        self.k.finish()
        return self.nc


def host_consts():
    cs = np.zeros((128, 512), np.float32)
    cs[:, 0:128] = np.eye(128, dtype=np.float32)
    bo = np.zeros((128, 128), np.float32)
    bo[0:64, 0:64] = 1.0
    bo[64:128, 64:128] = 1.0
    cs[:, 128:256] = bo
    pw = np.zeros((128, 128), np.float32)
    for blk in range(2):
        for n in range(64):
            pw[blk * 64 + n, blk * 64 + (n + 32) % 64] = 1.0
    cs[:, 256:384] = pw
    rows = SEQ // 64
    row = np.repeat(np.arange(rows, dtype=np.float32), 64)
    col = np.tile(np.arange(64, dtype=np.float32), rows)
    inv = (np.float32(10000.0) ** (-np.arange(0, 32, 2, dtype=np.float32) / np.float32(32))).astype(np.float32)
    ang = np.concatenate([row[:, None] * inv, col[:, None] * inv], axis=-1).astype(np.float32)
    c = np.cos(ang).astype(np.float32).T
    s_ = np.sin(ang).astype(np.float32).T
    cos64 = np.concatenate([c, c], 0)
    sin64 = np.concatenate([-s_, s_], 0)
    cosT = np.ascontiguousarray(np.concatenate([cos64, cos64], 0))
    sinT = np.ascontiguousarray(np.concatenate([sin64, sin64], 0))
    return cs, cosT, sinT


def fmaj(v):
    return np.ascontiguousarray(np.asarray(v, np.float32).reshape(-1, 128).T)


def host_vb(inp):
    vb = np.zeros((128, NVB), np.float32)

    def put(name, arr):
        arr = np.asarray(arr, np.float32)
        vb[:, VBM[name]:VBM[name] + arr.shape[1]] = arr
    for l in range(DEPTH):
        put("ng0_%d" % l, fmaj(inp["norm_g"][l, 0]))
        put("ng1_%d" % l, fmaj(inp["norm_g"][l, 1]))
        put("adab_%d" % l, fmaj(inp["ada_b"][l]))
    for i in range(2):
        gq = inp["hy_q_norm"][i][PERM]
        gk = inp["hy_k_norm"][i][PERM]
        put("gq_%d" % i, np.concatenate([gq, gq])[:, None])
        put("gk_%d" % i, np.concatenate([gk, gk])[:, None])
        put("convw_%d" % i, np.concatenate([fmaj(inp["hy_conv_w"][i][j]) for j in range(4)], 1))
        put("convb_%d" % i, fmaj(inp["hy_conv_b"][i]))
        put("gateb_%d" % i, np.concatenate([fmaj(inp["hy_gate_b"][i][d][g]) for d in range(2) for g in range(2)], 1))
        put("lam_%d" % i, np.concatenate([fmaj(inp["hy_lam"][i][d]) for d in range(2)], 1))
    for i in range(2):
        put("mu_%d" % i, np.concatenate([fmaj(inp["rw_mu"][i][j]) for j in range(6)], 1))
        put("kk_%d" % i, fmaj(inp["rw_k_k"][i]))
        put("ka_%d" % i, fmaj(inp["rw_k_a"][i]))
        put("rk_%d" % i, fmaj(inp["rw_r_k"][i].reshape(-1)))
        put("gng_%d" % i, fmaj(inp["rw_gn_g"][i]))
        put("gnb_%d" % i, fmaj(inp["rw_gn_b"][i]))
        put("lb_%d" % i, np.concatenate([fmaj(inp["rw_lora_bias"][i][d][j]) for d in range(2) for j in range(2)], 1))
    return vb


def host_shared(inp):
    sh = {}
    cs, cosT, sinT = host_consts()
    sh["consts"], sh["cosT"], sh["sinT"] = cs, cosT, sinT
    sh["vb"] = host_vb(inp)
    sh["ada_w"] = np.ascontiguousarray(inp["ada_w"], np.float32)
    sh["mlp_w1"] = np.ascontiguousarray(inp["mlp_w1"], np.float32)
    sh["mlp_w2"] = np.ascontiguousarray(inp["mlp_w2"], np.float32)
    win = inp["hy_w_in"]
    cols = []
    for h in range(8):
        cols.append(h * 64 + PERM)
    for kv in range(2):
        cols.append(512 + kv * 64 + PERM)
        cols.append(512 + kv * 64 + PERM)
    cols.append(np.arange(640, 768))
    cols.append(np.arange(768, 1792))
    cols = np.concatenate(cols)
    sh["hy_win"] = np.ascontiguousarray(win[:, :, cols], np.float32)
    sh["hy_wout"] = np.ascontiguousarray(inp["hy_w_out"], np.float32)
    gw = inp["hy_gate_w"]
    bd = np.zeros((2, 2, 2, 4, 128, 128), np.float32)
    for c in range(4):
        bd[:, :, :, c, 0:64, 0:64] = gw[:, :, :, 2 * c]
        bd[:, :, :, c, 64:128, 64:128] = gw[:, :, :, 2 * c + 1]
    sh["hy_gw"] = bd
    sh["rw_wrkv"] = np.ascontiguousarray(inp["rw_w_rkv"], np.float32)
    st = np.concatenate([np.arange((2 * p + hp) * 64, (2 * p + hp) * 64 + 64) for hp in range(2) for p in range(8)])
    sh["rw_wvst"] = np.ascontiguousarray(inp["rw_w_rkv"][:, 2][:, :, st], np.float32)
    sh["rw_wo"] = np.ascontiguousarray(inp["rw_w_o"], np.float32)
    ldn = inp["rw_lora_down"]
    sh["rw_ld"] = np.ascontiguousarray(np.concatenate([ldn[:, d, j] for d in range(2) for j in range(2)], axis=-1), np.float32)
    lup = inp["rw_lora_up"]
    sh["rw_lu"] = np.ascontiguousarray(np.stack([lup[:, d, j] for d in range(2) for j in range(2)], axis=1), np.float32)
    sh["rw_gd"] = np.ascontiguousarray(inp["rw_gate_down"], np.float32)
    sh["rw_gu"] = np.ascontiguousarray(inp["rw_gate_up"], np.float32)
    ii = np.arange(64)
    up = (ii[None, :] > ii[:, None]).astype(np.float32)
    le = (ii[:, None] <= ii[None, :]).astype(np.float32)
    lo_ = (ii[None, :] < ii[:, None]).astype(np.float32)

    def bdm(m):
        o = np.zeros((128, 128), np.float32)
        o[0:64, 0:64] = m
        o[64:128, 64:128] = m
        return o

    def stk(m):
        return np.concatenate([m, m], 0)
    wm = np.zeros((128, 2, 512), np.float32)
    for d, (u_, l_, s_) in enumerate(((up, lo_, le), (up.T, lo_.T, le.T))):
        wm[:, d, 0:128] = bdm(u_)
        wm[:, d, 128:192] = stk(s_)
        wm[:, d, 192:320] = bdm(u_)
        wm[:, d, 320:384] = stk(s_)
        wm[:, d, 384:512] = bdm(l_)
    sh["wmask"] = wm
    rm = np.ones((128, 2, 2048), np.float32)
    tt = np.arange(2048)
    rm[:, 0, tt % 64 == 0] = 0.0
    rm[:, 1, tt % 64 == 63] = 0.0
    sh["rmask"] = rm
    return sh


_CACHE = {}


def kernel(**inp):
    inp = {k_: np.asarray(v) for k_, v in inp.items()}
    sh = host_shared(inp)
    if "nc" not in _CACHE:
        _CACHE["nc"] = Prog().build()
    nc = _CACHE["nc"]
    in_maps = []
    for core in range(8):
        bs = [2 * core, 2 * core + 1]
        m = dict(sh)
        m["xT_in"] = np.ascontiguousarray(np.stack([inp["x"][b].T for b in bs]), np.float32)
        m["ctxT_in"] = np.ascontiguousarray(np.stack([inp["ctx"][b].T for b in bs]), np.float32)
        cv = np.stack([inp["c"][bs[0]], inp["c"][bs[1]], inp["c_ctx"]], 0)
        m["cT"] = np.ascontiguousarray(cv.reshape(3, 8, 128).transpose(2, 1, 0), np.float32)
        in_maps.append(m)
    res = run_bass_kernel_spmd(nc, in_maps, core_ids=list(range(8)))
    out = np.empty((16, SEQ, D), np.float32)
    for core in range(8):
        o = res.results[core]["outT"]
        for j in range(NB):
            out[2 * core + j] = o[j].T
    return out
```

```python
import math
import numpy as np
import concourse.bass as bass
import concourse.mybir as mybir
from concourse.bass_utils import run_bass_kernel_spmd

F32 = mybir.dt.float32
BF16 = mybir.dt.bfloat16
AF = mybir.ActivationFunctionType
ALU = mybir.AluOpType

D = 1024
SEQ = 4096
CTX = 256
T = SEQ + CTX
NB = 2
DEPTH = 4
EPS = 1e-6
DFF = 4096
TILES = [(0, CTX)] + [(CTX + 512 * i, 512) for i in range(8)]
WTILES = [(0, CTX)] + [(CTX + 256 * i, 256) for i in range(16)]
GELU_C = 2.0 * math.sqrt(2.0 / math.pi)
DECAY_SCALE = math.exp(-0.5)
GN_EPS = 64e-5


class Buf:
    __slots__ = ("w", "r")

    def __init__(self):
        self.w = None
        self.r = {}


class KB:
    NDMA = 40

    def __init__(self, nc):
        self.nc = nc
        self.eng = dict(pe=nc.tensor, act=nc.scalar, dve=nc.vector, pool=nc.gpsimd, sp=nc.sync)
        self.sems = {}
        self.cnt = {}
        for e in ("pe", "act", "dve", "pool"):
            self.sems[e] = nc.alloc_semaphore("s_" + e)
            self.cnt[e] = 0
        self.dsem = [nc.alloc_semaphore("d%d" % i) for i in range(self.NDMA)]
        self.dval = [0] * self.NDMA
        self.drr = 0
        self.waited = {e: {} for e in self.eng}
        self.sb_off = 16512
        self.sb_base = 16512
        self.nalloc = 0
        self.ninst = 0
        self.pending = []
        self.rec = None

    def sb(self, shape, dt=F32, name=None):
        self.nalloc += 1
        nm = "%s_%d" % (name or "t", self.nalloc)
        n = 1
        for s_ in shape[1:]:
            n *= s_
        nbytes = n * (4 if dt == F32 else 2)
        nbytes = (nbytes + 63) // 64 * 64
        off = self.sb_off
        self.sb_off += nbytes
        assert self.sb_off <= 229376, ("SBUF overflow", nm, self.sb_off)
        return self.nc.alloc_sbuf_tensor_at(nm, list(shape), dt, offset=off).ap()

    def phase_reset(self):
        self.barrier()
        self.sb_off = self.sb_base

    def persist_mark(self):
        self.sb_base = self.sb_off

    def _semh(self, key):
        return self.sems[key] if isinstance(key, str) else self.dsem[key[1]]

    def _wait(self, e, key, val, raw=False):
        if key == e and (not raw or e == "pe"):
            return
        w = self.waited[e]
        if w.get(key, 0) >= val:
            return
        w[key] = val
        self.pending.append((key, val))

    def _take(self):
        p = self.pending
        self.pending = []
        d = {}
        for k_, v_ in p:
            if d.get(k_, 0) < v_:
                d[k_] = v_
        return list(d.items())

    def _deps(self, e, reads, writes):
        for b in reads:
            if b.w is not None:
                self._wait(e, b.w[0], b.w[1], raw=True)
        for b in writes:
            if b.w is not None:
                self._wait(e, b.w[0], b.w[1])
            for k_, v_ in b.r.items():
                self._wait(e, k_, v_)

    def _mark(self, tok, reads, writes):
        k_, v_ = tok
        for b in reads:
            if b.r.get(k_, 0) < v_:
                b.r[k_] = v_
        for b in writes:
            b.w = tok
            b.r = {}

    def op(self, e, ins_fn, reads=(), writes=(), sig=True):
        self._deps(e, reads, writes)
        items = self._take()
        if self.rec is not None:
            self.rec.append((e, list(items), e if sig else None, 1))
        last = items.pop() if items else None
        for k_, v_ in items:
            self.eng[e].wait_ge(self._semh(k_), v_)
            self.ninst += 1
        ins = ins_fn(self.eng[e])
        if last is not None:
            ins._wait_ge(self._semh(last[0]), last[1])
        self.ninst += 1
        if sig:
            self.cnt[e] += 1
            ins.then_inc(self.sems[e], 1)
            tok = (e, self.cnt[e])
        else:
            tok = (e, self.cnt[e] + 1)
        self._mark(tok, reads, writes)
        return tok

    def dma(self, q, out, in_, reads=(), writes=(), **kw):
        i = self.drr
        self.drr = (self.drr + 1) % self.NDMA
        key = ("d", i)
        if self.dval[i] > 0:
            self._wait(q, key, self.dval[i])
        self._deps(q, reads, writes)
        its_ = self._take()
        if self.rec is not None:
            self.rec.append((q, list(its_), key, 16))
        for k_, v_ in its_:
            self.eng[q].wait_ge(self._semh(k_), v_)
            self.ninst += 1
        self.eng[q].dma_start(out=out, in_=in_, **kw).then_inc(self.dsem[i], 16)
        self.ninst += 1
        self.dval[i] += 16
        tok = (key, self.dval[i])
        self._mark(tok, reads, writes)
        return tok

    def barrier(self):
        for e in self.eng:
            for o in ("pe", "act", "dve", "pool"):
                if self.cnt[o] > 0:
                    self._wait(e, o, self.cnt[o])
            for i in range(self.NDMA):
                if self.dval[i] > 0:
                    self._wait(e, ("d", i), self.dval[i])
            its_ = self._take()
            if self.rec is not None:
                self.rec.append((e, list(its_), None, 0))
            for k_, v_ in its_:
                self.eng[e].wait_ge(self._semh(k_), v_)
                self.ninst += 1

    def finish(self):
        self.barrier()


def vb_layout():
    m = {}
    off = 0

    def add(name, n):
        nonlocal off
        m[name] = off
        off += n
    for l in range(DEPTH):
        add("ng0_%d" % l, 8)
        add("ng1_%d" % l, 8)
        add("adab_%d" % l, 48)
    for i in range(2):
        add("gq_%d" % i, 1)
        add("gk_%d" % i, 1)
        add("convw_%d" % i, 16)
        add("convb_%d" % i, 4)
        add("gateb_%d" % i, 16)
        add("lam_%d" % i, 8)
    for i in range(2):
        add("mu_%d" % i, 48)
        add("kk_%d" % i, 8)
        add("ka_%d" % i, 8)
        add("rk_%d" % i, 8)
        add("gng_%d" % i, 8)
        add("gnb_%d" % i, 8)
        add("lb_%d" % i, 32)
    return m, off


VBM, NVB = vb_layout()
PERM = np.concatenate([np.arange(0, 64, 2), np.arange(1, 64, 2)])


class Prog:
    def __init__(self, debug=(), nlayers=DEPTH, ext_in=()):
        self.debug = set(debug)
        self.ext_in = set(ext_in)
        self.nlayers = nlayers
        self.layers = list(range(nlayers))
        self.rw_stop = 99
        self.wkv_stage = 99
        self.wkv_ntiles = 99
        nc = self.nc = bass.Bass("TRN2", target_bir_lowering=False)
        k = self.k = KB(nc)
        self.dbuf = {}
        di = self.din
        self.xin = di("xT_in", [NB, D, SEQ])
        self.cin = di("ctxT_in", [NB, D, CTX])
        self.cT = di("cT", [128, 8, 3])
        self.vbd = di("vb", [128, NVB])
        self.ada_w = di("ada_w", [DEPTH, D, 6 * D])
        self.w1 = di("mlp_w1", [DEPTH, D, DFF])
        self.w2 = di("mlp_w2", [DEPTH, DFF, D])
        self.hy_win = di("hy_win", [2, D, 1920])
        self.hy_wout = di("hy_wout", [2, D, D])
        self.hy_gw = di("hy_gw", [2, 2, 2, 4, 128, 128])
        self.cst = di("consts", [128, 512])
        self.cosd = di("cosT", [128, SEQ])
        self.sind = di("sinT", [128, SEQ])
        self.rw_wrkv = di("rw_wrkv", [2, 3, D, D])
        self.rw_wvst = di("rw_wvst", [2, D, D])
        self.rw_wo = di("rw_wo", [2, D, D])
        self.rw_ld = di("rw_ld", [2, D, 256])
        self.rw_lu = di("rw_lu", [2, 4, 64, D])
        self.rw_gd = di("rw_gd", [2, D, 128])
        self.rw_gu = di("rw_gu", [2, 128, D])
        self.wmask = di("wmask", [128, 2, 512])
        self.rmask = di("rmask", [128, 2, 2048])
        self.out = nc.dram_tensor("outT", [NB, D, SEQ], F32, kind="ExternalOutput").ap()
        self.hnT = self.dscr("hnT", [D, T])
        self.rT = self.dscr("rT", [D, T])
        self.kkT = self.dscr("kkT", [D, T])
        self.kdT = self.dscr("kdT", [2, D, T])
        self.bT = self.dscr("bT", [2, D, T])
        self.lwT = self.dscr("lwT", [2, D, T])
        self.gT = self.dscr("gT", [D, T])
        self.bonT = self.dscr("bonT", [D, T])
        self.Vst = self.dscr("Vst", [T // 64, 128, 512])
        self.yT = self.dscr("yT", [2, D, T])
        self.xT = self.dscr("xT", [NB, D, T])
        self.qT = self.dscr("qT", [512, T], BF16)
        self.kT2 = self.dscr("kT2", [256, T], BF16)
        self.vtok = self.dscr("vtok", [T // 128, 128, 384], BF16)
        self.xr = self.dscr("xr", [512, T])
        self.gg = self.dscr("gg", [512, T], BF16)
        self.rec = self.dscr("rec", [512, T], BF16)
        self.ps = [nc.alloc_psum_tensor("ps%d" % i, [128, 512], F32).ap() for i in range(8)]
        self.PS = [Buf() for _ in range(8)]
        self.prr = 0
        self.vb = k.sb([128, NVB], F32, "vb")
        self.VB = Buf()
        self.cs = k.sb([128, 512], F32, "cs")
        self.CS = Buf()
        self.ident = self.cs[:, 0:128]
        self.ones_bf = k.sb([128, 128], BF16, "ones")
        self.bones_bf = k.sb([128, 128], BF16, "bones")
        self.pswap_bf = k.sb([128, 128], BF16, "pswap")
        self.scT = k.sb([128, 8, 3], BF16, "scT")
        self.mod = k.sb([128, 48, 3], F32, "mod")
        self.A1 = k.sb([128, 8, 3], F32, "A1")
        self.A2 = k.sb([128, 8, 3], F32, "A2")
        self.MOD = Buf()
        self.CONST = Buf()
        k.persist_mark()

    def din(self, name, shape, dt=F32):
        return self.nc.dram_tensor(name, list(shape), dt, kind="ExternalInput").ap()

    def dscr(self, name, shape, dt=F32):
        kind = "ExternalOutput" if name in self.debug else "Internal"
        if name in getattr(self, "ext_in", ()):
            kind = "ExternalInput"
        return self.nc.dram_tensor(name, list(shape), dt, kind=kind).ap()

    def DB(self, *key):
        b = self.dbuf.get(key)
        if b is None:
            b = self.dbuf[key] = Buf()
        return b

    def pb(self):
        i = self.prr
        self.prr = (self.prr + 1) % 8
        return self.ps[i], self.PS[i]

    @staticmethod
    def fm(ap2d, t0, n):
        return ap2d.rearrange("(fc p) t -> p fc t", p=128)[:, :, t0:t0 + n]

    def V(self, name, j=0, n=1):
        o = VBM[name] + j
        return self.vb[:, o:o + n]

    def setup(self):
        k = self.k
        k.dma("sp", self.vb, self.vbd, writes=[self.VB])
        k.dma("sp", self.cs, self.cst, writes=[self.CS])
        k.op("dve", lambda e: e.memset(self.ones_bf, 1.0), writes=[self.CONST])
        k.op("act", lambda e: e.activation(out=self.bones_bf, in_=self.cs[:, 128:256], func=AF.Copy),
             reads=[self.CS], writes=[self.CONST])
        k.op("act", lambda e: e.activation(out=self.pswap_bf, in_=self.cs[:, 256:384], func=AF.Copy),
             reads=[self.CS], writes=[self.CONST])
        ct = k.sb([128, 8, 3], F32)
        sg = k.sb([128, 8, 3], F32)
        CTB = Buf()
        k.dma("sp", ct, self.cT, writes=[CTB])
        k.op("act", lambda e: e.activation(out=sg, in_=ct, func=AF.Sigmoid), reads=[CTB], writes=[CTB])
        k.op("dve", lambda e: e.tensor_tensor(out=self.scT, in0=ct, in1=sg, op=ALU.mult), reads=[CTB],
             writes=[self.CONST])
        for b in range(NB):
            k.dma("sp", self.xT[b, :, 0:CTX], self.cin[b], writes=[self.DB("xT", b, 0)])
            for ti in range(1, 9):
                t0, n = TILES[ti]
                k.dma("sp", self.xT[b, :, t0:t0 + n], self.xin[b, :, t0 - CTX:t0 - CTX + n],
                      writes=[self.DB("xT", b, ti)])
        k.phase_reset()

    def phase_mod(self, l):
        k = self.k
        wa = k.sb([128, 8, 3072], BF16)
        WA = Buf()
        pm, PM = self.pb()
        for half in range(2):
            src = self.ada_w[l].rearrange("(kc p) n -> p kc n", p=128)[:, :, half * 3072:(half + 1) * 3072]
            for kc in range(8):
                k.dma("pool", wa[:, kc, :], src[:, kc, :], writes=[WA])
            for j in range(24):
                jj = half * 24 + j
                for kc in range(8):
                    k.op("pe", lambda e: e.matmul(pm[:, jj * 4:jj * 4 + 3], lhsT=wa[:, kc, j * 128:(j + 1) * 128],
                                                  rhs=self.scT[:, kc, :], start=(kc == 0), stop=(kc == 7)),
                         reads=[WA, self.CONST], writes=[PM], sig=(kc == 7))
        pmv = pm[:, 0:192].rearrange("p (j f) -> p j f", f=4)[:, :, 0:3]
        bias = self.V("adab_%d" % l, 0, 48).unsqueeze(2).broadcast_to([128, 48, 3])
        k.op("dve", lambda e: e.tensor_tensor(out=self.mod, in0=pmv, in1=bias, op=ALU.add),
             reads=[PM, self.VB], writes=[self.MOD])
        for (A, gname, sc0) in ((self.A1, "ng0_%d" % l, 8), (self.A2, "ng1_%d" % l, 32)):
            g = self.V(gname, 0, 8).unsqueeze(2).broadcast_to([128, 8, 3])
            k.op("dve", lambda e: e.scalar_tensor_tensor(out=A, in0=self.mod[:, sc0:sc0 + 8, :], scalar=1.0, in1=g,
                                                         op0=ALU.add, op1=ALU.mult),
                 reads=[self.MOD, self.VB], writes=[self.MOD])
        k.phase_reset()

    def norm_tile(self, xt, XTB, out, OUTB, A, Bsh, mi, n, W):
        k = self.k
        sq, rstd, tmp = W["sq"], W["rstd"], W["tmp"]
        k.op("act", lambda e: e.activation(out=sq[:, :, 0:n], in_=xt[:, :, 0:n], func=AF.Square),
             reads=[XTB], writes=[W["SQ"]])
        pn, PN = self.pb()
        for fc in range(8):
            k.op("pe", lambda e: e.matmul(pn[:, 0:n], lhsT=self.ones_bf, rhs=sq[:, fc, 0:n], start=(fc == 0),
                                          stop=(fc == 7)), reads=[W["SQ"], self.CONST], writes=[PN], sig=(fc == 7))
        k.op("act", lambda e: e.activation(out=rstd[:, 0:n], in_=pn[:, 0:n], func=AF.Sqrt, bias=EPS, scale=1.0 / D),
             reads=[PN], writes=[W["RSTD"]])
        k.op("dve", lambda e: e.reciprocal(out=rstd[:, 0:n], in_=rstd[:, 0:n]), reads=[W["RSTD"]], writes=[W["RSTD"]])
        for fc in range(8):
            j = fc % 2
            k.op("dve", lambda e: e.tensor_tensor(out=tmp[:, j, 0:n], in0=xt[:, fc, 0:n], in1=rstd[:, 0:n],
                                                  op=ALU.mult), reads=[XTB, W["RSTD"]], writes=[W["TMP"][j]])
            k.op("act", lambda e: e.activation(out=out[:, fc, 0:n], in_=tmp[:, j, 0:n], func=AF.Identity,
                                               bias=Bsh[:, fc, mi:mi + 1], scale=A[:, fc, mi:mi + 1]),
                 reads=[W["TMP"][j], self.MOD], writes=[OUTB])

    def norm_work(self):
        k = self.k
        return dict(sq=k.sb([128, 8, 512], BF16), rstd=k.sb([128, 512], F32), tmp=k.sb([128, 2, 512], F32),
                    SQ=Buf(), RSTD=Buf(), TMP=[Buf(), Buf()])

    def load_w(self, dst, DSTB, src2d, nkc, split=1):
        v = src2d.rearrange("(kc p) n -> p kc n", p=128)
        for kc in range(nkc):
            self.k.dma("pool", dst[:, kc, :], v[:, kc, :], writes=[DSTB])

    def phase_mlp(self, l, b, last):
        k = self.k
        w1 = k.sb([128, 8, DFF], BF16)
        w2 = k.sb([128, 32, D], BF16)
        W1B, W2B = Buf(), Buf()
        self.load_w(w1, W1B, self.w1[l], 8)
        self.load_w(w2, W2B, self.w2[l], 32)
        W = self.norm_work()
        xt = k.sb([128, 8, 512], F32)
        XTB = Buf()
        hn = k.sb([128, 8, 512], BF16)
        HNB = Buf()
        h1 = k.sb([128, 32, 512], BF16)
        H1B = [Buf() for _ in range(32)]
        rl = k.sb([128, 2, 512], BF16)
        RLB = [Buf(), Buf()]
        for ti, (t0, n) in enumerate(TILES):
            if last and ti == 0:
                continue
            mi = 2 if ti == 0 else b
            k.dma("sp", xt[:, :, 0:n], self.fm(self.xT[b], t0, n), reads=[self.DB("xT", b, ti)], writes=[XTB])
            self.norm_tile(xt, XTB, hn, HNB, self.A2, self.mod[:, 24:32, :], mi, n, W)
            for oc in range(32):
                p, P = self.pb()
                for kc in range(8):
                    k.op("pe", lambda e: e.matmul(p[:, 0:n], lhsT=w1[:, kc, oc * 128:(oc + 1) * 128], rhs=hn[:, kc, 0:n],
                                                  start=(kc == 0), stop=(kc == 7)),
                         reads=[W1B, HNB], writes=[P], sig=(kc == 7))
                j = oc % 2
                k.op("act", lambda e: e.activation(out=rl[:, j, 0:n], in_=p[:, 0:n], func=AF.Relu),
                     reads=[P], writes=[RLB[j]])
                k.op("pool", lambda e: e.tensor_tensor(out=h1[:, oc, 0:n], in0=rl[:, j, 0:n], in1=rl[:, j, 0:n],
                                                       op=ALU.mult), reads=[RLB[j]], writes=[H1B[oc]])
            for oc in range(8):
                p, P = self.pb()
                for kc in range(32):
                    k.op("pe", lambda e: e.matmul(p[:, 0:n], lhsT=w2[:, kc, oc * 128:(oc + 1) * 128], rhs=h1[:, kc, 0:n],
                                                  start=(kc == 0), stop=(kc == 31)),
                         reads=[W2B, H1B[kc]], writes=[P], sig=(kc == 31))
                k.op("dve", lambda e: e.scalar_tensor_tensor(out=xt[:, oc, 0:n], in0=p[:, 0:n],
                                                             scalar=self.mod[:, 40 + oc, mi:mi + 1],
                                                             in1=xt[:, oc, 0:n], op0=ALU.mult, op1=ALU.add),
                     reads=[P, self.MOD, XTB], writes=[XTB])
            if last:
                k.dma("sp", self.out[b].rearrange("(fc p) t -> p fc t", p=128)[:, :, t0 - CTX:t0 - CTX + n],
                      xt[:, :, 0:n], reads=[XTB], writes=[self.DB("out", b, ti)])
            else:
                k.dma("sp", self.fm(self.xT[b], t0, n), xt[:, :, 0:n], reads=[XTB], writes=[self.DB("xT", b, ti)])
        k.phase_reset()

    def phase_hy_inproj(self, l, b):
        k = self.k
        i = l // 2
        win = k.sb([128, 8, 1920], BF16)
        WB = Buf()
        self.load_w(win, WB, self.hy_win[i], 8)
        W = self.norm_work()
        xt = [k.sb([128, 8, 512], F32) for _ in range(2)]
        XTB = [Buf(), Buf()]
        hn = k.sb([128, 8, 512], BF16)
        HNB = Buf()
        cs_t = k.sb([128, 2, 512], F32)
        CSB = Buf()
        sq = k.sb([128, 512], BF16)
        SQB = Buf()
        rs = k.sb([128, 512], F32)
        RSB = Buf()
        xn = k.sb([128, 512], BF16)
        XNB = Buf()
        t1 = k.sb([128, 512], F32)
        t2 = k.sb([128, 512], F32)
        T1B, T2B = Buf(), Buf()
        qk = k.sb([128, 6, 512], BF16)
        QKB = Buf()
        vt = k.sb([128, 4, 384], BF16)
        VTB = Buf()
        xro = k.sb([128, 4, 512], F32)
        XRB = Buf()
        ggo = k.sb([128, 4, 512], BF16)
        GGB = Buf()
        z2 = k.sb([128, 512], F32)
        Z2B = Buf()
        k.op("dve", lambda e: e.memset(vt, 1.0), writes=[VTB])
        for ti, (t0, n) in enumerate(TILES):
            mi = 2 if ti == 0 else b
            x_ = xt[ti % 2]
            XB = XTB[ti % 2]
            k.dma("sp", x_[:, :, 0:n], self.fm(self.xT[b], t0, n), reads=[self.DB("xT", b, ti)], writes=[XB])
            if ti > 0:
                k.dma("sp", cs_t[:, 0, 0:n], self.cosd[:, t0 - CTX:t0 - CTX + n], writes=[CSB])
                k.dma("sp", cs_t[:, 1, 0:n], self.sind[:, t0 - CTX:t0 - CTX + n], writes=[CSB])
            self.norm_tile(x_, XB, hn, HNB, self.A1, self.mod[:, 0:8, :], mi, n, W)
            for oc in range(6):
                p, P = self.pb()
                for kc in range(8):
                    k.op("pe", lambda e: e.matmul(p[:, 0:n], lhsT=win[:, kc, oc * 128:(oc + 1) * 128], rhs=hn[:, kc, 0:n],
                                                  start=(kc == 0), stop=(kc == 7)), reads=[WB, HNB], writes=[P],
                         sig=(kc == 7))
                k.op("act", lambda e: e.activation(out=sq[:, 0:n], in_=p[:, 0:n], func=AF.Square), reads=[P], writes=[SQB])
                p2, P2 = self.pb()
                k.op("pe", lambda e: e.matmul(p2[:, 0:n], lhsT=self.bones_bf, rhs=sq[:, 0:n], start=True, stop=True),
                     reads=[SQB, self.CONST], writes=[P2])
                k.op("act", lambda e: e.activation(out=rs[:, 0:n], in_=p2[:, 0:n], func=AF.Sqrt, bias=EPS,
                                                   scale=1.0 / 64), reads=[P2], writes=[RSB])
                k.op("dve", lambda e: e.reciprocal(out=rs[:, 0:n], in_=rs[:, 0:n]), reads=[RSB], writes=[RSB])
                g = self.V("gq_%d" % i) if oc < 4 else self.V("gk_%d" % i)
                dst = qk[:, oc, 0:n] if ti == 0 else xn[:, 0:n]
                DSTB = QKB if ti == 0 else XNB
                k.op("dve", lambda e: e.scalar_tensor_tensor(out=dst, in0=p[:, 0:n], scalar=g, in1=rs[:, 0:n],
                                                             op0=ALU.mult, op1=ALU.mult),
                     reads=[P, RSB, self.VB], writes=[DSTB])
                if ti > 0:
                    p3, P3 = self.pb()
                    k.op("pe", lambda e: e.matmul(p3[:, 0:n], lhsT=self.pswap_bf, rhs=xn[:, 0:n], start=True, stop=True),
                         reads=[XNB, self.CONST], writes=[P3])
                    k.op("pool", lambda e: e.tensor_tensor(out=t1[:, 0:n], in0=xn[:, 0:n], in1=cs_t[:, 0, 0:n],
                                                           op=ALU.mult), reads=[XNB, CSB], writes=[T1B])
                    k.op("dve", lambda e: e.tensor_tensor(out=t2[:, 0:n], in0=p3[:, 0:n], in1=cs_t[:, 1, 0:n],
                                                          op=ALU.mult), reads=[P3, CSB], writes=[T2B])
                    k.op("dve", lambda e: e.tensor_tensor(out=qk[:, oc, 0:n], in0=t1[:, 0:n], in1=t2[:, 0:n],
                                                          op=ALU.add), reads=[T1B, T2B], writes=[QKB])
            k.dma("sp", self.fm(self.qT, t0, n), qk[:, 0:4, 0:n], reads=[QKB], writes=[self.DB("qT", ti)])
            k.dma("sp", self.fm(self.kT2, t0, n), qk[:, 4:6, 0:n], reads=[QKB], writes=[self.DB("kT2", ti)])
            nst = n // 128
            for st in range(nst):
                p, P = self.pb()
                for kc in range(8):
                    k.op("pe", lambda e: e.matmul(p[:, 0:128], lhsT=hn[:, kc, st * 128:(st + 1) * 128],
                                                  rhs=win[:, kc, 768:896], start=(kc == 0), stop=(kc == 7)),
                         reads=[WB, HNB], writes=[P], sig=(kc == 7))
                vv = vt[:, st, :].rearrange("p (h c) -> p h c", c=192)[:, :, 64:128]
                k.op("act", lambda e: e.activation(out=vv, in_=p[:, 0:128].rearrange("p (h c) -> p h c", c=64),
                                                   func=AF.Copy), reads=[P], writes=[VTB])
            c0 = t0 // 128
            k.dma("sp", self.vtok[c0:c0 + nst].rearrange("c p f -> p c f"), vt[:, 0:nst, :], reads=[VTB],
                  writes=[self.DB("vtok", ti)])
            for oc in range(4):
                p, P = self.pb()
                for kc in range(8):
                    k.op("pe", lambda e: e.matmul(p[:, 0:n], lhsT=win[:, kc, 896 + oc * 128:896 + (oc + 1) * 128],
                                                  rhs=hn[:, kc, 0:n], start=(kc == 0), stop=(kc == 7)),
                         reads=[WB, HNB], writes=[P], sig=(kc == 7))
                k.op("act", lambda e: e.activation(out=xro[:, oc, 0:n], in_=p[:, 0:n], func=AF.Copy), reads=[P],
                     writes=[XRB])
            k.dma("sp", self.fm(self.xr, t0, n), xro[:, :, 0:n], reads=[XRB], writes=[self.DB("xr", ti)])
            for oc in range(4):
                p, P = self.pb()
                for kc in range(8):
                    k.op("pe", lambda e: e.matmul(p[:, 0:n], lhsT=win[:, kc, 1408 + oc * 128:1408 + (oc + 1) * 128],
                                                  rhs=hn[:, kc, 0:n], start=(kc == 0), stop=(kc == 7)),
                         reads=[WB, HNB], writes=[P], sig=(kc == 7))
                self.gelu_from_psum(p, P, ggo[:, oc, 0:n], GGB, n, z2, Z2B, t1, T1B)
            k.dma("sp", self.fm(self.gg, t0, n), ggo[:, :, 0:n], reads=[GGB], writes=[self.DB("gg", ti)])
        k.phase_reset()

    def gelu_from_psum(self, p, P, dst, DSTB, n, z2, Z2B, t1, T1B):
        k = self.k
        k.op("act", lambda e: e.activation(out=z2[:, 0:n], in_=p[:, 0:n], func=AF.Square), reads=[P], writes=[Z2B])
        k.op("dve", lambda e: e.tensor_scalar(out=z2[:, 0:n], in0=z2[:, 0:n], scalar1=0.044715, scalar2=1.0,
                                              op0=ALU.mult, op1=ALU.add), reads=[Z2B], writes=[Z2B])
        k.op("dve", lambda e: e.tensor_tensor(out=z2[:, 0:n], in0=z2[:, 0:n], in1=p[:, 0:n], op=ALU.mult),
             reads=[Z2B, P], writes=[Z2B])
        k.op("act", lambda e: e.activation(out=t1[:, 0:n], in_=z2[:, 0:n], func=AF.Sigmoid, scale=GELU_C),
             reads=[Z2B], writes=[T1B])
        k.op("dve", lambda e: e.tensor_tensor(out=dst, in0=t1[:, 0:n], in1=p[:, 0:n], op=ALU.mult),
             reads=[T1B, P], writes=[DSTB])

    def phase_rglru(self, l, b):
        k = self.k
        i = l // 2
        gw = k.sb([128, 16, 128], BF16)
        GWB = Buf()
        k.dma("pool", gw, self.hy_gw[i].rearrange("d g c p m -> p (d g c) m"), writes=[GWB])
        c1 = k.sb([128, 8], F32)
        c2 = k.sb([128, 8], F32)
        C1B = Buf()
        k.op("act", lambda e: e.activation(out=c1, in_=self.V("lam_%d" % i, 0, 8), func=AF.Exp, scale=-1.0),
             reads=[self.VB], writes=[C1B])
        k.op("act", lambda e: e.activation(out=c1, in_=c1, func=AF.Ln, bias=1.0), reads=[C1B], writes=[C1B])
        k.op("dve", lambda e: e.tensor_scalar(out=c2, in0=c1, scalar1=-16.0, scalar2=None, op0=ALU.mult),
             reads=[C1B], writes=[C1B])
        k.op("dve", lambda e: e.tensor_scalar(out=c1, in0=c1, scalar1=-8.0, scalar2=None, op0=ALU.mult),
             reads=[C1B], writes=[C1B])
        x = k.sb([128, T], F32)
        xc = k.sb([128, T], F32)
        xcb = k.sb([128, T], BF16)
        r = k.sb([128, T], F32)
        ig = k.sb([128, T], F32)
        a = k.sb([128, T], F32)
        u = k.sb([128, T], F32)
        h = [k.sb([128, T], F32) for _ in range(2)]
        ggt = k.sb([128, T], BF16)
        rec = k.sb([128, T], BF16)
        XB, XCB, XCBB, RB, IB, AB, UB, GB, RECB = [Buf() for _ in range(9)]
        HB = [Buf(), Buf()]
        segs = [(0, CTX), (CTX, T)]
        for c in range(4):
            k.dma("sp", x, self.xr[c * 128:(c + 1) * 128, :], reads=[self.DB("xr", ti) for ti in range(9)], writes=[XB])
            k.dma("sp", ggt, self.gg[c * 128:(c + 1) * 128, :], reads=[self.DB("gg", ti) for ti in range(9)],
                  writes=[GB])
            k.op("act", lambda e: e.activation(out=xc, in_=x, func=AF.Identity, bias=self.V("convb_%d" % i, c),
                                               scale=self.V("convw_%d" % i, 2 * 4 + c)),
                 reads=[XB, self.VB], writes=[XCB])
            for (s0, s1) in segs:
                for j in (0, 1, 3):
                    sh = j - 2
                    a0 = max(s0, s0 - sh)
                    a1 = min(s1, s1 - sh)
                    k.op("dve", lambda e: e.scalar_tensor_tensor(out=xc[:, a0:a1], in0=x[:, a0 + sh:a1 + sh],
                                                                 scalar=self.V("convw_%d" % i, j * 4 + c),
                                                                 in1=xc[:, a0:a1], op0=ALU.mult, op1=ALU.add),
                         reads=[XB, XCB, self.VB], writes=[XCB])
            k.op("act", lambda e: e.activation(out=xcb, in_=xc, func=AF.Copy), reads=[XCB], writes=[XCBB])
            for d in range(2):
                for (t0, n) in TILES:
                    for g, (dst, DB_) in enumerate(((r, RB), (ig, IB))):
                        p, P = self.pb()
                        k.op("pe", lambda e: e.matmul(p[:, 0:n], lhsT=gw[:, (d * 2 + g) * 4 + c, :], rhs=xcb[:, t0:t0 + n],
                                                      start=True, stop=True), reads=[GWB, XCBB], writes=[P])
                        k.op("act", lambda e: e.activation(out=dst[:, t0:t0 + n], in_=p[:, 0:n], func=AF.Sigmoid,
                                                           bias=self.V("gateb_%d" % i, (d * 2 + g) * 4 + c)),
                             reads=[P, self.VB], writes=[DB_])
                k.op("act", lambda e: e.activation(out=a, in_=r, func=AF.Exp, scale=c1[:, d * 4 + c:d * 4 + c + 1]),
                     reads=[RB, C1B], writes=[AB])
                k.op("act", lambda e: e.activation(out=u, in_=r, func=AF.Exp, scale=c2[:, d * 4 + c:d * 4 + c + 1]),
                     reads=[RB, C1B], writes=[UB])
                k.op("act", lambda e: e.activation(out=u, in_=u, func=AF.Sqrt, bias=1.0, scale=-1.0),
                     reads=[UB], writes=[UB])
                k.op("dve", lambda e: e.tensor_tensor(out=ig, in0=ig, in1=xc, op=ALU.mult), reads=[IB, XCB], writes=[IB])
                k.op("dve", lambda e: e.tensor_tensor(out=u, in0=u, in1=ig, op=ALU.mult), reads=[UB, IB], writes=[UB])
                hd = h[d]
                if d == 0:
                    k.op("dve", lambda e: e.tensor_tensor_scan(out=hd, data0=a, data1=u, initial=0.0, op0=ALU.mult,
                                                               op1=ALU.add), reads=[AB, UB], writes=[HB[d]])
                else:
                    k.op("dve", lambda e: e.tensor_tensor_scan(out=hd[:, 0:CTX][:, ::-1], data0=a[:, 0:CTX][:, ::-1],
                                                               data1=u[:, 0:CTX][:, ::-1], initial=0.0, op0=ALU.mult,
                                                               op1=ALU.add), reads=[AB, UB], writes=[HB[d]])
                    k.op("dve", lambda e: e.tensor_tensor_scan(out=hd[:, CTX:T][:, ::-1], data0=a[:, CTX:T][:, ::-1],
                                                               data1=u[:, CTX:T][:, ::-1], initial=hd[:, 0:1],
                                                               op0=ALU.mult, op1=ALU.add), reads=[AB, UB, HB[d]],
                         writes=[HB[d]])
            if "rgd" in self.debug and c == 0 and b == 0:
                rgd = self.nc.dram_tensor("rgd", [8, 128, T], F32, kind="ExternalOutput").ap()
                for j_, (t_, B_) in enumerate(((x, XB), (xc, XCB), (r, RB), (ig, IB), (a, AB), (u, UB), (h[0], HB[0]),
                                               (h[1], HB[1]))):
                    k.dma("sp", rgd[j_], t_, reads=[B_], writes=[Buf()])
                c1d = self.nc.dram_tensor("c1d", [2, 128, 8], F32, kind="ExternalOutput").ap()
                k.dma("sp", c1d[0], c1, reads=[C1B], writes=[Buf()])
                k.dma("sp", c1d[1], c2, reads=[C1B], writes=[Buf()])
            k.op("dve", lambda e: e.tensor_tensor(out=h[0], in0=h[0], in1=h[1], op=ALU.add), reads=[HB[0], HB[1]],
                 writes=[HB[0]])
            k.op("dve", lambda e: e.tensor_tensor(out=rec, in0=h[0], in1=ggt, op=ALU.mult), reads=[HB[0], GB],
                 writes=[RECB])
            k.dma("sp", self.rec[c * 128:(c + 1) * 128, :], rec, reads=[RECB], writes=[self.DB("rec", c)])
        k.phase_reset()

    def phase_attn(self, l, b):
        k = self.k
        i = l // 2
        kt = k.sb([128, 2, T], BF16)
        KTB = Buf()
        k.dma("sp", kt, self.kT2.rearrange("(c p) t -> p c t", p=128), reads=[self.DB("kT2", ti) for ti in range(9)],
              writes=[KTB])
        vt = k.sb([128, T // 128, 384], BF16)
        VTB = Buf()
        k.dma("sp", vt, self.vtok.rearrange("c p f -> p c f"), reads=[self.DB("vtok", ti) for ti in range(9)],
              writes=[VTB])
        wo = k.sb([128, 8, D], BF16)
        WOB = Buf()
        self.load_w(wo, WOB, self.hy_wout[i], 8)
        xt = k.sb([128, 8, 512], F32)
        XB = Buf()
        q = k.sb([128, 4, 512], BF16)
        QB = Buf()
        rc = k.sb([128, 4, 512], BF16)
        RCB = Buf()
        att = k.sb([128, 4, 512], BF16)
        ATB = Buf()
        NPT = 6
        pt = [k.sb([128, 512], BF16) for _ in range(NPT)]
        PTB = [Buf() for _ in range(NPT)]
        den = k.sb([128, 512], F32)
        DNB = Buf()
        ptr = 0
        RECALL = [self.DB("rec", c) for c in range(4)]
        for ti, (t0, n) in enumerate(TILES):
            mi = 2 if ti == 0 else b
            nkc = 2 if ti == 0 else T // 128
            k.dma("sp", xt[:, :, 0:n], self.fm(self.xT[b], t0, n), reads=[self.DB("xT", b, ti)], writes=[XB])
            k.dma("sp", q[:, :, 0:n], self.fm(self.qT, t0, n), reads=[self.DB("qT", ti)], writes=[QB])
            k.dma("sp", rc[:, :, 0:n], self.fm(self.rec, t0, n), reads=RECALL, writes=[RCB])
            for hh in range(8):
                kv, hp, fc = hh // 4, hh % 2, hh // 2
                lo, hi = hp * 64, hp * 64 + 64
                olo, ohi = (1 - hp) * 64, (1 - hp) * 64 + 64
                po, PO = self.ps[6 + hh % 2], self.PS[6 + hh % 2]
                voff = kv * 192 + (64 if hp == 0 else 0)
                LA = 3
                slots = []

                def pv(kc, pj):
                    k.op("pe", lambda e: e.matmul(po[:, 0:n], lhsT=vt[:, kc, voff:voff + 128], rhs=pt[pj][:, 0:n],
                                                  start=(kc == 0), stop=(kc == nkc - 1)),
                         reads=[VTB, PTB[pj]], writes=[PO], sig=(kc == nkc - 1))
                for kc in range(nkc):
                    sbi = ptr % 6
                    ps_, PSB = self.ps[sbi], self.PS[sbi]
                    k.op("pe", lambda e: e.matmul(ps_[:, 0:n], lhsT=kt[lo:hi, kv, kc * 128:(kc + 1) * 128],
                                                  rhs=q[lo:hi, fc, 0:n], start=True, stop=True),
                         reads=[KTB, QB], writes=[PSB])
                    pj = ptr % NPT
                    ptr += 1
                    k.op("act", lambda e: e.activation(out=pt[pj][:, 0:n], in_=ps_[:, 0:n], func=AF.Exp, scale=0.125),
                         reads=[PSB], writes=[PTB[pj]])
                    slots.append((kc, pj))
                    if len(slots) > LA:
                        pv(*slots.pop(0))
                while slots:
                    pv(*slots.pop(0))
                k.op("act", lambda e: e.activation(out=den[lo:hi, 0:n], in_=po[olo:ohi, 0:n], func=AF.Copy),
                     reads=[PO], writes=[DNB])
                k.op("dve", lambda e: e.reciprocal(out=den[lo:hi, 0:n], in_=den[lo:hi, 0:n]), reads=[DNB], writes=[DNB])
                k.op("dve", lambda e: e.tensor_tensor(out=att[lo:hi, fc, 0:n], in0=po[lo:hi, 0:n], in1=den[lo:hi, 0:n],
                                                      op=ALU.mult), reads=[PO, DNB], writes=[ATB])
            for oc in range(8):
                p, P = self.pb()
                for kc in range(8):
                    src = att[:, kc, 0:n] if kc < 4 else rc[:, kc - 4, 0:n]
                    k.op("pe", lambda e: e.matmul(p[:, 0:n], lhsT=wo[:, kc, oc * 128:(oc + 1) * 128], rhs=src,
                                                  start=(kc == 0), stop=(kc == 7)),
                         reads=[WOB, ATB, RCB], writes=[P], sig=(kc == 7))
                k.op("dve", lambda e: e.scalar_tensor_tensor(out=xt[:, oc, 0:n], in0=p[:, 0:n],
                                                             scalar=self.mod[:, 16 + oc, mi:mi + 1], in1=xt[:, oc, 0:n],
                                                             op0=ALU.mult, op1=ALU.add),
                     reads=[P, self.MOD, XB], writes=[XB])
            k.dma("sp", self.fm(self.xT[b], t0, n), xt[:, :, 0:n], reads=[XB], writes=[self.DB("xT", b, ti)])
        k.phase_reset()

    def phase_rwkv(self, l, b):
        last = (l == DEPTH - 1)
        self.rw_norm(l, b)
        if self.rw_stop >= 1:
            self.rw_proj(l, b)
        for d in range(2):
            if self.rw_stop >= 2 + d:
                self.rw_wkv(l, b, d)
        if self.rw_stop >= 4:
            self.rw_out(l, b, last)

    def rw_norm(self, l, b):
        k = self.k
        W = self.norm_work()
        xt = [k.sb([128, 8, 512], F32) for _ in range(2)]
        XB = [Buf(), Buf()]
        hn = [k.sb([128, 8, 512], F32) for _ in range(2)]
        HB = [Buf(), Buf()]
        for ti, (t0, n) in enumerate(TILES):
            mi = 2 if ti == 0 else b
            j = ti % 2
            k.dma("sp", xt[j][:, :, 0:n], self.fm(self.xT[b], t0, n), reads=[self.DB("xT", b, ti)], writes=[XB[j]])
            self.norm_tile(xt[j], XB[j], hn[j], HB[j], self.A1, self.mod[:, 0:8, :], mi, n, W)
            k.dma("sp", self.fm(self.hnT, t0, n), hn[j][:, :, 0:n], reads=[HB[j]], writes=[self.DB("hnT", ti)])
        k.phase_reset()

    def rw_proj(self, l, b):
        k = self.k
        i = l // 2
        wr = k.sb([128, 8, D], BF16)
        wk = k.sb([128, 8, D], BF16)
        wvs = k.sb([128, 8, D], BF16)
        ld = k.sb([128, 8, 256], BF16)
        lu = k.sb([64, 4, D], BF16)
        gd = k.sb([128, 8, 128], BF16)
        gu = k.sb([128, D], BF16)
        WB = Buf()
        self.load_w(wr, WB, self.rw_wrkv[i, 0], 8)
        self.load_w(wk, WB, self.rw_wrkv[i, 1], 8)
        self.load_w(wvs, WB, self.rw_wvst[i], 8)
        self.load_w(ld, WB, self.rw_ld[i], 8)
        self.load_w(gd, WB, self.rw_gd[i], 8)
        k.dma("pool", lu, self.rw_lu[i].rearrange("q p n -> p q n"), writes=[WB])
        k.dma("pool", gu, self.rw_gu[i], writes=[WB])
        omka = k.sb([128, 8], F32)
        OMB = Buf()
        k.op("dve", lambda e: e.tensor_scalar(out=omka, in0=self.V("ka_%d" % i, 0, 8), scalar1=-1.0, scalar2=1.0,
                                              op0=ALU.mult, op1=ALU.add), reads=[self.VB], writes=[OMB])
        hh = k.sb([128, 8, 258], F32)
        HHB = Buf()
        xx = k.sb([128, 8, 256], F32)
        XXB = Buf()
        L = [k.sb([128, 8, 256], BF16) for _ in range(6)]
        LB = [Buf() for _ in range(6)]
        rt = k.sb([128, 8, 256], F32)
        kt = k.sb([128, 8, 256], F32)
        kkt = k.sb([128, 8, 256], F32)
        kd = [k.sb([128, 8, 256], F32) for _ in range(2)]
        o1 = k.sb([128, 8, 256], F32)
        RTB, KTB, KKB, O1B = Buf(), Buf(), Buf(), Buf()
        KDB = [Buf(), Buf()]
        vst = k.sb([128, 2, 512], F32)
        VSB = [Buf(), Buf()]
        sm = k.sb([128, 512], BF16)
        SMB = Buf()
        at = k.sb([128, 2, 512], F32)
        ATB = [Buf(), Buf()]
        w1 = k.sb([128, 2, 512], F32)
        W1B = [Buf(), Buf()]
        sqb = k.sb([128, 512], BF16)
        SQB = Buf()
        bones_f = self.cs[:, 128:256]
        for ti, (t0, n) in enumerate(WTILES):
            seg0, seg1 = (0, CTX) if ti == 0 else (CTX, T)
            lo = max(t0 - 1, seg0)
            hi = min(t0 + n + 1, seg1)
            k.dma("sp", hh[:, :, lo - (t0 - 1):hi - (t0 - 1)], self.fm(self.hnT, lo, hi - lo),
                  reads=[self.DB("hnT", j) for j in range(9)], writes=[HHB])
            if lo != t0 - 1:
                k.op("dve", lambda e: e.memset(hh[:, :, 0:1], 0.0), writes=[HHB])
            if hi != t0 + n + 1:
                k.op("dve", lambda e: e.memset(hh[:, :, n + 1:n + 2], 0.0), writes=[HHB])
            h = hh[:, :, 1:n + 1]
            k.op("dve", lambda e: e.tensor_tensor(out=xx[:, :, 0:n], in0=hh[:, :, 0:n], in1=hh[:, :, 2:n + 2], op=ALU.add),
                 reads=[HHB], writes=[XXB])
            k.op("dve", lambda e: e.scalar_tensor_tensor(out=xx[:, :, 0:n], in0=xx[:, :, 0:n], scalar=0.5, in1=h,
                                                         op0=ALU.mult, op1=ALU.subtract), reads=[XXB, HHB], writes=[XXB])
            for j in range(6):
                for fc in range(8):
                    eng = "dve"
                    k.op(eng, lambda e: e.scalar_tensor_tensor(out=L[j][:, fc, 0:n], in0=xx[:, fc, 0:n],
                                                               scalar=self.V("mu_%d" % i, j * 8 + fc), in1=hh[:, fc, 1:n + 1],
                                                               op0=ALU.mult, op1=ALU.add),
                         reads=[XXB, HHB, self.VB], writes=[LB[j]])

            def proj(w, Lj, LjB, dst, DSTB):
                for oc in range(8):
                    p, P = self.pb()
                    for kc in range(8):
                        k.op("pe", lambda e: e.matmul(p[:, 0:n], lhsT=w[:, kc, oc * 128:(oc + 1) * 128], rhs=Lj[:, kc, 0:n],
                                                      start=(kc == 0), stop=(kc == 7)), reads=[WB, LjB], writes=[P],
                             sig=(kc == 7))
                    k.op("act", lambda e: e.activation(out=dst[:, oc, 0:n], in_=p[:, 0:n], func=AF.Copy), reads=[P],
                         writes=[DSTB])
            proj(wr, L[0], LB[0], rt, RTB)
            k.dma("sp", self.fm(self.rT, t0, n), rt[:, :, 0:n], reads=[RTB], writes=[self.DB("rT", ti)])
            proj(wk, L[2], LB[2], kt, KTB)
            for ch in range(n // 64):
                p, P = self.pb()
                for hp in range(2):
                    for kc in range(8):
                        k.op("pe", lambda e: e.matmul(p[hp * 64:(hp + 1) * 64, :], lhsT=L[3][:, kc, ch * 64:(ch + 1) * 64],
                                                      rhs=wvs[:, kc, hp * 512:(hp + 1) * 512], start=(kc == 0),
                                                      stop=(kc == 7)), reads=[WB, LB[3]], writes=[P],
                             sig=(kc == 7 and hp == 1))
                j = ch % 2
                k.op("act", lambda e: e.activation(out=vst[:, j, :], in_=p, func=AF.Copy), reads=[P], writes=[VSB[j]])
                k.dma("sp", self.Vst[t0 // 64 + ch], vst[:, j, :], reads=[VSB[j]], writes=[self.DB("Vst", ti)])
            for oc in range(8):
                k.op("dve", lambda e: e.tensor_scalar(out=kkt[:, oc, 0:n], in0=kt[:, oc, 0:n],
                                                      scalar1=self.V("kk_%d" % i, oc), scalar2=None, op0=ALU.mult),
                     reads=[KTB, self.VB], writes=[KKB])
                k.op("act", lambda e: e.activation(out=sqb[:, 0:n], in_=kkt[:, oc, 0:n], func=AF.Square), reads=[KKB],
                     writes=[SQB])
                p, P = self.pb()
                k.op("pe", lambda e: e.matmul(p[:, 0:n], lhsT=self.bones_bf, rhs=sqb[:, 0:n], start=True, stop=True),
                     reads=[SQB, self.CONST], writes=[P])
                j = oc % 2
                k.op("act", lambda e: e.activation(out=w1[:, j, 0:n], in_=p[:, 0:n], func=AF.Sqrt), reads=[P],
                     writes=[W1B[j]])
                k.op("dve", lambda e: e.tensor_scalar(out=w1[:, j, 0:n], in0=w1[:, j, 0:n], scalar1=1e-12, scalar2=None,
                                                      op0=ALU.max), reads=[W1B[j]], writes=[W1B[j]])
                k.op("dve", lambda e: e.reciprocal(out=w1[:, j, 0:n], in_=w1[:, j, 0:n]), reads=[W1B[j]], writes=[W1B[j]])
                k.op("dve", lambda e: e.tensor_tensor(out=kkt[:, oc, 0:n], in0=kkt[:, oc, 0:n], in1=w1[:, j, 0:n],
                                                      op=ALU.mult), reads=[KKB, W1B[j]], writes=[KKB])
            k.dma("sp", self.fm(self.kkT, t0, n), kkt[:, :, 0:n], reads=[KKB], writes=[self.DB("kkT", ti)])
            p, P = self.pb()
            for kc in range(8):
                k.op("pe", lambda e: e.matmul(p[:, 0:n], lhsT=gd[:, kc, :], rhs=L[5][:, kc, 0:n], start=(kc == 0),
                                              stop=(kc == 7)), reads=[WB, LB[5]], writes=[P], sig=(kc == 7))
            k.op("act", lambda e: e.activation(out=sm[:, 0:n], in_=p[:, 0:n], func=AF.Sigmoid), reads=[P], writes=[SMB])
            for oc in range(8):
                p, P = self.pb()
                k.op("pe", lambda e: e.matmul(p[:, 0:n], lhsT=gu[:, oc * 128:(oc + 1) * 128], rhs=sm[:, 0:n], start=True,
                                              stop=True), reads=[WB, SMB], writes=[P])
                k.op("act", lambda e: e.activation(out=o1[:, oc, 0:n], in_=p[:, 0:n], func=AF.Copy), reads=[P],
                     writes=[O1B])
            k.dma("sp", self.fm(self.gT, t0, n), o1[:, :, 0:n], reads=[O1B], writes=[self.DB("gT", ti)])
            for d in range(2):
                p, P = self.pb()
                for kc in range(8):
                    k.op("pe", lambda e: e.matmul(p[0:64, 0:n], lhsT=ld[:, kc, (d * 2) * 64:(d * 2 + 1) * 64],
                                                  rhs=L[1][:, kc, 0:n], start=(kc == 0), stop=(kc == 7)),
                         reads=[WB, LB[1]], writes=[P], sig=(kc == 7))
                k.op("act", lambda e: e.activation(out=sm[0:64, 0:n], in_=p[0:64, 0:n], func=AF.Tanh), reads=[P],
                     writes=[SMB])
                for oc in range(8):
                    p, P = self.pb()
                    k.op("pe", lambda e: e.matmul(p[:, 0:n], lhsT=lu[:, d * 2, oc * 128:(oc + 1) * 128], rhs=sm[0:64, 0:n],
                                                  start=True, stop=True), reads=[WB, SMB], writes=[P])
                    k.op("act", lambda e: e.activation(out=o1[:, oc, 0:n], in_=p[:, 0:n], func=AF.Sigmoid,
                                                       bias=self.V("lb_%d" % i, (d * 2) * 8 + oc)),
                         reads=[P, self.VB], writes=[O1B])
                    k.op("dve", lambda e: e.tensor_scalar(out=o1[:, oc, 0:n], in0=o1[:, oc, 0:n], scalar1=-DECAY_SCALE,
                                                           scalar2=None, op0=ALU.mult), reads=[O1B], writes=[O1B])
                k.dma("sp", self.fm(self.lwT[d], t0, n), o1[:, :, 0:n], reads=[O1B], writes=[self.DB("lwT", d, ti)])
                p, P = self.pb()
                for kc in range(8):
                    k.op("pe", lambda e: e.matmul(p[0:64, 0:n], lhsT=ld[:, kc, (d * 2 + 1) * 64:(d * 2 + 2) * 64],
                                                  rhs=L[4][:, kc, 0:n], start=(kc == 0), stop=(kc == 7)),
                         reads=[WB, LB[4]], writes=[P], sig=(kc == 7))
                k.op("act", lambda e: e.activation(out=sm[0:64, 0:n], in_=p[0:64, 0:n], func=AF.Copy), reads=[P],
                     writes=[SMB])
                for oc in range(8):
                    p, P = self.pb()
                    k.op("pe", lambda e: e.matmul(p[:, 0:n], lhsT=lu[:, d * 2 + 1, oc * 128:(oc + 1) * 128],
                                                  rhs=sm[0:64, 0:n], start=True, stop=True), reads=[WB, SMB], writes=[P])
                    j = oc % 2
                    k.op("act", lambda e: e.activation(out=at[:, j, 0:n], in_=p[:, 0:n], func=AF.Sigmoid,
                                                       bias=self.V("lb_%d" % i, (d * 2 + 1) * 8 + oc)),
                         reads=[P, self.VB], writes=[ATB[j]])
                    k.op("dve", lambda e: e.tensor_tensor(out=o1[:, oc, 0:n], in0=kkt[:, oc, 0:n], in1=at[:, j, 0:n],
                                                          op=ALU.mult), reads=[KKB, ATB[j]], writes=[O1B])
                    k.op("dve", lambda e: e.tensor_scalar(out=at[:, j, 0:n], in0=at[:, j, 0:n],
                                                           scalar1=self.V("ka_%d" % i, oc), scalar2=omka[:, oc:oc + 1],
                                                           op0=ALU.mult, op1=ALU.add),
                         reads=[ATB[j], self.VB, OMB], writes=[ATB[j]])
                    k.op("dve", lambda e: e.tensor_tensor(out=kd[d][:, oc, 0:n], in0=at[:, j, 0:n], in1=kt[:, oc, 0:n],
                                                          op=ALU.mult), reads=[ATB[j], KTB], writes=[KDB[d]])
                k.dma("sp", self.fm(self.bT[d], t0, n), o1[:, :, 0:n], reads=[O1B], writes=[self.DB("bT", d, ti)])
                k.dma("sp", self.fm(self.kdT[d], t0, n), kd[d][:, :, 0:n], reads=[KDB[d]], writes=[self.DB("kdT", d, ti)])
            for oc in range(8):
                j = oc % 2
                k.op("dve", lambda e: e.tensor_tensor(out=at[:, j, 0:n], in0=kd[0][:, oc, 0:n], in1=kd[1][:, oc, 0:n],
                                                      op=ALU.add), reads=[KDB[0], KDB[1]], writes=[ATB[j]])
                k.op("dve", lambda e: e.scalar_tensor_tensor(out=at[:, j, 0:n], in0=rt[:, oc, 0:n],
                                                             scalar=self.V("rk_%d" % i, oc), in1=at[:, j, 0:n],
                                                             op0=ALU.mult, op1=ALU.mult),
                     reads=[RTB, ATB[j], self.VB], writes=[ATB[j]])
                p, P = self.pb()
                k.op("pe", lambda e: e.matmul(p[:, 0:n], lhsT=bones_f, rhs=at[:, j, 0:n], start=True, stop=True),
                     reads=[ATB[j], self.CS], writes=[P])
                k.op("act", lambda e: e.activation(out=w1[:, j, 0:n], in_=p[:, 0:n], func=AF.Copy), reads=[P],
                     writes=[W1B[j]])
                hpv = [(2 * oc) % 2, (2 * oc + 1) % 2]
                p2, P2 = self.pb()
                for half in range(2):
                    hd_ = 2 * oc + half
                    c0 = (hd_ % 2) * 512 + (hd_ // 2) * 64
                    for kc in range(8):
                        k.op("pe", lambda e: e.matmul(p2[half * 64:(half + 1) * 64, 0:n], lhsT=wvs[:, kc, c0:c0 + 64],
                                                      rhs=L[3][:, kc, 0:n], start=(kc == 0), stop=(kc == 7)),
                             reads=[WB, LB[3]], writes=[P2], sig=(kc == 7 and half == 1))
                k.op("dve", lambda e: e.tensor_tensor(out=o1[:, oc, 0:n], in0=p2[:, 0:n], in1=w1[:, j, 0:n], op=ALU.mult),
                     reads=[P2, W1B[j]], writes=[O1B])
            k.dma("sp", self.fm(self.bonT, t0, n), o1[:, :, 0:n], reads=[O1B], writes=[self.DB("bonT", ti)])
        k.phase_reset()

    def rw_wkv(self, l, b, d):
        k = self.k
        NW = 256
        rev = (d == 1)
        msk = k.sb([128, 512], F32)
        rmk = k.sb([128, 2048], F32)
        MB = Buf()
        k.dma("sp", msk, self.wmask[:, d, :], writes=[MB])
        k.dma("sp", rmk, self.rmask[:, d, :], writes=[MB])

        def t8():
            return k.sb([128, 8, NW], F32)
        rt, kk, bb, kdt, lw, cum, ec = t8(), t8(), t8(), t8(), t8(), t8(), t8()
        INB = Buf()
        ECB = Buf()
        vst = k.sb([128, 4, 512], F32)
        VB_ = Buf()
        yt = k.sb([128, 8, NW], F32)
        YB = Buf()
        XR = [k.sb([128, 8, 192], F32) for _ in range(2)]
        BE = [k.sb([128, 8, 128], F32) for _ in range(2)]
        KT = [k.sb([128, 8, 128], F32) for _ in range(2)]
        CHB = [Buf(), Buf()]
        AM2 = [k.sb([128, 8, 512], F32) for _ in range(2)]
        AMB2 = [[Buf() for _ in range(8)] for _ in range(2)]
        X2 = [k.sb([128, 8, 128], F32) for _ in range(2)]
        XB2 = [[Buf(), Buf()], [Buf(), Buf()]]
        Pn = k.sb([128, 8, 128], F32)
        PTn = k.sb([128, 8, 128], F32)
        PNB = [Buf(), Buf()]
        PTB = [Buf(), Buf()]
        BEt2 = [k.sb([128, 8, 128], F32) for _ in range(2)]
        KTt2 = [k.sb([128, 8, 128], F32) for _ in range(2)]
        BTB2, KTTB2 = [Buf(), Buf()], [Buf(), Buf()]
        Vbd2 = [k.sb([128, 8, 128], F32) for _ in range(2)]
        VBB2 = [Buf(), Buf()]
        Wsb = k.sb([128, 8, 64], F32)
        Ust = k.sb([128, 8, 64], F32)
        Ubd = k.sb([128, 8, 128], F32)
        Sst = k.sb([128, 8, 64], F32)
        Sbd = k.sb([128, 8, 128], F32)
        WSB, USB, UBB, SSB, SBB = [Buf() for _ in range(5)]
        for t_, B_ in ((XR[0], CHB[0]), (XR[1], CHB[1]), (BE[0], CHB[0]), (BE[1], CHB[1]), (KT[0], CHB[0]),
                       (KT[1], CHB[1]), (Ubd, UBB), (Vbd2[0], VBB2[0]), (Vbd2[1], VBB2[1]), (Sbd, SBB), (Sst, SSB)):
            k.op("pool", lambda e: e.memset(t_, 0.0), writes=[B_])
        r3 = lambda t_: t_.rearrange("p (q c) -> p q c", c=64)
        r4 = lambda t_: t_.rearrange("p (q c) -> p q c", c=128)

        def pre_gen(ch, jb):
            c0 = ch * 64
            cs_ = slice(c0, c0 + 64)
            xr_, be_, kt_, CB = XR[jb], BE[jb], KT[jb], CHB[jb]
            AM, AMB, X, XB = AM2[jb], AMB2[jb], X2[jb], XB2[jb]
            BEt, KTt, BTB, KTTB, Vbd, VBB = BEt2[jb], KTt2[jb], BTB2[jb], KTTB2[jb], Vbd2[jb], VBB2[jb]
            for hp in range(2):
                ps_ = slice(hp * 64, hp * 64 + 64)
                k.op("dve", lambda e: e.scalar_tensor_tensor(out=xr_[ps_, :, hp * 64:hp * 64 + 64], in0=kk[ps_, :, cs_],
                                                             scalar=-1.0, in1=lw[ps_, :, cs_], op0=ALU.mult,
                                                             op1=ALU.mult), reads=[INB], writes=[CB])
                k.op("pool", lambda e: e.tensor_tensor(out=be_[ps_, :, hp * 64:hp * 64 + 64], in0=bb[ps_, :, cs_],
                                                       in1=cum[ps_, :, cs_], op=ALU.mult), reads=[INB, ECB], writes=[CB])
                k.op("pool", lambda e: e.tensor_tensor(out=kt_[ps_, :, hp * 64:hp * 64 + 64], in0=kdt[ps_, :, cs_],
                                                       in1=cum[ps_, :, cs_], op=ALU.mult), reads=[INB, ECB], writes=[CB])
            k.op("dve", lambda e: e.tensor_tensor(out=xr_[:, :, 128:192], in0=rt[:, :, cs_], in1=ec[:, :, cs_],
                                                  op=ALU.mult), reads=[INB, ECB], writes=[CB])
            vs_ = vst[:, ch, :].rearrange("p (q v) -> p q v", v=64)
            for hp in range(2):
                ps_ = slice(hp * 64, hp * 64 + 64)
                k.op("act", lambda e: e.activation(out=Vbd[ps_, :, hp * 64:hp * 64 + 64], in_=vs_[ps_, :, :],
                                                   func=AF.Copy), reads=[VB_], writes=[VBB])
            yield
            for p_ in range(8):
                pa, PA = self.pb()
                k.op("pe", lambda e: e.matmul(pa[:, 0:192], lhsT=be_[:, p_, :], rhs=xr_[:, p_, :], start=True,
                                              stop=True), reads=[CB], writes=[PA], sig=False)
                k.op("pe", lambda e: e.matmul(pa[:, 192:384], lhsT=kt_[:, p_, :], rhs=xr_[:, p_, :], start=True,
                                              stop=True), reads=[CB], writes=[PA], sig=False)
                k.op("pe", lambda e: e.matmul(pa[:, 384:512], lhsT=xr_[:, p_, 0:128], rhs=be_[:, p_, :], start=True,
                                              stop=True), reads=[CB], writes=[PA])
                k.op("dve", lambda e: e.tensor_tensor(out=AM[:, p_, :], in0=pa, in1=msk, op=ALU.mult),
                     reads=[PA, MB], writes=[AMB[p_]])
                if p_ == 3:
                    yield
            yield
            for (src_, dst_, DB_) in ((be_, BEt, BTB), (kt_, KTt, KTTB)):
                for q_ in range(2):
                    pt_, PT_ = self.pb()
                    for pi in range(4):
                        p_ = q_ * 4 + pi
                        k.op("pe", lambda e: e.transpose(pt_[:, pi * 128:(pi + 1) * 128], src_[:, p_, :], self.ident),
                             reads=[CB, self.CS], writes=[PT_], sig=(pi == 3))
                    k.op("act", lambda e: e.activation(out=dst_[:, q_ * 4:q_ * 4 + 4, :], in_=r4(pt_), func=AF.Copy),
                         reads=[PT_], writes=[DB_])
            for q_ in range(2):
                k.op("dve", lambda e: e.tensor_tensor(out=X[:, q_ * 4:q_ * 4 + 4, :], in0=AM[:, q_ * 4:q_ * 4 + 4, 0:128],
                                                      in1=self.ident.unsqueeze(1).broadcast_to([128, 4, 128]), op=ALU.add),
                     reads=AMB[q_ * 4:q_ * 4 + 4] + [self.CS], writes=[XB[q_]])
            yield
            for kk_ in range(1, 6):
                Pq = (lambda p_: AM[:, p_, 0:128]) if kk_ == 1 else (lambda p_: Pn[:, p_, :])
                PTq = (lambda p_: AM[:, p_, 384:512]) if kk_ == 1 else (lambda p_: PTn[:, p_, :])
                bk = {}
                for q_ in range(2):
                    RD = (AMB[q_ * 4:q_ * 4 + 4]) if kk_ == 1 else [PNB[q_], PTB[q_]]
                    pb_, PB_ = self.pb()
                    for pi in range(4):
                        p_ = q_ * 4 + pi
                        k.op("pe", lambda e: e.matmul(pb_[:, pi * 128:(pi + 1) * 128], lhsT=Pq(p_), rhs=PTq(p_),
                                                      start=True, stop=True), reads=RD, writes=[PB_], sig=(pi == 3))
                    bk[("b", q_)] = (pb_, PB_)
                    if kk_ < 5:
                        pa_, PA_ = self.pb()
                        for pi in range(4):
                            p_ = q_ * 4 + pi
                            k.op("pe", lambda e: e.matmul(pa_[:, pi * 128:(pi + 1) * 128], lhsT=PTq(p_), rhs=Pq(p_),
                                                          start=True, stop=True), reads=RD, writes=[PA_], sig=(pi == 3))
                        bk[("a", q_)] = (pa_, PA_)
                yield
                for q_ in range(2):
                    pb_, PB_ = bk[("b", q_)]
                    k.op("act", lambda e: e.activation(out=PTn[:, q_ * 4:q_ * 4 + 4, :], in_=r4(pb_), func=AF.Copy),
                         reads=[PB_], writes=[PTB[q_]])
                    if kk_ < 5:
                        pa_, PA_ = bk[("a", q_)]
                        k.op("dve", lambda e: e.tensor_copy(out=Pn[:, q_ * 4:q_ * 4 + 4, :], in_=r4(pa_)),
                             reads=[PA_], writes=[PNB[q_]])
                for q_ in range(2):
                    pc_, PC_ = self.pb()
                    for pi in range(4):
                        p_ = q_ * 4 + pi
                        k.op("pe", lambda e: e.matmul(pc_[:, pi * 128:(pi + 1) * 128], lhsT=PTn[:, p_, :], rhs=X[:, p_, :],
                                                      start=True, stop=True), reads=[PTB[q_], XB[q_]], writes=[PC_],
                             sig=(pi == 3))
                    bk[("c", q_)] = (pc_, PC_)
                yield
                for q_ in range(2):
                    pc_, PC_ = bk[("c", q_)]
                    k.op("dve", lambda e: e.tensor_tensor(out=X[:, q_ * 4:q_ * 4 + 4, :], in0=X[:, q_ * 4:q_ * 4 + 4, :],
                                                          in1=r4(pc_), op=ALU.add), reads=[PC_, XB[q_]], writes=[XB[q_]])

        def state_gen(ch, jb):
            c0 = ch * 64
            cs_ = slice(c0, c0 + 64)
            xr_, CB = XR[jb], CHB[jb]
            AM, AMB, X, XB = AM2[jb], AMB2[jb], X2[jb], XB2[jb]
            BEt, KTt, BTB, KTTB, Vbd, VBB = BEt2[jb], KTt2[jb], BTB2[jb], KTTB2[jb], Vbd2[jb], VBB2[jb]
            vs_ = vst[:, ch, :].rearrange("p (q v) -> p q v", v=64)
            pw, PW = self.pb()
            pw2, PW2 = self.pb()
            for p_ in range(8):
                k.op("pe", lambda e: e.matmul(pw[:, p_ * 64:(p_ + 1) * 64], lhsT=xr_[:, p_, 0:128], rhs=Sst[:, p_, :],
                                              start=True, stop=True), reads=[CB, SSB], writes=[PW], sig=(p_ == 7))
            for p_ in range(8):
                k.op("pe", lambda e: e.matmul(pw2[:, p_ * 64:(p_ + 1) * 64], lhsT=AM[:, p_, 192:320], rhs=vs_[:, p_, :],
                                              start=True, stop=True), reads=[AMB[p_], VB_], writes=[PW2], sig=(p_ == 7))
            py1, PY1 = self.pb()
            py3, PY3 = self.pb()
            for p_ in range(8):
                k.op("pe", lambda e: e.matmul(py1[:, p_ * 64:(p_ + 1) * 64], lhsT=Sbd[:, p_, :], rhs=xr_[:, p_, 128:192],
                                              start=True, stop=True), reads=[SBB, CB], writes=[PY1], sig=(p_ == 7))
            for p_ in range(8):
                k.op("pe", lambda e: e.matmul(py3[:, p_ * 64:(p_ + 1) * 64], lhsT=Vbd[:, p_, :], rhs=AM[:, p_, 320:384],
                                              start=True, stop=True), reads=[VBB, AMB[p_]], writes=[PY3], sig=(p_ == 7))
            k.op("act", lambda e: e.activation(out=Wsb, in_=r3(pw), func=AF.Copy), reads=[PW], writes=[WSB])
            k.op("dve", lambda e: e.tensor_tensor(out=Wsb, in0=Wsb, in1=r3(pw2), op=ALU.add), reads=[WSB, PW2],
                 writes=[WSB])
            k.op("act", lambda e: e.activation(out=yt[:, :, cs_], in_=r3(py1), func=AF.Copy), reads=[PY1], writes=[YB])
            k.op("dve", lambda e: e.tensor_tensor(out=yt[:, :, cs_], in0=yt[:, :, cs_], in1=r3(py3), op=ALU.add),
                 reads=[YB, PY3], writes=[YB])
            yield
            pu, PU = self.pb()
            for p_ in range(8):
                k.op("pe", lambda e: e.matmul(pu[:, p_ * 64:(p_ + 1) * 64], lhsT=X[:, p_, :], rhs=Wsb[:, p_, :], start=True,
                                              stop=True), reads=[XB[p_ // 4], WSB], writes=[PU], sig=(p_ == 7))
            pu3 = r3(pu)
            k.op("act", lambda e: e.activation(out=Ust, in_=pu3, func=AF.Copy), reads=[PU], writes=[USB])
            for hp in range(2):
                ps_ = slice(hp * 64, hp * 64 + 64)
                k.op("act", lambda e: e.activation(out=Ubd[ps_, :, hp * 64:hp * 64 + 64], in_=pu3[ps_, :, :],
                                                   func=AF.Copy), reads=[PU], writes=[UBB])
            yield
            py2, PY2 = self.pb()
            pS1, PS1 = self.pb()
            pS2, PS2 = self.pb()
            for p_ in range(8):
                k.op("pe", lambda e: e.matmul(pS2[:, p_ * 64:(p_ + 1) * 64], lhsT=KTt[:, p_, :], rhs=vs_[:, p_, :],
                                              start=True, stop=True), reads=[KTTB, VB_], writes=[PS2], sig=(p_ == 7))
            for p_ in range(8):
                k.op("pe", lambda e: e.matmul(pS1[:, p_ * 64:(p_ + 1) * 64], lhsT=BEt[:, p_, :], rhs=Ust[:, p_, :],
                                              start=True, stop=True), reads=[BTB, USB], writes=[PS1], sig=(p_ == 7))
            for p_ in range(8):
                k.op("pe", lambda e: e.matmul(py2[:, p_ * 64:(p_ + 1) * 64], lhsT=Ubd[:, p_, :], rhs=AM[:, p_, 128:192],
                                              start=True, stop=True), reads=[UBB, AMB[p_]], writes=[PY2], sig=(p_ == 7))
            k.op("act", lambda e: e.activation(out=Wsb, in_=r3(pS1), func=AF.Copy), reads=[PS1], writes=[WSB])
            k.op("dve", lambda e: e.tensor_tensor(out=Wsb, in0=Wsb, in1=r3(pS2), op=ALU.add), reads=[WSB, PS2], writes=[WSB])
            k.op("dve", lambda e: e.tensor_tensor(out=Wsb, in0=Wsb, in1=Sst, op=ALU.add), reads=[WSB, SSB], writes=[WSB])
            gcol = (c0 + 63) if not rev else c0
            k.op("dve", lambda e: e.tensor_tensor(out=Sst, in0=Wsb, in1=ec[:, :, gcol:gcol + 1].broadcast_to([128, 8, 64]),
                                                  op=ALU.mult), reads=[WSB, ECB], writes=[SSB])
            for hp in range(2):
                ps_ = slice(hp * 64, hp * 64 + 64)
                k.op("act", lambda e: e.activation(out=Sbd[ps_, :, hp * 64:hp * 64 + 64], in_=Sst[ps_, :, :],
                                                   func=AF.Copy), reads=[SSB], writes=[SBB])
            k.op("dve", lambda e: e.tensor_tensor(out=yt[:, :, cs_], in0=yt[:, :, cs_], in1=r3(py2), op=ALU.add),
                 reads=[YB, PY2], writes=[YB])

        def drain(g):
            for _ in g:
                pass

        def interleave(pre, st):
            pre_done = pre is None
            st_done = False
            while not (pre_done and st_done):
                for _ in range(3):
                    if not pre_done:
                        try:
                            next(pre)
                        except StopIteration:
                            pre_done = True
                if not st_done:
                    try:
                        next(st)
                    except StopIteration:
                        st_done = True

        wt = [(0, 0)] + [(1 + w, CTX + NW * w) for w in range(16)]
        order = wt if not rev else [wt[0]] + wt[:0:-1]
        nchunk = 0
        for (wi, t0) in order[:self.wkv_ntiles]:
            ti = wi
            for (dst, src, key) in ((rt, self.rT, ("rT", ti)), (kk, self.kkT, ("kkT", ti)), (bb, self.bT[d], ("bT", d, ti)),
                                    (kdt, self.kdT[d], ("kdT", d, ti)), (lw, self.lwT[d], ("lwT", d, ti))):
                k.dma("sp", dst, self.fm(src, t0, NW), reads=[self.DB(*key)], writes=[INB])
            k.dma("sp", vst, self.Vst[t0 // 64:t0 // 64 + 4].rearrange("c p f -> p c f"), reads=[self.DB("Vst", ti)],
                  writes=[VB_])
            fl = lambda a: a.rearrange("p f t -> p (f t)")
            rv = (lambda a: a[:, ::-1]) if rev else (lambda a: a)
            k.op("dve", lambda e: e.tensor_tensor_scan(out=rv(fl(cum)), data0=rv(rmk), data1=rv(fl(lw)), initial=0.0,
                                                       op0=ALU.mult, op1=ALU.add), reads=[INB, MB, ECB], writes=[ECB])
            k.op("dve", lambda e: e.tensor_tensor(out=lw, in0=cum, in1=lw, op=ALU.subtract), reads=[ECB, INB], writes=[INB])
            k.op("act", lambda e: e.activation(out=lw, in_=lw, func=AF.Exp), reads=[INB], writes=[INB])
            k.op("act", lambda e: e.activation(out=ec, in_=cum, func=AF.Exp), reads=[ECB], writes=[ECB])
            k.op("act", lambda e: e.activation(out=cum, in_=cum, func=AF.Exp, scale=-1.0), reads=[ECB], writes=[ECB])
            chs = list(range(4)) if not rev else list(range(3, -1, -1))
            jbs = [(nchunk + i_) % 2 for i_ in range(4)]
            nchunk += 4
            drain(pre_gen(chs[0], jbs[0]))
            for i_ in range(4):
                nxt = pre_gen(chs[i_ + 1], jbs[i_ + 1]) if i_ < 3 else None
                interleave(nxt, state_gen(chs[i_], jbs[i_]))
            k.dma("sp", self.fm(self.yT[d], t0, NW), yt, reads=[YB], writes=[self.DB("yT", d, wi)])
        k.phase_reset()

    def rw_out(self, l, b, last):
        k = self.k
        i = l // 2
        wo = k.sb([128, 8, D], BF16)
        WOB = Buf()
        self.load_w(wo, WOB, self.rw_wo[i], 8)
        bones_f = self.cs[:, 128:256]
        y0 = k.sb([128, 8, 512], F32)
        y1 = k.sb([128, 8, 512], F32)
        bon = k.sb([128, 8, 512], F32)
        gt = k.sb([128, 8, 512], F32)
        xt = k.sb([128, 8, 512], F32)
        ob = k.sb([128, 8, 512], BF16)
        Y0B, Y1B, BNB, GTB, XB, OBB = [Buf() for _ in range(6)]
        yc = k.sb([128, 2, 512], F32)
        sq = k.sb([128, 2, 512], F32)
        rs = k.sb([128, 2, 512], F32)
        YCB, SQB, RSB = [Buf(), Buf()], [Buf(), Buf()], [Buf(), Buf()]
        for ti, (t0, n) in enumerate(TILES):
            if last and ti == 0:
                continue
            mi = 2 if ti == 0 else b
            wis = [0] if ti == 0 else [2 * ti - 1, 2 * ti]
            k.dma("sp", y0[:, :, 0:n], self.fm(self.yT[0], t0, n), reads=[self.DB("yT", 0, w) for w in wis], writes=[Y0B])
            k.dma("sp", y1[:, :, 0:n], self.fm(self.yT[1], t0, n), reads=[self.DB("yT", 1, w) for w in wis], writes=[Y1B])
            k.dma("sp", bon[:, :, 0:n], self.fm(self.bonT, t0, n), reads=[self.DB("bonT", w) for w in wis], writes=[BNB])
            k.dma("sp", gt[:, :, 0:n], self.fm(self.gT, t0, n), reads=[self.DB("gT", w) for w in wis], writes=[GTB])
            k.dma("sp", xt[:, :, 0:n], self.fm(self.xT[b], t0, n), reads=[self.DB("xT", b, ti)], writes=[XB])
            k.op("pool", lambda e: e.tensor_tensor(out=y0[:, :, 0:n], in0=y0[:, :, 0:n], in1=y1[:, :, 0:n], op=ALU.add),
                 reads=[Y0B, Y1B], writes=[Y0B])
            for fc in range(8):
                j = fc % 2
                p, P = self.pb()
                k.op("pe", lambda e: e.matmul(p[:, 0:n], lhsT=bones_f, rhs=y0[:, fc, 0:n], start=True, stop=True),
                     reads=[Y0B, self.CS], writes=[P])
                k.op("dve", lambda e: e.scalar_tensor_tensor(out=yc[:, j, 0:n], in0=p[:, 0:n], scalar=-1.0 / 64,
                                                             in1=y0[:, fc, 0:n], op0=ALU.mult, op1=ALU.add),
                     reads=[P, Y0B], writes=[YCB[j]])
                k.op("act", lambda e: e.activation(out=sq[:, j, 0:n], in_=yc[:, j, 0:n], func=AF.Square), reads=[YCB[j]],
                     writes=[SQB[j]])
                p2, P2 = self.pb()
                k.op("pe", lambda e: e.matmul(p2[:, 0:n], lhsT=bones_f, rhs=sq[:, j, 0:n], start=True, stop=True),
                     reads=[SQB[j], self.CS], writes=[P2])
                k.op("act", lambda e: e.activation(out=rs[:, j, 0:n], in_=p2[:, 0:n], func=AF.Sqrt, bias=GN_EPS,
                                                   scale=1.0 / 64), reads=[P2], writes=[RSB[j]])
                k.op("dve", lambda e: e.reciprocal(out=rs[:, j, 0:n], in_=rs[:, j, 0:n]), reads=[RSB[j]], writes=[RSB[j]])
                k.op("dve", lambda e: e.tensor_tensor(out=yc[:, j, 0:n], in0=yc[:, j, 0:n], in1=rs[:, j, 0:n], op=ALU.mult),
                     reads=[YCB[j], RSB[j]], writes=[YCB[j]])
                k.op("act", lambda e: e.activation(out=yc[:, j, 0:n], in_=yc[:, j, 0:n], func=AF.Identity,
                                                   bias=self.V("gnb_%d" % i, fc), scale=self.V("gng_%d" % i, fc)),
                     reads=[YCB[j], self.VB], writes=[YCB[j]])
                k.op("pool", lambda e: e.tensor_tensor(out=yc[:, j, 0:n], in0=yc[:, j, 0:n], in1=bon[:, fc, 0:n], op=ALU.add),
                     reads=[YCB[j], BNB], writes=[YCB[j]])
                k.op("dve", lambda e: e.tensor_tensor(out=ob[:, fc, 0:n], in0=yc[:, j, 0:n], in1=gt[:, fc, 0:n], op=ALU.mult),
                     reads=[YCB[j], GTB], writes=[OBB])
            for oc in range(8):
                p, P = self.pb()
                for kc in range(8):
                    k.op("pe", lambda e: e.matmul(p[:, 0:n], lhsT=wo[:, kc, oc * 128:(oc + 1) * 128], rhs=ob[:, kc, 0:n],
                                                  start=(kc == 0), stop=(kc == 7)), reads=[WOB, OBB], writes=[P],
                         sig=(kc == 7))
                k.op("dve", lambda e: e.scalar_tensor_tensor(out=xt[:, oc, 0:n], in0=p[:, 0:n],
                                                             scalar=self.mod[:, 16 + oc, mi:mi + 1], in1=xt[:, oc, 0:n],
                                                             op0=ALU.mult, op1=ALU.add), reads=[P, self.MOD, XB], writes=[XB])
            k.dma("sp", self.fm(self.xT[b], t0, n), xt[:, :, 0:n], reads=[XB], writes=[self.DB("xT", b, ti)])
        k.phase_reset()

    def build(self, nphases=None):
        ph = [lambda: self.setup()]
        for l in self.layers:
            last = (l == DEPTH - 1)
            ph.append(lambda l=l: self.phase_mod(l))
            for b in range(NB):
                if l % 2 == 0:
                    ph.append(lambda l=l, b=b: self.phase_hy_inproj(l, b))
                    ph.append(lambda l=l, b=b: self.phase_rglru(l, b))
                    ph.append(lambda l=l, b=b: self.phase_attn(l, b))
                else:
                    ph.append(lambda l=l, b=b: self.phase_rwkv(l, b))
                ph.append(lambda l=l, b=b, last=last: self.phase_mlp(l, b, last))
        for f in (ph if nphases is None else ph[:nphases]):
            f()
        self.k.finish()
        return self.nc


def host_consts():
    cs = np.zeros((128, 512), np.float32)
    cs[:, 0:128] = np.eye(128, dtype=np.float32)
    bo = np.zeros((128, 128), np.float32)
    bo[0:64, 0:64] = 1.0
    bo[64:128, 64:128] = 1.0
    cs[:, 128:256] = bo
    pw = np.zeros((128, 128), np.float32)
    for blk in range(2):
        for n in range(64):
            pw[blk * 64 + n, blk * 64 + (n + 32) % 64] = 1.0
    cs[:, 256:384] = pw
    rows = SEQ // 64
    row = np.repeat(np.arange(rows, dtype=np.float32), 64)
    col = np.tile(np.arange(64, dtype=np.float32), rows)
    inv = (np.float32(10000.0) ** (-np.arange(0, 32, 2, dtype=np.float32) / np.float32(32))).astype(np.float32)
    ang = np.concatenate([row[:, None] * inv, col[:, None] * inv], axis=-1).astype(np.float32)
    c = np.cos(ang).astype(np.float32).T
    s_ = np.sin(ang).astype(np.float32).T
    cos64 = np.concatenate([c, c], 0)
    sin64 = np.concatenate([-s_, s_], 0)
    cosT = np.ascontiguousarray(np.concatenate([cos64, cos64], 0))
    sinT = np.ascontiguousarray(np.concatenate([sin64, sin64], 0))
    return cs, cosT, sinT


def fmaj(v):
    return np.ascontiguousarray(np.asarray(v, np.float32).reshape(-1, 128).T)


def host_vb(inp):
    vb = np.zeros((128, NVB), np.float32)

    def put(name, arr):
        arr = np.asarray(arr, np.float32)
        vb[:, VBM[name]:VBM[name] + arr.shape[1]] = arr
    for l in range(DEPTH):
        put("ng0_%d" % l, fmaj(inp["norm_g"][l, 0]))
        put("ng1_%d" % l, fmaj(inp["norm_g"][l, 1]))
        put("adab_%d" % l, fmaj(inp["ada_b"][l]))
    for i in range(2):
        gq = inp["hy_q_norm"][i][PERM]
        gk = inp["hy_k_norm"][i][PERM]
        put("gq_%d" % i, np.concatenate([gq, gq])[:, None])
        put("gk_%d" % i, np.concatenate([gk, gk])[:, None])
        put("convw_%d" % i, np.concatenate([fmaj(inp["hy_conv_w"][i][j]) for j in range(4)], 1))
        put("convb_%d" % i, fmaj(inp["hy_conv_b"][i]))
        put("gateb_%d" % i, np.concatenate([fmaj(inp["hy_gate_b"][i][d][g]) for d in range(2) for g in range(2)], 1))
        put("lam_%d" % i, np.concatenate([fmaj(inp["hy_lam"][i][d]) for d in range(2)], 1))
    for i in range(2):
        put("mu_%d" % i, np.concatenate([fmaj(inp["rw_mu"][i][j]) for j in range(6)], 1))
        put("kk_%d" % i, fmaj(inp["rw_k_k"][i]))
        put("ka_%d" % i, fmaj(inp["rw_k_a"][i]))
        put("rk_%d" % i, fmaj(inp["rw_r_k"][i].reshape(-1)))
        put("gng_%d" % i, fmaj(inp["rw_gn_g"][i]))
        put("gnb_%d" % i, fmaj(inp["rw_gn_b"][i]))
        put("lb_%d" % i, np.concatenate([fmaj(inp["rw_lora_bias"][i][d][j]) for d in range(2) for j in range(2)], 1))
    return vb


def host_shared(inp):
    sh = {}
    cs, cosT, sinT = host_consts()
    sh["consts"], sh["cosT"], sh["sinT"] = cs, cosT, sinT
    sh["vb"] = host_vb(inp)
    sh["ada_w"] = np.ascontiguousarray(inp["ada_w"], np.float32)
    sh["mlp_w1"] = np.ascontiguousarray(inp["mlp_w1"], np.float32)
    sh["mlp_w2"] = np.ascontiguousarray(inp["mlp_w2"], np.float32)
    win = inp["hy_w_in"]
    cols = []
    for h in range(8):
        cols.append(h * 64 + PERM)
    for kv in range(2):
        cols.append(512 + kv * 64 + PERM)
        cols.append(512 + kv * 64 + PERM)
    cols.append(np.arange(640, 768))
    cols.append(np.arange(768, 1792))
    cols = np.concatenate(cols)
    sh["hy_win"] = np.ascontiguousarray(win[:, :, cols], np.float32)
    sh["hy_wout"] = np.ascontiguousarray(inp["hy_w_out"], np.float32)
    gw = inp["hy_gate_w"]
    bd = np.zeros((2, 2, 2, 4, 128, 128), np.float32)
    for c in range(4):
        bd[:, :, :, c, 0:64, 0:64] = gw[:, :, :, 2 * c]
        bd[:, :, :, c, 64:128, 64:128] = gw[:, :, :, 2 * c + 1]
    sh["hy_gw"] = bd
    sh["rw_wrkv"] = np.ascontiguousarray(inp["rw_w_rkv"], np.float32)
    st = np.concatenate([np.arange((2 * p + hp) * 64, (2 * p + hp) * 64 + 64) for hp in range(2) for p in range(8)])
    sh["rw_wvst"] = np.ascontiguousarray(inp["rw_w_rkv"][:, 2][:, :, st], np.float32)
    sh["rw_wo"] = np.ascontiguousarray(inp["rw_w_o"], np.float32)
    ldn = inp["rw_lora_down"]
    sh["rw_ld"] = np.ascontiguousarray(np.concatenate([ldn[:, d, j] for d in range(2) for j in range(2)], axis=-1), np.float32)
    lup = inp["rw_lora_up"]
    sh["rw_lu"] = np.ascontiguousarray(np.stack([lup[:, d, j] for d in range(2) for j in range(2)], axis=1), np.float32)
    sh["rw_gd"] = np.ascontiguousarray(inp["rw_gate_down"], np.float32)
    sh["rw_gu"] = np.ascontiguousarray(inp["rw_gate_up"], np.float32)
    ii = np.arange(64)
    up = (ii[None, :] > ii[:, None]).astype(np.float32)
    le = (ii[:, None] <= ii[None, :]).astype(np.float32)
    lo_ = (ii[None, :] < ii[:, None]).astype(np.float32)

    def bdm(m):
        o = np.zeros((128, 128), np.float32)
        o[0:64, 0:64] = m
        o[64:128, 64:128] = m
        return o

    def stk(m):
        return np.concatenate([m, m], 0)
    wm = np.zeros((128, 2, 512), np.float32)
    for d, (u_, l_, s_) in enumerate(((up, lo_, le), (up.T, lo_.T, le.T))):
        wm[:, d, 0:128] = bdm(u_)
        wm[:, d, 128:192] = stk(s_)
        wm[:, d, 192:320] = bdm(u_)
        wm[:, d, 320:384] = stk(s_)
        wm[:, d, 384:512] = bdm(l_)
    sh["wmask"] = wm
    rm = np.ones((128, 2, 2048), np.float32)
    tt = np.arange(2048)
    rm[:, 0, tt % 64 == 0] = 0.0
    rm[:, 1, tt % 64 == 63] = 0.0
    sh["rmask"] = rm
    return sh


_CACHE = {}


def kernel(**inp):
    inp = {k_: np.asarray(v) for k_, v in inp.items()}
    sh = host_shared(inp)
    if "nc" not in _CACHE:
        _CACHE["nc"] = Prog().build()
    nc = _CACHE["nc"]
    in_maps = []
    for core in range(8):
        bs = [2 * core, 2 * core + 1]
        m = dict(sh)
        m["xT_in"] = np.ascontiguousarray(np.stack([inp["x"][b].T for b in bs]), np.float32)
        m["ctxT_in"] = np.ascontiguousarray(np.stack([inp["ctx"][b].T for b in bs]), np.float32)
        cv = np.stack([inp["c"][bs[0]], inp["c"][bs[1]], inp["c_ctx"]], 0)
        m["cT"] = np.ascontiguousarray(cv.reshape(3, 8, 128).transpose(2, 1, 0), np.float32)
        in_maps.append(m)
    res = run_bass_kernel_spmd(nc, in_maps, core_ids=list(range(8)))
    out = np.empty((16, SEQ, D), np.float32)
    for core in range(8):
        o = res.results[core]["outT"]
        for j in range(NB):
            out[2 * core + j] = o[j].T
    return out
```

```python
import math
import numpy as np
import concourse.bass as bass
import concourse.mybir as mybir
from concourse.bass_utils import run_bass_kernel_spmd

F32 = mybir.dt.float32
BF16 = mybir.dt.bfloat16
AF = mybir.ActivationFunctionType
ALU = mybir.AluOpType

D = 1024
SEQ = 4096
CTX = 256
T = SEQ + CTX
NB = 2
DEPTH = 4
EPS = 1e-6
DFF = 4096
TILES = [(0, CTX)] + [(CTX + 512 * i, 512) for i in range(8)]
WTILES = [(0, CTX)] + [(CTX + 256 * i, 256) for i in range(16)]
GELU_C = 2.0 * math.sqrt(2.0 / math.pi)
DECAY_SCALE = math.exp(-0.5)
GN_EPS = 64e-5


class Buf:
    __slots__ = ("w", "r")

    def __init__(self):
        self.w = None
        self.r = {}


class KB:
    NDMA = 40

    def __init__(self, nc):
        self.nc = nc
        self.eng = dict(pe=nc.tensor, act=nc.scalar, dve=nc.vector, pool=nc.gpsimd, sp=nc.sync)
        self.sems = {}
        self.cnt = {}
        for e in ("pe", "act", "dve", "pool"):
            self.sems[e] = nc.alloc_semaphore("s_" + e)
            self.cnt[e] = 0
        self.dsem = [nc.alloc_semaphore("d%d" % i) for i in range(self.NDMA)]
        self.dval = [0] * self.NDMA
        self.drr = 0
        self.waited = {e: {} for e in self.eng}
        self.sb_off = 16512
        self.sb_base = 16512
        self.nalloc = 0
        self.ninst = 0
        self.pending = []
        self.rec = None

    def sb(self, shape, dt=F32, name=None):
        self.nalloc += 1
        nm = "%s_%d" % (name or "t", self.nalloc)
        n = 1
        for s_ in shape[1:]:
            n *= s_
        nbytes = n * (4 if dt == F32 else 2)
        nbytes = (nbytes + 63) // 64 * 64
        off = self.sb_off
        self.sb_off += nbytes
        assert self.sb_off <= 229376, ("SBUF overflow", nm, self.sb_off)
        return self.nc.alloc_sbuf_tensor_at(nm, list(shape), dt, offset=off).ap()

    def phase_reset(self):
        self.barrier()
        self.sb_off = self.sb_base

    def persist_mark(self):
        self.sb_base = self.sb_off

    def _semh(self, key):
        return self.sems[key] if isinstance(key, str) else self.dsem[key[1]]

    def _wait(self, e, key, val, raw=False):
        if key == e and (not raw or e == "pe"):
            return
        w = self.waited[e]
        if w.get(key, 0) >= val:
            return
        w[key] = val
        self.pending.append((key, val))

    def _take(self):
        p = self.pending
        self.pending = []
        d = {}
        for k_, v_ in p:
            if d.get(k_, 0) < v_:
                d[k_] = v_
        return list(d.items())

    def _deps(self, e, reads, writes):
        for b in reads:
            if b.w is not None:
                self._wait(e, b.w[0], b.w[1], raw=True)
        for b in writes:
            if b.w is not None:
                self._wait(e, b.w[0], b.w[1])
            for k_, v_ in b.r.items():
                self._wait(e, k_, v_)

    def _mark(self, tok, reads, writes):
        k_, v_ = tok
        for b in reads:
            if b.r.get(k_, 0) < v_:
                b.r[k_] = v_
        for b in writes:
            b.w = tok
            b.r = {}

    def op(self, e, ins_fn, reads=(), writes=(), sig=True):
        self._deps(e, reads, writes)
        items = self._take()
        if self.rec is not None:
            self.rec.append((e, list(items), e if sig else None, 1))
        last = items.pop() if items else None
        for k_, v_ in items:
            self.eng[e].wait_ge(self._semh(k_), v_)
            self.ninst += 1
        ins = ins_fn(self.eng[e])
        if last is not None:
            ins._wait_ge(self._semh(last[0]), last[1])
        self.ninst += 1
        if sig:
            self.cnt[e] += 1
            ins.then_inc(self.sems[e], 1)
            tok = (e, self.cnt[e])
        else:
            tok = (e, self.cnt[e] + 1)
        self._mark(tok, reads, writes)
        return tok

    def dma(self, q, out, in_, reads=(), writes=(), **kw):
        i = self.drr
        self.drr = (self.drr + 1) % self.NDMA
        key = ("d", i)
        if self.dval[i] > 0:
            self._wait(q, key, self.dval[i])
        self._deps(q, reads, writes)
        its_ = self._take()
        if self.rec is not None:
            self.rec.append((q, list(its_), key, 16))
        for k_, v_ in its_:
            self.eng[q].wait_ge(self._semh(k_), v_)
            self.ninst += 1
        self.eng[q].dma_start(out=out, in_=in_, **kw).then_inc(self.dsem[i], 16)
        self.ninst += 1
        self.dval[i] += 16
        tok = (key, self.dval[i])
        self._mark(tok, reads, writes)
        return tok

    def barrier(self):
        for e in self.eng:
            for o in ("pe", "act", "dve", "pool"):
                if self.cnt[o] > 0:
                    self._wait(e, o, self.cnt[o])
            for i in range(self.NDMA):
                if self.dval[i] > 0:
                    self._wait(e, ("d", i), self.dval[i])
            its_ = self._take()
            if self.rec is not None:
                self.rec.append((e, list(its_), None, 0))
            for k_, v_ in its_:
                self.eng[e].wait_ge(self._semh(k_), v_)
                self.ninst += 1

    def finish(self):
        self.barrier()


def vb_layout():
    m = {}
    off = 0

    def add(name, n):
        nonlocal off
        m[name] = off
        off += n
    for l in range(DEPTH):
        add("ng0_%d" % l, 8)
        add("ng1_%d" % l, 8)
        add("adab_%d" % l, 48)
    for i in range(2):
        add("gq_%d" % i, 1)
        add("gk_%d" % i, 1)
        add("convw_%d" % i, 16)
        add("convb_%d" % i, 4)
        add("gateb_%d" % i, 16)
        add("lam_%d" % i, 8)
    for i in range(2):
        add("mu_%d" % i, 48)
        add("kk_%d" % i, 8)
        add("ka_%d" % i, 8)
        add("rk_%d" % i, 8)
        add("gng_%d" % i, 8)
        add("gnb_%d" % i, 8)
        add("lb_%d" % i, 32)
    return m, off


VBM, NVB = vb_layout()
PERM = np.concatenate([np.arange(0, 64, 2), np.arange(1, 64, 2)])


class Prog:
    def __init__(self, debug=(), nlayers=DEPTH, ext_in=()):
        self.debug = set(debug)
        self.ext_in = set(ext_in)
        self.nlayers = nlayers
        self.layers = list(range(nlayers))
        self.rw_stop = 99
        self.wkv_stage = 99
        self.wkv_ntiles = 99
        nc = self.nc = bass.Bass("TRN2", target_bir_lowering=False)
        k = self.k = KB(nc)
        self.dbuf = {}
        di = self.din
        self.xin = di("xT_in", [NB, D, SEQ])
        self.cin = di("ctxT_in", [NB, D, CTX])
        self.cT = di("cT", [128, 8, 3])
        self.vbd = di("vb", [128, NVB])
        self.ada_w = di("ada_w", [DEPTH, D, 6 * D])
        self.w1 = di("mlp_w1", [DEPTH, D, DFF])
        self.w2 = di("mlp_w2", [DEPTH, DFF, D])
        self.hy_win = di("hy_win", [2, D, 1920])
        self.hy_wout = di("hy_wout", [2, D, D])
        self.hy_gw = di("hy_gw", [2, 2, 2, 4, 128, 128])
        self.cst = di("consts", [128, 512])
        self.cosd = di("cosT", [128, SEQ])
        self.sind = di("sinT", [128, SEQ])
        self.rw_wrkv = di("rw_wrkv", [2, 3, D, D])
        self.rw_wvst = di("rw_wvst", [2, D, D])
        self.rw_wo = di("rw_wo", [2, D, D])
        self.rw_ld = di("rw_ld", [2, D, 256])
        self.rw_lu = di("rw_lu", [2, 4, 64, D])
        self.rw_gd = di("rw_gd", [2, D, 128])
        self.rw_gu = di("rw_gu", [2, 128, D])
        self.wmask = di("wmask", [128, 2, 512])
        self.rmask = di("rmask", [128, 2, 2048])
        self.out = nc.dram_tensor("outT", [NB, D, SEQ], F32, kind="ExternalOutput").ap()
        self.hnT = self.dscr("hnT", [D, T])
        self.rT = self.dscr("rT", [D, T])
        self.kkT = self.dscr("kkT", [D, T])
        self.kdT = self.dscr("kdT", [2, D, T])
        self.bT = self.dscr("bT", [2, D, T])
        self.lwT = self.dscr("lwT", [2, D, T])
        self.gT = self.dscr("gT", [D, T])
        self.bonT = self.dscr("bonT", [D, T])
        self.Vst = self.dscr("Vst", [T // 64, 128, 512])
        self.yT = self.dscr("yT", [2, D, T])
        self.xT = self.dscr("xT", [NB, D, T])
        self.qT = self.dscr("qT", [512, T], BF16)
        self.kT2 = self.dscr("kT2", [256, T], BF16)
        self.vtok = self.dscr("vtok", [T // 128, 128, 384], BF16)
        self.xr = self.dscr("xr", [512, T])
        self.gg = self.dscr("gg", [512, T], BF16)
        self.rec = self.dscr("rec", [512, T], BF16)
        self.ps = [nc.alloc_psum_tensor("ps%d" % i, [128, 512], F32).ap() for i in range(8)]
        self.PS = [Buf() for _ in range(8)]
        self.prr = 0
        self.vb = k.sb([128, NVB], F32, "vb")
        self.VB = Buf()
        self.cs = k.sb([128, 512], F32, "cs")
        self.CS = Buf()
        self.ident = self.cs[:, 0:128]
        self.ones_bf = k.sb([128, 128], BF16, "ones")
        self.bones_bf = k.sb([128, 128], BF16, "bones")
        self.pswap_bf = k.sb([128, 128], BF16, "pswap")
        self.scT = k.sb([128, 8, 3], BF16, "scT")
        self.mod = k.sb([128, 48, 3], F32, "mod")
        self.A1 = k.sb([128, 8, 3], F32, "A1")
        self.A2 = k.sb([128, 8, 3], F32, "A2")
        self.MOD = Buf()
        self.CONST = Buf()
        k.persist_mark()

    def din(self, name, shape, dt=F32):
        return self.nc.dram_tensor(name, list(shape), dt, kind="ExternalInput").ap()

    def dscr(self, name, shape, dt=F32):
        kind = "ExternalOutput" if name in self.debug else "Internal"
        if name in getattr(self, "ext_in", ()):
            kind = "ExternalInput"
        return self.nc.dram_tensor(name, list(shape), dt, kind=kind).ap()

    def DB(self, *key):
        b = self.dbuf.get(key)
        if b is None:
            b = self.dbuf[key] = Buf()
        return b

    def pb(self):
        i = self.prr
        self.prr = (self.prr + 1) % 8
        return self.ps[i], self.PS[i]

    @staticmethod
    def fm(ap2d, t0, n):
        return ap2d.rearrange("(fc p) t -> p fc t", p=128)[:, :, t0:t0 + n]

    def V(self, name, j=0, n=1):
        o = VBM[name] + j
        return self.vb[:, o:o + n]

    def setup(self):
        k = self.k
        k.dma("sp", self.vb, self.vbd, writes=[self.VB])
        k.dma("sp", self.cs, self.cst, writes=[self.CS])
        k.op("dve", lambda e: e.memset(self.ones_bf, 1.0), writes=[self.CONST])
        k.op("act", lambda e: e.activation(out=self.bones_bf, in_=self.cs[:, 128:256], func=AF.Copy),
             reads=[self.CS], writes=[self.CONST])
        k.op("act", lambda e: e.activation(out=self.pswap_bf, in_=self.cs[:, 256:384], func=AF.Copy),
             reads=[self.CS], writes=[self.CONST])
        ct = k.sb([128, 8, 3], F32)
        sg = k.sb([128, 8, 3], F32)
        CTB = Buf()
        k.dma("sp", ct, self.cT, writes=[CTB])
        k.op("act", lambda e: e.activation(out=sg, in_=ct, func=AF.Sigmoid), reads=[CTB], writes=[CTB])
        k.op("dve", lambda e: e.tensor_tensor(out=self.scT, in0=ct, in1=sg, op=ALU.mult), reads=[CTB],
             writes=[self.CONST])
        for b in range(NB):
            k.dma("sp", self.xT[b, :, 0:CTX], self.cin[b], writes=[self.DB("xT", b, 0)])
            for ti in range(1, 9):
                t0, n = TILES[ti]
                k.dma("sp", self.xT[b, :, t0:t0 + n], self.xin[b, :, t0 - CTX:t0 - CTX + n],
                      writes=[self.DB("xT", b, ti)])
        k.phase_reset()

    def phase_mod(self, l):
        k = self.k
        wa = k.sb([128, 8, 3072], BF16)
        WA = Buf()
        pm, PM = self.pb()
        for half in range(2):
            src = self.ada_w[l].rearrange("(kc p) n -> p kc n", p=128)[:, :, half * 3072:(half + 1) * 3072]
            for kc in range(8):
                k.dma("pool", wa[:, kc, :], src[:, kc, :], writes=[WA])
            for j in range(24):
                jj = half * 24 + j
                for kc in range(8):
                    k.op("pe", lambda e: e.matmul(pm[:, jj * 4:jj * 4 + 3], lhsT=wa[:, kc, j * 128:(j + 1) * 128],
                                                  rhs=self.scT[:, kc, :], start=(kc == 0), stop=(kc == 7)),
                         reads=[WA, self.CONST], writes=[PM], sig=(kc == 7))
        pmv = pm[:, 0:192].rearrange("p (j f) -> p j f", f=4)[:, :, 0:3]
        bias = self.V("adab_%d" % l, 0, 48).unsqueeze(2).broadcast_to([128, 48, 3])
        k.op("dve", lambda e: e.tensor_tensor(out=self.mod, in0=pmv, in1=bias, op=ALU.add),
             reads=[PM, self.VB], writes=[self.MOD])
        for (A, gname, sc0) in ((self.A1, "ng0_%d" % l, 8), (self.A2, "ng1_%d" % l, 32)):
            g = self.V(gname, 0, 8).unsqueeze(2).broadcast_to([128, 8, 3])
            k.op("dve", lambda e: e.scalar_tensor_tensor(out=A, in0=self.mod[:, sc0:sc0 + 8, :], scalar=1.0, in1=g,
                                                         op0=ALU.add, op1=ALU.mult),
                 reads=[self.MOD, self.VB], writes=[self.MOD])
        k.phase_reset()

    def norm_tile(self, xt, XTB, out, OUTB, A, Bsh, mi, n, W):
        k = self.k
        sq, rstd, tmp = W["sq"], W["rstd"], W["tmp"]
        k.op("act", lambda e: e.activation(out=sq[:, :, 0:n], in_=xt[:, :, 0:n], func=AF.Square),
             reads=[XTB], writes=[W["SQ"]])
        pn, PN = self.pb()
        for fc in range(8):
            k.op("pe", lambda e: e.matmul(pn[:, 0:n], lhsT=self.ones_bf, rhs=sq[:, fc, 0:n], start=(fc == 0),
                                          stop=(fc == 7)), reads=[W["SQ"], self.CONST], writes=[PN], sig=(fc == 7))
        k.op("act", lambda e: e.activation(out=rstd[:, 0:n], in_=pn[:, 0:n], func=AF.Sqrt, bias=EPS, scale=1.0 / D),
             reads=[PN], writes=[W["RSTD"]])
        k.op("dve", lambda e: e.reciprocal(out=rstd[:, 0:n], in_=rstd[:, 0:n]), reads=[W["RSTD"]], writes=[W["RSTD"]])
        for fc in range(8):
            j = fc % 2
            k.op("dve", lambda e: e.tensor_tensor(out=tmp[:, j, 0:n], in0=xt[:, fc, 0:n], in1=rstd[:, 0:n],
                                                  op=ALU.mult), reads=[XTB, W["RSTD"]], writes=[W["TMP"][j]])
            k.op("act", lambda e: e.activation(out=out[:, fc, 0:n], in_=tmp[:, j, 0:n], func=AF.Identity,
                                               bias=Bsh[:, fc, mi:mi + 1], scale=A[:, fc, mi:mi + 1]),
                 reads=[W["TMP"][j], self.MOD], writes=[OUTB])

    def norm_work(self):
        k = self.k
        return dict(sq=k.sb([128, 8, 512], BF16), rstd=k.sb([128, 512], F32), tmp=k.sb([128, 2, 512], F32),
                    SQ=Buf(), RSTD=Buf(), TMP=[Buf(), Buf()])

    def load_w(self, dst, DSTB, src2d, nkc, split=1):
        v = src2d.rearrange("(kc p) n -> p kc n", p=128)
        for kc in range(nkc):
            self.k.dma("pool", dst[:, kc, :], v[:, kc, :], writes=[DSTB])

    def phase_mlp(self, l, b, last):
        k = self.k
        w1 = k.sb([128, 8, DFF], BF16)
        w2 = k.sb([128, 32, D], BF16)
        W1B, W2B = Buf(), Buf()
        self.load_w(w1, W1B, self.w1[l], 8)
        self.load_w(w2, W2B, self.w2[l], 32)
        W = self.norm_work()
        xt = k.sb([128, 8, 512], F32)
        XTB = Buf()
        hn = k.sb([128, 8, 512], BF16)
        HNB = Buf()
        h1 = k.sb([128, 32, 512], BF16)
        H1B = [Buf() for _ in range(32)]
        rl = k.sb([128, 2, 512], BF16)
        RLB = [Buf(), Buf()]
        for ti, (t0, n) in enumerate(TILES):
            if last and ti == 0:
                continue
            mi = 2 if ti == 0 else b
            k.dma("sp", xt[:, :, 0:n], self.fm(self.xT[b], t0, n), reads=[self.DB("xT", b, ti)], writes=[XTB])
            self.norm_tile(xt, XTB, hn, HNB, self.A2, self.mod[:, 24:32, :], mi, n, W)
            for oc in range(32):
                p, P = self.pb()
                for kc in range(8):
                    k.op("pe", lambda e: e.matmul(p[:, 0:n], lhsT=w1[:, kc, oc * 128:(oc + 1) * 128], rhs=hn[:, kc, 0:n],
                                                  start=(kc == 0), stop=(kc == 7)),
                         reads=[W1B, HNB], writes=[P], sig=(kc == 7))
                j = oc % 2
                k.op("act", lambda e: e.activation(out=rl[:, j, 0:n], in_=p[:, 0:n], func=AF.Relu),
                     reads=[P], writes=[RLB[j]])
                k.op("pool", lambda e: e.tensor_tensor(out=h1[:, oc, 0:n], in0=rl[:, j, 0:n], in1=rl[:, j, 0:n],
                                                       op=ALU.mult), reads=[RLB[j]], writes=[H1B[oc]])
            for oc in range(8):
                p, P = self.pb()
                for kc in range(32):
                    k.op("pe", lambda e: e.matmul(p[:, 0:n], lhsT=w2[:, kc, oc * 128:(oc + 1) * 128], rhs=h1[:, kc, 0:n],
                                                  start=(kc == 0), stop=(kc == 31)),
                         reads=[W2B, H1B[kc]], writes=[P], sig=(kc == 31))
                k.op("dve", lambda e: e.scalar_tensor_tensor(out=xt[:, oc, 0:n], in0=p[:, 0:n],
                                                             scalar=self.mod[:, 40 + oc, mi:mi + 1],
                                                             in1=xt[:, oc, 0:n], op0=ALU.mult, op1=ALU.add),
                     reads=[P, self.MOD, XTB], writes=[XTB])
            if last:
                k.dma("sp", self.out[b].rearrange("(fc p) t -> p fc t", p=128)[:, :, t0 - CTX:t0 - CTX + n],
                      xt[:, :, 0:n], reads=[XTB], writes=[self.DB("out", b, ti)])
            else:
                k.dma("sp", self.fm(self.xT[b], t0, n), xt[:, :, 0:n], reads=[XTB], writes=[self.DB("xT", b, ti)])
        k.phase_reset()

    def phase_hy_inproj(self, l, b):
        k = self.k
        i = l // 2
        win = k.sb([128, 8, 1920], BF16)
        WB = Buf()
        self.load_w(win, WB, self.hy_win[i], 8)
        W = self.norm_work()
        xt = [k.sb([128, 8, 512], F32) for _ in range(2)]
        XTB = [Buf(), Buf()]
        hn = k.sb([128, 8, 512], BF16)
        HNB = Buf()
        cs_t = k.sb([128, 2, 512], F32)
        CSB = Buf()
        sq = k.sb([128, 512], BF16)
        SQB = Buf()
        rs = k.sb([128, 512], F32)
        RSB = Buf()
        xn = k.sb([128, 512], BF16)
        XNB = Buf()
        t1 = k.sb([128, 512], F32)
        t2 = k.sb([128, 512], F32)
        T1B, T2B = Buf(), Buf()
        qk = k.sb([128, 6, 512], BF16)
        QKB = Buf()
        vt = k.sb([128, 4, 384], BF16)
        VTB = Buf()
        xro = k.sb([128, 4, 512], F32)
        XRB = Buf()
        ggo = k.sb([128, 4, 512], BF16)
        GGB = Buf()
        z2 = k.sb([128, 512], F32)
        Z2B = Buf()
        k.op("dve", lambda e: e.memset(vt, 1.0), writes=[VTB])
        for ti, (t0, n) in enumerate(TILES):
            mi = 2 if ti == 0 else b
            x_ = xt[ti % 2]
            XB = XTB[ti % 2]
            k.dma("sp", x_[:, :, 0:n], self.fm(self.xT[b], t0, n), reads=[self.DB("xT", b, ti)], writes=[XB])
            if ti > 0:
                k.dma("sp", cs_t[:, 0, 0:n], self.cosd[:, t0 - CTX:t0 - CTX + n], writes=[CSB])
                k.dma("sp", cs_t[:, 1, 0:n], self.sind[:, t0 - CTX:t0 - CTX + n], writes=[CSB])
            self.norm_tile(x_, XB, hn, HNB, self.A1, self.mod[:, 0:8, :], mi, n, W)
            for oc in range(6):
                p, P = self.pb()
                for kc in range(8):
                    k.op("pe", lambda e: e.matmul(p[:, 0:n], lhsT=win[:, kc, oc * 128:(oc + 1) * 128], rhs=hn[:, kc, 0:n],
                                                  start=(kc == 0), stop=(kc == 7)), reads=[WB, HNB], writes=[P],
                         sig=(kc == 7))
                k.op("act", lambda e: e.activation(out=sq[:, 0:n], in_=p[:, 0:n], func=AF.Square), reads=[P], writes=[SQB])
                p2, P2 = self.pb()
                k.op("pe", lambda e: e.matmul(p2[:, 0:n], lhsT=self.bones_bf, rhs=sq[:, 0:n], start=True, stop=True),
                     reads=[SQB, self.CONST], writes=[P2])
                k.op("act", lambda e: e.activation(out=rs[:, 0:n], in_=p2[:, 0:n], func=AF.Sqrt, bias=EPS,
                                                   scale=1.0 / 64), reads=[P2], writes=[RSB])
                k.op("dve", lambda e: e.reciprocal(out=rs[:, 0:n], in_=rs[:, 0:n]), reads=[RSB], writes=[RSB])
                g = self.V("gq_%d" % i) if oc < 4 else self.V("gk_%d" % i)
                dst = qk[:, oc, 0:n] if ti == 0 else xn[:, 0:n]
                DSTB = QKB if ti == 0 else XNB
                k.op("dve", lambda e: e.scalar_tensor_tensor(out=dst, in0=p[:, 0:n], scalar=g, in1=rs[:, 0:n],
                                                             op0=ALU.mult, op1=ALU.mult),
                     reads=[P, RSB, self.VB], writes=[DSTB])
                if ti > 0:
                    p3, P3 = self.pb()
                    k.op("pe", lambda e: e.matmul(p3[:, 0:n], lhsT=self.pswap_bf, rhs=xn[:, 0:n], start=True, stop=True),
                         reads=[XNB, self.CONST], writes=[P3])
                    k.op("pool", lambda e: e.tensor_tensor(out=t1[:, 0:n], in0=xn[:, 0:n], in1=cs_t[:, 0, 0:n],
                                                           op=ALU.mult), reads=[XNB, CSB], writes=[T1B])
                    k.op("dve", lambda e: e.tensor_tensor(out=t2[:, 0:n], in0=p3[:, 0:n], in1=cs_t[:, 1, 0:n],
                                                          op=ALU.mult), reads=[P3, CSB], writes=[T2B])
                    k.op("dve", lambda e: e.tensor_tensor(out=qk[:, oc, 0:n], in0=t1[:, 0:n], in1=t2[:, 0:n],
                                                          op=ALU.add), reads=[T1B, T2B], writes=[QKB])
            k.dma("sp", self.fm(self.qT, t0, n), qk[:, 0:4, 0:n], reads=[QKB], writes=[self.DB("qT", ti)])
            k.dma("sp", self.fm(self.kT2, t0, n), qk[:, 4:6, 0:n], reads=[QKB], writes=[self.DB("kT2", ti)])
            nst = n // 128
            for st in range(nst):
                p, P = self.pb()
                for kc in range(8):
                    k.op("pe", lambda e: e.matmul(p[:, 0:128], lhsT=hn[:, kc, st * 128:(st + 1) * 128],
                                                  rhs=win[:, kc, 768:896], start=(kc == 0), stop=(kc == 7)),
                         reads=[WB, HNB], writes=[P], sig=(kc == 7))
                vv = vt[:, st, :].rearrange("p (h c) -> p h c", c=192)[:, :, 64:128]
                k.op("act", lambda e: e.activation(out=vv, in_=p[:, 0:128].rearrange("p (h c) -> p h c", c=64),
                                                   func=AF.Copy), reads=[P], writes=[VTB])
            c0 = t0 // 128
            k.dma("sp", self.vtok[c0:c0 + nst].rearrange("c p f -> p c f"), vt[:, 0:nst, :], reads=[VTB],
                  writes=[self.DB("vtok", ti)])
            for oc in range(4):
                p, P = self.pb()
                for kc in range(8):
                    k.op("pe", lambda e: e.matmul(p[:, 0:n], lhsT=win[:, kc, 896 + oc * 128:896 + (oc + 1) * 128],
                                                  rhs=hn[:, kc, 0:n], start=(kc == 0), stop=(kc == 7)),
                         reads=[WB, HNB], writes=[P], sig=(kc == 7))
                k.op("act", lambda e: e.activation(out=xro[:, oc, 0:n], in_=p[:, 0:n], func=AF.Copy), reads=[P],
                     writes=[XRB])
            k.dma("sp", self.fm(self.xr, t0, n), xro[:, :, 0:n], reads=[XRB], writes=[self.DB("xr", ti)])
            for oc in range(4):
                p, P = self.pb()
                for kc in range(8):
                    k.op("pe", lambda e: e.matmul(p[:, 0:n], lhsT=win[:, kc, 1408 + oc * 128:1408 + (oc + 1) * 128],
                                                  rhs=hn[:, kc, 0:n], start=(kc == 0), stop=(kc == 7)),
                         reads=[WB, HNB], writes=[P], sig=(kc == 7))
                self.gelu_from_psum(p, P, ggo[:, oc, 0:n], GGB, n, z2, Z2B, t1, T1B)
            k.dma("sp", self.fm(self.gg, t0, n), ggo[:, :, 0:n], reads=[GGB], writes=[self.DB("gg", ti)])
        k.phase_reset()

    def gelu_from_psum(self, p, P, dst, DSTB, n, z2, Z2B, t1, T1B):
        k = self.k
        k.op("act", lambda e: e.activation(out=z2[:, 0:n], in_=p[:, 0:n], func=AF.Square), reads=[P], writes=[Z2B])
        k.op("dve", lambda e: e.tensor_scalar(out=z2[:, 0:n], in0=z2[:, 0:n], scalar1=0.044715, scalar2=1.0,
                                              op0=ALU.mult, op1=ALU.add), reads=[Z2B], writes=[Z2B])
        k.op("dve", lambda e: e.tensor_tensor(out=z2[:, 0:n], in0=z2[:, 0:n], in1=p[:, 0:n], op=ALU.mult),
             reads=[Z2B, P], writes=[Z2B])
        k.op("act", lambda e: e.activation(out=t1[:, 0:n], in_=z2[:, 0:n], func=AF.Sigmoid, scale=GELU_C),
             reads=[Z2B], writes=[T1B])
        k.op("dve", lambda e: e.tensor_tensor(out=dst, in0=t1[:, 0:n], in1=p[:, 0:n], op=ALU.mult),
             reads=[T1B, P], writes=[DSTB])

    def phase_rglru(self, l, b):
        k = self.k
        i = l // 2
        gw = k.sb([128, 16, 128], BF16)
        GWB = Buf()
        k.dma("pool", gw, self.hy_gw[i].rearrange("d g c p m -> p (d g c) m"), writes=[GWB])
        c1 = k.sb([128, 8], F32)
        c2 = k.sb([128, 8], F32)
        C1B = Buf()
        k.op("act", lambda e: e.activation(out=c1, in_=self.V("lam_%d" % i, 0, 8), func=AF.Exp, scale=-1.0),
             reads=[self.VB], writes=[C1B])
        k.op("act", lambda e: e.activation(out=c1, in_=c1, func=AF.Ln, bias=1.0), reads=[C1B], writes=[C1B])
        k.op("dve", lambda e: e.tensor_scalar(out=c2, in0=c1, scalar1=-16.0, scalar2=None, op0=ALU.mult),
             reads=[C1B], writes=[C1B])
        k.op("dve", lambda e: e.tensor_scalar(out=c1, in0=c1, scalar1=-8.0, scalar2=None, op0=ALU.mult),
             reads=[C1B], writes=[C1B])
        x = k.sb([128, T], F32)
        xc = k.sb([128, T], F32)
        xcb = k.sb([128, T], BF16)
        r = k.sb([128, T], F32)
        ig = k.sb([128, T], F32)
        a = k.sb([128, T], F32)
        u = k.sb([128, T], F32)
        h = [k.sb([128, T], F32) for _ in range(2)]
        ggt = k.sb([128, T], BF16)
        rec = k.sb([128, T], BF16)
        XB, XCB, XCBB, RB, IB, AB, UB, GB, RECB = [Buf() for _ in range(9)]
        HB = [Buf(), Buf()]
        segs = [(0, CTX), (CTX, T)]
        for c in range(4):
            k.dma("sp", x, self.xr[c * 128:(c + 1) * 128, :], reads=[self.DB("xr", ti) for ti in range(9)], writes=[XB])
            k.dma("sp", ggt, self.gg[c * 128:(c + 1) * 128, :], reads=[self.DB("gg", ti) for ti in range(9)],
                  writes=[GB])
            k.op("act", lambda e: e.activation(out=xc, in_=x, func=AF.Identity, bias=self.V("convb_%d" % i, c),
                                               scale=self.V("convw_%d" % i, 2 * 4 + c)),
                 reads=[XB, self.VB], writes=[XCB])
            for (s0, s1) in segs:
                for j in (0, 1, 3):
                    sh = j - 2
                    a0 = max(s0, s0 - sh)
                    a1 = min(s1, s1 - sh)
                    k.op("dve", lambda e: e.scalar_tensor_tensor(out=xc[:, a0:a1], in0=x[:, a0 + sh:a1 + sh],
                                                                 scalar=self.V("convw_%d" % i, j * 4 + c),
                                                                 in1=xc[:, a0:a1], op0=ALU.mult, op1=ALU.add),
                         reads=[XB, XCB, self.VB], writes=[XCB])
            k.op("act", lambda e: e.activation(out=xcb, in_=xc, func=AF.Copy), reads=[XCB], writes=[XCBB])
            for d in range(2):
                for (t0, n) in TILES:
                    for g, (dst, DB_) in enumerate(((r, RB), (ig, IB))):
                        p, P = self.pb()
                        k.op("pe", lambda e: e.matmul(p[:, 0:n], lhsT=gw[:, (d * 2 + g) * 4 + c, :], rhs=xcb[:, t0:t0 + n],
                                                      start=True, stop=True), reads=[GWB, XCBB], writes=[P])
                        k.op("act", lambda e: e.activation(out=dst[:, t0:t0 + n], in_=p[:, 0:n], func=AF.Sigmoid,
                                                           bias=self.V("gateb_%d" % i, (d * 2 + g) * 4 + c)),
                             reads=[P, self.VB], writes=[DB_])
                k.op("act", lambda e: e.activation(out=a, in_=r, func=AF.Exp, scale=c1[:, d * 4 + c:d * 4 + c + 1]),
                     reads=[RB, C1B], writes=[AB])
                k.op("act", lambda e: e.activation(out=u, in_=r, func=AF.Exp, scale=c2[:, d * 4 + c:d * 4 + c + 1]),
                     reads=[RB, C1B], writes=[UB])
                k.op("act", lambda e: e.activation(out=u, in_=u, func=AF.Sqrt, bias=1.0, scale=-1.0),
                     reads=[UB], writes=[UB])
                k.op("dve", lambda e: e.tensor_tensor(out=ig, in0=ig, in1=xc, op=ALU.mult), reads=[IB, XCB], writes=[IB])
                k.op("dve", lambda e: e.tensor_tensor(out=u, in0=u, in1=ig, op=ALU.mult), reads=[UB, IB], writes=[UB])
                hd = h[d]
                if d == 0:
                    k.op("dve", lambda e: e.tensor_tensor_scan(out=hd, data0=a, data1=u, initial=0.0, op0=ALU.mult,
                                                               op1=ALU.add), reads=[AB, UB], writes=[HB[d]])
                else:
                    k.op("dve", lambda e: e.tensor_tensor_scan(out=hd[:, 0:CTX][:, ::-1], data0=a[:, 0:CTX][:, ::-1],
                                                               data1=u[:, 0:CTX][:, ::-1], initial=0.0, op0=ALU.mult,
                                                               op1=ALU.add), reads=[AB, UB], writes=[HB[d]])
                    k.op("dve", lambda e: e.tensor_tensor_scan(out=hd[:, CTX:T][:, ::-1], data0=a[:, CTX:T][:, ::-1],
                                                               data1=u[:, CTX:T][:, ::-1], initial=hd[:, 0:1],
                                                               op0=ALU.mult, op1=ALU.add), reads=[AB, UB, HB[d]],
                         writes=[HB[d]])
            if "rgd" in self.debug and c == 0 and b == 0:
                rgd = self.nc.dram_tensor("rgd", [8, 128, T], F32, kind="ExternalOutput").ap()
                for j_, (t_, B_) in enumerate(((x, XB), (xc, XCB), (r, RB), (ig, IB), (a, AB), (u, UB), (h[0], HB[0]),
                                               (h[1], HB[1]))):
                    k.dma("sp", rgd[j_], t_, reads=[B_], writes=[Buf()])
                c1d = self.nc.dram_tensor("c1d", [2, 128, 8], F32, kind="ExternalOutput").ap()
                k.dma("sp", c1d[0], c1, reads=[C1B], writes=[Buf()])
                k.dma("sp", c1d[1], c2, reads=[C1B], writes=[Buf()])
            k.op("dve", lambda e: e.tensor_tensor(out=h[0], in0=h[0], in1=h[1], op=ALU.add), reads=[HB[0], HB[1]],
                 writes=[HB[0]])
            k.op("dve", lambda e: e.tensor_tensor(out=rec, in0=h[0], in1=ggt, op=ALU.mult), reads=[HB[0], GB],
                 writes=[RECB])
            k.dma("sp", self.rec[c * 128:(c + 1) * 128, :], rec, reads=[RECB], writes=[self.DB("rec", c)])
        k.phase_reset()

    def phase_attn(self, l, b):
        k = self.k
        i = l // 2
        kt = k.sb([128, 2, T], BF16)
        KTB = Buf()
        k.dma("sp", kt, self.kT2.rearrange("(c p) t -> p c t", p=128), reads=[self.DB("kT2", ti) for ti in range(9)],
              writes=[KTB])
        vt = k.sb([128, T // 128, 384], BF16)
        VTB = Buf()
        k.dma("sp", vt, self.vtok.rearrange("c p f -> p c f"), reads=[self.DB("vtok", ti) for ti in range(9)],
              writes=[VTB])
        wo = k.sb([128, 8, D], BF16)
        WOB = Buf()
        self.load_w(wo, WOB, self.hy_wout[i], 8)
        xt = k.sb([128, 8, 512], F32)
        XB = Buf()
        q = k.sb([128, 4, 512], BF16)
        QB = Buf()
        rc = k.sb([128, 4, 512], BF16)
        RCB = Buf()
        att = k.sb([128, 4, 512], BF16)
        ATB = Buf()
        NPT = 8
        pt = [k.sb([128, 512], BF16) for _ in range(NPT)]
        PTB = [Buf() for _ in range(NPT)]
        den = k.sb([128, 512], F32)
        DNB = Buf()
        ptr = 0
        RECALL = [self.DB("rec", c) for c in range(4)]
        for ti, (t0, n) in enumerate(TILES):
            mi = 2 if ti == 0 else b
            nkc = 2 if ti == 0 else T // 128
            k.dma("sp", xt[:, :, 0:n], self.fm(self.xT[b], t0, n), reads=[self.DB("xT", b, ti)], writes=[XB])
            k.dma("sp", q[:, :, 0:n], self.fm(self.qT, t0, n), reads=[self.DB("qT", ti)], writes=[QB])
            k.dma("sp", rc[:, :, 0:n], self.fm(self.rec, t0, n), reads=RECALL, writes=[RCB])
            for hh in range(8):
                kv, hp, fc = hh // 4, hh % 2, hh // 2
                lo, hi = hp * 64, hp * 64 + 64
                olo, ohi = (1 - hp) * 64, (1 - hp) * 64 + 64
                po, PO = self.ps[6 + hh % 2], self.PS[6 + hh % 2]
                voff = kv * 192 + (64 if hp == 0 else 0)
                LA = 5
                slots = []

                def pv(kc, pj):
                    k.op("pe", lambda e: e.matmul(po[:, 0:n], lhsT=vt[:, kc, voff:voff + 128], rhs=pt[pj][:, 0:n],
                                                  start=(kc == 0), stop=(kc == nkc - 1)),
                         reads=[VTB, PTB[pj]], writes=[PO], sig=(kc == nkc - 1))
                for kc in range(nkc):
                    sbi = ptr % 6
                    ps_, PSB = self.ps[sbi], self.PS[sbi]
                    k.op("pe", lambda e: e.matmul(ps_[:, 0:n], lhsT=kt[lo:hi, kv, kc * 128:(kc + 1) * 128],
                                                  rhs=q[lo:hi, fc, 0:n], start=True, stop=True),
                         reads=[KTB, QB], writes=[PSB])
                    pj = ptr % NPT
                    ptr += 1
                    k.op("act", lambda e: e.activation(out=pt[pj][:, 0:n], in_=ps_[:, 0:n], func=AF.Exp, scale=0.125),
                         reads=[PSB], writes=[PTB[pj]])
                    slots.append((kc, pj))
                    if len(slots) > LA:
                        pv(*slots.pop(0))
                while slots:
                    pv(*slots.pop(0))
                k.op("act", lambda e: e.activation(out=den[lo:hi, 0:n], in_=po[olo:ohi, 0:n], func=AF.Copy),
                     reads=[PO], writes=[DNB])
                k.op("dve", lambda e: e.reciprocal(out=den[lo:hi, 0:n], in_=den[lo:hi, 0:n]), reads=[DNB], writes=[DNB])
                k.op("dve", lambda e: e.tensor_tensor(out=att[lo:hi, fc, 0:n], in0=po[lo:hi, 0:n], in1=den[lo:hi, 0:n],
                                                      op=ALU.mult), reads=[PO, DNB], writes=[ATB])
            for oc in range(8):
                p, P = self.pb()
                for kc in range(8):
                    src = att[:, kc, 0:n] if kc < 4 else rc[:, kc - 4, 0:n]
                    k.op("pe", lambda e: e.matmul(p[:, 0:n], lhsT=wo[:, kc, oc * 128:(oc + 1) * 128], rhs=src,
                                                  start=(kc == 0), stop=(kc == 7)),
                         reads=[WOB, ATB, RCB], writes=[P], sig=(kc == 7))
                k.op("dve", lambda e: e.scalar_tensor_tensor(out=xt[:, oc, 0:n], in0=p[:, 0:n],
                                                             scalar=self.mod[:, 16 + oc, mi:mi + 1], in1=xt[:, oc, 0:n],
                                                             op0=ALU.mult, op1=ALU.add),
                     reads=[P, self.MOD, XB], writes=[XB])
            k.dma("sp", self.fm(self.xT[b], t0, n), xt[:, :, 0:n], reads=[XB], writes=[self.DB("xT", b, ti)])
        k.phase_reset()

    def phase_rwkv(self, l, b):
        last = (l == DEPTH - 1)
        self.rw_norm(l, b)
        if self.rw_stop >= 1:
            self.rw_proj(l, b)
        for d in range(2):
            if self.rw_stop >= 2 + d:
                self.rw_wkv(l, b, d)
        if self.rw_stop >= 4:
            self.rw_out(l, b, last)

    def rw_norm(self, l, b):
        k = self.k
        W = self.norm_work()
        xt = [k.sb([128, 8, 512], F32) for _ in range(2)]
        XB = [Buf(), Buf()]
        hn = [k.sb([128, 8, 512], F32) for _ in range(2)]
        HB = [Buf(), Buf()]
        for ti, (t0, n) in enumerate(TILES):
            mi = 2 if ti == 0 else b
            j = ti % 2
            k.dma("sp", xt[j][:, :, 0:n], self.fm(self.xT[b], t0, n), reads=[self.DB("xT", b, ti)], writes=[XB[j]])
            self.norm_tile(xt[j], XB[j], hn[j], HB[j], self.A1, self.mod[:, 0:8, :], mi, n, W)
            k.dma("sp", self.fm(self.hnT, t0, n), hn[j][:, :, 0:n], reads=[HB[j]], writes=[self.DB("hnT", ti)])
        k.phase_reset()

    def rw_proj(self, l, b):
        k = self.k
        i = l // 2
        wr = k.sb([128, 8, D], BF16)
        wk = k.sb([128, 8, D], BF16)
        wvs = k.sb([128, 8, D], BF16)
        ld = k.sb([128, 8, 256], BF16)
        lu = k.sb([64, 4, D], BF16)
        gd = k.sb([128, 8, 128], BF16)
        gu = k.sb([128, D], BF16)
        WB = Buf()
        self.load_w(wr, WB, self.rw_wrkv[i, 0], 8)
        self.load_w(wk, WB, self.rw_wrkv[i, 1], 8)
        self.load_w(wvs, WB, self.rw_wvst[i], 8)
        self.load_w(ld, WB, self.rw_ld[i], 8)
        self.load_w(gd, WB, self.rw_gd[i], 8)
        k.dma("pool", lu, self.rw_lu[i].rearrange("q p n -> p q n"), writes=[WB])
        k.dma("pool", gu, self.rw_gu[i], writes=[WB])
        omka = k.sb([128, 8], F32)
        OMB = Buf()
        k.op("dve", lambda e: e.tensor_scalar(out=omka, in0=self.V("ka_%d" % i, 0, 8), scalar1=-1.0, scalar2=1.0,
                                              op0=ALU.mult, op1=ALU.add), reads=[self.VB], writes=[OMB])
        hh = k.sb([128, 8, 258], F32)
        HHB = Buf()
        xx = k.sb([128, 8, 256], F32)
        XXB = Buf()
        L = [k.sb([128, 8, 256], BF16) for _ in range(6)]
        LB = [Buf() for _ in range(6)]
        rt = k.sb([128, 8, 256], F32)
        kt = k.sb([128, 8, 256], F32)
        kkt = k.sb([128, 8, 256], F32)
        kd = [k.sb([128, 8, 256], F32) for _ in range(2)]
        o1 = k.sb([128, 8, 256], F32)
        RTB, KTB, KKB, O1B = Buf(), Buf(), Buf(), Buf()
        KDB = [Buf(), Buf()]
        vst = k.sb([128, 2, 512], F32)
        VSB = [Buf(), Buf()]
        sm = k.sb([128, 512], BF16)
        SMB = Buf()
        at = k.sb([128, 2, 512], F32)
        ATB = [Buf(), Buf()]
        w1 = k.sb([128, 2, 512], F32)
        W1B = [Buf(), Buf()]
        sqb = k.sb([128, 512], BF16)
        SQB = Buf()
        bones_f = self.cs[:, 128:256]
        for ti, (t0, n) in enumerate(WTILES):
            seg0, seg1 = (0, CTX) if ti == 0 else (CTX, T)
            lo = max(t0 - 1, seg0)
            hi = min(t0 + n + 1, seg1)
            k.dma("sp", hh[:, :, lo - (t0 - 1):hi - (t0 - 1)], self.fm(self.hnT, lo, hi - lo),
                  reads=[self.DB("hnT", j) for j in range(9)], writes=[HHB])
            if lo != t0 - 1:
                k.op("dve", lambda e: e.memset(hh[:, :, 0:1], 0.0), writes=[HHB])
            if hi != t0 + n + 1:
                k.op("dve", lambda e: e.memset(hh[:, :, n + 1:n + 2], 0.0), writes=[HHB])
            h = hh[:, :, 1:n + 1]
            k.op("dve", lambda e: e.tensor_tensor(out=xx[:, :, 0:n], in0=hh[:, :, 0:n], in1=hh[:, :, 2:n + 2], op=ALU.add),
                 reads=[HHB], writes=[XXB])
            k.op("dve", lambda e: e.scalar_tensor_tensor(out=xx[:, :, 0:n], in0=xx[:, :, 0:n], scalar=0.5, in1=h,
                                                         op0=ALU.mult, op1=ALU.subtract), reads=[XXB, HHB], writes=[XXB])
            for j in range(6):
                for fc in range(8):
                    eng = "dve"
                    k.op(eng, lambda e: e.scalar_tensor_tensor(out=L[j][:, fc, 0:n], in0=xx[:, fc, 0:n],
                                                               scalar=self.V("mu_%d" % i, j * 8 + fc), in1=hh[:, fc, 1:n + 1],
                                                               op0=ALU.mult, op1=ALU.add),
                         reads=[XXB, HHB, self.VB], writes=[LB[j]])

            def proj(w, Lj, LjB, dst, DSTB):
                for oc in range(8):
                    p, P = self.pb()
                    for kc in range(8):
                        k.op("pe", lambda e: e.matmul(p[:, 0:n], lhsT=w[:, kc, oc * 128:(oc + 1) * 128], rhs=Lj[:, kc, 0:n],
                                                      start=(kc == 0), stop=(kc == 7)), reads=[WB, LjB], writes=[P],
                             sig=(kc == 7))
                    k.op("act", lambda e: e.activation(out=dst[:, oc, 0:n], in_=p[:, 0:n], func=AF.Copy), reads=[P],
                         writes=[DSTB])
            proj(wr, L[0], LB[0], rt, RTB)
            k.dma("sp", self.fm(self.rT, t0, n), rt[:, :, 0:n], reads=[RTB], writes=[self.DB("rT", ti)])
            proj(wk, L[2], LB[2], kt, KTB)
            for ch in range(n // 64):
                p, P = self.pb()
                for hp in range(2):
                    for kc in range(8):
                        k.op("pe", lambda e: e.matmul(p[hp * 64:(hp + 1) * 64, :], lhsT=L[3][:, kc, ch * 64:(ch + 1) * 64],
                                                      rhs=wvs[:, kc, hp * 512:(hp + 1) * 512], start=(kc == 0),
                                                      stop=(kc == 7)), reads=[WB, LB[3]], writes=[P],
                             sig=(kc == 7 and hp == 1))
                j = ch % 2
                k.op("act", lambda e: e.activation(out=vst[:, j, :], in_=p, func=AF.Copy), reads=[P], writes=[VSB[j]])
                k.dma("sp", self.Vst[t0 // 64 + ch], vst[:, j, :], reads=[VSB[j]], writes=[self.DB("Vst", ti)])
            for oc in range(8):
                k.op("dve", lambda e: e.tensor_scalar(out=kkt[:, oc, 0:n], in0=kt[:, oc, 0:n],
                                                      scalar1=self.V("kk_%d" % i, oc), scalar2=None, op0=ALU.mult),
                     reads=[KTB, self.VB], writes=[KKB])
                k.op("act", lambda e: e.activation(out=sqb[:, 0:n], in_=kkt[:, oc, 0:n], func=AF.Square), reads=[KKB],
                     writes=[SQB])
                p, P = self.pb()
                k.op("pe", lambda e: e.matmul(p[:, 0:n], lhsT=self.bones_bf, rhs=sqb[:, 0:n], start=True, stop=True),
                     reads=[SQB, self.CONST], writes=[P])
                j = oc % 2
                k.op("act", lambda e: e.activation(out=w1[:, j, 0:n], in_=p[:, 0:n], func=AF.Sqrt), reads=[P],
                     writes=[W1B[j]])
                k.op("dve", lambda e: e.tensor_scalar(out=w1[:, j, 0:n], in0=w1[:, j, 0:n], scalar1=1e-12, scalar2=None,
                                                      op0=ALU.max), reads=[W1B[j]], writes=[W1B[j]])
                k.op("dve", lambda e: e.reciprocal(out=w1[:, j, 0:n], in_=w1[:, j, 0:n]), reads=[W1B[j]], writes=[W1B[j]])
                k.op("dve", lambda e: e.tensor_tensor(out=kkt[:, oc, 0:n], in0=kkt[:, oc, 0:n], in1=w1[:, j, 0:n],
                                                      op=ALU.mult), reads=[KKB, W1B[j]], writes=[KKB])
            k.dma("sp", self.fm(self.kkT, t0, n), kkt[:, :, 0:n], reads=[KKB], writes=[self.DB("kkT", ti)])
            p, P = self.pb()
            for kc in range(8):
                k.op("pe", lambda e: e.matmul(p[:, 0:n], lhsT=gd[:, kc, :], rhs=L[5][:, kc, 0:n], start=(kc == 0),
                                              stop=(kc == 7)), reads=[WB, LB[5]], writes=[P], sig=(kc == 7))
            k.op("act", lambda e: e.activation(out=sm[:, 0:n], in_=p[:, 0:n], func=AF.Sigmoid), reads=[P], writes=[SMB])
            for oc in range(8):
                p, P = self.pb()
                k.op("pe", lambda e: e.matmul(p[:, 0:n], lhsT=gu[:, oc * 128:(oc + 1) * 128], rhs=sm[:, 0:n], start=True,
                                              stop=True), reads=[WB, SMB], writes=[P])
                k.op("act", lambda e: e.activation(out=o1[:, oc, 0:n], in_=p[:, 0:n], func=AF.Copy), reads=[P],
                     writes=[O1B])
            k.dma("sp", self.fm(self.gT, t0, n), o1[:, :, 0:n], reads=[O1B], writes=[self.DB("gT", ti)])
            for d in range(2):
                p, P = self.pb()
                for kc in range(8):
                    k.op("pe", lambda e: e.matmul(p[0:64, 0:n], lhsT=ld[:, kc, (d * 2) * 64:(d * 2 + 1) * 64],
                                                  rhs=L[1][:, kc, 0:n], start=(kc == 0), stop=(kc == 7)),
                         reads=[WB, LB[1]], writes=[P], sig=(kc == 7))
                k.op("act", lambda e: e.activation(out=sm[0:64, 0:n], in_=p[0:64, 0:n], func=AF.Tanh), reads=[P],
                     writes=[SMB])
                for oc in range(8):
                    p, P = self.pb()
                    k.op("pe", lambda e: e.matmul(p[:, 0:n], lhsT=lu[:, d * 2, oc * 128:(oc + 1) * 128], rhs=sm[0:64, 0:n],
                                                  start=True, stop=True), reads=[WB, SMB], writes=[P])
                    k.op("act", lambda e: e.activation(out=o1[:, oc, 0:n], in_=p[:, 0:n], func=AF.Sigmoid,
                                                       bias=self.V("lb_%d" % i, (d * 2) * 8 + oc)),
                         reads=[P, self.VB], writes=[O1B])
                    k.op("dve", lambda e: e.tensor_scalar(out=o1[:, oc, 0:n], in0=o1[:, oc, 0:n], scalar1=-DECAY_SCALE,
                                                           scalar2=None, op0=ALU.mult), reads=[O1B], writes=[O1B])
                k.dma("sp", self.fm(self.lwT[d], t0, n), o1[:, :, 0:n], reads=[O1B], writes=[self.DB("lwT", d, ti)])
                p, P = self.pb()
                for kc in range(8):
                    k.op("pe", lambda e: e.matmul(p[0:64, 0:n], lhsT=ld[:, kc, (d * 2 + 1) * 64:(d * 2 + 2) * 64],
                                                  rhs=L[4][:, kc, 0:n], start=(kc == 0), stop=(kc == 7)),
                         reads=[WB, LB[4]], writes=[P], sig=(kc == 7))
                k.op("act", lambda e: e.activation(out=sm[0:64, 0:n], in_=p[0:64, 0:n], func=AF.Copy), reads=[P],
                     writes=[SMB])
                for oc in range(8):
                    p, P = self.pb()
                    k.op("pe", lambda e: e.matmul(p[:, 0:n], lhsT=lu[:, d * 2 + 1, oc * 128:(oc + 1) * 128],
                                                  rhs=sm[0:64, 0:n], start=True, stop=True), reads=[WB, SMB], writes=[P])
                    j = oc % 2
                    k.op("act", lambda e: e.activation(out=at[:, j, 0:n], in_=p[:, 0:n], func=AF.Sigmoid,
                                                       bias=self.V("lb_%d" % i, (d * 2 + 1) * 8 + oc)),
                         reads=[P, self.VB], writes=[ATB[j]])
                    k.op("dve", lambda e: e.tensor_tensor(out=o1[:, oc, 0:n], in0=kkt[:, oc, 0:n], in1=at[:, j, 0:n],
                                                          op=ALU.mult), reads=[KKB, ATB[j]], writes=[O1B])
                    k.op("dve", lambda e: e.tensor_scalar(out=at[:, j, 0:n], in0=at[:, j, 0:n],
                                                           scalar1=self.V("ka_%d" % i, oc), scalar2=omka[:, oc:oc + 1],
                                                           op0=ALU.mult, op1=ALU.add),
                         reads=[ATB[j], self.VB, OMB], writes=[ATB[j]])
                    k.op("dve", lambda e: e.tensor_tensor(out=kd[d][:, oc, 0:n], in0=at[:, j, 0:n], in1=kt[:, oc, 0:n],
                                                          op=ALU.mult), reads=[ATB[j], KTB], writes=[KDB[d]])
                k.dma("sp", self.fm(self.bT[d], t0, n), o1[:, :, 0:n], reads=[O1B], writes=[self.DB("bT", d, ti)])
                k.dma("sp", self.fm(self.kdT[d], t0, n), kd[d][:, :, 0:n], reads=[KDB[d]], writes=[self.DB("kdT", d, ti)])
            for oc in range(8):
                j = oc % 2
                k.op("dve", lambda e: e.tensor_tensor(out=at[:, j, 0:n], in0=kd[0][:, oc, 0:n], in1=kd[1][:, oc, 0:n],
                                                      op=ALU.add), reads=[KDB[0], KDB[1]], writes=[ATB[j]])
                k.op("dve", lambda e: e.scalar_tensor_tensor(out=at[:, j, 0:n], in0=rt[:, oc, 0:n],
                                                             scalar=self.V("rk_%d" % i, oc), in1=at[:, j, 0:n],
                                                             op0=ALU.mult, op1=ALU.mult),
                     reads=[RTB, ATB[j], self.VB], writes=[ATB[j]])
                p, P = self.pb()
                k.op("pe", lambda e: e.matmul(p[:, 0:n], lhsT=bones_f, rhs=at[:, j, 0:n], start=True, stop=True),
                     reads=[ATB[j], self.CS], writes=[P])
                k.op("act", lambda e: e.activation(out=w1[:, j, 0:n], in_=p[:, 0:n], func=AF.Copy), reads=[P],
                     writes=[W1B[j]])
                hpv = [(2 * oc) % 2, (2 * oc + 1) % 2]
                p2, P2 = self.pb()
                for half in range(2):
                    hd_ = 2 * oc + half
                    c0 = (hd_ % 2) * 512 + (hd_ // 2) * 64
                    for kc in range(8):
                        k.op("pe", lambda e: e.matmul(p2[half * 64:(half + 1) * 64, 0:n], lhsT=wvs[:, kc, c0:c0 + 64],
                                                      rhs=L[3][:, kc, 0:n], start=(kc == 0), stop=(kc == 7)),
                             reads=[WB, LB[3]], writes=[P2], sig=(kc == 7 and half == 1))
                k.op("dve", lambda e: e.tensor_tensor(out=o1[:, oc, 0:n], in0=p2[:, 0:n], in1=w1[:, j, 0:n], op=ALU.mult),
                     reads=[P2, W1B[j]], writes=[O1B])
            k.dma("sp", self.fm(self.bonT, t0, n), o1[:, :, 0:n], reads=[O1B], writes=[self.DB("bonT", ti)])
        k.phase_reset()

    def rw_wkv(self, l, b, d):
        k = self.k
        NW = 256
        rev = (d == 1)
        msk = k.sb([128, 512], F32)
        rmk = k.sb([128, 2048], F32)
        MB = Buf()
        k.dma("sp", msk, self.wmask[:, d, :], writes=[MB])
        k.dma("sp", rmk, self.rmask[:, d, :], writes=[MB])

        def t8():
            return k.sb([128, 8, NW], F32)
        rt, kk, bb, kdt, lw, cum, ec = t8(), t8(), t8(), t8(), t8(), t8(), t8()
        INB = Buf()
        ECB = Buf()
        vst = k.sb([128, 4, 512], F32)
        VB_ = Buf()
        yt = k.sb([128, 8, NW], F32)
        YB = Buf()
        XR = [k.sb([128, 8, 192], F32) for _ in range(2)]
        BE = [k.sb([128, 8, 128], F32) for _ in range(2)]
        KT = [k.sb([128, 8, 128], F32) for _ in range(2)]
        CHB = [Buf(), Buf()]
        AM2 = [k.sb([128, 8, 512], F32) for _ in range(2)]
        AMB2 = [[Buf() for _ in range(8)] for _ in range(2)]
        X2 = [k.sb([128, 8, 128], F32) for _ in range(2)]
        XB2 = [[Buf(), Buf()], [Buf(), Buf()]]
        Pn = k.sb([128, 8, 128], F32)
        PTn = k.sb([128, 8, 128], F32)
        PNB = [Buf(), Buf()]
        PTB = [Buf(), Buf()]
        BEt2 = [k.sb([128, 8, 128], F32) for _ in range(2)]
        KTt2 = [k.sb([128, 8, 128], F32) for _ in range(2)]
        BTB2, KTTB2 = [Buf(), Buf()], [Buf(), Buf()]
        Vbd2 = [k.sb([128, 8, 128], F32) for _ in range(2)]
        VBB2 = [Buf(), Buf()]
        Wsb = k.sb([128, 8, 64], F32)
        Ust = k.sb([128, 8, 64], F32)
        Ubd = k.sb([128, 8, 128], F32)
        Sst = k.sb([128, 8, 64], F32)
        Sbd = k.sb([128, 8, 128], F32)
        WSB, USB, UBB, SSB, SBB = [Buf() for _ in range(5)]
        for t_, B_ in ((XR[0], CHB[0]), (XR[1], CHB[1]), (BE[0], CHB[0]), (BE[1], CHB[1]), (KT[0], CHB[0]),
                       (KT[1], CHB[1]), (Ubd, UBB), (Vbd2[0], VBB2[0]), (Vbd2[1], VBB2[1]), (Sbd, SBB), (Sst, SSB)):
            k.op("pool", lambda e: e.memset(t_, 0.0), writes=[B_])
        r3 = lambda t_: t_.rearrange("p (q c) -> p q c", c=64)
        r4 = lambda t_: t_.rearrange("p (q c) -> p q c", c=128)

        def pre_gen(ch, jb):
            c0 = ch * 64
            cs_ = slice(c0, c0 + 64)
            xr_, be_, kt_, CB = XR[jb], BE[jb], KT[jb], CHB[jb]
            AM, AMB, X, XB = AM2[jb], AMB2[jb], X2[jb], XB2[jb]
            BEt, KTt, BTB, KTTB, Vbd, VBB = BEt2[jb], KTt2[jb], BTB2[jb], KTTB2[jb], Vbd2[jb], VBB2[jb]
            for hp in range(2):
                ps_ = slice(hp * 64, hp * 64 + 64)
                k.op("dve", lambda e: e.scalar_tensor_tensor(out=xr_[ps_, :, hp * 64:hp * 64 + 64], in0=kk[ps_, :, cs_],
                                                             scalar=-1.0, in1=lw[ps_, :, cs_], op0=ALU.mult,
                                                             op1=ALU.mult), reads=[INB], writes=[CB])
                k.op("pool", lambda e: e.tensor_tensor(out=be_[ps_, :, hp * 64:hp * 64 + 64], in0=bb[ps_, :, cs_],
                                                       in1=cum[ps_, :, cs_], op=ALU.mult), reads=[INB, ECB], writes=[CB])
                k.op("pool", lambda e: e.tensor_tensor(out=kt_[ps_, :, hp * 64:hp * 64 + 64], in0=kdt[ps_, :, cs_],
                                                       in1=cum[ps_, :, cs_], op=ALU.mult), reads=[INB, ECB], writes=[CB])
            k.op("dve", lambda e: e.tensor_tensor(out=xr_[:, :, 128:192], in0=rt[:, :, cs_], in1=ec[:, :, cs_],
                                                  op=ALU.mult), reads=[INB, ECB], writes=[CB])
            vs_ = vst[:, ch, :].rearrange("p (q v) -> p q v", v=64)
            for hp in range(2):
                ps_ = slice(hp * 64, hp * 64 + 64)
                k.op("act", lambda e: e.activation(out=Vbd[ps_, :, hp * 64:hp * 64 + 64], in_=vs_[ps_, :, :],
                                                   func=AF.Copy), reads=[VB_], writes=[VBB])
            yield
            for p_ in range(8):
                pa, PA = self.pb()
                k.op("pe", lambda e: e.matmul(pa[:, 0:192], lhsT=be_[:, p_, :], rhs=xr_[:, p_, :], start=True,
                                              stop=True), reads=[CB], writes=[PA], sig=False)
                k.op("pe", lambda e: e.matmul(pa[:, 192:384], lhsT=kt_[:, p_, :], rhs=xr_[:, p_, :], start=True,
                                              stop=True), reads=[CB], writes=[PA], sig=False)
                k.op("pe", lambda e: e.matmul(pa[:, 384:512], lhsT=xr_[:, p_, 0:128], rhs=be_[:, p_, :], start=True,
                                              stop=True), reads=[CB], writes=[PA])
                k.op("dve", lambda e: e.tensor_tensor(out=AM[:, p_, :], in0=pa, in1=msk, op=ALU.mult),
                     reads=[PA, MB], writes=[AMB[p_]])
                if p_ == 3:
                    yield
            yield
            for (src_, dst_, DB_) in ((be_, BEt, BTB), (kt_, KTt, KTTB)):
                for q_ in range(2):
                    pt_, PT_ = self.pb()
                    for pi in range(4):
                        p_ = q_ * 4 + pi
                        k.op("pe", lambda e: e.transpose(pt_[:, pi * 128:(pi + 1) * 128], src_[:, p_, :], self.ident),
                             reads=[CB, self.CS], writes=[PT_], sig=(pi == 3))
                    k.op("act", lambda e: e.activation(out=dst_[:, q_ * 4:q_ * 4 + 4, :], in_=r4(pt_), func=AF.Copy),
                         reads=[PT_], writes=[DB_])
            for q_ in range(2):
                k.op("dve", lambda e: e.tensor_tensor(out=X[:, q_ * 4:q_ * 4 + 4, :], in0=AM[:, q_ * 4:q_ * 4 + 4, 0:128],
                                                      in1=self.ident.unsqueeze(1).broadcast_to([128, 4, 128]), op=ALU.add),
                     reads=AMB[q_ * 4:q_ * 4 + 4] + [self.CS], writes=[XB[q_]])
            yield
            for kk_ in range(1, 6):
                Pq = (lambda p_: AM[:, p_, 0:128]) if kk_ == 1 else (lambda p_: Pn[:, p_, :])
                PTq = (lambda p_: AM[:, p_, 384:512]) if kk_ == 1 else (lambda p_: PTn[:, p_, :])
                bk = {}
                for q_ in range(2):
                    RD = (AMB[q_ * 4:q_ * 4 + 4]) if kk_ == 1 else [PNB[q_], PTB[q_]]
                    pb_, PB_ = self.pb()
                    for pi in range(4):
                        p_ = q_ * 4 + pi
                        k.op("pe", lambda e: e.matmul(pb_[:, pi * 128:(pi + 1) * 128], lhsT=Pq(p_), rhs=PTq(p_),
                                                      start=True, stop=True), reads=RD, writes=[PB_], sig=(pi == 3))
                    bk[("b", q_)] = (pb_, PB_)
                    if kk_ < 5:
                        pa_, PA_ = self.pb()
                        for pi in range(4):
                            p_ = q_ * 4 + pi
                            k.op("pe", lambda e: e.matmul(pa_[:, pi * 128:(pi + 1) * 128], lhsT=PTq(p_), rhs=Pq(p_),
                                                          start=True, stop=True), reads=RD, writes=[PA_], sig=(pi == 3))
                        bk[("a", q_)] = (pa_, PA_)
                yield
                for q_ in range(2):
                    pb_, PB_ = bk[("b", q_)]
                    k.op("act", lambda e: e.activation(out=PTn[:, q_ * 4:q_ * 4 + 4, :], in_=r4(pb_), func=AF.Copy),
                         reads=[PB_], writes=[PTB[q_]])
                    if kk_ < 5:
                        pa_, PA_ = bk[("a", q_)]
                        k.op("dve", lambda e: e.tensor_copy(out=Pn[:, q_ * 4:q_ * 4 + 4, :], in_=r4(pa_)),
                             reads=[PA_], writes=[PNB[q_]])
                for q_ in range(2):
                    pc_, PC_ = self.pb()
                    for pi in range(4):
                        p_ = q_ * 4 + pi
                        k.op("pe", lambda e: e.matmul(pc_[:, pi * 128:(pi + 1) * 128], lhsT=PTn[:, p_, :], rhs=X[:, p_, :],
                                                      start=True, stop=True), reads=[PTB[q_], XB[q_]], writes=[PC_],
                             sig=(pi == 3))
                    bk[("c", q_)] = (pc_, PC_)
                yield
                for q_ in range(2):
                    pc_, PC_ = bk[("c", q_)]
                    k.op("dve", lambda e: e.tensor_tensor(out=X[:, q_ * 4:q_ * 4 + 4, :], in0=X[:, q_ * 4:q_ * 4 + 4, :],
                                                          in1=r4(pc_), op=ALU.add), reads=[PC_, XB[q_]], writes=[XB[q_]])

        def state_gen(ch, jb):
            c0 = ch * 64
            cs_ = slice(c0, c0 + 64)
            xr_, CB = XR[jb], CHB[jb]
            AM, AMB, X, XB = AM2[jb], AMB2[jb], X2[jb], XB2[jb]
            BEt, KTt, BTB, KTTB, Vbd, VBB = BEt2[jb], KTt2[jb], BTB2[jb], KTTB2[jb], Vbd2[jb], VBB2[jb]
            vs_ = vst[:, ch, :].rearrange("p (q v) -> p q v", v=64)
            pw, PW = self.pb()
            pw2, PW2 = self.pb()
            for p_ in range(8):
                k.op("pe", lambda e: e.matmul(pw[:, p_ * 64:(p_ + 1) * 64], lhsT=xr_[:, p_, 0:128], rhs=Sst[:, p_, :],
                                              start=True, stop=True), reads=[CB, SSB], writes=[PW], sig=(p_ == 7))
            for p_ in range(8):
                k.op("pe", lambda e: e.matmul(pw2[:, p_ * 64:(p_ + 1) * 64], lhsT=AM[:, p_, 192:320], rhs=vs_[:, p_, :],
                                              start=True, stop=True), reads=[AMB[p_], VB_], writes=[PW2], sig=(p_ == 7))
            py1, PY1 = self.pb()
            py3, PY3 = self.pb()
            for p_ in range(8):
                k.op("pe", lambda e: e.matmul(py1[:, p_ * 64:(p_ + 1) * 64], lhsT=Sbd[:, p_, :], rhs=xr_[:, p_, 128:192],
                                              start=True, stop=True), reads=[SBB, CB], writes=[PY1], sig=(p_ == 7))
            for p_ in range(8):
                k.op("pe", lambda e: e.matmul(py3[:, p_ * 64:(p_ + 1) * 64], lhsT=Vbd[:, p_, :], rhs=AM[:, p_, 320:384],
                                              start=True, stop=True), reads=[VBB, AMB[p_]], writes=[PY3], sig=(p_ == 7))
            k.op("act", lambda e: e.activation(out=Wsb, in_=r3(pw), func=AF.Copy), reads=[PW], writes=[WSB])
            k.op("dve", lambda e: e.tensor_tensor(out=Wsb, in0=Wsb, in1=r3(pw2), op=ALU.add), reads=[WSB, PW2],
                 writes=[WSB])
            k.op("act", lambda e: e.activation(out=yt[:, :, cs_], in_=r3(py1), func=AF.Copy), reads=[PY1], writes=[YB])
            k.op("dve", lambda e: e.tensor_tensor(out=yt[:, :, cs_], in0=yt[:, :, cs_], in1=r3(py3), op=ALU.add),
                 reads=[YB, PY3], writes=[YB])
            yield
            pu, PU = self.pb()
            for p_ in range(8):
                k.op("pe", lambda e: e.matmul(pu[:, p_ * 64:(p_ + 1) * 64], lhsT=X[:, p_, :], rhs=Wsb[:, p_, :], start=True,
                                              stop=True), reads=[XB[p_ // 4], WSB], writes=[PU], sig=(p_ == 7))
            pu3 = r3(pu)
            k.op("act", lambda e: e.activation(out=Ust, in_=pu3, func=AF.Copy), reads=[PU], writes=[USB])
            for hp in range(2):
                ps_ = slice(hp * 64, hp * 64 + 64)
                k.op("act", lambda e: e.activation(out=Ubd[ps_, :, hp * 64:hp * 64 + 64], in_=pu3[ps_, :, :],
                                                   func=AF.Copy), reads=[PU], writes=[UBB])
            yield
            py2, PY2 = self.pb()
            pS1, PS1 = self.pb()
            pS2, PS2 = self.pb()
            for p_ in range(8):
                k.op("pe", lambda e: e.matmul(pS2[:, p_ * 64:(p_ + 1) * 64], lhsT=KTt[:, p_, :], rhs=vs_[:, p_, :],
                                              start=True, stop=True), reads=[KTTB, VB_], writes=[PS2], sig=(p_ == 7))
            for p_ in range(8):
                k.op("pe", lambda e: e.matmul(pS1[:, p_ * 64:(p_ + 1) * 64], lhsT=BEt[:, p_, :], rhs=Ust[:, p_, :],
                                              start=True, stop=True), reads=[BTB, USB], writes=[PS1], sig=(p_ == 7))
            for p_ in range(8):
                k.op("pe", lambda e: e.matmul(py2[:, p_ * 64:(p_ + 1) * 64], lhsT=Ubd[:, p_, :], rhs=AM[:, p_, 128:192],
                                              start=True, stop=True), reads=[UBB, AMB[p_]], writes=[PY2], sig=(p_ == 7))
            k.op("act", lambda e: e.activation(out=Wsb, in_=r3(pS1), func=AF.Copy), reads=[PS1], writes=[WSB])
            k.op("dve", lambda e: e.tensor_tensor(out=Wsb, in0=Wsb, in1=r3(pS2), op=ALU.add), reads=[WSB, PS2], writes=[WSB])
            k.op("dve", lambda e: e.tensor_tensor(out=Wsb, in0=Wsb, in1=Sst, op=ALU.add), reads=[WSB, SSB], writes=[WSB])
            gcol = (c0 + 63) if not rev else c0
            k.op("dve", lambda e: e.tensor_tensor(out=Sst, in0=Wsb, in1=ec[:, :, gcol:gcol + 1].broadcast_to([128, 8, 64]),
                                                  op=ALU.mult), reads=[WSB, ECB], writes=[SSB])
            for hp in range(2):
                ps_ = slice(hp * 64, hp * 64 + 64)
                k.op("act", lambda e: e.activation(out=Sbd[ps_, :, hp * 64:hp * 64 + 64], in_=Sst[ps_, :, :],
                                                   func=AF.Copy), reads=[SSB], writes=[SBB])
            k.op("dve", lambda e: e.tensor_tensor(out=yt[:, :, cs_], in0=yt[:, :, cs_], in1=r3(py2), op=ALU.add),
                 reads=[YB, PY2], writes=[YB])

        def drain(g):
            for _ in g:
                pass

        def interleave(pre, st):
            pre_done = pre is None
            st_done = False
            while not (pre_done and st_done):
                for _ in range(3):
                    if not pre_done:
                        try:
                            next(pre)
                        except StopIteration:
                            pre_done = True
                if not st_done:
                    try:
                        next(st)
                    except StopIteration:
                        st_done = True

        wt = [(0, 0)] + [(1 + w, CTX + NW * w) for w in range(16)]
        order = wt if not rev else [wt[0]] + wt[:0:-1]
        nchunk = 0
        for (wi, t0) in order[:self.wkv_ntiles]:
            ti = wi
            for (dst, src, key) in ((rt, self.rT, ("rT", ti)), (kk, self.kkT, ("kkT", ti)), (bb, self.bT[d], ("bT", d, ti)),
                                    (kdt, self.kdT[d], ("kdT", d, ti)), (lw, self.lwT[d], ("lwT", d, ti))):
                k.dma("sp", dst, self.fm(src, t0, NW), reads=[self.DB(*key)], writes=[INB])
            k.dma("sp", vst, self.Vst[t0 // 64:t0 // 64 + 4].rearrange("c p f -> p c f"), reads=[self.DB("Vst", ti)],
                  writes=[VB_])
            fl = lambda a: a.rearrange("p f t -> p (f t)")
            rv = (lambda a: a[:, ::-1]) if rev else (lambda a: a)
            k.op("dve", lambda e: e.tensor_tensor_scan(out=rv(fl(cum)), data0=rv(rmk), data1=rv(fl(lw)), initial=0.0,
                                                       op0=ALU.mult, op1=ALU.add), reads=[INB, MB, ECB], writes=[ECB])
            k.op("dve", lambda e: e.tensor_tensor(out=lw, in0=cum, in1=lw, op=ALU.subtract), reads=[ECB, INB], writes=[INB])
            k.op("act", lambda e: e.activation(out=lw, in_=lw, func=AF.Exp), reads=[INB], writes=[INB])
            k.op("act", lambda e: e.activation(out=ec, in_=cum, func=AF.Exp), reads=[ECB], writes=[ECB])
            k.op("act", lambda e: e.activation(out=cum, in_=cum, func=AF.Exp, scale=-1.0), reads=[ECB], writes=[ECB])
            chs = list(range(4)) if not rev else list(range(3, -1, -1))
            jbs = [(nchunk + i_) % 2 for i_ in range(4)]
            nchunk += 4
            drain(pre_gen(chs[0], jbs[0]))
            for i_ in range(4):
                nxt = pre_gen(chs[i_ + 1], jbs[i_ + 1]) if i_ < 3 else None
                interleave(nxt, state_gen(chs[i_], jbs[i_]))
            k.dma("sp", self.fm(self.yT[d], t0, NW), yt, reads=[YB], writes=[self.DB("yT", d, wi)])
        k.phase_reset()

    def rw_out(self, l, b, last):
        k = self.k
        i = l // 2
        wo = k.sb([128, 8, D], BF16)
        WOB = Buf()
        self.load_w(wo, WOB, self.rw_wo[i], 8)
        bones_f = self.cs[:, 128:256]
        y0 = k.sb([128, 8, 512], F32)
        y1 = k.sb([128, 8, 512], F32)
        bon = k.sb([128, 8, 512], F32)
        gt = k.sb([128, 8, 512], F32)
        xt = k.sb([128, 8, 512], F32)
        ob = k.sb([128, 8, 512], BF16)
        Y0B, Y1B, BNB, GTB, XB, OBB = [Buf() for _ in range(6)]
        yc = k.sb([128, 2, 512], F32)
        sq = k.sb([128, 2, 512], F32)
        rs = k.sb([128, 2, 512], F32)
        YCB, SQB, RSB = [Buf(), Buf()], [Buf(), Buf()], [Buf(), Buf()]
        for ti, (t0, n) in enumerate(TILES):
            if last and ti == 0:
                continue
            mi = 2 if ti == 0 else b
            wis = [0] if ti == 0 else [2 * ti - 1, 2 * ti]
            k.dma("sp", y0[:, :, 0:n], self.fm(self.yT[0], t0, n), reads=[self.DB("yT", 0, w) for w in wis], writes=[Y0B])
            k.dma("sp", y1[:, :, 0:n], self.fm(self.yT[1], t0, n), reads=[self.DB("yT", 1, w) for w in wis], writes=[Y1B])
            k.dma("sp", bon[:, :, 0:n], self.fm(self.bonT, t0, n), reads=[self.DB("bonT", w) for w in wis], writes=[BNB])
            k.dma("sp", gt[:, :, 0:n], self.fm(self.gT, t0, n), reads=[self.DB("gT", w) for w in wis], writes=[GTB])
            k.dma("sp", xt[:, :, 0:n], self.fm(self.xT[b], t0, n), reads=[self.DB("xT", b, ti)], writes=[XB])
            k.op("pool", lambda e: e.tensor_tensor(out=y0[:, :, 0:n], in0=y0[:, :, 0:n], in1=y1[:, :, 0:n], op=ALU.add),
                 reads=[Y0B, Y1B], writes=[Y0B])
            for fc in range(8):
                j = fc % 2
                p, P = self.pb()
                k.op("pe", lambda e: e.matmul(p[:, 0:n], lhsT=bones_f, rhs=y0[:, fc, 0:n], start=True, stop=True),
                     reads=[Y0B, self.CS], writes=[P])
                k.op("dve", lambda e: e.scalar_tensor_tensor(out=yc[:, j, 0:n], in0=p[:, 0:n], scalar=-1.0 / 64,
                                                             in1=y0[:, fc, 0:n], op0=ALU.mult, op1=ALU.add),
                     reads=[P, Y0B], writes=[YCB[j]])
                k.op("act", lambda e: e.activation(out=sq[:, j, 0:n], in_=yc[:, j, 0:n], func=AF.Square), reads=[YCB[j]],
                     writes=[SQB[j]])
                p2, P2 = self.pb()
                k.op("pe", lambda e: e.matmul(p2[:, 0:n], lhsT=bones_f, rhs=sq[:, j, 0:n], start=True, stop=True),
                     reads=[SQB[j], self.CS], writes=[P2])
                k.op("act", lambda e: e.activation(out=rs[:, j, 0:n], in_=p2[:, 0:n], func=AF.Sqrt, bias=GN_EPS,
                                                   scale=1.0 / 64), reads=[P2], writes=[RSB[j]])
                k.op("dve", lambda e: e.reciprocal(out=rs[:, j, 0:n], in_=rs[:, j, 0:n]), reads=[RSB[j]], writes=[RSB[j]])
                k.op("dve", lambda e: e.tensor_tensor(out=yc[:, j, 0:n], in0=yc[:, j, 0:n], in1=rs[:, j, 0:n], op=ALU.mult),
                     reads=[YCB[j], RSB[j]], writes=[YCB[j]])
                k.op("act", lambda e: e.activation(out=yc[:, j, 0:n], in_=yc[:, j, 0:n], func=AF.Identity,
                                                   bias=self.V("gnb_%d" % i, fc), scale=self.V("gng_%d" % i, fc)),
                     reads=[YCB[j], self.VB], writes=[YCB[j]])
                k.op("pool", lambda e: e.tensor_tensor(out=yc[:, j, 0:n], in0=yc[:, j, 0:n], in1=bon[:, fc, 0:n], op=ALU.add),
                     reads=[YCB[j], BNB], writes=[YCB[j]])
                k.op("dve", lambda e: e.tensor_tensor(out=ob[:, fc, 0:n], in0=yc[:, j, 0:n], in1=gt[:, fc, 0:n], op=ALU.mult),
                     reads=[YCB[j], GTB], writes=[OBB])
            for oc in range(8):
                p, P = self.pb()
                for kc in range(8):
                    k.op("pe", lambda e: e.matmul(p[:, 0:n], lhsT=wo[:, kc, oc * 128:(oc + 1) * 128], rhs=ob[:, kc, 0:n],
                                                  start=(kc == 0), stop=(kc == 7)), reads=[WOB, OBB], writes=[P],
                         sig=(kc == 7))
                k.op("dve", lambda e: e.scalar_tensor_tensor(out=xt[:, oc, 0:n], in0=p[:, 0:n],
                                                             scalar=self.mod[:, 16 + oc, mi:mi + 1], in1=xt[:, oc, 0:n],
                                                             op0=ALU.mult, op1=ALU.add), reads=[P, self.MOD, XB], writes=[XB])
            k.dma("sp", self.fm(self.xT[b], t0, n), xt[:, :, 0:n], reads=[XB], writes=[self.DB("xT", b, ti)])
        k.phase_reset()

    def build(self, nphases=None):
        ph = [lambda: self.setup()]
        for l in self.layers:
            last = (l == DEPTH - 1)
            ph.append(lambda l=l: self.phase_mod(l))
            for b in range(NB):
                if l % 2 == 0:
                    ph.append(lambda l=l, b=b: self.phase_hy_inproj(l, b))
                    ph.append(lambda l=l, b=b: self.phase_rglru(l, b))
                    ph.append(lambda l=l, b=b: self.phase_attn(l, b))
                else:
                    ph.append(lambda l=l, b=b: self.phase_rwkv(l, b))
                ph.append(lambda l=l, b=b, last=last: self.phase_mlp(l, b, last))
        for f in (ph if nphases is None else ph[:nphases]):
            f()
        self.k.finish()
        return self.nc


def host_consts():
    cs = np.zeros((128, 512), np.float32)
    cs[:, 0:128] = np.eye(128, dtype=np.float32)
    bo = np.zeros((128, 128), np.float32)
    bo[0:64, 0:64] = 1.0
    bo[64:128, 64:128] = 1.0
    cs[:, 128:256] = bo
    pw = np.zeros((128, 128), np.float32)
    for blk in range(2):
        for n in range(64):
            pw[blk * 64 + n, blk * 64 + (n + 32) % 64] = 1.0
    cs[:, 256:384] = pw
    rows = SEQ // 64
    row = np.repeat(np.arange(rows, dtype=np.float32), 64)
    col = np.tile(np.arange(64, dtype=np.float32), rows)
    inv = (np.float32(10000.0) ** (-np.arange(0, 32, 2, dtype=np.float32) / np.float32(32))).astype(np.float32)
    ang = np.concatenate([row[:, None] * inv, col[:, None] * inv], axis=-1).astype(np.float32)
    c = np.cos(ang).astype(np.float32).T
    s_ = np.sin(ang).astype(np.float32).T
    cos64 = np.concatenate([c, c], 0)
    sin64 = np.concatenate([-s_, s_], 0)
    cosT = np.ascontiguousarray(np.concatenate([cos64, cos64], 0))
    sinT = np.ascontiguousarray(np.concatenate([sin64, sin64], 0))
    return cs, cosT, sinT


def fmaj(v):
    return np.ascontiguousarray(np.asarray(v, np.float32).reshape(-1, 128).T)


def host_vb(inp):
    vb = np.zeros((128, NVB), np.float32)

    def put(name, arr):
        arr = np.asarray(arr, np.float32)
        vb[:, VBM[name]:VBM[name] + arr.shape[1]] = arr
    for l in range(DEPTH):
        put("ng0_%d" % l, fmaj(inp["norm_g"][l, 0]))
        put("ng1_%d" % l, fmaj(inp["norm_g"][l, 1]))
        put("adab_%d" % l, fmaj(inp["ada_b"][l]))
    for i in range(2):
        gq = inp["hy_q_norm"][i][PERM]
        gk = inp["hy_k_norm"][i][PERM]
        put("gq_%d" % i, np.concatenate([gq, gq])[:, None])
        put("gk_%d" % i, np.concatenate([gk, gk])[:, None])
        put("convw_%d" % i, np.concatenate([fmaj(inp["hy_conv_w"][i][j]) for j in range(4)], 1))
        put("convb_%d" % i, fmaj(inp["hy_conv_b"][i]))
        put("gateb_%d" % i, np.concatenate([fmaj(inp["hy_gate_b"][i][d][g]) for d in range(2) for g in range(2)], 1))
        put("lam_%d" % i, np.concatenate([fmaj(inp["hy_lam"][i][d]) for d in range(2)], 1))
    for i in range(2):
        put("mu_%d" % i, np.concatenate([fmaj(inp["rw_mu"][i][j]) for j in range(6)], 1))
        put("kk_%d" % i, fmaj(inp["rw_k_k"][i]))
        put("ka_%d" % i, fmaj(inp["rw_k_a"][i]))
        put("rk_%d" % i, fmaj(inp["rw_r_k"][i].reshape(-1)))
        put("gng_%d" % i, fmaj(inp["rw_gn_g"][i]))
        put("gnb_%d" % i, fmaj(inp["rw_gn_b"][i]))
        put("lb_%d" % i, np.concatenate([fmaj(inp["rw_lora_bias"][i][d][j]) for d in range(2) for j in range(2)], 1))
    return vb


def host_shared(inp):
    sh = {}
    cs, cosT, sinT = host_consts()
    sh["consts"], sh["cosT"], sh["sinT"] = cs, cosT, sinT
    sh["vb"] = host_vb(inp)
    sh["ada_w"] = np.ascontiguousarray(inp["ada_w"], np.float32)
    sh["mlp_w1"] = np.ascontiguousarray(inp["mlp_w1"], np.float32)
    sh["mlp_w2"] = np.ascontiguousarray(inp["mlp_w2"], np.float32)
    win = inp["hy_w_in"]
    cols = []
    for h in range(8):
        cols.append(h * 64 + PERM)
    for kv in range(2):
        cols.append(512 + kv * 64 + PERM)
        cols.append(512 + kv * 64 + PERM)
    cols.append(np.arange(640, 768))
    cols.append(np.arange(768, 1792))
    cols = np.concatenate(cols)
    sh["hy_win"] = np.ascontiguousarray(win[:, :, cols], np.float32)
    sh["hy_wout"] = np.ascontiguousarray(inp["hy_w_out"], np.float32)
    gw = inp["hy_gate_w"]
    bd = np.zeros((2, 2, 2, 4, 128, 128), np.float32)
    for c in range(4):
        bd[:, :, :, c, 0:64, 0:64] = gw[:, :, :, 2 * c]
        bd[:, :, :, c, 64:128, 64:128] = gw[:, :, :, 2 * c + 1]
    sh["hy_gw"] = bd
    sh["rw_wrkv"] = np.ascontiguousarray(inp["rw_w_rkv"], np.float32)
    st = np.concatenate([np.arange((2 * p + hp) * 64, (2 * p + hp) * 64 + 64) for hp in range(2) for p in range(8)])
    sh["rw_wvst"] = np.ascontiguousarray(inp["rw_w_rkv"][:, 2][:, :, st], np.float32)
    sh["rw_wo"] = np.ascontiguousarray(inp["rw_w_o"], np.float32)
    ldn = inp["rw_lora_down"]
    sh["rw_ld"] = np.ascontiguousarray(np.concatenate([ldn[:, d, j] for d in range(2) for j in range(2)], axis=-1), np.float32)
    lup = inp["rw_lora_up"]
    sh["rw_lu"] = np.ascontiguousarray(np.stack([lup[:, d, j] for d in range(2) for j in range(2)], axis=1), np.float32)
    sh["rw_gd"] = np.ascontiguousarray(inp["rw_gate_down"], np.float32)
    sh["rw_gu"] = np.ascontiguousarray(inp["rw_gate_up"], np.float32)
    ii = np.arange(64)
    up = (ii[None, :] > ii[:, None]).astype(np.float32)
    le = (ii[:, None] <= ii[None, :]).astype(np.float32)
    lo_ = (ii[None, :] < ii[:, None]).astype(np.float32)

    def bdm(m):
        o = np.zeros((128, 128), np.float32)
        o[0:64, 0:64] = m
        o[64:128, 64:128] = m
        return o

    def stk(m):
        return np.concatenate([m, m], 0)
    wm = np.zeros((128, 2, 512), np.float32)
    for d, (u_, l_, s_) in enumerate(((up, lo_, le), (up.T, lo_.T, le.T))):
        wm[:, d, 0:128] = bdm(u_)
        wm[:, d, 128:192] = stk(s_)
        wm[:, d, 192:320] = bdm(u_)
        wm[:, d, 320:384] = stk(s_)
        wm[:, d, 384:512] = bdm(l_)
    sh["wmask"] = wm
    rm = np.ones((128, 2, 2048), np.float32)
    tt = np.arange(2048)
    rm[:, 0, tt % 64 == 0] = 0.0
    rm[:, 1, tt % 64 == 63] = 0.0
    sh["rmask"] = rm
    return sh


_CACHE = {}


def kernel(**inp):
    inp = {k_: np.asarray(v) for k_, v in inp.items()}
    sh = host_shared(inp)
    if "nc" not in _CACHE:
        _CACHE["nc"] = Prog().build()
    nc = _CACHE["nc"]
    in_maps = []
    for core in range(8):
        bs = [2 * core, 2 * core + 1]
        m = dict(sh)
        m["xT_in"] = np.ascontiguousarray(np.stack([inp["x"][b].T for b in bs]), np.float32)
        m["ctxT_in"] = np.ascontiguousarray(np.stack([inp["ctx"][b].T for b in bs]), np.float32)
        cv = np.stack([inp["c"][bs[0]], inp["c"][bs[1]], inp["c_ctx"]], 0)
        m["cT"] = np.ascontiguousarray(cv.reshape(3, 8, 128).transpose(2, 1, 0), np.float32)
        in_maps.append(m)
    res = run_bass_kernel_spmd(nc, in_maps, core_ids=list(range(8)))
    out = np.empty((16, SEQ, D), np.float32)
    for core in range(8):
        o = res.results[core]["outT"]
        for j in range(NB):
            out[2 * core + j] = o[j].T
    return out
```

```python
import math
import numpy as np
import concourse.bass as bass
import concourse.mybir as mybir
from concourse.bass_utils import run_bass_kernel_spmd

F32 = mybir.dt.float32
BF16 = mybir.dt.bfloat16
AF = mybir.ActivationFunctionType
ALU = mybir.AluOpType

D = 1024
SEQ = 4096
CTX = 256
T = SEQ + CTX
NB = 2
DEPTH = 4
EPS = 1e-6
DFF = 4096
TILES = [(0, CTX)] + [(CTX + 512 * i, 512) for i in range(8)]
WTILES = [(0, CTX)] + [(CTX + 256 * i, 256) for i in range(16)]
GELU_C = 2.0 * math.sqrt(2.0 / math.pi)
DECAY_SCALE = math.exp(-0.5)
GN_EPS = 64e-5


class Buf:
    __slots__ = ("w", "r")

    def __init__(self):
        self.w = None
        self.r = {}


class KB:
    NDMA = 40

    def __init__(self, nc):
        self.nc = nc
        self.eng = dict(pe=nc.tensor, act=nc.scalar, dve=nc.vector, pool=nc.gpsimd, sp=nc.sync)
        self.sems = {}
        self.cnt = {}
        for e in ("pe", "act", "dve", "pool"):
            self.sems[e] = nc.alloc_semaphore("s_" + e)
            self.cnt[e] = 0
        self.dsem = [nc.alloc_semaphore("d%d" % i) for i in range(self.NDMA)]
        self.dval = [0] * self.NDMA
        self.drr = 0
        self.waited = {e: {} for e in self.eng}
        self.sb_off = 16512
        self.sb_base = 16512
        self.nalloc = 0
        self.ninst = 0
        self.pending = []
        self.rec = None

    def sb(self, shape, dt=F32, name=None):
        self.nalloc += 1
        nm = "%s_%d" % (name or "t", self.nalloc)
        n = 1
        for s_ in shape[1:]:
            n *= s_
        nbytes = n * (4 if dt == F32 else 2)
        nbytes = (nbytes + 63) // 64 * 64
        off = self.sb_off
        self.sb_off += nbytes
        assert self.sb_off <= 229376, ("SBUF overflow", nm, self.sb_off)
        return self.nc.alloc_sbuf_tensor_at(nm, list(shape), dt, offset=off).ap()

    def phase_reset(self):
        self.barrier()
        self.sb_off = self.sb_base

    def persist_mark(self):
        self.sb_base = self.sb_off

    def _semh(self, key):
        return self.sems[key] if isinstance(key, str) else self.dsem[key[1]]

    def _wait(self, e, key, val, raw=False):
        if key == e and (not raw or e == "pe"):
            return
        w = self.waited[e]
        if w.get(key, 0) >= val:
            return
        w[key] = val
        self.pending.append((key, val))

    def _take(self):
        p = self.pending
        self.pending = []
        d = {}
        for k_, v_ in p:
            if d.get(k_, 0) < v_:
                d[k_] = v_
        return list(d.items())

    def _deps(self, e, reads, writes):
        for b in reads:
            if b.w is not None:
                self._wait(e, b.w[0], b.w[1], raw=True)
        for b in writes:
            if b.w is not None:
                self._wait(e, b.w[0], b.w[1])
            for k_, v_ in b.r.items():
                self._wait(e, k_, v_)

    def _mark(self, tok, reads, writes):
        k_, v_ = tok
        for b in reads:
            if b.r.get(k_, 0) < v_:
                b.r[k_] = v_
        for b in writes:
            b.w = tok
            b.r = {}

    def op(self, e, ins_fn, reads=(), writes=(), sig=True):
        self._deps(e, reads, writes)
        items = self._take()
        if self.rec is not None:
            self.rec.append((e, list(items), e if sig else None, 1))
        last = items.pop() if items else None
        for k_, v_ in items:
            self.eng[e].wait_ge(self._semh(k_), v_)
            self.ninst += 1
        ins = ins_fn(self.eng[e])
        if last is not None:
            ins._wait_ge(self._semh(last[0]), last[1])
        self.ninst += 1
        if sig:
            self.cnt[e] += 1
            ins.then_inc(self.sems[e], 1)
            tok = (e, self.cnt[e])
        else:
            tok = (e, self.cnt[e] + 1)
        self._mark(tok, reads, writes)
        return tok

    def dma(self, q, out, in_, reads=(), writes=(), **kw):
        i = self.drr
        self.drr = (self.drr + 1) % self.NDMA
        key = ("d", i)
        if self.dval[i] > 0:
            self._wait(q, key, self.dval[i])
        self._deps(q, reads, writes)
        its_ = self._take()
        if self.rec is not None:
            self.rec.append((q, list(its_), key, 16))
        for k_, v_ in its_:
            self.eng[q].wait_ge(self._semh(k_), v_)
            self.ninst += 1
        self.eng[q].dma_start(out=out, in_=in_, **kw).then_inc(self.dsem[i], 16)
        self.ninst += 1
        self.dval[i] += 16
        tok = (key, self.dval[i])
        self._mark(tok, reads, writes)
        return tok

    def barrier(self):
        for e in self.eng:
            for o in ("pe", "act", "dve", "pool"):
                if self.cnt[o] > 0:
                    self._wait(e, o, self.cnt[o])
            for i in range(self.NDMA):
                if self.dval[i] > 0:
                    self._wait(e, ("d", i), self.dval[i])
            its_ = self._take()
            if self.rec is not None:
                self.rec.append((e, list(its_), None, 0))
            for k_, v_ in its_:
                self.eng[e].wait_ge(self._semh(k_), v_)
                self.ninst += 1

    def finish(self):
        self.barrier()


def vb_layout():
    m = {}
    off = 0

    def add(name, n):
        nonlocal off
        m[name] = off
        off += n
    for l in range(DEPTH):
        add("ng0_%d" % l, 8)
        add("ng1_%d" % l, 8)
        add("adab_%d" % l, 48)
    for i in range(2):
        add("gq_%d" % i, 1)
        add("gk_%d" % i, 1)
        add("convw_%d" % i, 16)
        add("convb_%d" % i, 4)
        add("gateb_%d" % i, 16)
        add("lam_%d" % i, 8)
    for i in range(2):
        add("mu_%d" % i, 48)
        add("kk_%d" % i, 8)
        add("ka_%d" % i, 8)
        add("rk_%d" % i, 8)
        add("gng_%d" % i, 8)
        add("gnb_%d" % i, 8)
        add("lb_%d" % i, 32)
    return m, off


VBM, NVB = vb_layout()
PERM = np.concatenate([np.arange(0, 64, 2), np.arange(1, 64, 2)])


class Prog:
    def __init__(self, debug=(), nlayers=DEPTH, ext_in=()):
        self.debug = set(debug)
        self.ext_in = set(ext_in)
        self.nlayers = nlayers
        self.layers = list(range(nlayers))
        self.rw_stop = 99
        self.wkv_stage = 99
        self.wkv_ntiles = 99
        nc = self.nc = bass.Bass("TRN2", target_bir_lowering=False)
        k = self.k = KB(nc)
        self.dbuf = {}
        di = self.din
        self.xin = di("xT_in", [NB, D, SEQ])
        self.cin = di("ctxT_in", [NB, D, CTX])
        self.cT = di("cT", [128, 8, 3])
        self.vbd = di("vb", [128, NVB])
        self.ada_w = di("ada_w", [DEPTH, D, 6 * D])
        self.w1 = di("mlp_w1", [DEPTH, D, DFF])
        self.w2 = di("mlp_w2", [DEPTH, DFF, D])
        self.hy_win = di("hy_win", [2, D, 1920])
        self.hy_wout = di("hy_wout", [2, D, D])
        self.hy_gw = di("hy_gw", [2, 2, 2, 4, 128, 128])
        self.cst = di("consts", [128, 512])
        self.cosd = di("cosT", [128, SEQ])
        self.sind = di("sinT", [128, SEQ])
        self.rw_wrkv = di("rw_wrkv", [2, 3, D, D])
        self.rw_wvst = di("rw_wvst", [2, D, D])
        self.rw_wo = di("rw_wo", [2, D, D])
        self.rw_ld = di("rw_ld", [2, D, 256])
        self.rw_lu = di("rw_lu", [2, 4, 64, D])
        self.rw_gd = di("rw_gd", [2, D, 128])
        self.rw_gu = di("rw_gu", [2, 128, D])
        self.wmask = di("wmask", [128, 2, 512])
        self.rmask = di("rmask", [128, 2, 2048])
        self.out = nc.dram_tensor("outT", [NB, D, SEQ], F32, kind="ExternalOutput").ap()
        self.hnT = self.dscr("hnT", [D, T])
        self.rT = self.dscr("rT", [D, T])
        self.kkT = self.dscr("kkT", [D, T])
        self.kdT = self.dscr("kdT", [2, D, T])
        self.bT = self.dscr("bT", [2, D, T])
        self.lwT = self.dscr("lwT", [2, D, T])
        self.gT = self.dscr("gT", [D, T])
        self.bonT = self.dscr("bonT", [D, T])
        self.Vst = self.dscr("Vst", [T // 64, 128, 512])
        self.yT = self.dscr("yT", [2, D, T])
        self.xT = self.dscr("xT", [NB, D, T])
        self.qT = self.dscr("qT", [512, T], BF16)
        self.kT2 = self.dscr("kT2", [256, T], BF16)
        self.vtok = self.dscr("vtok", [T // 128, 128, 384], BF16)
        self.xr = self.dscr("xr", [512, T])
        self.gg = self.dscr("gg", [512, T], BF16)
        self.rec = self.dscr("rec", [512, T], BF16)
        self.ps = [nc.alloc_psum_tensor("ps%d" % i, [128, 512], F32).ap() for i in range(8)]
        self.PS = [Buf() for _ in range(8)]
        self.prr = 0
        self.vb = k.sb([128, NVB], F32, "vb")
        self.VB = Buf()
        self.cs = k.sb([128, 512], F32, "cs")
        self.CS = Buf()
        self.ident = self.cs[:, 0:128]
        self.ones_bf = k.sb([128, 128], BF16, "ones")
        self.bones_bf = k.sb([128, 128], BF16, "bones")
        self.pswap_bf = k.sb([128, 128], BF16, "pswap")
        self.scT = k.sb([128, 8, 3], BF16, "scT")
        self.mod = k.sb([128, 48, 3], F32, "mod")
        self.A1 = k.sb([128, 8, 3], F32, "A1")
        self.A2 = k.sb([128, 8, 3], F32, "A2")
        self.MOD = Buf()
        self.CONST = Buf()
        k.persist_mark()

    def din(self, name, shape, dt=F32):
        return self.nc.dram_tensor(name, list(shape), dt, kind="ExternalInput").ap()

    def dscr(self, name, shape, dt=F32):
        kind = "ExternalOutput" if name in self.debug else "Internal"
        if name in getattr(self, "ext_in", ()):
            kind = "ExternalInput"
        return self.nc.dram_tensor(name, list(shape), dt, kind=kind).ap()

    def DB(self, *key):
        b = self.dbuf.get(key)
        if b is None:
            b = self.dbuf[key] = Buf()
        return b

    def pb(self):
        i = self.prr
        self.prr = (self.prr + 1) % 8
        return self.ps[i], self.PS[i]

    @staticmethod
    def fm(ap2d, t0, n):
        return ap2d.rearrange("(fc p) t -> p fc t", p=128)[:, :, t0:t0 + n]

    def V(self, name, j=0, n=1):
        o = VBM[name] + j
        return self.vb[:, o:o + n]

    def setup(self):
        k = self.k
        k.dma("sp", self.vb, self.vbd, writes=[self.VB])
        k.dma("sp", self.cs, self.cst, writes=[self.CS])
        k.op("dve", lambda e: e.memset(self.ones_bf, 1.0), writes=[self.CONST])
        k.op("act", lambda e: e.activation(out=self.bones_bf, in_=self.cs[:, 128:256], func=AF.Copy),
             reads=[self.CS], writes=[self.CONST])
        k.op("act", lambda e: e.activation(out=self.pswap_bf, in_=self.cs[:, 256:384], func=AF.Copy),
             reads=[self.CS], writes=[self.CONST])
        ct = k.sb([128, 8, 3], F32)
        sg = k.sb([128, 8, 3], F32)
        CTB = Buf()
        k.dma("sp", ct, self.cT, writes=[CTB])
        k.op("act", lambda e: e.activation(out=sg, in_=ct, func=AF.Sigmoid), reads=[CTB], writes=[CTB])
        k.op("dve", lambda e: e.tensor_tensor(out=self.scT, in0=ct, in1=sg, op=ALU.mult), reads=[CTB],
             writes=[self.CONST])
        for b in range(NB):
            k.dma("sp", self.xT[b, :, 0:CTX], self.cin[b], writes=[self.DB("xT", b, 0)])
            for ti in range(1, 9):
                t0, n = TILES[ti]
                k.dma("sp", self.xT[b, :, t0:t0 + n], self.xin[b, :, t0 - CTX:t0 - CTX + n],
                      writes=[self.DB("xT", b, ti)])
        k.phase_reset()

    def phase_mod(self, l):
        k = self.k
        wa = k.sb([128, 8, 3072], BF16)
        WA = Buf()
        pm, PM = self.pb()
        for half in range(2):
            src = self.ada_w[l].rearrange("(kc p) n -> p kc n", p=128)[:, :, half * 3072:(half + 1) * 3072]
            for kc in range(8):
                k.dma("pool", wa[:, kc, :], src[:, kc, :], writes=[WA])
            for j in range(24):
                jj = half * 24 + j
                for kc in range(8):
                    k.op("pe", lambda e: e.matmul(pm[:, jj * 4:jj * 4 + 3], lhsT=wa[:, kc, j * 128:(j + 1) * 128],
                                                  rhs=self.scT[:, kc, :], start=(kc == 0), stop=(kc == 7)),
                         reads=[WA, self.CONST], writes=[PM], sig=(kc == 7))
        pmv = pm[:, 0:192].rearrange("p (j f) -> p j f", f=4)[:, :, 0:3]
        bias = self.V("adab_%d" % l, 0, 48).unsqueeze(2).broadcast_to([128, 48, 3])
        k.op("dve", lambda e: e.tensor_tensor(out=self.mod, in0=pmv, in1=bias, op=ALU.add),
             reads=[PM, self.VB], writes=[self.MOD])
        for (A, gname, sc0) in ((self.A1, "ng0_%d" % l, 8), (self.A2, "ng1_%d" % l, 32)):
            g = self.V(gname, 0, 8).unsqueeze(2).broadcast_to([128, 8, 3])
            k.op("dve", lambda e: e.scalar_tensor_tensor(out=A, in0=self.mod[:, sc0:sc0 + 8, :], scalar=1.0, in1=g,
                                                         op0=ALU.add, op1=ALU.mult),
                 reads=[self.MOD, self.VB], writes=[self.MOD])
        k.phase_reset()

    def norm_tile(self, xt, XTB, out, OUTB, A, Bsh, mi, n, W):
        k = self.k
        sq, rstd, tmp = W["sq"], W["rstd"], W["tmp"]
        k.op("act", lambda e: e.activation(out=sq[:, :, 0:n], in_=xt[:, :, 0:n], func=AF.Square),
             reads=[XTB], writes=[W["SQ"]])
        pn, PN = self.pb()
        for fc in range(8):
            k.op("pe", lambda e: e.matmul(pn[:, 0:n], lhsT=self.ones_bf, rhs=sq[:, fc, 0:n], start=(fc == 0),
                                          stop=(fc == 7)), reads=[W["SQ"], self.CONST], writes=[PN], sig=(fc == 7))
        k.op("act", lambda e: e.activation(out=rstd[:, 0:n], in_=pn[:, 0:n], func=AF.Sqrt, bias=EPS, scale=1.0 / D),
             reads=[PN], writes=[W["RSTD"]])
        k.op("dve", lambda e: e.reciprocal(out=rstd[:, 0:n], in_=rstd[:, 0:n]), reads=[W["RSTD"]], writes=[W["RSTD"]])
        for fc in range(8):
            j = fc % 2
            k.op("dve", lambda e: e.tensor_tensor(out=tmp[:, j, 0:n], in0=xt[:, fc, 0:n], in1=rstd[:, 0:n],
                                                  op=ALU.mult), reads=[XTB, W["RSTD"]], writes=[W["TMP"][j]])
            k.op("act", lambda e: e.activation(out=out[:, fc, 0:n], in_=tmp[:, j, 0:n], func=AF.Identity,
                                               bias=Bsh[:, fc, mi:mi + 1], scale=A[:, fc, mi:mi + 1]),
                 reads=[W["TMP"][j], self.MOD], writes=[OUTB])

    def norm_work(self):
        k = self.k
        return dict(sq=k.sb([128, 8, 512], BF16), rstd=k.sb([128, 512], F32), tmp=k.sb([128, 2, 512], F32),
                    SQ=Buf(), RSTD=Buf(), TMP=[Buf(), Buf()])

    def load_w(self, dst, DSTB, src2d, nkc, split=1):
        v = src2d.rearrange("(kc p) n -> p kc n", p=128)
        for kc in range(nkc):
            self.k.dma("pool", dst[:, kc, :], v[:, kc, :], writes=[DSTB])

    def phase_mlp(self, l, b, last):
        k = self.k
        w1 = k.sb([128, 8, DFF], BF16)
        w2 = k.sb([128, 32, D], BF16)
        W1B, W2B = Buf(), Buf()
        self.load_w(w1, W1B, self.w1[l], 8)
        self.load_w(w2, W2B, self.w2[l], 32)
        W = self.norm_work()
        xt = k.sb([128, 8, 512], F32)
        XTB = Buf()
        hn = k.sb([128, 8, 512], BF16)
        HNB = Buf()
        h1 = k.sb([128, 32, 512], BF16)
        H1B = [Buf() for _ in range(32)]
        rl = k.sb([128, 2, 512], BF16)
        RLB = [Buf(), Buf()]
        for ti, (t0, n) in enumerate(TILES):
            if last and ti == 0:
                continue
            mi = 2 if ti == 0 else b
            k.dma("sp", xt[:, :, 0:n], self.fm(self.xT[b], t0, n), reads=[self.DB("xT", b, ti)], writes=[XTB])
            self.norm_tile(xt, XTB, hn, HNB, self.A2, self.mod[:, 24:32, :], mi, n, W)
            for oc in range(32):
                p, P = self.pb()
                for kc in range(8):
                    k.op("pe", lambda e: e.matmul(p[:, 0:n], lhsT=w1[:, kc, oc * 128:(oc + 1) * 128], rhs=hn[:, kc, 0:n],
                                                  start=(kc == 0), stop=(kc == 7)),
                         reads=[W1B, HNB], writes=[P], sig=(kc == 7))
                j = oc % 2
                k.op("act", lambda e: e.activation(out=rl[:, j, 0:n], in_=p[:, 0:n], func=AF.Relu),
                     reads=[P], writes=[RLB[j]])
                k.op("pool", lambda e: e.tensor_tensor(out=h1[:, oc, 0:n], in0=rl[:, j, 0:n], in1=rl[:, j, 0:n],
                                                       op=ALU.mult), reads=[RLB[j]], writes=[H1B[oc]])
            for oc in range(8):
                p, P = self.pb()
                for kc in range(32):
                    k.op("pe", lambda e: e.matmul(p[:, 0:n], lhsT=w2[:, kc, oc * 128:(oc + 1) * 128], rhs=h1[:, kc, 0:n],
                                                  start=(kc == 0), stop=(kc == 31)),
                         reads=[W2B, H1B[kc]], writes=[P], sig=(kc == 31))
                k.op("dve", lambda e: e.scalar_tensor_tensor(out=xt[:, oc, 0:n], in0=p[:, 0:n],
                                                             scalar=self.mod[:, 40 + oc, mi:mi + 1],
                                                             in1=xt[:, oc, 0:n], op0=ALU.mult, op1=ALU.add),
                     reads=[P, self.MOD, XTB], writes=[XTB])
            if last:
                k.dma("sp", self.out[b].rearrange("(fc p) t -> p fc t", p=128)[:, :, t0 - CTX:t0 - CTX + n],
                      xt[:, :, 0:n], reads=[XTB], writes=[self.DB("out", b, ti)])
            else:
                k.dma("sp", self.fm(self.xT[b], t0, n), xt[:, :, 0:n], reads=[XTB], writes=[self.DB("xT", b, ti)])
        k.phase_reset()

    def phase_hy_inproj(self, l, b):
        k = self.k
        i = l // 2
        win = k.sb([128, 8, 1920], BF16)
        WB = Buf()
        self.load_w(win, WB, self.hy_win[i], 8)
        W = self.norm_work()
        xt = [k.sb([128, 8, 512], F32) for _ in range(2)]
        XTB = [Buf(), Buf()]
        hn = k.sb([128, 8, 512], BF16)
        HNB = Buf()
        cs_t = k.sb([128, 2, 512], F32)
        CSB = Buf()
        sq = k.sb([128, 512], BF16)
        SQB = Buf()
        rs = k.sb([128, 512], F32)
        RSB = Buf()
        xn = k.sb([128, 512], BF16)
        XNB = Buf()
        t1 = k.sb([128, 512], F32)
        t2 = k.sb([128, 512], F32)
        T1B, T2B = Buf(), Buf()
        qk = k.sb([128, 6, 512], BF16)
        QKB = Buf()
        vt = k.sb([128, 4, 384], BF16)
        VTB = Buf()
        xro = k.sb([128, 4, 512], F32)
        XRB = Buf()
        ggo = k.sb([128, 4, 512], BF16)
        GGB = Buf()
        z2 = k.sb([128, 512], F32)
        Z2B = Buf()
        k.op("dve", lambda e: e.memset(vt, 1.0), writes=[VTB])
        for ti, (t0, n) in enumerate(TILES):
            mi = 2 if ti == 0 else b
            x_ = xt[ti % 2]
            XB = XTB[ti % 2]
            k.dma("sp", x_[:, :, 0:n], self.fm(self.xT[b], t0, n), reads=[self.DB("xT", b, ti)], writes=[XB])
            if ti > 0:
                k.dma("sp", cs_t[:, 0, 0:n], self.cosd[:, t0 - CTX:t0 - CTX + n], writes=[CSB])
                k.dma("sp", cs_t[:, 1, 0:n], self.sind[:, t0 - CTX:t0 - CTX + n], writes=[CSB])
            self.norm_tile(x_, XB, hn, HNB, self.A1, self.mod[:, 0:8, :], mi, n, W)
            for oc in range(6):
                p, P = self.pb()
                for kc in range(8):
                    k.op("pe", lambda e: e.matmul(p[:, 0:n], lhsT=win[:, kc, oc * 128:(oc + 1) * 128], rhs=hn[:, kc, 0:n],
                                                  start=(kc == 0), stop=(kc == 7)), reads=[WB, HNB], writes=[P],
                         sig=(kc == 7))
                k.op("act", lambda e: e.activation(out=sq[:, 0:n], in_=p[:, 0:n], func=AF.Square), reads=[P], writes=[SQB])
                p2, P2 = self.pb()
                k.op("pe", lambda e: e.matmul(p2[:, 0:n], lhsT=self.bones_bf, rhs=sq[:, 0:n], start=True, stop=True),
                     reads=[SQB, self.CONST], writes=[P2])
                k.op("act", lambda e: e.activation(out=rs[:, 0:n], in_=p2[:, 0:n], func=AF.Sqrt, bias=EPS,
                                                   scale=1.0 / 64), reads=[P2], writes=[RSB])
                k.op("dve", lambda e: e.reciprocal(out=rs[:, 0:n], in_=rs[:, 0:n]), reads=[RSB], writes=[RSB])
                g = self.V("gq_%d" % i) if oc < 4 else self.V("gk_%d" % i)
                dst = qk[:, oc, 0:n] if ti == 0 else xn[:, 0:n]
                DSTB = QKB if ti == 0 else XNB
                k.op("dve", lambda e: e.scalar_tensor_tensor(out=dst, in0=p[:, 0:n], scalar=g, in1=rs[:, 0:n],
                                                             op0=ALU.mult, op1=ALU.mult),
                     reads=[P, RSB, self.VB], writes=[DSTB])
                if ti > 0:
                    p3, P3 = self.pb()
                    k.op("pe", lambda e: e.matmul(p3[:, 0:n], lhsT=self.pswap_bf, rhs=xn[:, 0:n], start=True, stop=True),
                         reads=[XNB, self.CONST], writes=[P3])
                    k.op("pool", lambda e: e.tensor_tensor(out=t1[:, 0:n], in0=xn[:, 0:n], in1=cs_t[:, 0, 0:n],
                                                           op=ALU.mult), reads=[XNB, CSB], writes=[T1B])
                    k.op("dve", lambda e: e.tensor_tensor(out=t2[:, 0:n], in0=p3[:, 0:n], in1=cs_t[:, 1, 0:n],
                                                          op=ALU.mult), reads=[P3, CSB], writes=[T2B])
                    k.op("dve", lambda e: e.tensor_tensor(out=qk[:, oc, 0:n], in0=t1[:, 0:n], in1=t2[:, 0:n],
                                                          op=ALU.add), reads=[T1B, T2B], writes=[QKB])
            k.dma("sp", self.fm(self.qT, t0, n), qk[:, 0:4, 0:n], reads=[QKB], writes=[self.DB("qT", ti)])
            k.dma("sp", self.fm(self.kT2, t0, n), qk[:, 4:6, 0:n], reads=[QKB], writes=[self.DB("kT2", ti)])
            nst = n // 128
            for st in range(nst):
                p, P = self.pb()
                for kc in range(8):
                    k.op("pe", lambda e: e.matmul(p[:, 0:128], lhsT=hn[:, kc, st * 128:(st + 1) * 128],
                                                  rhs=win[:, kc, 768:896], start=(kc == 0), stop=(kc == 7)),
                         reads=[WB, HNB], writes=[P], sig=(kc == 7))
                vv = vt[:, st, :].rearrange("p (h c) -> p h c", c=192)[:, :, 64:128]
                k.op("act", lambda e: e.activation(out=vv, in_=p[:, 0:128].rearrange("p (h c) -> p h c", c=64),
                                                   func=AF.Copy), reads=[P], writes=[VTB])
            c0 = t0 // 128
            k.dma("sp", self.vtok[c0:c0 + nst].rearrange("c p f -> p c f"), vt[:, 0:nst, :], reads=[VTB],
                  writes=[self.DB("vtok", ti)])
            for oc in range(4):
                p, P = self.pb()
                for kc in range(8):
                    k.op("pe", lambda e: e.matmul(p[:, 0:n], lhsT=win[:, kc, 896 + oc * 128:896 + (oc + 1) * 128],
                                                  rhs=hn[:, kc, 0:n], start=(kc == 0), stop=(kc == 7)),
                         reads=[WB, HNB], writes=[P], sig=(kc == 7))
                k.op("act", lambda e: e.activation(out=xro[:, oc, 0:n], in_=p[:, 0:n], func=AF.Copy), reads=[P],
                     writes=[XRB])
            k.dma("sp", self.fm(self.xr, t0, n), xro[:, :, 0:n], reads=[XRB], writes=[self.DB("xr", ti)])
            for oc in range(4):
                p, P = self.pb()
                for kc in range(8):
                    k.op("pe", lambda e: e.matmul(p[:, 0:n], lhsT=win[:, kc, 1408 + oc * 128:1408 + (oc + 1) * 128],
                                                  rhs=hn[:, kc, 0:n], start=(kc == 0), stop=(kc == 7)),
                         reads=[WB, HNB], writes=[P], sig=(kc == 7))
                self.gelu_from_psum(p, P, ggo[:, oc, 0:n], GGB, n, z2, Z2B, t1, T1B)
            k.dma("sp", self.fm(self.gg, t0, n), ggo[:, :, 0:n], reads=[GGB], writes=[self.DB("gg", ti)])
        k.phase_reset()

    def gelu_from_psum(self, p, P, dst, DSTB, n, z2, Z2B, t1, T1B):
        k = self.k
        k.op("act", lambda e: e.activation(out=z2[:, 0:n], in_=p[:, 0:n], func=AF.Square), reads=[P], writes=[Z2B])
        k.op("dve", lambda e: e.tensor_scalar(out=z2[:, 0:n], in0=z2[:, 0:n], scalar1=0.044715, scalar2=1.0,
                                              op0=ALU.mult, op1=ALU.add), reads=[Z2B], writes=[Z2B])
        k.op("dve", lambda e: e.tensor_tensor(out=z2[:, 0:n], in0=z2[:, 0:n], in1=p[:, 0:n], op=ALU.mult),
             reads=[Z2B, P], writes=[Z2B])
        k.op("act", lambda e: e.activation(out=t1[:, 0:n], in_=z2[:, 0:n], func=AF.Sigmoid, scale=GELU_C),
             reads=[Z2B], writes=[T1B])
        k.op("dve", lambda e: e.tensor_tensor(out=dst, in0=t1[:, 0:n], in1=p[:, 0:n], op=ALU.mult),
             reads=[T1B, P], writes=[DSTB])

    def phase_rglru(self, l, b):
        k = self.k
        i = l // 2
        gw = k.sb([128, 16, 128], BF16)
        GWB = Buf()
        k.dma("pool", gw, self.hy_gw[i].rearrange("d g c p m -> p (d g c) m"), writes=[GWB])
        c1 = k.sb([128, 8], F32)
        c2 = k.sb([128, 8], F32)
        C1B = Buf()
        k.op("act", lambda e: e.activation(out=c1, in_=self.V("lam_%d" % i, 0, 8), func=AF.Exp, scale=-1.0),
             reads=[self.VB], writes=[C1B])
        k.op("act", lambda e: e.activation(out=c1, in_=c1, func=AF.Ln, bias=1.0), reads=[C1B], writes=[C1B])
        k.op("dve", lambda e: e.tensor_scalar(out=c2, in0=c1, scalar1=-16.0, scalar2=None, op0=ALU.mult),
             reads=[C1B], writes=[C1B])
        k.op("dve", lambda e: e.tensor_scalar(out=c1, in0=c1, scalar1=-8.0, scalar2=None, op0=ALU.mult),
             reads=[C1B], writes=[C1B])
        x = k.sb([128, T], F32)
        xc = k.sb([128, T], F32)
        xcb = k.sb([128, T], BF16)
        r = k.sb([128, T], F32)
        ig = k.sb([128, T], F32)
        a = k.sb([128, T], F32)
        u = k.sb([128, T], F32)
        h = [k.sb([128, T], F32) for _ in range(2)]
        ggt = k.sb([128, T], BF16)
        rec = k.sb([128, T], BF16)
        XB, XCB, XCBB, RB, IB, AB, UB, GB, RECB = [Buf() for _ in range(9)]
        HB = [Buf(), Buf()]
        segs = [(0, CTX), (CTX, T)]
        for c in range(4):
            k.dma("sp", x, self.xr[c * 128:(c + 1) * 128, :], reads=[self.DB("xr", ti) for ti in range(9)], writes=[XB])
            k.dma("sp", ggt, self.gg[c * 128:(c + 1) * 128, :], reads=[self.DB("gg", ti) for ti in range(9)],
                  writes=[GB])
            k.op("act", lambda e: e.activation(out=xc, in_=x, func=AF.Identity, bias=self.V("convb_%d" % i, c),
                                               scale=self.V("convw_%d" % i, 2 * 4 + c)),
                 reads=[XB, self.VB], writes=[XCB])
            for (s0, s1) in segs:
                for j in (0, 1, 3):
                    sh = j - 2
                    a0 = max(s0, s0 - sh)
                    a1 = min(s1, s1 - sh)
                    k.op("dve", lambda e: e.scalar_tensor_tensor(out=xc[:, a0:a1], in0=x[:, a0 + sh:a1 + sh],
                                                                 scalar=self.V("convw_%d" % i, j * 4 + c),
                                                                 in1=xc[:, a0:a1], op0=ALU.mult, op1=ALU.add),
                         reads=[XB, XCB, self.VB], writes=[XCB])
            k.op("act", lambda e: e.activation(out=xcb, in_=xc, func=AF.Copy), reads=[XCB], writes=[XCBB])
            for d in range(2):
                for (t0, n) in TILES:
                    for g, (dst, DB_) in enumerate(((r, RB), (ig, IB))):
                        p, P = self.pb()
                        k.op("pe", lambda e: e.matmul(p[:, 0:n], lhsT=gw[:, (d * 2 + g) * 4 + c, :], rhs=xcb[:, t0:t0 + n],
                                                      start=True, stop=True), reads=[GWB, XCBB], writes=[P])
                        k.op("act", lambda e: e.activation(out=dst[:, t0:t0 + n], in_=p[:, 0:n], func=AF.Sigmoid,
                                                           bias=self.V("gateb_%d" % i, (d * 2 + g) * 4 + c)),
                             reads=[P, self.VB], writes=[DB_])
                k.op("act", lambda e: e.activation(out=a, in_=r, func=AF.Exp, scale=c1[:, d * 4 + c:d * 4 + c + 1]),
                     reads=[RB, C1B], writes=[AB])
                k.op("act", lambda e: e.activation(out=u, in_=r, func=AF.Exp, scale=c2[:, d * 4 + c:d * 4 + c + 1]),
                     reads=[RB, C1B], writes=[UB])
                k.op("act", lambda e: e.activation(out=u, in_=u, func=AF.Sqrt, bias=1.0, scale=-1.0),
                     reads=[UB], writes=[UB])
                k.op("dve", lambda e: e.tensor_tensor(out=ig, in0=ig, in1=xc, op=ALU.mult), reads=[IB, XCB], writes=[IB])
                k.op("dve", lambda e: e.tensor_tensor(out=u, in0=u, in1=ig, op=ALU.mult), reads=[UB, IB], writes=[UB])
                hd = h[d]
                if d == 0:
                    k.op("dve", lambda e: e.tensor_tensor_scan(out=hd, data0=a, data1=u, initial=0.0, op0=ALU.mult,
                                                               op1=ALU.add), reads=[AB, UB], writes=[HB[d]])
                else:
                    k.op("dve", lambda e: e.tensor_tensor_scan(out=hd[:, 0:CTX][:, ::-1], data0=a[:, 0:CTX][:, ::-1],
                                                               data1=u[:, 0:CTX][:, ::-1], initial=0.0, op0=ALU.mult,
                                                               op1=ALU.add), reads=[AB, UB], writes=[HB[d]])
                    k.op("dve", lambda e: e.tensor_tensor_scan(out=hd[:, CTX:T][:, ::-1], data0=a[:, CTX:T][:, ::-1],
                                                               data1=u[:, CTX:T][:, ::-1], initial=hd[:, 0:1],
                                                               op0=ALU.mult, op1=ALU.add), reads=[AB, UB, HB[d]],
                         writes=[HB[d]])
            if "rgd" in self.debug and c == 0 and b == 0:
                rgd = self.nc.dram_tensor("rgd", [8, 128, T], F32, kind="ExternalOutput").ap()
                for j_, (t_, B_) in enumerate(((x, XB), (xc, XCB), (r, RB), (ig, IB), (a, AB), (u, UB), (h[0], HB[0]),
                                               (h[1], HB[1]))):
                    k.dma("sp", rgd[j_], t_, reads=[B_], writes=[Buf()])
                c1d = self.nc.dram_tensor("c1d", [2, 128, 8], F32, kind="ExternalOutput").ap()
                k.dma("sp", c1d[0], c1, reads=[C1B], writes=[Buf()])
                k.dma("sp", c1d[1], c2, reads=[C1B], writes=[Buf()])
            k.op("dve", lambda e: e.tensor_tensor(out=h[0], in0=h[0], in1=h[1], op=ALU.add), reads=[HB[0], HB[1]],
                 writes=[HB[0]])
            k.op("dve", lambda e: e.tensor_tensor(out=rec, in0=h[0], in1=ggt, op=ALU.mult), reads=[HB[0], GB],
                 writes=[RECB])
            k.dma("sp", self.rec[c * 128:(c + 1) * 128, :], rec, reads=[RECB], writes=[self.DB("rec", c)])
        k.phase_reset()

    def phase_attn(self, l, b):
        k = self.k
        i = l // 2
        kt = k.sb([128, 2, T], BF16)
        KTB = Buf()
        k.dma("sp", kt, self.kT2.rearrange("(c p) t -> p c t", p=128), reads=[self.DB("kT2", ti) for ti in range(9)],
              writes=[KTB])
        vt = k.sb([128, T // 128, 384], BF16)
        VTB = Buf()
        k.dma("sp", vt, self.vtok.rearrange("c p f -> p c f"), reads=[self.DB("vtok", ti) for ti in range(9)],
              writes=[VTB])
        wo = k.sb([128, 8, D], BF16)
        WOB = Buf()
        self.load_w(wo, WOB, self.hy_wout[i], 8)
        xt = k.sb([128, 8, 512], F32)
        XB = Buf()
        q = k.sb([128, 4, 512], BF16)
        QB = Buf()
        rc = k.sb([128, 4, 512], BF16)
        RCB = Buf()
        att = k.sb([128, 4, 512], BF16)
        ATB = Buf()
        NPT = 8
        pt = [k.sb([128, 512], BF16) for _ in range(NPT)]
        PTB = [Buf() for _ in range(NPT)]
        den = k.sb([128, 512], F32)
        DNB = Buf()
        ptr = 0
        RECALL = [self.DB("rec", c) for c in range(4)]
        for ti, (t0, n) in enumerate(TILES):
            mi = 2 if ti == 0 else b
            nkc = 2 if ti == 0 else T // 128
            k.dma("sp", xt[:, :, 0:n], self.fm(self.xT[b], t0, n), reads=[self.DB("xT", b, ti)], writes=[XB])
            k.dma("sp", q[:, :, 0:n], self.fm(self.qT, t0, n), reads=[self.DB("qT", ti)], writes=[QB])
            k.dma("sp", rc[:, :, 0:n], self.fm(self.rec, t0, n), reads=RECALL, writes=[RCB])
            for hh in range(8):
                kv, hp, fc = hh // 4, hh % 2, hh // 2
                lo, hi = hp * 64, hp * 64 + 64
                olo, ohi = (1 - hp) * 64, (1 - hp) * 64 + 64
                po, PO = self.ps[6 + hh % 2], self.PS[6 + hh % 2]
                voff = kv * 192 + (64 if hp == 0 else 0)
                LA = 5
                slots = []

                def pv(kc, pj):
                    k.op("pe", lambda e: e.matmul(po[:, 0:n], lhsT=vt[:, kc, voff:voff + 128], rhs=pt[pj][:, 0:n],
                                                  start=(kc == 0), stop=(kc == nkc - 1)),
                         reads=[VTB, PTB[pj]], writes=[PO], sig=(kc == nkc - 1))
                for kc in range(nkc):
                    sbi = ptr % 6
                    ps_, PSB = self.ps[sbi], self.PS[sbi]
                    k.op("pe", lambda e: e.matmul(ps_[:, 0:n], lhsT=kt[lo:hi, kv, kc * 128:(kc + 1) * 128],
                                                  rhs=q[lo:hi, fc, 0:n], start=True, stop=True),
                         reads=[KTB, QB], writes=[PSB])
                    pj = ptr % NPT
                    ptr += 1
                    k.op("act", lambda e: e.activation(out=pt[pj][:, 0:n], in_=ps_[:, 0:n], func=AF.Exp, scale=0.125),
                         reads=[PSB], writes=[PTB[pj]])
                    slots.append((kc, pj))
                    if len(slots) > LA:
                        pv(*slots.pop(0))
                while slots:
                    pv(*slots.pop(0))
                k.op("act", lambda e: e.activation(out=den[lo:hi, 0:n], in_=po[olo:ohi, 0:n], func=AF.Copy),
                     reads=[PO], writes=[DNB])
                k.op("dve", lambda e: e.reciprocal(out=den[lo:hi, 0:n], in_=den[lo:hi, 0:n]), reads=[DNB], writes=[DNB])
                k.op("dve", lambda e: e.tensor_tensor(out=att[lo:hi, fc, 0:n], in0=po[lo:hi, 0:n], in1=den[lo:hi, 0:n],
                                                      op=ALU.mult), reads=[PO, DNB], writes=[ATB])
            for oc in range(8):
                p, P = self.pb()
                for kc in range(8):
                    src = att[:, kc, 0:n] if kc < 4 else rc[:, kc - 4, 0:n]
                    k.op("pe", lambda e: e.matmul(p[:, 0:n], lhsT=wo[:, kc, oc * 128:(oc + 1) * 128], rhs=src,
                                                  start=(kc == 0), stop=(kc == 7)),
                         reads=[WOB, ATB, RCB], writes=[P], sig=(kc == 7))
                k.op("dve", lambda e: e.scalar_tensor_tensor(out=xt[:, oc, 0:n], in0=p[:, 0:n],
                                                             scalar=self.mod[:, 16 + oc, mi:mi + 1], in1=xt[:, oc, 0:n],
                                                             op0=ALU.mult, op1=ALU.add),
                     reads=[P, self.MOD, XB], writes=[XB])
            k.dma("sp", self.fm(self.xT[b], t0, n), xt[:, :, 0:n], reads=[XB], writes=[self.DB("xT", b, ti)])
        k.phase_reset()

    def phase_rwkv(self, l, b):
        last = (l == DEPTH - 1)
        self.rw_norm(l, b)
        if self.rw_stop >= 1:
            self.rw_proj(l, b)
        for d in range(2):
            if self.rw_stop >= 2 + d:
                self.rw_wkv(l, b, d)
        if self.rw_stop >= 4:
            self.rw_out(l, b, last)

    def rw_norm(self, l, b):
        k = self.k
        W = self.norm_work()
        xt = [k.sb([128, 8, 512], F32) for _ in range(2)]
        XB = [Buf(), Buf()]
        hn = [k.sb([128, 8, 512], F32) for _ in range(2)]
        HB = [Buf(), Buf()]
        for ti, (t0, n) in enumerate(TILES):
            mi = 2 if ti == 0 else b
            j = ti % 2
            k.dma("sp", xt[j][:, :, 0:n], self.fm(self.xT[b], t0, n), reads=[self.DB("xT", b, ti)], writes=[XB[j]])
            self.norm_tile(xt[j], XB[j], hn[j], HB[j], self.A1, self.mod[:, 0:8, :], mi, n, W)
            k.dma("sp", self.fm(self.hnT, t0, n), hn[j][:, :, 0:n], reads=[HB[j]], writes=[self.DB("hnT", ti)])
        k.phase_reset()

    def rw_proj(self, l, b):
        k = self.k
        i = l // 2
        wr = k.sb([128, 8, D], BF16)
        wk = k.sb([128, 8, D], BF16)
        wvs = k.sb([128, 8, D], BF16)
        ld = k.sb([128, 8, 256], BF16)
        lu = k.sb([64, 4, D], BF16)
        gd = k.sb([128, 8, 128], BF16)
        gu = k.sb([128, D], BF16)
        WB = Buf()
        self.load_w(wr, WB, self.rw_wrkv[i, 0], 8)
        self.load_w(wk, WB, self.rw_wrkv[i, 1], 8)
        self.load_w(wvs, WB, self.rw_wvst[i], 8)
        self.load_w(ld, WB, self.rw_ld[i], 8)
        self.load_w(gd, WB, self.rw_gd[i], 8)
        k.dma("pool", lu, self.rw_lu[i].rearrange("q p n -> p q n"), writes=[WB])
        k.dma("pool", gu, self.rw_gu[i], writes=[WB])
        omka = k.sb([128, 8], F32)
        OMB = Buf()
        k.op("dve", lambda e: e.tensor_scalar(out=omka, in0=self.V("ka_%d" % i, 0, 8), scalar1=-1.0, scalar2=1.0,
                                              op0=ALU.mult, op1=ALU.add), reads=[self.VB], writes=[OMB])
        NT = 256
        hh = k.sb([128, 8, NT + 2], F32)
        HHB = Buf()
        xx = k.sb([128, 8, NT], F32)
        XXB = Buf()
        L = [k.sb([128, 8, NT], BF16) for _ in range(6)]
        LB = [Buf() for _ in range(6)]
        rt = k.sb([128, 8, NT], F32)
        kt = k.sb([128, 8, NT], F32)
        kkt = k.sb([128, 8, NT], F32)
        kd = [k.sb([128, 8, NT], F32) for _ in range(2)]
        o1 = k.sb([128, 8, NT], F32)
        RTB, KTB, KKB, O1B = Buf(), Buf(), Buf(), Buf()
        KDB = [Buf(), Buf()]
        vst = k.sb([128, 2, 512], F32)
        VSB = [Buf(), Buf()]
        sm = k.sb([128, NT], BF16)
        SMB = Buf()
        at8 = k.sb([128, 8, NT], F32)
        AT8 = Buf()
        tp = k.sb([128, 8, NT], F32)
        TPB = Buf()
        w18 = k.sb([128, 8, NT], F32)
        W18 = Buf()
        sq8 = k.sb([128, 8, NT], BF16)
        SQ8 = Buf()
        bones_f = self.cs[:, 128:256]

        def bc(name, j0, n):
            return self.V(name, j0, 8).unsqueeze(2).broadcast_to([128, 8, n])

        for ti, (t0, n) in enumerate(WTILES):
            seg0, seg1 = (0, CTX) if ti == 0 else (CTX, T)
            lo = max(t0 - 1, seg0)
            hi = min(t0 + n + 1, seg1)
            k.dma("sp", hh[:, :, lo - (t0 - 1):hi - (t0 - 1)], self.fm(self.hnT, lo, hi - lo),
                  reads=[self.DB("hnT", j) for j in range(9)], writes=[HHB])
            if lo != t0 - 1:
                k.op("dve", lambda e: e.memset(hh[:, :, 0:1], 0.0), writes=[HHB])
            if hi != t0 + n + 1:
                k.op("dve", lambda e: e.memset(hh[:, :, n + 1:n + 2], 0.0), writes=[HHB])
            h = hh[:, :, 1:n + 1]
            k.op("dve", lambda e: e.tensor_tensor(out=xx[:, :, 0:n], in0=hh[:, :, 0:n], in1=hh[:, :, 2:n + 2], op=ALU.add),
                 reads=[HHB], writes=[XXB])
            k.op("dve", lambda e: e.scalar_tensor_tensor(out=xx[:, :, 0:n], in0=xx[:, :, 0:n], scalar=0.5, in1=h,
                                                         op0=ALU.mult, op1=ALU.subtract), reads=[XXB, HHB], writes=[XXB])
            for j in (0, 2, 3, 1, 4, 5):
                if j in (0, 2, 3):
                    eng, tmp, TB = "dve", at8, AT8
                else:
                    eng, tmp, TB = "pool", tp, TPB
                k.op(eng, lambda e: e.tensor_tensor(out=tmp[:, :, 0:n], in0=xx[:, :, 0:n], in1=bc("mu_%d" % i, j * 8, n),
                                                    op=ALU.mult), reads=[XXB, self.VB], writes=[TB])
                k.op(eng, lambda e: e.tensor_tensor(out=L[j][:, :, 0:n], in0=tmp[:, :, 0:n], in1=h, op=ALU.add),
                     reads=[TB, HHB], writes=[LB[j]])

            def proj(w, Lj, LjB, dst, DSTB):
                for oc in range(8):
                    p, P = self.pb()
                    for kc in range(8):
                        k.op("pe", lambda e: e.matmul(p[:, 0:n], lhsT=w[:, kc, oc * 128:(oc + 1) * 128], rhs=Lj[:, kc, 0:n],
                                                      start=(kc == 0), stop=(kc == 7)), reads=[WB, LjB], writes=[P],
                             sig=(kc == 7))
                    k.op("act", lambda e: e.activation(out=dst[:, oc, 0:n], in_=p[:, 0:n], func=AF.Copy), reads=[P],
                         writes=[DSTB])
            proj(wr, L[0], LB[0], rt, RTB)
            k.dma("sp", self.fm(self.rT, t0, n), rt[:, :, 0:n], reads=[RTB], writes=[self.DB("rT", ti)])
            proj(wk, L[2], LB[2], kt, KTB)
            for ch in range(n // 64):
                p, P = self.pb()
                for hp in range(2):
                    for kc in range(8):
                        k.op("pe", lambda e: e.matmul(p[hp * 64:(hp + 1) * 64, :], lhsT=L[3][:, kc, ch * 64:(ch + 1) * 64],
                                                      rhs=wvs[:, kc, hp * 512:(hp + 1) * 512], start=(kc == 0),
                                                      stop=(kc == 7)), reads=[WB, LB[3]], writes=[P],
                             sig=(kc == 7 and hp == 1))
                j = ch % 2
                k.op("act", lambda e: e.activation(out=vst[:, j, :], in_=p, func=AF.Copy), reads=[P], writes=[VSB[j]])
                k.dma("sp", self.Vst[t0 // 64 + ch], vst[:, j, :], reads=[VSB[j]], writes=[self.DB("Vst", ti)])
            p, P = self.pb()
            for kc in range(8):
                k.op("pe", lambda e: e.matmul(p[:, 0:n], lhsT=gd[:, kc, :], rhs=L[5][:, kc, 0:n], start=(kc == 0),
                                              stop=(kc == 7)), reads=[WB, LB[5]], writes=[P], sig=(kc == 7))
            k.op("act", lambda e: e.activation(out=sm[:, 0:n], in_=p[:, 0:n], func=AF.Sigmoid), reads=[P], writes=[SMB])
            for oc in range(8):
                p, P = self.pb()
                k.op("pe", lambda e: e.matmul(p[:, 0:n], lhsT=gu[:, oc * 128:(oc + 1) * 128], rhs=sm[:, 0:n], start=True,
                                              stop=True), reads=[WB, SMB], writes=[P])
                k.op("act", lambda e: e.activation(out=o1[:, oc, 0:n], in_=p[:, 0:n], func=AF.Copy), reads=[P],
                     writes=[O1B])
            k.dma("sp", self.fm(self.gT, t0, n), o1[:, :, 0:n], reads=[O1B], writes=[self.DB("gT", ti)])
            k.op("dve", lambda e: e.tensor_tensor(out=kkt[:, :, 0:n], in0=kt[:, :, 0:n], in1=bc("kk_%d" % i, 0, n),
                                                  op=ALU.mult), reads=[KTB, self.VB], writes=[KKB])
            k.op("act", lambda e: e.activation(out=sq8[:, :, 0:n], in_=kkt[:, :, 0:n], func=AF.Square), reads=[KKB],
                 writes=[SQ8])
            for oc in range(8):
                p, P = self.pb()
                k.op("pe", lambda e: e.matmul(p[:, 0:n], lhsT=self.bones_bf, rhs=sq8[:, oc, 0:n], start=True, stop=True),
                     reads=[SQ8, self.CONST], writes=[P])
                k.op("act", lambda e: e.activation(out=w18[:, oc, 0:n], in_=p[:, 0:n], func=AF.Sqrt), reads=[P],
                     writes=[W18])
            k.op("dve", lambda e: e.tensor_scalar(out=w18[:, :, 0:n], in0=w18[:, :, 0:n], scalar1=1e-12, scalar2=None,
                                                  op0=ALU.max), reads=[W18], writes=[W18])
            k.op("dve", lambda e: e.reciprocal(out=w18[:, :, 0:n], in_=w18[:, :, 0:n]), reads=[W18], writes=[W18])
            k.op("dve", lambda e: e.tensor_tensor(out=kkt[:, :, 0:n], in0=kkt[:, :, 0:n], in1=w18[:, :, 0:n], op=ALU.mult),
                 reads=[KKB, W18], writes=[KKB])
            k.dma("sp", self.fm(self.kkT, t0, n), kkt[:, :, 0:n], reads=[KKB], writes=[self.DB("kkT", ti)])
            for d in range(2):
                p, P = self.pb()
                for kc in range(8):
                    k.op("pe", lambda e: e.matmul(p[0:64, 0:n], lhsT=ld[:, kc, (d * 2) * 64:(d * 2 + 1) * 64],
                                                  rhs=L[1][:, kc, 0:n], start=(kc == 0), stop=(kc == 7)),
                         reads=[WB, LB[1]], writes=[P], sig=(kc == 7))
                k.op("act", lambda e: e.activation(out=sm[0:64, 0:n], in_=p[0:64, 0:n], func=AF.Tanh), reads=[P],
                     writes=[SMB])
                for oc in range(8):
                    p, P = self.pb()
                    k.op("pe", lambda e: e.matmul(p[:, 0:n], lhsT=lu[:, d * 2, oc * 128:(oc + 1) * 128], rhs=sm[0:64, 0:n],
                                                  start=True, stop=True), reads=[WB, SMB], writes=[P])
                    k.op("act", lambda e: e.activation(out=o1[:, oc, 0:n], in_=p[:, 0:n], func=AF.Sigmoid,
                                                       bias=self.V("lb_%d" % i, (d * 2) * 8 + oc)),
                         reads=[P, self.VB], writes=[O1B])
                k.op("dve", lambda e: e.tensor_scalar(out=o1[:, :, 0:n], in0=o1[:, :, 0:n], scalar1=-DECAY_SCALE,
                                                      scalar2=None, op0=ALU.mult), reads=[O1B], writes=[O1B])
                k.dma("sp", self.fm(self.lwT[d], t0, n), o1[:, :, 0:n], reads=[O1B], writes=[self.DB("lwT", d, ti)])
                p, P = self.pb()
                for kc in range(8):
                    k.op("pe", lambda e: e.matmul(p[0:64, 0:n], lhsT=ld[:, kc, (d * 2 + 1) * 64:(d * 2 + 2) * 64],
                                                  rhs=L[4][:, kc, 0:n], start=(kc == 0), stop=(kc == 7)),
                         reads=[WB, LB[4]], writes=[P], sig=(kc == 7))
                k.op("act", lambda e: e.activation(out=sm[0:64, 0:n], in_=p[0:64, 0:n], func=AF.Copy), reads=[P],
                     writes=[SMB])
                for oc in range(8):
                    p, P = self.pb()
                    k.op("pe", lambda e: e.matmul(p[:, 0:n], lhsT=lu[:, d * 2 + 1, oc * 128:(oc + 1) * 128],
                                                  rhs=sm[0:64, 0:n], start=True, stop=True), reads=[WB, SMB], writes=[P])
                    k.op("act", lambda e: e.activation(out=at8[:, oc, 0:n], in_=p[:, 0:n], func=AF.Sigmoid,
                                                       bias=self.V("lb_%d" % i, (d * 2 + 1) * 8 + oc)),
                         reads=[P, self.VB], writes=[AT8])
                k.op("dve", lambda e: e.tensor_tensor(out=o1[:, :, 0:n], in0=kkt[:, :, 0:n], in1=at8[:, :, 0:n], op=ALU.mult),
                     reads=[KKB, AT8], writes=[O1B])
                k.dma("sp", self.fm(self.bT[d], t0, n), o1[:, :, 0:n], reads=[O1B], writes=[self.DB("bT", d, ti)])
                k.op("dve", lambda e: e.tensor_tensor(out=at8[:, :, 0:n], in0=at8[:, :, 0:n], in1=bc("ka_%d" % i, 0, n),
                                                      op=ALU.mult), reads=[AT8, self.VB], writes=[AT8])
                k.op("dve", lambda e: e.tensor_tensor(out=at8[:, :, 0:n], in0=at8[:, :, 0:n],
                                                      in1=omka.unsqueeze(2).broadcast_to([128, 8, n]), op=ALU.add),
                     reads=[AT8, OMB], writes=[AT8])
                k.op("dve", lambda e: e.tensor_tensor(out=kd[d][:, :, 0:n], in0=at8[:, :, 0:n], in1=kt[:, :, 0:n], op=ALU.mult),
                     reads=[AT8, KTB], writes=[KDB[d]])
                k.dma("sp", self.fm(self.kdT[d], t0, n), kd[d][:, :, 0:n], reads=[KDB[d]], writes=[self.DB("kdT", d, ti)])
            k.op("dve", lambda e: e.tensor_tensor(out=at8[:, :, 0:n], in0=kd[0][:, :, 0:n], in1=kd[1][:, :, 0:n], op=ALU.add),
                 reads=[KDB[0], KDB[1]], writes=[AT8])
            k.op("dve", lambda e: e.tensor_tensor(out=at8[:, :, 0:n], in0=at8[:, :, 0:n], in1=rt[:, :, 0:n], op=ALU.mult),
                 reads=[AT8, RTB], writes=[AT8])
            k.op("dve", lambda e: e.tensor_tensor(out=at8[:, :, 0:n], in0=at8[:, :, 0:n], in1=bc("rk_%d" % i, 0, n),
                                                  op=ALU.mult), reads=[AT8, self.VB], writes=[AT8])
            for oc in range(8):
                p, P = self.pb()
                k.op("pe", lambda e: e.matmul(p[:, 0:n], lhsT=bones_f, rhs=at8[:, oc, 0:n], start=True, stop=True),
                     reads=[AT8, self.CS], writes=[P])
                k.op("act", lambda e: e.activation(out=w18[:, oc, 0:n], in_=p[:, 0:n], func=AF.Copy), reads=[P],
                     writes=[W18])
            for oc in range(8):
                p2, P2 = self.pb()
                for half in range(2):
                    hd_ = 2 * oc + half
                    c0 = (hd_ % 2) * 512 + (hd_ // 2) * 64
                    for kc in range(8):
                        k.op("pe", lambda e: e.matmul(p2[half * 64:(half + 1) * 64, 0:n], lhsT=wvs[:, kc, c0:c0 + 64],
                                                      rhs=L[3][:, kc, 0:n], start=(kc == 0), stop=(kc == 7)),
                             reads=[WB, LB[3]], writes=[P2], sig=(kc == 7 and half == 1))
                k.op("dve", lambda e: e.tensor_tensor(out=o1[:, oc, 0:n], in0=p2[:, 0:n], in1=w18[:, oc, 0:n], op=ALU.mult),
                     reads=[P2, W18], writes=[O1B])
            k.dma("sp", self.fm(self.bonT, t0, n), o1[:, :, 0:n], reads=[O1B], writes=[self.DB("bonT", ti)])
        k.phase_reset()

    def rw_wkv(self, l, b, d):
        k = self.k
        NW = 256
        rev = (d == 1)
        msk = k.sb([128, 512], F32)
        rmk = k.sb([128, 2048], F32)
        MB = Buf()
        k.dma("sp", msk, self.wmask[:, d, :], writes=[MB])
        k.dma("sp", rmk, self.rmask[:, d, :], writes=[MB])

        def t8():
            return k.sb([128, 8, NW], F32)
        rt, kk, bb, kdt, lw, cum, ec = t8(), t8(), t8(), t8(), t8(), t8(), t8()
        INB = Buf()
        ECB = Buf()
        vst = k.sb([128, 4, 512], F32)
        VB_ = Buf()
        yt = k.sb([128, 8, NW], F32)
        YB = Buf()
        XR = [k.sb([128, 8, 192], F32) for _ in range(2)]
        BE = [k.sb([128, 8, 128], F32) for _ in range(2)]
        KT = [k.sb([128, 8, 128], F32) for _ in range(2)]
        CHB = [Buf(), Buf()]
        AM2 = [k.sb([128, 8, 512], F32) for _ in range(2)]
        AMB2 = [[Buf() for _ in range(8)] for _ in range(2)]
        X2 = [k.sb([128, 8, 128], F32) for _ in range(2)]
        XB2 = [[Buf(), Buf()], [Buf(), Buf()]]
        Pn = k.sb([128, 8, 128], F32)
        PTn = k.sb([128, 8, 128], F32)
        PNB = [Buf(), Buf()]
        PTB = [Buf(), Buf()]
        BEt2 = [k.sb([128, 8, 128], F32) for _ in range(2)]
        KTt2 = [k.sb([128, 8, 128], F32) for _ in range(2)]
        BTB2, KTTB2 = [Buf(), Buf()], [Buf(), Buf()]
        Vbd2 = [k.sb([128, 8, 128], F32) for _ in range(2)]
        VBB2 = [Buf(), Buf()]
        Wsb = k.sb([128, 8, 64], F32)
        Ust = k.sb([128, 8, 64], F32)
        Ubd = k.sb([128, 8, 128], F32)
        Sst = k.sb([128, 8, 64], F32)
        Sbd = k.sb([128, 8, 128], F32)
        WSB, USB, UBB, SSB, SBB = [Buf() for _ in range(5)]
        for t_, B_ in ((XR[0], CHB[0]), (XR[1], CHB[1]), (BE[0], CHB[0]), (BE[1], CHB[1]), (KT[0], CHB[0]),
                       (KT[1], CHB[1]), (Ubd, UBB), (Vbd2[0], VBB2[0]), (Vbd2[1], VBB2[1]), (Sbd, SBB), (Sst, SSB)):
            k.op("pool", lambda e: e.memset(t_, 0.0), writes=[B_])
        r3 = lambda t_: t_.rearrange("p (q c) -> p q c", c=64)
        r4 = lambda t_: t_.rearrange("p (q c) -> p q c", c=128)

        def pre_gen(ch, jb):
            c0 = ch * 64
            cs_ = slice(c0, c0 + 64)
            xr_, be_, kt_, CB = XR[jb], BE[jb], KT[jb], CHB[jb]
            AM, AMB, X, XB = AM2[jb], AMB2[jb], X2[jb], XB2[jb]
            BEt, KTt, BTB, KTTB, Vbd, VBB = BEt2[jb], KTt2[jb], BTB2[jb], KTTB2[jb], Vbd2[jb], VBB2[jb]
            for hp in range(2):
                ps_ = slice(hp * 64, hp * 64 + 64)
                k.op("dve", lambda e: e.scalar_tensor_tensor(out=xr_[ps_, :, hp * 64:hp * 64 + 64], in0=kk[ps_, :, cs_],
                                                             scalar=-1.0, in1=lw[ps_, :, cs_], op0=ALU.mult,
                                                             op1=ALU.mult), reads=[INB], writes=[CB])
                k.op("pool", lambda e: e.tensor_tensor(out=be_[ps_, :, hp * 64:hp * 64 + 64], in0=bb[ps_, :, cs_],
                                                       in1=cum[ps_, :, cs_], op=ALU.mult), reads=[INB, ECB], writes=[CB])
                k.op("pool", lambda e: e.tensor_tensor(out=kt_[ps_, :, hp * 64:hp * 64 + 64], in0=kdt[ps_, :, cs_],
                                                       in1=cum[ps_, :, cs_], op=ALU.mult), reads=[INB, ECB], writes=[CB])
            k.op("dve", lambda e: e.tensor_tensor(out=xr_[:, :, 128:192], in0=rt[:, :, cs_], in1=ec[:, :, cs_],
                                                  op=ALU.mult), reads=[INB, ECB], writes=[CB])
            vs_ = vst[:, ch, :].rearrange("p (q v) -> p q v", v=64)
            for hp in range(2):
                ps_ = slice(hp * 64, hp * 64 + 64)
                k.op("act", lambda e: e.activation(out=Vbd[ps_, :, hp * 64:hp * 64 + 64], in_=vs_[ps_, :, :],
                                                   func=AF.Copy), reads=[VB_], writes=[VBB])
            yield
            for p_ in range(8):
                pa, PA = self.pb()
                k.op("pe", lambda e: e.matmul(pa[:, 0:192], lhsT=be_[:, p_, :], rhs=xr_[:, p_, :], start=True,
                                              stop=True), reads=[CB], writes=[PA], sig=False)
                k.op("pe", lambda e: e.matmul(pa[:, 192:384], lhsT=kt_[:, p_, :], rhs=xr_[:, p_, :], start=True,
                                              stop=True), reads=[CB], writes=[PA], sig=False)
                k.op("pe", lambda e: e.matmul(pa[:, 384:512], lhsT=xr_[:, p_, 0:128], rhs=be_[:, p_, :], start=True,
                                              stop=True), reads=[CB], writes=[PA])
                k.op("dve", lambda e: e.tensor_tensor(out=AM[:, p_, :], in0=pa, in1=msk, op=ALU.mult),
                     reads=[PA, MB], writes=[AMB[p_]])
                if p_ == 3:
                    yield
            yield
            for (src_, dst_, DB_) in ((be_, BEt, BTB), (kt_, KTt, KTTB)):
                for q_ in range(2):
                    pt_, PT_ = self.pb()
                    for pi in range(4):
                        p_ = q_ * 4 + pi
                        k.op("pe", lambda e: e.transpose(pt_[:, pi * 128:(pi + 1) * 128], src_[:, p_, :], self.ident),
                             reads=[CB, self.CS], writes=[PT_], sig=(pi == 3))
                    k.op("act", lambda e: e.activation(out=dst_[:, q_ * 4:q_ * 4 + 4, :], in_=r4(pt_), func=AF.Copy),
                         reads=[PT_], writes=[DB_])
            for q_ in range(2):
                k.op("dve", lambda e: e.tensor_tensor(out=X[:, q_ * 4:q_ * 4 + 4, :], in0=AM[:, q_ * 4:q_ * 4 + 4, 0:128],
                                                      in1=self.ident.unsqueeze(1).broadcast_to([128, 4, 128]), op=ALU.add),
                     reads=AMB[q_ * 4:q_ * 4 + 4] + [self.CS], writes=[XB[q_]])
            yield
            for kk_ in range(1, 6):
                Pq = (lambda p_: AM[:, p_, 0:128]) if kk_ == 1 else (lambda p_: Pn[:, p_, :])
                PTq = (lambda p_: AM[:, p_, 384:512]) if kk_ == 1 else (lambda p_: PTn[:, p_, :])
                bk = {}
                for q_ in range(2):
                    RD = (AMB[q_ * 4:q_ * 4 + 4]) if kk_ == 1 else [PNB[q_], PTB[q_]]
                    pb_, PB_ = self.pb()
                    for pi in range(4):
                        p_ = q_ * 4 + pi
                        k.op("pe", lambda e: e.matmul(pb_[:, pi * 128:(pi + 1) * 128], lhsT=Pq(p_), rhs=PTq(p_),
                                                      start=True, stop=True), reads=RD, writes=[PB_], sig=(pi == 3))
                    bk[("b", q_)] = (pb_, PB_)
                    if kk_ < 5:
                        pa_, PA_ = self.pb()
                        for pi in range(4):
                            p_ = q_ * 4 + pi
                            k.op("pe", lambda e: e.matmul(pa_[:, pi * 128:(pi + 1) * 128], lhsT=PTq(p_), rhs=Pq(p_),
                                                          start=True, stop=True), reads=RD, writes=[PA_], sig=(pi == 3))
                        bk[("a", q_)] = (pa_, PA_)
                yield
                for q_ in range(2):
                    pb_, PB_ = bk[("b", q_)]
                    k.op("act", lambda e: e.activation(out=PTn[:, q_ * 4:q_ * 4 + 4, :], in_=r4(pb_), func=AF.Copy),
                         reads=[PB_], writes=[PTB[q_]])
                    if kk_ < 5:
                        pa_, PA_ = bk[("a", q_)]
                        k.op("dve", lambda e: e.tensor_copy(out=Pn[:, q_ * 4:q_ * 4 + 4, :], in_=r4(pa_)),
                             reads=[PA_], writes=[PNB[q_]])
                for q_ in range(2):
                    pc_, PC_ = self.pb()
                    for pi in range(4):
                        p_ = q_ * 4 + pi
                        k.op("pe", lambda e: e.matmul(pc_[:, pi * 128:(pi + 1) * 128], lhsT=PTn[:, p_, :], rhs=X[:, p_, :],
                                                      start=True, stop=True), reads=[PTB[q_], XB[q_]], writes=[PC_],
                             sig=(pi == 3))
                    bk[("c", q_)] = (pc_, PC_)
                yield
                for q_ in range(2):
                    pc_, PC_ = bk[("c", q_)]
                    k.op("dve", lambda e: e.tensor_tensor(out=X[:, q_ * 4:q_ * 4 + 4, :], in0=X[:, q_ * 4:q_ * 4 + 4, :],
                                                          in1=r4(pc_), op=ALU.add), reads=[PC_, XB[q_]], writes=[XB[q_]])

        def state_gen(ch, jb):
            c0 = ch * 64
            cs_ = slice(c0, c0 + 64)
            xr_, CB = XR[jb], CHB[jb]
            AM, AMB, X, XB = AM2[jb], AMB2[jb], X2[jb], XB2[jb]
            BEt, KTt, BTB, KTTB, Vbd, VBB = BEt2[jb], KTt2[jb], BTB2[jb], KTTB2[jb], Vbd2[jb], VBB2[jb]
            vs_ = vst[:, ch, :].rearrange("p (q v) -> p q v", v=64)
            pw, PW = self.pb()
            pw2, PW2 = self.pb()
            for p_ in range(8):
                k.op("pe", lambda e: e.matmul(pw[:, p_ * 64:(p_ + 1) * 64], lhsT=xr_[:, p_, 0:128], rhs=Sst[:, p_, :],
                                              start=True, stop=True), reads=[CB, SSB], writes=[PW], sig=(p_ == 7))
            for p_ in range(8):
                k.op("pe", lambda e: e.matmul(pw2[:, p_ * 64:(p_ + 1) * 64], lhsT=AM[:, p_, 192:320], rhs=vs_[:, p_, :],
                                              start=True, stop=True), reads=[AMB[p_], VB_], writes=[PW2], sig=(p_ == 7))
            py1, PY1 = self.pb()
            py3, PY3 = self.pb()
            for p_ in range(8):
                k.op("pe", lambda e: e.matmul(py1[:, p_ * 64:(p_ + 1) * 64], lhsT=Sbd[:, p_, :], rhs=xr_[:, p_, 128:192],
                                              start=True, stop=True), reads=[SBB, CB], writes=[PY1], sig=(p_ == 7))
            for p_ in range(8):
                k.op("pe", lambda e: e.matmul(py3[:, p_ * 64:(p_ + 1) * 64], lhsT=Vbd[:, p_, :], rhs=AM[:, p_, 320:384],
                                              start=True, stop=True), reads=[VBB, AMB[p_]], writes=[PY3], sig=(p_ == 7))
            k.op("act", lambda e: e.activation(out=Wsb, in_=r3(pw), func=AF.Copy), reads=[PW], writes=[WSB])
            k.op("dve", lambda e: e.tensor_tensor(out=Wsb, in0=Wsb, in1=r3(pw2), op=ALU.add), reads=[WSB, PW2],
                 writes=[WSB])
            k.op("act", lambda e: e.activation(out=yt[:, :, cs_], in_=r3(py1), func=AF.Copy), reads=[PY1], writes=[YB])
            k.op("dve", lambda e: e.tensor_tensor(out=yt[:, :, cs_], in0=yt[:, :, cs_], in1=r3(py3), op=ALU.add),
                 reads=[YB, PY3], writes=[YB])
            yield
            pu, PU = self.pb()
            for p_ in range(8):
                k.op("pe", lambda e: e.matmul(pu[:, p_ * 64:(p_ + 1) * 64], lhsT=X[:, p_, :], rhs=Wsb[:, p_, :], start=True,
                                              stop=True), reads=[XB[p_ // 4], WSB], writes=[PU], sig=(p_ == 7))
            pu3 = r3(pu)
            k.op("act", lambda e: e.activation(out=Ust, in_=pu3, func=AF.Copy), reads=[PU], writes=[USB])
            for hp in range(2):
                ps_ = slice(hp * 64, hp * 64 + 64)
                k.op("act", lambda e: e.activation(out=Ubd[ps_, :, hp * 64:hp * 64 + 64], in_=pu3[ps_, :, :],
                                                   func=AF.Copy), reads=[PU], writes=[UBB])
            yield
            py2, PY2 = self.pb()
            pS1, PS1 = self.pb()
            pS2, PS2 = self.pb()
            for p_ in range(8):
                k.op("pe", lambda e: e.matmul(pS2[:, p_ * 64:(p_ + 1) * 64], lhsT=KTt[:, p_, :], rhs=vs_[:, p_, :],
                                              start=True, stop=True), reads=[KTTB, VB_], writes=[PS2], sig=(p_ == 7))
            for p_ in range(8):
                k.op("pe", lambda e: e.matmul(pS1[:, p_ * 64:(p_ + 1) * 64], lhsT=BEt[:, p_, :], rhs=Ust[:, p_, :],
                                              start=True, stop=True), reads=[BTB, USB], writes=[PS1], sig=(p_ == 7))
            for p_ in range(8):
                k.op("pe", lambda e: e.matmul(py2[:, p_ * 64:(p_ + 1) * 64], lhsT=Ubd[:, p_, :], rhs=AM[:, p_, 128:192],
                                              start=True, stop=True), reads=[UBB, AMB[p_]], writes=[PY2], sig=(p_ == 7))
            k.op("act", lambda e: e.activation(out=Wsb, in_=r3(pS1), func=AF.Copy), reads=[PS1], writes=[WSB])
            k.op("dve", lambda e: e.tensor_tensor(out=Wsb, in0=Wsb, in1=r3(pS2), op=ALU.add), reads=[WSB, PS2], writes=[WSB])
            k.op("dve", lambda e: e.tensor_tensor(out=Wsb, in0=Wsb, in1=Sst, op=ALU.add), reads=[WSB, SSB], writes=[WSB])
            gcol = (c0 + 63) if not rev else c0
            k.op("dve", lambda e: e.tensor_tensor(out=Sst, in0=Wsb, in1=ec[:, :, gcol:gcol + 1].broadcast_to([128, 8, 64]),
                                                  op=ALU.mult), reads=[WSB, ECB], writes=[SSB])
            for hp in range(2):
                ps_ = slice(hp * 64, hp * 64 + 64)
                k.op("act", lambda e: e.activation(out=Sbd[ps_, :, hp * 64:hp * 64 + 64], in_=Sst[ps_, :, :],
                                                   func=AF.Copy), reads=[SSB], writes=[SBB])
            k.op("dve", lambda e: e.tensor_tensor(out=yt[:, :, cs_], in0=yt[:, :, cs_], in1=r3(py2), op=ALU.add),
                 reads=[YB, PY2], writes=[YB])

        def drain(g):
            for _ in g:
                pass

        def interleave(pre, st):
            pre_done = pre is None
            st_done = False
            while not (pre_done and st_done):
                for _ in range(3):
                    if not pre_done:
                        try:
                            next(pre)
                        except StopIteration:
                            pre_done = True
                if not st_done:
                    try:
                        next(st)
                    except StopIteration:
                        st_done = True

        wt = [(0, 0)] + [(1 + w, CTX + NW * w) for w in range(16)]
        order = wt if not rev else [wt[0]] + wt[:0:-1]
        nchunk = 0
        for (wi, t0) in order[:self.wkv_ntiles]:
            ti = wi
            for (dst, src, key) in ((rt, self.rT, ("rT", ti)), (kk, self.kkT, ("kkT", ti)), (bb, self.bT[d], ("bT", d, ti)),
                                    (kdt, self.kdT[d], ("kdT", d, ti)), (lw, self.lwT[d], ("lwT", d, ti))):
                k.dma("sp", dst, self.fm(src, t0, NW), reads=[self.DB(*key)], writes=[INB])
            k.dma("sp", vst, self.Vst[t0 // 64:t0 // 64 + 4].rearrange("c p f -> p c f"), reads=[self.DB("Vst", ti)],
                  writes=[VB_])
            fl = lambda a: a.rearrange("p f t -> p (f t)")
            rv = (lambda a: a[:, ::-1]) if rev else (lambda a: a)
            k.op("dve", lambda e: e.tensor_tensor_scan(out=rv(fl(cum)), data0=rv(rmk), data1=rv(fl(lw)), initial=0.0,
                                                       op0=ALU.mult, op1=ALU.add), reads=[INB, MB, ECB], writes=[ECB])
            k.op("dve", lambda e: e.tensor_tensor(out=lw, in0=cum, in1=lw, op=ALU.subtract), reads=[ECB, INB], writes=[INB])
            k.op("act", lambda e: e.activation(out=lw, in_=lw, func=AF.Exp), reads=[INB], writes=[INB])
            k.op("act", lambda e: e.activation(out=ec, in_=cum, func=AF.Exp), reads=[ECB], writes=[ECB])
            k.op("act", lambda e: e.activation(out=cum, in_=cum, func=AF.Exp, scale=-1.0), reads=[ECB], writes=[ECB])
            chs = list(range(4)) if not rev else list(range(3, -1, -1))
            jbs = [(nchunk + i_) % 2 for i_ in range(4)]
            nchunk += 4
            drain(pre_gen(chs[0], jbs[0]))
            for i_ in range(4):
                nxt = pre_gen(chs[i_ + 1], jbs[i_ + 1]) if i_ < 3 else None
                interleave(nxt, state_gen(chs[i_], jbs[i_]))
            k.dma("sp", self.fm(self.yT[d], t0, NW), yt, reads=[YB], writes=[self.DB("yT", d, wi)])
        k.phase_reset()

    def rw_out(self, l, b, last):
        k = self.k
        i = l // 2
        wo = k.sb([128, 8, D], BF16)
        WOB = Buf()
        self.load_w(wo, WOB, self.rw_wo[i], 8)
        bones_f = self.cs[:, 128:256]
        y0 = k.sb([128, 8, 512], F32)
        y1 = k.sb([128, 8, 512], F32)
        bon = k.sb([128, 8, 512], F32)
        gt = k.sb([128, 8, 512], F32)
        xt = k.sb([128, 8, 512], F32)
        yc = k.sb([128, 8, 512], F32)
        rs = k.sb([128, 8, 512], F32)
        ob = k.sb([128, 8, 512], BF16)
        Y0B, Y1B, BNB, GTB, XB, OBB, YCB, RSB = [Buf() for _ in range(8)]

        def bc(name, n):
            return self.V(name, 0, 8).unsqueeze(2).broadcast_to([128, 8, n])
        for ti, (t0, n) in enumerate(TILES):
            if last and ti == 0:
                continue
            mi = 2 if ti == 0 else b
            wis = [0] if ti == 0 else [2 * ti - 1, 2 * ti]
            k.dma("sp", y0[:, :, 0:n], self.fm(self.yT[0], t0, n), reads=[self.DB("yT", 0, w) for w in wis], writes=[Y0B])
            k.dma("sp", y1[:, :, 0:n], self.fm(self.yT[1], t0, n), reads=[self.DB("yT", 1, w) for w in wis], writes=[Y1B])
            k.dma("sp", bon[:, :, 0:n], self.fm(self.bonT, t0, n), reads=[self.DB("bonT", w) for w in wis], writes=[BNB])
            k.dma("sp", gt[:, :, 0:n], self.fm(self.gT, t0, n), reads=[self.DB("gT", w) for w in wis], writes=[GTB])
            k.dma("sp", xt[:, :, 0:n], self.fm(self.xT[b], t0, n), reads=[self.DB("xT", b, ti)], writes=[XB])
            k.op("pool", lambda e: e.tensor_tensor(out=y0[:, :, 0:n], in0=y0[:, :, 0:n], in1=y1[:, :, 0:n], op=ALU.add),
                 reads=[Y0B, Y1B], writes=[Y0B])
            for fc in range(8):
                p, P = self.pb()
                k.op("pe", lambda e: e.matmul(p[:, 0:n], lhsT=bones_f, rhs=y0[:, fc, 0:n], start=True, stop=True),
                     reads=[Y0B, self.CS], writes=[P])
                k.op("dve", lambda e: e.scalar_tensor_tensor(out=yc[:, fc, 0:n], in0=p[:, 0:n], scalar=-1.0 / 64,
                                                             in1=y0[:, fc, 0:n], op0=ALU.mult, op1=ALU.add),
                     reads=[P, Y0B], writes=[YCB])
            k.op("act", lambda e: e.activation(out=y1[:, :, 0:n], in_=yc[:, :, 0:n], func=AF.Square), reads=[YCB, Y1B],
                 writes=[Y1B])
            for fc in range(8):
                p2, P2 = self.pb()
                k.op("pe", lambda e: e.matmul(p2[:, 0:n], lhsT=bones_f, rhs=y1[:, fc, 0:n], start=True, stop=True),
                     reads=[Y1B, self.CS], writes=[P2])
                k.op("act", lambda e: e.activation(out=rs[:, fc, 0:n], in_=p2[:, 0:n], func=AF.Sqrt, bias=GN_EPS,
                                                   scale=1.0 / 64), reads=[P2], writes=[RSB])
            k.op("dve", lambda e: e.reciprocal(out=rs[:, :, 0:n], in_=rs[:, :, 0:n]), reads=[RSB], writes=[RSB])
            k.op("dve", lambda e: e.tensor_tensor(out=yc[:, :, 0:n], in0=yc[:, :, 0:n], in1=rs[:, :, 0:n], op=ALU.mult),
                 reads=[YCB, RSB], writes=[YCB])
            k.op("pool", lambda e: e.tensor_tensor(out=yc[:, :, 0:n], in0=yc[:, :, 0:n], in1=bc("gng_%d" % i, n), op=ALU.mult),
                 reads=[YCB, self.VB], writes=[YCB])
            k.op("pool", lambda e: e.tensor_tensor(out=bon[:, :, 0:n], in0=bon[:, :, 0:n], in1=bc("gnb_%d" % i, n), op=ALU.add),
                 reads=[BNB, self.VB], writes=[BNB])
            k.op("dve", lambda e: e.tensor_tensor(out=yc[:, :, 0:n], in0=yc[:, :, 0:n], in1=bon[:, :, 0:n], op=ALU.add),
                 reads=[YCB, BNB], writes=[YCB])
            k.op("dve", lambda e: e.tensor_tensor(out=ob[:, :, 0:n], in0=yc[:, :, 0:n], in1=gt[:, :, 0:n], op=ALU.mult),
                 reads=[YCB, GTB], writes=[OBB])
            for oc in range(8):
                p, P = self.pb()
                for kc in range(8):
                    k.op("pe", lambda e: e.matmul(p[:, 0:n], lhsT=wo[:, kc, oc * 128:(oc + 1) * 128], rhs=ob[:, kc, 0:n],
                                                  start=(kc == 0), stop=(kc == 7)), reads=[WOB, OBB], writes=[P],
                         sig=(kc == 7))
                k.op("dve", lambda e: e.scalar_tensor_tensor(out=xt[:, oc, 0:n], in0=p[:, 0:n],
                                                             scalar=self.mod[:, 16 + oc, mi:mi + 1], in1=xt[:, oc, 0:n],
                                                             op0=ALU.mult, op1=ALU.add), reads=[P, self.MOD, XB], writes=[XB])
            k.dma("sp", self.fm(self.xT[b], t0, n), xt[:, :, 0:n], reads=[XB], writes=[self.DB("xT", b, ti)])
        k.phase_reset()

    def build(self, nphases=None):
        ph = [lambda: self.setup()]
        for l in self.layers:
            last = (l == DEPTH - 1)
            ph.append(lambda l=l: self.phase_mod(l))
            for b in range(NB):
                if l % 2 == 0:
                    ph.append(lambda l=l, b=b: self.phase_hy_inproj(l, b))
                    ph.append(lambda l=l, b=b: self.phase_rglru(l, b))
                    ph.append(lambda l=l, b=b: self.phase_attn(l, b))
                else:
                    ph.append(lambda l=l, b=b: self.phase_rwkv(l, b))
                ph.append(lambda l=l, b=b, last=last: self.phase_mlp(l, b, last))
        for f in (ph if nphases is None else ph[:nphases]):
            f()
        self.k.finish()
        return self.nc


def host_consts():
    cs = np.zeros((128, 512), np.float32)
    cs[:, 0:128] = np.eye(128, dtype=np.float32)
    bo = np.zeros((128, 128), np.float32)
    bo[0:64, 0:64] = 1.0
    bo[64:128, 64:128] = 1.0
    cs[:, 128:256] = bo
    pw = np.zeros((128, 128), np.float32)
    for blk in range(2):
        for n in range(64):
            pw[blk * 64 + n, blk * 64 + (n + 32) % 64] = 1.0
    cs[:, 256:384] = pw
    rows = SEQ // 64
    row = np.repeat(np.arange(rows, dtype=np.float32), 64)
    col = np.tile(np.arange(64, dtype=np.float32), rows)
    inv = (np.float32(10000.0) ** (-np.arange(0, 32, 2, dtype=np.float32) / np.float32(32))).astype(np.float32)
    ang = np.concatenate([row[:, None] * inv, col[:, None] * inv], axis=-1).astype(np.float32)
    c = np.cos(ang).astype(np.float32).T
    s_ = np.sin(ang).astype(np.float32).T
    cos64 = np.concatenate([c, c], 0)
    sin64 = np.concatenate([-s_, s_], 0)
    cosT = np.ascontiguousarray(np.concatenate([cos64, cos64], 0))
    sinT = np.ascontiguousarray(np.concatenate([sin64, sin64], 0))
    return cs, cosT, sinT


def fmaj(v):
    return np.ascontiguousarray(np.asarray(v, np.float32).reshape(-1, 128).T)


def host_vb(inp):
    vb = np.zeros((128, NVB), np.float32)

    def put(name, arr):
        arr = np.asarray(arr, np.float32)
        vb[:, VBM[name]:VBM[name] + arr.shape[1]] = arr
    for l in range(DEPTH):
        put("ng0_%d" % l, fmaj(inp["norm_g"][l, 0]))
        put("ng1_%d" % l, fmaj(inp["norm_g"][l, 1]))
        put("adab_%d" % l, fmaj(inp["ada_b"][l]))
    for i in range(2):
        gq = inp["hy_q_norm"][i][PERM]
        gk = inp["hy_k_norm"][i][PERM]
        put("gq_%d" % i, np.concatenate([gq, gq])[:, None])
        put("gk_%d" % i, np.concatenate([gk, gk])[:, None])
        put("convw_%d" % i, np.concatenate([fmaj(inp["hy_conv_w"][i][j]) for j in range(4)], 1))
        put("convb_%d" % i, fmaj(inp["hy_conv_b"][i]))
        put("gateb_%d" % i, np.concatenate([fmaj(inp["hy_gate_b"][i][d][g]) for d in range(2) for g in range(2)], 1))
        put("lam_%d" % i, np.concatenate([fmaj(inp["hy_lam"][i][d]) for d in range(2)], 1))
    for i in range(2):
        put("mu_%d" % i, np.concatenate([fmaj(inp["rw_mu"][i][j]) for j in range(6)], 1))
        put("kk_%d" % i, fmaj(inp["rw_k_k"][i]))
        put("ka_%d" % i, fmaj(inp["rw_k_a"][i]))
        put("rk_%d" % i, fmaj(inp["rw_r_k"][i].reshape(-1)))
        put("gng_%d" % i, fmaj(inp["rw_gn_g"][i]))
        put("gnb_%d" % i, fmaj(inp["rw_gn_b"][i]))
        put("lb_%d" % i, np.concatenate([fmaj(inp["rw_lora_bias"][i][d][j]) for d in range(2) for j in range(2)], 1))
    return vb


def host_shared(inp):
    sh = {}
    cs, cosT, sinT = host_consts()
    sh["consts"], sh["cosT"], sh["sinT"] = cs, cosT, sinT
    sh["vb"] = host_vb(inp)
    sh["ada_w"] = np.ascontiguousarray(inp["ada_w"], np.float32)
    sh["mlp_w1"] = np.ascontiguousarray(inp["mlp_w1"], np.float32)
    sh["mlp_w2"] = np.ascontiguousarray(inp["mlp_w2"], np.float32)
    win = inp["hy_w_in"]
    cols = []
    for h in range(8):
        cols.append(h * 64 + PERM)
    for kv in range(2):
        cols.append(512 + kv * 64 + PERM)
        cols.append(512 + kv * 64 + PERM)
    cols.append(np.arange(640, 768))
    cols.append(np.arange(768, 1792))
    cols = np.concatenate(cols)
    sh["hy_win"] = np.ascontiguousarray(win[:, :, cols], np.float32)
    sh["hy_wout"] = np.ascontiguousarray(inp["hy_w_out"], np.float32)
    gw = inp["hy_gate_w"]
    bd = np.zeros((2, 2, 2, 4, 128, 128), np.float32)
    for c in range(4):
        bd[:, :, :, c, 0:64, 0:64] = gw[:, :, :, 2 * c]
        bd[:, :, :, c, 64:128, 64:128] = gw[:, :, :, 2 * c + 1]
    sh["hy_gw"] = bd
    sh["rw_wrkv"] = np.ascontiguousarray(inp["rw_w_rkv"], np.float32)
    st = np.concatenate([np.arange((2 * p + hp) * 64, (2 * p + hp) * 64 + 64) for hp in range(2) for p in range(8)])
    sh["rw_wvst"] = np.ascontiguousarray(inp["rw_w_rkv"][:, 2][:, :, st], np.float32)
    sh["rw_wo"] = np.ascontiguousarray(inp["rw_w_o"], np.float32)
    ldn = inp["rw_lora_down"]
    sh["rw_ld"] = np.ascontiguousarray(np.concatenate([ldn[:, d, j] for d in range(2) for j in range(2)], axis=-1), np.float32)
    lup = inp["rw_lora_up"]
    sh["rw_lu"] = np.ascontiguousarray(np.stack([lup[:, d, j] for d in range(2) for j in range(2)], axis=1), np.float32)
    sh["rw_gd"] = np.ascontiguousarray(inp["rw_gate_down"], np.float32)
    sh["rw_gu"] = np.ascontiguousarray(inp["rw_gate_up"], np.float32)
    ii = np.arange(64)
    up = (ii[None, :] > ii[:, None]).astype(np.float32)
    le = (ii[:, None] <= ii[None, :]).astype(np.float32)
    lo_ = (ii[None, :] < ii[:, None]).astype(np.float32)

    def bdm(m):
        o = np.zeros((128, 128), np.float32)
        o[0:64, 0:64] = m
        o[64:128, 64:128] = m
        return o

    def stk(m):
        return np.concatenate([m, m], 0)
    wm = np.zeros((128, 2, 512), np.float32)
    for d, (u_, l_, s_) in enumerate(((up, lo_, le), (up.T, lo_.T, le.T))):
        wm[:, d, 0:128] = bdm(u_)
        wm[:, d, 128:192] = stk(s_)
        wm[:, d, 192:320] = bdm(u_)
        wm[:, d, 320:384] = stk(s_)
        wm[:, d, 384:512] = bdm(l_)
    sh["wmask"] = wm
    rm = np.ones((128, 2, 2048), np.float32)
    tt = np.arange(2048)
    rm[:, 0, tt % 64 == 0] = 0.0
    rm[:, 1, tt % 64 == 63] = 0.0
    sh["rmask"] = rm
    return sh


_CACHE = {}


def kernel(**inp):
    inp = {k_: np.asarray(v) for k_, v in inp.items()}
    sh = host_shared(inp)
    if "nc" not in _CACHE:
        _CACHE["nc"] = Prog().build()
    nc = _CACHE["nc"]
    in_maps = []
    for core in range(8):
        bs = [2 * core, 2 * core + 1]
        m = dict(sh)
        m["xT_in"] = np.ascontiguousarray(np.stack([inp["x"][b].T for b in bs]), np.float32)
        m["ctxT_in"] = np.ascontiguousarray(np.stack([inp["ctx"][b].T for b in bs]), np.float32)
        cv = np.stack([inp["c"][bs[0]], inp["c"][bs[1]], inp["c_ctx"]], 0)
        m["cT"] = np.ascontiguousarray(cv.reshape(3, 8, 128).transpose(2, 1, 0), np.float32)
        in_maps.append(m)
    res = run_bass_kernel_spmd(nc, in_maps, core_ids=list(range(8)))
    out = np.empty((16, SEQ, D), np.float32)
    for core in range(8):
        o = res.results[core]["outT"]
        for j in range(NB):
            out[2 * core + j] = o[j].T
    return out
```

```python
import math
import numpy as np
import concourse.bass as bass
import concourse.mybir as mybir
from concourse.bass_utils import run_bass_kernel_spmd

F32 = mybir.dt.float32
BF16 = mybir.dt.bfloat16
AF = mybir.ActivationFunctionType
ALU = mybir.AluOpType

D = 1024
SEQ = 4096
CTX = 256
T = SEQ + CTX
NB = 2
DEPTH = 4
EPS = 1e-6
DFF = 4096
TILES = [(0, CTX)] + [(CTX + 512 * i, 512) for i in range(8)]
WTILES = [(0, CTX)] + [(CTX + 256 * i, 256) for i in range(16)]
GELU_C = 2.0 * math.sqrt(2.0 / math.pi)
DECAY_SCALE = math.exp(-0.5)
GN_EPS = 64e-5


class Buf:
    __slots__ = ("w", "r")

    def __init__(self):
        self.w = None
        self.r = {}


class KB:
    NDMA = 40

    def __init__(self, nc):
        self.nc = nc
        self.eng = dict(pe=nc.tensor, act=nc.scalar, dve=nc.vector, pool=nc.gpsimd, sp=nc.sync)
        self.sems = {}
        self.cnt = {}
        for e in ("pe", "act", "dve", "pool"):
            self.sems[e] = nc.alloc_semaphore("s_" + e)
            self.cnt[e] = 0
        self.dsem = [nc.alloc_semaphore("d%d" % i) for i in range(self.NDMA)]
        self.dval = [0] * self.NDMA
        self.drr = 0
        self.waited = {e: {} for e in self.eng}
        self.sb_off = 16512
        self.sb_base = 16512
        self.nalloc = 0
        self.ninst = 0
        self.pending = []
        self.rec = None

    def sb(self, shape, dt=F32, name=None):
        self.nalloc += 1
        nm = "%s_%d" % (name or "t", self.nalloc)
        n = 1
        for s_ in shape[1:]:
            n *= s_
        nbytes = n * (4 if dt == F32 else 2)
        nbytes = (nbytes + 63) // 64 * 64
        off = self.sb_off
        self.sb_off += nbytes
        assert self.sb_off <= 229376, ("SBUF overflow", nm, self.sb_off)
        return self.nc.alloc_sbuf_tensor_at(nm, list(shape), dt, offset=off).ap()

    def phase_reset(self):
        self.barrier()
        self.sb_off = self.sb_base

    def persist_mark(self):
        self.sb_base = self.sb_off

    def _semh(self, key):
        return self.sems[key] if isinstance(key, str) else self.dsem[key[1]]

    def _wait(self, e, key, val, raw=False):
        if key == e and (not raw or e == "pe"):
            return
        w = self.waited[e]
        if w.get(key, 0) >= val:
            return
        w[key] = val
        self.pending.append((key, val))

    def _take(self):
        p = self.pending
        self.pending = []
        d = {}
        for k_, v_ in p:
            if d.get(k_, 0) < v_:
                d[k_] = v_
        return list(d.items())

    def _deps(self, e, reads, writes):
        for b in reads:
            if b.w is not None:
                self._wait(e, b.w[0], b.w[1], raw=True)
        for b in writes:
            if b.w is not None:
                self._wait(e, b.w[0], b.w[1])
            for k_, v_ in b.r.items():
                self._wait(e, k_, v_)

    def _mark(self, tok, reads, writes):
        k_, v_ = tok
        for b in reads:
            if b.r.get(k_, 0) < v_:
                b.r[k_] = v_
        for b in writes:
            b.w = tok
            b.r = {}

    def op(self, e, ins_fn, reads=(), writes=(), sig=True):
        self._deps(e, reads, writes)
        items = self._take()
        if self.rec is not None:
            self.rec.append((e, list(items), e if sig else None, 1))
        last = items.pop() if items else None
        for k_, v_ in items:
            self.eng[e].wait_ge(self._semh(k_), v_)
            self.ninst += 1
        ins = ins_fn(self.eng[e])
        if last is not None:
            ins._wait_ge(self._semh(last[0]), last[1])
        self.ninst += 1
        if sig:
            self.cnt[e] += 1
            ins.then_inc(self.sems[e], 1)
            tok = (e, self.cnt[e])
        else:
            tok = (e, self.cnt[e] + 1)
        self._mark(tok, reads, writes)
        return tok

    def dma(self, q, out, in_, reads=(), writes=(), **kw):
        i = self.drr
        self.drr = (self.drr + 1) % self.NDMA
        key = ("d", i)
        if self.dval[i] > 0:
            self._wait(q, key, self.dval[i])
        self._deps(q, reads, writes)
        its_ = self._take()
        if self.rec is not None:
            self.rec.append((q, list(its_), key, 16))
        for k_, v_ in its_:
            self.eng[q].wait_ge(self._semh(k_), v_)
            self.ninst += 1
        self.eng[q].dma_start(out=out, in_=in_, **kw).then_inc(self.dsem[i], 16)
        self.ninst += 1
        self.dval[i] += 16
        tok = (key, self.dval[i])
        self._mark(tok, reads, writes)
        return tok

    def barrier(self):
        for e in self.eng:
            for o in ("pe", "act", "dve", "pool"):
                if self.cnt[o] > 0:
                    self._wait(e, o, self.cnt[o])
            for i in range(self.NDMA):
                if self.dval[i] > 0:
                    self._wait(e, ("d", i), self.dval[i])
            its_ = self._take()
            if self.rec is not None:
                self.rec.append((e, list(its_), None, 0))
            for k_, v_ in its_:
                self.eng[e].wait_ge(self._semh(k_), v_)
                self.ninst += 1

    def finish(self):
        self.barrier()


def vb_layout():
    m = {}
    off = 0

    def add(name, n):
        nonlocal off
        m[name] = off
        off += n
    for l in range(DEPTH):
        add("ng0_%d" % l, 8)
        add("ng1_%d" % l, 8)
        add("adab_%d" % l, 48)
    for i in range(2):
        add("gq_%d" % i, 1)
        add("gk_%d" % i, 1)
        add("convw_%d" % i, 16)
        add("convb_%d" % i, 4)
        add("gateb_%d" % i, 16)
        add("lam_%d" % i, 8)
    for i in range(2):
        add("mu_%d" % i, 48)
        add("kk_%d" % i, 8)
        add("ka_%d" % i, 8)
        add("rk_%d" % i, 8)
        add("gng_%d" % i, 8)
        add("gnb_%d" % i, 8)
        add("lb_%d" % i, 32)
    return m, off


VBM, NVB = vb_layout()
PERM = np.concatenate([np.arange(0, 64, 2), np.arange(1, 64, 2)])


class Prog:
    def __init__(self, debug=(), nlayers=DEPTH, ext_in=()):
        self.debug = set(debug)
        self.ext_in = set(ext_in)
        self.nlayers = nlayers
        self.layers = list(range(nlayers))
        self.rw_stop = 99
        self.wkv_stage = 99
        self.wkv_ntiles = 99
        self.wkv_ratio = 1
        nc = self.nc = bass.Bass("TRN2", target_bir_lowering=False)
        k = self.k = KB(nc)
        self.dbuf = {}
        di = self.din
        self.xin = di("xT_in", [NB, D, SEQ])
        self.cin = di("ctxT_in", [NB, D, CTX])
        self.cT = di("cT", [128, 8, 3])
        self.vbd = di("vb", [128, NVB])
        self.ada_w = di("ada_w", [DEPTH, D, 6 * D])
        self.w1 = di("mlp_w1", [DEPTH, D, DFF])
        self.w2 = di("mlp_w2", [DEPTH, DFF, D])
        self.hy_win = di("hy_win", [2, D, 1920])
        self.hy_wout = di("hy_wout", [2, D, D])
        self.hy_gw = di("hy_gw", [2, 2, 2, 4, 128, 128])
        self.cst = di("consts", [128, 512])
        self.cosd = di("cosT", [128, SEQ])
        self.sind = di("sinT", [128, SEQ])
        self.rw_wrkv = di("rw_wrkv", [2, 3, D, D])
        self.rw_wvst = di("rw_wvst", [2, D, D])
        self.rw_wo = di("rw_wo", [2, D, D])
        self.rw_ld = di("rw_ld", [2, D, 256])
        self.rw_lu = di("rw_lu", [2, 4, 64, D])
        self.rw_gd = di("rw_gd", [2, D, 128])
        self.rw_gu = di("rw_gu", [2, 128, D])
        self.wmask = di("wmask", [128, 2, 512])
        self.rmask = di("rmask", [128, 2, 2048])
        self.out = nc.dram_tensor("outT", [NB, D, SEQ], F32, kind="ExternalOutput").ap()
        self.hnT = self.dscr("hnT", [D, T])
        self.rT = self.dscr("rT", [D, T])
        self.kkT = self.dscr("kkT", [D, T])
        self.kdT = self.dscr("kdT", [2, D, T])
        self.bT = self.dscr("bT", [2, D, T])
        self.lwT = self.dscr("lwT", [2, D, T])
        self.gT = self.dscr("gT", [D, T])
        self.bonT = self.dscr("bonT", [D, T])
        self.Vst = self.dscr("Vst", [T // 64, 128, 512])
        self.yT = self.dscr("yT", [2, D, T])
        self.xT = self.dscr("xT", [NB, D, T])
        self.qT = self.dscr("qT", [512, T], BF16)
        self.kT2 = self.dscr("kT2", [256, T], BF16)
        self.vtok = self.dscr("vtok", [T // 128, 128, 384], BF16)
        self.xr = self.dscr("xr", [512, T])
        self.gg = self.dscr("gg", [512, T], BF16)
        self.rec = self.dscr("rec", [512, T], BF16)
        self.ps = [nc.alloc_psum_tensor("ps%d" % i, [128, 512], F32).ap() for i in range(8)]
        self.PS = [Buf() for _ in range(8)]
        self.prr = 0
        self.vb = k.sb([128, NVB], F32, "vb")
        self.VB = Buf()
        self.cs = k.sb([128, 512], F32, "cs")
        self.CS = Buf()
        self.ident = self.cs[:, 0:128]
        self.ones_bf = k.sb([128, 128], BF16, "ones")
        self.bones_bf = k.sb([128, 128], BF16, "bones")
        self.pswap_bf = k.sb([128, 128], BF16, "pswap")
        self.scT = k.sb([128, 8, 3], BF16, "scT")
        self.mod = k.sb([128, 48, 3], F32, "mod")
        self.A1 = k.sb([128, 8, 3], F32, "A1")
        self.A2 = k.sb([128, 8, 3], F32, "A2")
        self.MOD = Buf()
        self.CONST = Buf()
        k.persist_mark()

    def din(self, name, shape, dt=F32):
        return self.nc.dram_tensor(name, list(shape), dt, kind="ExternalInput").ap()

    def dscr(self, name, shape, dt=F32):
        kind = "ExternalOutput" if name in self.debug else "Internal"
        if name in getattr(self, "ext_in", ()):
            kind = "ExternalInput"
        return self.nc.dram_tensor(name, list(shape), dt, kind=kind).ap()

    def DB(self, *key):
        b = self.dbuf.get(key)
        if b is None:
            b = self.dbuf[key] = Buf()
        return b

    def pb(self):
        i = self.prr
        self.prr = (self.prr + 1) % 8
        return self.ps[i], self.PS[i]

    @staticmethod
    def fm(ap2d, t0, n):
        return ap2d.rearrange("(fc p) t -> p fc t", p=128)[:, :, t0:t0 + n]

    def V(self, name, j=0, n=1):
        o = VBM[name] + j
        return self.vb[:, o:o + n]

    def setup(self):
        k = self.k
        k.dma("sp", self.vb, self.vbd, writes=[self.VB])
        k.dma("sp", self.cs, self.cst, writes=[self.CS])
        k.op("dve", lambda e: e.memset(self.ones_bf, 1.0), writes=[self.CONST])
        k.op("act", lambda e: e.activation(out=self.bones_bf, in_=self.cs[:, 128:256], func=AF.Copy),
             reads=[self.CS], writes=[self.CONST])
        k.op("act", lambda e: e.activation(out=self.pswap_bf, in_=self.cs[:, 256:384], func=AF.Copy),
             reads=[self.CS], writes=[self.CONST])
        ct = k.sb([128, 8, 3], F32)
        sg = k.sb([128, 8, 3], F32)
        CTB = Buf()
        k.dma("sp", ct, self.cT, writes=[CTB])
        k.op("act", lambda e: e.activation(out=sg, in_=ct, func=AF.Sigmoid), reads=[CTB], writes=[CTB])
        k.op("dve", lambda e: e.tensor_tensor(out=self.scT, in0=ct, in1=sg, op=ALU.mult), reads=[CTB],
             writes=[self.CONST])
        for b in range(NB):
            k.dma("sp", self.xT[b, :, 0:CTX], self.cin[b], writes=[self.DB("xT", b, 0)])
            for ti in range(1, 9):
                t0, n = TILES[ti]
                k.dma("sp", self.xT[b, :, t0:t0 + n], self.xin[b, :, t0 - CTX:t0 - CTX + n],
                      writes=[self.DB("xT", b, ti)])
        k.phase_reset()

    def phase_mod(self, l):
        k = self.k
        wa = k.sb([128, 8, 3072], BF16)
        WA = Buf()
        pm, PM = self.pb()
        for half in range(2):
            src = self.ada_w[l].rearrange("(kc p) n -> p kc n", p=128)[:, :, half * 3072:(half + 1) * 3072]
            for kc in range(8):
                k.dma("pool", wa[:, kc, :], src[:, kc, :], writes=[WA])
            for j in range(24):
                jj = half * 24 + j
                for kc in range(8):
                    k.op("pe", lambda e: e.matmul(pm[:, jj * 4:jj * 4 + 3], lhsT=wa[:, kc, j * 128:(j + 1) * 128],
                                                  rhs=self.scT[:, kc, :], start=(kc == 0), stop=(kc == 7)),
                         reads=[WA, self.CONST], writes=[PM], sig=(kc == 7))
        pmv = pm[:, 0:192].rearrange("p (j f) -> p j f", f=4)[:, :, 0:3]
        bias = self.V("adab_%d" % l, 0, 48).unsqueeze(2).broadcast_to([128, 48, 3])
        k.op("dve", lambda e: e.tensor_tensor(out=self.mod, in0=pmv, in1=bias, op=ALU.add),
             reads=[PM, self.VB], writes=[self.MOD])
        for (A, gname, sc0) in ((self.A1, "ng0_%d" % l, 8), (self.A2, "ng1_%d" % l, 32)):
            g = self.V(gname, 0, 8).unsqueeze(2).broadcast_to([128, 8, 3])
            k.op("dve", lambda e: e.scalar_tensor_tensor(out=A, in0=self.mod[:, sc0:sc0 + 8, :], scalar=1.0, in1=g,
                                                         op0=ALU.add, op1=ALU.mult),
                 reads=[self.MOD, self.VB], writes=[self.MOD])
        k.phase_reset()

    def norm_tile(self, xt, XTB, out, OUTB, A, Bsh, mi, n, W):
        k = self.k
        sq, rstd, tmp = W["sq"], W["rstd"], W["tmp"]
        k.op("act", lambda e: e.activation(out=sq[:, :, 0:n], in_=xt[:, :, 0:n], func=AF.Square),
             reads=[XTB], writes=[W["SQ"]])
        pn, PN = self.pb()
        for fc in range(8):
            k.op("pe", lambda e: e.matmul(pn[:, 0:n], lhsT=self.ones_bf, rhs=sq[:, fc, 0:n], start=(fc == 0),
                                          stop=(fc == 7)), reads=[W["SQ"], self.CONST], writes=[PN], sig=(fc == 7))
        k.op("act", lambda e: e.activation(out=rstd[:, 0:n], in_=pn[:, 0:n], func=AF.Sqrt, bias=EPS, scale=1.0 / D),
             reads=[PN], writes=[W["RSTD"]])
        k.op("dve", lambda e: e.reciprocal(out=rstd[:, 0:n], in_=rstd[:, 0:n]), reads=[W["RSTD"]], writes=[W["RSTD"]])
        for fc in range(8):
            j = fc % 2
            k.op("dve", lambda e: e.tensor_tensor(out=tmp[:, j, 0:n], in0=xt[:, fc, 0:n], in1=rstd[:, 0:n],
                                                  op=ALU.mult), reads=[XTB, W["RSTD"]], writes=[W["TMP"][j]])
            k.op("act", lambda e: e.activation(out=out[:, fc, 0:n], in_=tmp[:, j, 0:n], func=AF.Identity,
                                               bias=Bsh[:, fc, mi:mi + 1], scale=A[:, fc, mi:mi + 1]),
                 reads=[W["TMP"][j], self.MOD], writes=[OUTB])

    def norm_work(self):
        k = self.k
        return dict(sq=k.sb([128, 8, 512], BF16), rstd=k.sb([128, 512], F32), tmp=k.sb([128, 2, 512], F32),
                    SQ=Buf(), RSTD=Buf(), TMP=[Buf(), Buf()])

    def load_w(self, dst, DSTB, src2d, nkc, split=1):
        v = src2d.rearrange("(kc p) n -> p kc n", p=128)
        for kc in range(nkc):
            self.k.dma("pool", dst[:, kc, :], v[:, kc, :], writes=[DSTB])

    def phase_mlp(self, l, last):
        k = self.k
        w1 = k.sb([128, 8, DFF], BF16)
        w2 = k.sb([128, 32, D], BF16)
        W1B, W2B = Buf(), Buf()
        self.load_w(w1, W1B, self.w1[l], 8)
        self.load_w(w2, W2B, self.w2[l], 32)
        W = self.norm_work()
        xt = k.sb([128, 8, 512], F32)
        XTB = Buf()
        hn = k.sb([128, 8, 512], BF16)
        HNB = Buf()
        h1 = k.sb([128, 32, 512], BF16)
        H1B = [Buf() for _ in range(32)]
        rl = k.sb([128, 2, 512], BF16)
        RLB = [Buf(), Buf()]
        for b, (ti, (t0, n)) in [(b_, t_) for b_ in range(NB) for t_ in enumerate(TILES)]:
            if last and ti == 0:
                continue
            mi = 2 if ti == 0 else b
            k.dma("sp", xt[:, :, 0:n], self.fm(self.xT[b], t0, n), reads=[self.DB("xT", b, ti)], writes=[XTB])
            self.norm_tile(xt, XTB, hn, HNB, self.A2, self.mod[:, 24:32, :], mi, n, W)
            for oc in range(32):
                p, P = self.pb()
                for kc in range(8):
                    k.op("pe", lambda e: e.matmul(p[:, 0:n], lhsT=w1[:, kc, oc * 128:(oc + 1) * 128], rhs=hn[:, kc, 0:n],
                                                  start=(kc == 0), stop=(kc == 7)),
                         reads=[W1B, HNB], writes=[P], sig=(kc == 7))
                j = oc % 2
                k.op("act", lambda e: e.activation(out=rl[:, j, 0:n], in_=p[:, 0:n], func=AF.Relu),
                     reads=[P], writes=[RLB[j]])
                k.op("pool", lambda e: e.tensor_tensor(out=h1[:, oc, 0:n], in0=rl[:, j, 0:n], in1=rl[:, j, 0:n],
                                                       op=ALU.mult), reads=[RLB[j]], writes=[H1B[oc]])
            for oc in range(8):
                p, P = self.pb()
                for kc in range(32):
                    k.op("pe", lambda e: e.matmul(p[:, 0:n], lhsT=w2[:, kc, oc * 128:(oc + 1) * 128], rhs=h1[:, kc, 0:n],
                                                  start=(kc == 0), stop=(kc == 31)),
                         reads=[W2B, H1B[kc]], writes=[P], sig=(kc == 31))
                k.op("dve", lambda e: e.scalar_tensor_tensor(out=xt[:, oc, 0:n], in0=p[:, 0:n],
                                                             scalar=self.mod[:, 40 + oc, mi:mi + 1],
                                                             in1=xt[:, oc, 0:n], op0=ALU.mult, op1=ALU.add),
                     reads=[P, self.MOD, XTB], writes=[XTB])
            if last:
                k.dma("sp", self.out[b].rearrange("(fc p) t -> p fc t", p=128)[:, :, t0 - CTX:t0 - CTX + n],
                      xt[:, :, 0:n], reads=[XTB], writes=[self.DB("out", b, ti)])
            else:
                k.dma("sp", self.fm(self.xT[b], t0, n), xt[:, :, 0:n], reads=[XTB], writes=[self.DB("xT", b, ti)])
        k.phase_reset()

    def phase_hy_inproj(self, l, b):
        k = self.k
        i = l // 2
        win = k.sb([128, 8, 1920], BF16)
        WB = Buf()
        self.load_w(win, WB, self.hy_win[i], 8)
        W = self.norm_work()
        xt = [k.sb([128, 8, 512], F32) for _ in range(2)]
        XTB = [Buf(), Buf()]
        hn = k.sb([128, 8, 512], BF16)
        HNB = Buf()
        cs_t = k.sb([128, 2, 512], F32)
        CSB = Buf()
        sq = k.sb([128, 512], BF16)
        SQB = Buf()
        rs = k.sb([128, 512], F32)
        RSB = Buf()
        xn = k.sb([128, 512], BF16)
        XNB = Buf()
        t1 = k.sb([128, 512], F32)
        t2 = k.sb([128, 512], F32)
        T1B, T2B = Buf(), Buf()
        qk = k.sb([128, 6, 512], BF16)
        QKB = Buf()
        vt = k.sb([128, 4, 384], BF16)
        VTB = Buf()
        xro = k.sb([128, 4, 512], F32)
        XRB = Buf()
        ggo = k.sb([128, 4, 512], BF16)
        GGB = Buf()
        z2 = k.sb([128, 512], F32)
        Z2B = Buf()
        k.op("dve", lambda e: e.memset(vt, 1.0), writes=[VTB])
        for ti, (t0, n) in enumerate(TILES):
            mi = 2 if ti == 0 else b
            x_ = xt[ti % 2]
            XB = XTB[ti % 2]
            k.dma("sp", x_[:, :, 0:n], self.fm(self.xT[b], t0, n), reads=[self.DB("xT", b, ti)], writes=[XB])
            if ti > 0:
                k.dma("sp", cs_t[:, 0, 0:n], self.cosd[:, t0 - CTX:t0 - CTX + n], writes=[CSB])
                k.dma("sp", cs_t[:, 1, 0:n], self.sind[:, t0 - CTX:t0 - CTX + n], writes=[CSB])
            self.norm_tile(x_, XB, hn, HNB, self.A1, self.mod[:, 0:8, :], mi, n, W)
            for oc in range(6):
                p, P = self.pb()
                for kc in range(8):
                    k.op("pe", lambda e: e.matmul(p[:, 0:n], lhsT=win[:, kc, oc * 128:(oc + 1) * 128], rhs=hn[:, kc, 0:n],
                                                  start=(kc == 0), stop=(kc == 7)), reads=[WB, HNB], writes=[P],
                         sig=(kc == 7))
                k.op("act", lambda e: e.activation(out=sq[:, 0:n], in_=p[:, 0:n], func=AF.Square), reads=[P], writes=[SQB])
                p2, P2 = self.pb()
                k.op("pe", lambda e: e.matmul(p2[:, 0:n], lhsT=self.bones_bf, rhs=sq[:, 0:n], start=True, stop=True),
                     reads=[SQB, self.CONST], writes=[P2])
                k.op("act", lambda e: e.activation(out=rs[:, 0:n], in_=p2[:, 0:n], func=AF.Sqrt, bias=EPS,
                                                   scale=1.0 / 64), reads=[P2], writes=[RSB])
                k.op("dve", lambda e: e.reciprocal(out=rs[:, 0:n], in_=rs[:, 0:n]), reads=[RSB], writes=[RSB])
                g = self.V("gq_%d" % i) if oc < 4 else self.V("gk_%d" % i)
                dst = qk[:, oc, 0:n] if ti == 0 else xn[:, 0:n]
                DSTB = QKB if ti == 0 else XNB
                k.op("dve", lambda e: e.scalar_tensor_tensor(out=dst, in0=p[:, 0:n], scalar=g, in1=rs[:, 0:n],
                                                             op0=ALU.mult, op1=ALU.mult),
                     reads=[P, RSB, self.VB], writes=[DSTB])
                if ti > 0:
                    p3, P3 = self.pb()
                    k.op("pe", lambda e: e.matmul(p3[:, 0:n], lhsT=self.pswap_bf, rhs=xn[:, 0:n], start=True, stop=True),
                         reads=[XNB, self.CONST], writes=[P3])
                    k.op("pool", lambda e: e.tensor_tensor(out=t1[:, 0:n], in0=xn[:, 0:n], in1=cs_t[:, 0, 0:n],
                                                           op=ALU.mult), reads=[XNB, CSB], writes=[T1B])
                    k.op("dve", lambda e: e.tensor_tensor(out=t2[:, 0:n], in0=p3[:, 0:n], in1=cs_t[:, 1, 0:n],
                                                          op=ALU.mult), reads=[P3, CSB], writes=[T2B])
                    k.op("dve", lambda e: e.tensor_tensor(out=qk[:, oc, 0:n], in0=t1[:, 0:n], in1=t2[:, 0:n],
                                                          op=ALU.add), reads=[T1B, T2B], writes=[QKB])
            k.dma("sp", self.fm(self.qT, t0, n), qk[:, 0:4, 0:n], reads=[QKB], writes=[self.DB("qT", ti)])
            k.dma("sp", self.fm(self.kT2, t0, n), qk[:, 4:6, 0:n], reads=[QKB], writes=[self.DB("kT2", ti)])
            nst = n // 128
            for st in range(nst):
                p, P = self.pb()
                for kc in range(8):
                    k.op("pe", lambda e: e.matmul(p[:, 0:128], lhsT=hn[:, kc, st * 128:(st + 1) * 128],
                                                  rhs=win[:, kc, 768:896], start=(kc == 0), stop=(kc == 7)),
                         reads=[WB, HNB], writes=[P], sig=(kc == 7))
                vv = vt[:, st, :].rearrange("p (h c) -> p h c", c=192)[:, :, 64:128]
                k.op("act", lambda e: e.activation(out=vv, in_=p[:, 0:128].rearrange("p (h c) -> p h c", c=64),
                                                   func=AF.Copy), reads=[P], writes=[VTB])
            c0 = t0 // 128
            k.dma("sp", self.vtok[c0:c0 + nst].rearrange("c p f -> p c f"), vt[:, 0:nst, :], reads=[VTB],
                  writes=[self.DB("vtok", ti)])
            for oc in range(4):
                p, P = self.pb()
                for kc in range(8):
                    k.op("pe", lambda e: e.matmul(p[:, 0:n], lhsT=win[:, kc, 896 + oc * 128:896 + (oc + 1) * 128],
                                                  rhs=hn[:, kc, 0:n], start=(kc == 0), stop=(kc == 7)),
                         reads=[WB, HNB], writes=[P], sig=(kc == 7))
                k.op("act", lambda e: e.activation(out=xro[:, oc, 0:n], in_=p[:, 0:n], func=AF.Copy), reads=[P],
                     writes=[XRB])
            k.dma("sp", self.fm(self.xr, t0, n), xro[:, :, 0:n], reads=[XRB], writes=[self.DB("xr", ti)])
            for oc in range(4):
                p, P = self.pb()
                for kc in range(8):
                    k.op("pe", lambda e: e.matmul(p[:, 0:n], lhsT=win[:, kc, 1408 + oc * 128:1408 + (oc + 1) * 128],
                                                  rhs=hn[:, kc, 0:n], start=(kc == 0), stop=(kc == 7)),
                         reads=[WB, HNB], writes=[P], sig=(kc == 7))
                self.gelu_from_psum(p, P, ggo[:, oc, 0:n], GGB, n, z2, Z2B, t1, T1B)
            k.dma("sp", self.fm(self.gg, t0, n), ggo[:, :, 0:n], reads=[GGB], writes=[self.DB("gg", ti)])
        k.phase_reset()

    def gelu_from_psum(self, p, P, dst, DSTB, n, z2, Z2B, t1, T1B):
        k = self.k
        k.op("act", lambda e: e.activation(out=z2[:, 0:n], in_=p[:, 0:n], func=AF.Square), reads=[P], writes=[Z2B])
        k.op("dve", lambda e: e.tensor_scalar(out=z2[:, 0:n], in0=z2[:, 0:n], scalar1=0.044715, scalar2=1.0,
                                              op0=ALU.mult, op1=ALU.add), reads=[Z2B], writes=[Z2B])
        k.op("dve", lambda e: e.tensor_tensor(out=z2[:, 0:n], in0=z2[:, 0:n], in1=p[:, 0:n], op=ALU.mult),
             reads=[Z2B, P], writes=[Z2B])
        k.op("act", lambda e: e.activation(out=t1[:, 0:n], in_=z2[:, 0:n], func=AF.Sigmoid, scale=GELU_C),
             reads=[Z2B], writes=[T1B])
        k.op("dve", lambda e: e.tensor_tensor(out=dst, in0=t1[:, 0:n], in1=p[:, 0:n], op=ALU.mult),
             reads=[T1B, P], writes=[DSTB])

    def phase_rglru(self, l, b):
        k = self.k
        i = l // 2
        gw = k.sb([128, 16, 128], BF16)
        GWB = Buf()
        k.dma("pool", gw, self.hy_gw[i].rearrange("d g c p m -> p (d g c) m"), writes=[GWB])
        c1 = k.sb([128, 8], F32)
        c2 = k.sb([128, 8], F32)
        C1B = Buf()
        k.op("act", lambda e: e.activation(out=c1, in_=self.V("lam_%d" % i, 0, 8), func=AF.Exp, scale=-1.0),
             reads=[self.VB], writes=[C1B])
        k.op("act", lambda e: e.activation(out=c1, in_=c1, func=AF.Ln, bias=1.0), reads=[C1B], writes=[C1B])
        k.op("dve", lambda e: e.tensor_scalar(out=c2, in0=c1, scalar1=-16.0, scalar2=None, op0=ALU.mult),
             reads=[C1B], writes=[C1B])
        k.op("dve", lambda e: e.tensor_scalar(out=c1, in0=c1, scalar1=-8.0, scalar2=None, op0=ALU.mult),
             reads=[C1B], writes=[C1B])
        x = k.sb([128, T], F32)
        xc = k.sb([128, T], F32)
        xcb = k.sb([128, T], BF16)
        r = k.sb([128, T], F32)
        ig = k.sb([128, T], F32)
        a = k.sb([128, T], F32)
        u = k.sb([128, T], F32)
        h = [k.sb([128, T], F32) for _ in range(2)]
        ggt = k.sb([128, T], BF16)
        rec = k.sb([128, T], BF16)
        XB, XCB, XCBB, RB, IB, AB, UB, GB, RECB = [Buf() for _ in range(9)]
        HB = [Buf(), Buf()]
        segs = [(0, CTX), (CTX, T)]
        for c in range(4):
            k.dma("sp", x, self.xr[c * 128:(c + 1) * 128, :], reads=[self.DB("xr", ti) for ti in range(9)], writes=[XB])
            k.dma("sp", ggt, self.gg[c * 128:(c + 1) * 128, :], reads=[self.DB("gg", ti) for ti in range(9)],
                  writes=[GB])
            k.op("act", lambda e: e.activation(out=xc, in_=x, func=AF.Identity, bias=self.V("convb_%d" % i, c),
                                               scale=self.V("convw_%d" % i, 2 * 4 + c)),
                 reads=[XB, self.VB], writes=[XCB])
            for (s0, s1) in segs:
                for j in (0, 1, 3):
                    sh = j - 2
                    a0 = max(s0, s0 - sh)
                    a1 = min(s1, s1 - sh)
                    k.op("dve", lambda e: e.scalar_tensor_tensor(out=xc[:, a0:a1], in0=x[:, a0 + sh:a1 + sh],
                                                                 scalar=self.V("convw_%d" % i, j * 4 + c),
                                                                 in1=xc[:, a0:a1], op0=ALU.mult, op1=ALU.add),
                         reads=[XB, XCB, self.VB], writes=[XCB])
            k.op("act", lambda e: e.activation(out=xcb, in_=xc, func=AF.Copy), reads=[XCB], writes=[XCBB])
            for d in range(2):
                for (t0, n) in TILES:
                    for g, (dst, DB_) in enumerate(((r, RB), (ig, IB))):
                        p, P = self.pb()
                        k.op("pe", lambda e: e.matmul(p[:, 0:n], lhsT=gw[:, (d * 2 + g) * 4 + c, :], rhs=xcb[:, t0:t0 + n],
                                                      start=True, stop=True), reads=[GWB, XCBB], writes=[P])
                        k.op("act", lambda e: e.activation(out=dst[:, t0:t0 + n], in_=p[:, 0:n], func=AF.Sigmoid,
                                                           bias=self.V("gateb_%d" % i, (d * 2 + g) * 4 + c)),
                             reads=[P, self.VB], writes=[DB_])
                k.op("act", lambda e: e.activation(out=a, in_=r, func=AF.Exp, scale=c1[:, d * 4 + c:d * 4 + c + 1]),
                     reads=[RB, C1B], writes=[AB])
                k.op("act", lambda e: e.activation(out=u, in_=r, func=AF.Exp, scale=c2[:, d * 4 + c:d * 4 + c + 1]),
                     reads=[RB, C1B], writes=[UB])
                k.op("act", lambda e: e.activation(out=u, in_=u, func=AF.Sqrt, bias=1.0, scale=-1.0),
                     reads=[UB], writes=[UB])
                k.op("dve", lambda e: e.tensor_tensor(out=ig, in0=ig, in1=xc, op=ALU.mult), reads=[IB, XCB], writes=[IB])
                k.op("dve", lambda e: e.tensor_tensor(out=u, in0=u, in1=ig, op=ALU.mult), reads=[UB, IB], writes=[UB])
                hd = h[d]
                if d == 0:
                    k.op("dve", lambda e: e.tensor_tensor_scan(out=hd, data0=a, data1=u, initial=0.0, op0=ALU.mult,
                                                               op1=ALU.add), reads=[AB, UB], writes=[HB[d]])
                else:
                    k.op("dve", lambda e: e.tensor_tensor_scan(out=hd[:, 0:CTX][:, ::-1], data0=a[:, 0:CTX][:, ::-1],
                                                               data1=u[:, 0:CTX][:, ::-1], initial=0.0, op0=ALU.mult,
                                                               op1=ALU.add), reads=[AB, UB], writes=[HB[d]])
                    k.op("dve", lambda e: e.tensor_tensor_scan(out=hd[:, CTX:T][:, ::-1], data0=a[:, CTX:T][:, ::-1],
                                                               data1=u[:, CTX:T][:, ::-1], initial=hd[:, 0:1],
                                                               op0=ALU.mult, op1=ALU.add), reads=[AB, UB, HB[d]],
                         writes=[HB[d]])
            if "rgd" in self.debug and c == 0 and b == 0:
                rgd = self.nc.dram_tensor("rgd", [8, 128, T], F32, kind="ExternalOutput").ap()
                for j_, (t_, B_) in enumerate(((x, XB), (xc, XCB), (r, RB), (ig, IB), (a, AB), (u, UB), (h[0], HB[0]),
                                               (h[1], HB[1]))):
                    k.dma("sp", rgd[j_], t_, reads=[B_], writes=[Buf()])
                c1d = self.nc.dram_tensor("c1d", [2, 128, 8], F32, kind="ExternalOutput").ap()
                k.dma("sp", c1d[0], c1, reads=[C1B], writes=[Buf()])
                k.dma("sp", c1d[1], c2, reads=[C1B], writes=[Buf()])
            k.op("dve", lambda e: e.tensor_tensor(out=h[0], in0=h[0], in1=h[1], op=ALU.add), reads=[HB[0], HB[1]],
                 writes=[HB[0]])
            k.op("dve", lambda e: e.tensor_tensor(out=rec, in0=h[0], in1=ggt, op=ALU.mult), reads=[HB[0], GB],
                 writes=[RECB])
            k.dma("sp", self.rec[c * 128:(c + 1) * 128, :], rec, reads=[RECB], writes=[self.DB("rec", c)])
        k.phase_reset()

    def phase_attn(self, l, b):
        k = self.k
        i = l // 2
        kt = k.sb([128, 2, T], BF16)
        KTB = Buf()
        k.dma("sp", kt, self.kT2.rearrange("(c p) t -> p c t", p=128), reads=[self.DB("kT2", ti) for ti in range(9)],
              writes=[KTB])
        vt = k.sb([128, T // 128, 384], BF16)
        VTB = Buf()
        k.dma("sp", vt, self.vtok.rearrange("c p f -> p c f"), reads=[self.DB("vtok", ti) for ti in range(9)],
              writes=[VTB])
        wo = k.sb([128, 8, D], BF16)
        WOB = Buf()
        self.load_w(wo, WOB, self.hy_wout[i], 8)
        xt = k.sb([128, 8, 512], F32)
        XB = Buf()
        q = k.sb([128, 4, 512], BF16)
        QB = Buf()
        rc = k.sb([128, 4, 512], BF16)
        RCB = Buf()
        att = k.sb([128, 4, 512], BF16)
        ATB = Buf()
        NPT = 8
        pt = [k.sb([128, 512], BF16) for _ in range(NPT)]
        PTB = [Buf() for _ in range(NPT)]
        den = k.sb([128, 512], F32)
        DNB = Buf()
        ptr = 0
        RECALL = [self.DB("rec", c) for c in range(4)]
        for ti, (t0, n) in enumerate(TILES):
            mi = 2 if ti == 0 else b
            nkc = 2 if ti == 0 else T // 128
            k.dma("sp", xt[:, :, 0:n], self.fm(self.xT[b], t0, n), reads=[self.DB("xT", b, ti)], writes=[XB])
            k.dma("sp", q[:, :, 0:n], self.fm(self.qT, t0, n), reads=[self.DB("qT", ti)], writes=[QB])
            k.dma("sp", rc[:, :, 0:n], self.fm(self.rec, t0, n), reads=RECALL, writes=[RCB])
            for hh in range(8):
                kv, hp, fc = hh // 4, hh % 2, hh // 2
                lo, hi = hp * 64, hp * 64 + 64
                olo, ohi = (1 - hp) * 64, (1 - hp) * 64 + 64
                po, PO = self.ps[6 + hh % 2], self.PS[6 + hh % 2]
                voff = kv * 192 + (64 if hp == 0 else 0)
                LA = 5
                slots = []

                def pv(kc, pj):
                    k.op("pe", lambda e: e.matmul(po[:, 0:n], lhsT=vt[:, kc, voff:voff + 128], rhs=pt[pj][:, 0:n],
                                                  start=(kc == 0), stop=(kc == nkc - 1)),
                         reads=[VTB, PTB[pj]], writes=[PO], sig=(kc == nkc - 1))
                for kc in range(nkc):
                    sbi = ptr % 6
                    ps_, PSB = self.ps[sbi], self.PS[sbi]
                    k.op("pe", lambda e: e.matmul(ps_[:, 0:n], lhsT=kt[lo:hi, kv, kc * 128:(kc + 1) * 128],
                                                  rhs=q[lo:hi, fc, 0:n], start=True, stop=True),
                         reads=[KTB, QB], writes=[PSB])
                    pj = ptr % NPT
                    ptr += 1
                    k.op("act", lambda e: e.activation(out=pt[pj][:, 0:n], in_=ps_[:, 0:n], func=AF.Exp, scale=0.125),
                         reads=[PSB], writes=[PTB[pj]])
                    slots.append((kc, pj))
                    if len(slots) > LA:
                        pv(*slots.pop(0))
                while slots:
                    pv(*slots.pop(0))
                k.op("act", lambda e: e.activation(out=den[lo:hi, 0:n], in_=po[olo:ohi, 0:n], func=AF.Copy),
                     reads=[PO], writes=[DNB])
                k.op("dve", lambda e: e.reciprocal(out=den[lo:hi, 0:n], in_=den[lo:hi, 0:n]), reads=[DNB], writes=[DNB])
                k.op("dve", lambda e: e.tensor_tensor(out=att[lo:hi, fc, 0:n], in0=po[lo:hi, 0:n], in1=den[lo:hi, 0:n],
                                                      op=ALU.mult), reads=[PO, DNB], writes=[ATB])
            for oc in range(8):
                p, P = self.pb()
                for kc in range(8):
                    src = att[:, kc, 0:n] if kc < 4 else rc[:, kc - 4, 0:n]
                    k.op("pe", lambda e: e.matmul(p[:, 0:n], lhsT=wo[:, kc, oc * 128:(oc + 1) * 128], rhs=src,
                                                  start=(kc == 0), stop=(kc == 7)),
                         reads=[WOB, ATB, RCB], writes=[P], sig=(kc == 7))
                k.op("dve", lambda e: e.scalar_tensor_tensor(out=xt[:, oc, 0:n], in0=p[:, 0:n],
                                                             scalar=self.mod[:, 16 + oc, mi:mi + 1], in1=xt[:, oc, 0:n],
                                                             op0=ALU.mult, op1=ALU.add),
                     reads=[P, self.MOD, XB], writes=[XB])
            k.dma("sp", self.fm(self.xT[b], t0, n), xt[:, :, 0:n], reads=[XB], writes=[self.DB("xT", b, ti)])
        k.phase_reset()

    def phase_rwkv(self, l, b):
        last = (l == DEPTH - 1)
        self.rw_norm(l, b)
        if self.rw_stop >= 1:
            self.rw_proj(l, b)
        for d in range(2):
            if self.rw_stop >= 2 + d:
                self.rw_wkv(l, b, d)
        if self.rw_stop >= 4:
            self.rw_out(l, b, last)

    def rw_norm(self, l, b):
        k = self.k
        W = self.norm_work()
        xt = [k.sb([128, 8, 512], F32) for _ in range(2)]
        XB = [Buf(), Buf()]
        hn = [k.sb([128, 8, 512], F32) for _ in range(2)]
        HB = [Buf(), Buf()]
        for ti, (t0, n) in enumerate(TILES):
            mi = 2 if ti == 0 else b
            j = ti % 2
            k.dma("sp", xt[j][:, :, 0:n], self.fm(self.xT[b], t0, n), reads=[self.DB("xT", b, ti)], writes=[XB[j]])
            self.norm_tile(xt[j], XB[j], hn[j], HB[j], self.A1, self.mod[:, 0:8, :], mi, n, W)
            k.dma("sp", self.fm(self.hnT, t0, n), hn[j][:, :, 0:n], reads=[HB[j]], writes=[self.DB("hnT", ti)])
        k.phase_reset()

    def rw_proj(self, l, b):
        k = self.k
        i = l // 2
        wr = k.sb([128, 8, D], BF16)
        wk = k.sb([128, 8, D], BF16)
        wvs = k.sb([128, 8, D], BF16)
        ld = k.sb([128, 8, 256], BF16)
        lu = k.sb([64, 4, D], BF16)
        gd = k.sb([128, 8, 128], BF16)
        gu = k.sb([128, D], BF16)
        WB = Buf()
        self.load_w(wr, WB, self.rw_wrkv[i, 0], 8)
        self.load_w(wk, WB, self.rw_wrkv[i, 1], 8)
        self.load_w(wvs, WB, self.rw_wvst[i], 8)
        self.load_w(ld, WB, self.rw_ld[i], 8)
        self.load_w(gd, WB, self.rw_gd[i], 8)
        k.dma("pool", lu, self.rw_lu[i].rearrange("q p n -> p q n"), writes=[WB])
        k.dma("pool", gu, self.rw_gu[i], writes=[WB])
        omka = k.sb([128, 8], F32)
        OMB = Buf()
        k.op("dve", lambda e: e.tensor_scalar(out=omka, in0=self.V("ka_%d" % i, 0, 8), scalar1=-1.0, scalar2=1.0,
                                              op0=ALU.mult, op1=ALU.add), reads=[self.VB], writes=[OMB])
        NT = 256
        hh = k.sb([128, 8, NT + 2], F32)
        HHB = Buf()
        xx = k.sb([128, 8, NT], F32)
        XXB = Buf()
        L = [k.sb([128, 8, NT], BF16) for _ in range(6)]
        LB = [Buf() for _ in range(6)]
        rt = k.sb([128, 8, NT], F32)
        kt = k.sb([128, 8, NT], F32)
        kkt = k.sb([128, 8, NT], F32)
        kd = [k.sb([128, 8, NT], F32) for _ in range(2)]
        o1 = k.sb([128, 8, NT], F32)
        RTB, KTB, KKB, O1B = Buf(), Buf(), Buf(), Buf()
        KDB = [Buf(), Buf()]
        vst = k.sb([128, 2, 512], F32)
        VSB = [Buf(), Buf()]
        sm = k.sb([128, NT], BF16)
        SMB = Buf()
        at8 = k.sb([128, 8, NT], F32)
        AT8 = Buf()
        tp = k.sb([128, 8, NT], F32)
        TPB = Buf()
        w18 = k.sb([128, 8, NT], F32)
        W18 = Buf()
        sq8 = k.sb([128, 8, NT], BF16)
        SQ8 = Buf()
        bones_f = self.cs[:, 128:256]

        def bc(name, j0, n):
            return self.V(name, j0, 8).unsqueeze(2).broadcast_to([128, 8, n])

        for ti, (t0, n) in enumerate(WTILES):
            seg0, seg1 = (0, CTX) if ti == 0 else (CTX, T)
            lo = max(t0 - 1, seg0)
            hi = min(t0 + n + 1, seg1)
            k.dma("sp", hh[:, :, lo - (t0 - 1):hi - (t0 - 1)], self.fm(self.hnT, lo, hi - lo),
                  reads=[self.DB("hnT", j) for j in range(9)], writes=[HHB])
            if lo != t0 - 1:
                k.op("dve", lambda e: e.memset(hh[:, :, 0:1], 0.0), writes=[HHB])
            if hi != t0 + n + 1:
                k.op("dve", lambda e: e.memset(hh[:, :, n + 1:n + 2], 0.0), writes=[HHB])
            h = hh[:, :, 1:n + 1]
            k.op("dve", lambda e: e.tensor_tensor(out=xx[:, :, 0:n], in0=hh[:, :, 0:n], in1=hh[:, :, 2:n + 2], op=ALU.add),
                 reads=[HHB], writes=[XXB])
            k.op("dve", lambda e: e.scalar_tensor_tensor(out=xx[:, :, 0:n], in0=xx[:, :, 0:n], scalar=0.5, in1=h,
                                                         op0=ALU.mult, op1=ALU.subtract), reads=[XXB, HHB], writes=[XXB])
            for j in (0, 2, 3, 1, 4, 5):
                if j in (0, 2, 3):
                    eng, tmp, TB = "dve", at8, AT8
                else:
                    eng, tmp, TB = "pool", tp, TPB
                k.op(eng, lambda e: e.tensor_tensor(out=tmp[:, :, 0:n], in0=xx[:, :, 0:n], in1=bc("mu_%d" % i, j * 8, n),
                                                    op=ALU.mult), reads=[XXB, self.VB], writes=[TB])
                k.op(eng, lambda e: e.tensor_tensor(out=L[j][:, :, 0:n], in0=tmp[:, :, 0:n], in1=h, op=ALU.add),
                     reads=[TB, HHB], writes=[LB[j]])

            def proj(w, Lj, LjB, dst, DSTB):
                for oc in range(8):
                    p, P = self.pb()
                    for kc in range(8):
                        k.op("pe", lambda e: e.matmul(p[:, 0:n], lhsT=w[:, kc, oc * 128:(oc + 1) * 128], rhs=Lj[:, kc, 0:n],
                                                      start=(kc == 0), stop=(kc == 7)), reads=[WB, LjB], writes=[P],
                             sig=(kc == 7))
                    k.op("act", lambda e: e.activation(out=dst[:, oc, 0:n], in_=p[:, 0:n], func=AF.Copy), reads=[P],
                         writes=[DSTB])
            proj(wr, L[0], LB[0], rt, RTB)
            k.dma("sp", self.fm(self.rT, t0, n), rt[:, :, 0:n], reads=[RTB], writes=[self.DB("rT", ti)])
            proj(wk, L[2], LB[2], kt, KTB)
            for ch in range(n // 64):
                p, P = self.pb()
                for hp in range(2):
                    for kc in range(8):
                        k.op("pe", lambda e: e.matmul(p[hp * 64:(hp + 1) * 64, :], lhsT=L[3][:, kc, ch * 64:(ch + 1) * 64],
                                                      rhs=wvs[:, kc, hp * 512:(hp + 1) * 512], start=(kc == 0),
                                                      stop=(kc == 7)), reads=[WB, LB[3]], writes=[P],
                             sig=(kc == 7 and hp == 1))
                j = ch % 2
                k.op("act", lambda e: e.activation(out=vst[:, j, :], in_=p, func=AF.Copy), reads=[P], writes=[VSB[j]])
                k.dma("sp", self.Vst[t0 // 64 + ch], vst[:, j, :], reads=[VSB[j]], writes=[self.DB("Vst", ti)])
            p, P = self.pb()
            for kc in range(8):
                k.op("pe", lambda e: e.matmul(p[:, 0:n], lhsT=gd[:, kc, :], rhs=L[5][:, kc, 0:n], start=(kc == 0),
                                              stop=(kc == 7)), reads=[WB, LB[5]], writes=[P], sig=(kc == 7))
            k.op("act", lambda e: e.activation(out=sm[:, 0:n], in_=p[:, 0:n], func=AF.Sigmoid), reads=[P], writes=[SMB])
            for oc in range(8):
                p, P = self.pb()
                k.op("pe", lambda e: e.matmul(p[:, 0:n], lhsT=gu[:, oc * 128:(oc + 1) * 128], rhs=sm[:, 0:n], start=True,
                                              stop=True), reads=[WB, SMB], writes=[P])
                k.op("act", lambda e: e.activation(out=o1[:, oc, 0:n], in_=p[:, 0:n], func=AF.Copy), reads=[P],
                     writes=[O1B])
            k.dma("sp", self.fm(self.gT, t0, n), o1[:, :, 0:n], reads=[O1B], writes=[self.DB("gT", ti)])
            k.op("dve", lambda e: e.tensor_tensor(out=kkt[:, :, 0:n], in0=kt[:, :, 0:n], in1=bc("kk_%d" % i, 0, n),
                                                  op=ALU.mult), reads=[KTB, self.VB], writes=[KKB])
            k.op("act", lambda e: e.activation(out=sq8[:, :, 0:n], in_=kkt[:, :, 0:n], func=AF.Square), reads=[KKB],
                 writes=[SQ8])
            for oc in range(8):
                p, P = self.pb()
                k.op("pe", lambda e: e.matmul(p[:, 0:n], lhsT=self.bones_bf, rhs=sq8[:, oc, 0:n], start=True, stop=True),
                     reads=[SQ8, self.CONST], writes=[P])
                k.op("act", lambda e: e.activation(out=w18[:, oc, 0:n], in_=p[:, 0:n], func=AF.Sqrt), reads=[P],
                     writes=[W18])
            k.op("dve", lambda e: e.tensor_scalar(out=w18[:, :, 0:n], in0=w18[:, :, 0:n], scalar1=1e-12, scalar2=None,
                                                  op0=ALU.max), reads=[W18], writes=[W18])
            k.op("dve", lambda e: e.reciprocal(out=w18[:, :, 0:n], in_=w18[:, :, 0:n]), reads=[W18], writes=[W18])
            k.op("dve", lambda e: e.tensor_tensor(out=kkt[:, :, 0:n], in0=kkt[:, :, 0:n], in1=w18[:, :, 0:n], op=ALU.mult),
                 reads=[KKB, W18], writes=[KKB])
            k.dma("sp", self.fm(self.kkT, t0, n), kkt[:, :, 0:n], reads=[KKB], writes=[self.DB("kkT", ti)])
            for d in range(2):
                p, P = self.pb()
                for kc in range(8):
                    k.op("pe", lambda e: e.matmul(p[0:64, 0:n], lhsT=ld[:, kc, (d * 2) * 64:(d * 2 + 1) * 64],
                                                  rhs=L[1][:, kc, 0:n], start=(kc == 0), stop=(kc == 7)),
                         reads=[WB, LB[1]], writes=[P], sig=(kc == 7))
                k.op("act", lambda e: e.activation(out=sm[0:64, 0:n], in_=p[0:64, 0:n], func=AF.Tanh), reads=[P],
                     writes=[SMB])
                for oc in range(8):
                    p, P = self.pb()
                    k.op("pe", lambda e: e.matmul(p[:, 0:n], lhsT=lu[:, d * 2, oc * 128:(oc + 1) * 128], rhs=sm[0:64, 0:n],
                                                  start=True, stop=True), reads=[WB, SMB], writes=[P])
                    k.op("act", lambda e: e.activation(out=o1[:, oc, 0:n], in_=p[:, 0:n], func=AF.Sigmoid,
                                                       bias=self.V("lb_%d" % i, (d * 2) * 8 + oc)),
                         reads=[P, self.VB], writes=[O1B])
                k.op("dve", lambda e: e.tensor_scalar(out=o1[:, :, 0:n], in0=o1[:, :, 0:n], scalar1=-DECAY_SCALE,
                                                      scalar2=None, op0=ALU.mult), reads=[O1B], writes=[O1B])
                k.dma("sp", self.fm(self.lwT[d], t0, n), o1[:, :, 0:n], reads=[O1B], writes=[self.DB("lwT", d, ti)])
                p, P = self.pb()
                for kc in range(8):
                    k.op("pe", lambda e: e.matmul(p[0:64, 0:n], lhsT=ld[:, kc, (d * 2 + 1) * 64:(d * 2 + 2) * 64],
                                                  rhs=L[4][:, kc, 0:n], start=(kc == 0), stop=(kc == 7)),
                         reads=[WB, LB[4]], writes=[P], sig=(kc == 7))
                k.op("act", lambda e: e.activation(out=sm[0:64, 0:n], in_=p[0:64, 0:n], func=AF.Copy), reads=[P],
                     writes=[SMB])
                for oc in range(8):
                    p, P = self.pb()
                    k.op("pe", lambda e: e.matmul(p[:, 0:n], lhsT=lu[:, d * 2 + 1, oc * 128:(oc + 1) * 128],
                                                  rhs=sm[0:64, 0:n], start=True, stop=True), reads=[WB, SMB], writes=[P])
                    k.op("act", lambda e: e.activation(out=at8[:, oc, 0:n], in_=p[:, 0:n], func=AF.Sigmoid,
                                                       bias=self.V("lb_%d" % i, (d * 2 + 1) * 8 + oc)),
                         reads=[P, self.VB], writes=[AT8])
                k.op("dve", lambda e: e.tensor_tensor(out=o1[:, :, 0:n], in0=kkt[:, :, 0:n], in1=at8[:, :, 0:n], op=ALU.mult),
                     reads=[KKB, AT8], writes=[O1B])
                k.dma("sp", self.fm(self.bT[d], t0, n), o1[:, :, 0:n], reads=[O1B], writes=[self.DB("bT", d, ti)])
                k.op("dve", lambda e: e.tensor_tensor(out=at8[:, :, 0:n], in0=at8[:, :, 0:n], in1=bc("ka_%d" % i, 0, n),
                                                      op=ALU.mult), reads=[AT8, self.VB], writes=[AT8])
                k.op("dve", lambda e: e.tensor_tensor(out=at8[:, :, 0:n], in0=at8[:, :, 0:n],
                                                      in1=omka.unsqueeze(2).broadcast_to([128, 8, n]), op=ALU.add),
                     reads=[AT8, OMB], writes=[AT8])
                k.op("dve", lambda e: e.tensor_tensor(out=kd[d][:, :, 0:n], in0=at8[:, :, 0:n], in1=kt[:, :, 0:n], op=ALU.mult),
                     reads=[AT8, KTB], writes=[KDB[d]])
                k.dma("sp", self.fm(self.kdT[d], t0, n), kd[d][:, :, 0:n], reads=[KDB[d]], writes=[self.DB("kdT", d, ti)])
            k.op("dve", lambda e: e.tensor_tensor(out=at8[:, :, 0:n], in0=kd[0][:, :, 0:n], in1=kd[1][:, :, 0:n], op=ALU.add),
                 reads=[KDB[0], KDB[1]], writes=[AT8])
            k.op("dve", lambda e: e.tensor_tensor(out=at8[:, :, 0:n], in0=at8[:, :, 0:n], in1=rt[:, :, 0:n], op=ALU.mult),
                 reads=[AT8, RTB], writes=[AT8])
            k.op("dve", lambda e: e.tensor_tensor(out=at8[:, :, 0:n], in0=at8[:, :, 0:n], in1=bc("rk_%d" % i, 0, n),
                                                  op=ALU.mult), reads=[AT8, self.VB], writes=[AT8])
            for oc in range(8):
                p, P = self.pb()
                k.op("pe", lambda e: e.matmul(p[:, 0:n], lhsT=bones_f, rhs=at8[:, oc, 0:n], start=True, stop=True),
                     reads=[AT8, self.CS], writes=[P])
                k.op("act", lambda e: e.activation(out=w18[:, oc, 0:n], in_=p[:, 0:n], func=AF.Copy), reads=[P],
                     writes=[W18])
            for oc in range(8):
                p2, P2 = self.pb()
                for half in range(2):
                    hd_ = 2 * oc + half
                    c0 = (hd_ % 2) * 512 + (hd_ // 2) * 64
                    for kc in range(8):
                        k.op("pe", lambda e: e.matmul(p2[half * 64:(half + 1) * 64, 0:n], lhsT=wvs[:, kc, c0:c0 + 64],
                                                      rhs=L[3][:, kc, 0:n], start=(kc == 0), stop=(kc == 7)),
                             reads=[WB, LB[3]], writes=[P2], sig=(kc == 7 and half == 1))
                k.op("dve", lambda e: e.tensor_tensor(out=o1[:, oc, 0:n], in0=p2[:, 0:n], in1=w18[:, oc, 0:n], op=ALU.mult),
                     reads=[P2, W18], writes=[O1B])
            k.dma("sp", self.fm(self.bonT, t0, n), o1[:, :, 0:n], reads=[O1B], writes=[self.DB("bonT", ti)])
        k.phase_reset()

    def rw_wkv(self, l, b, d):
        k = self.k
        NW = 256
        rev = (d == 1)
        msk = k.sb([128, 512], F32)
        rmk = k.sb([128, 2048], F32)
        MB = Buf()
        k.dma("sp", msk, self.wmask[:, d, :], writes=[MB])
        k.dma("sp", rmk, self.rmask[:, d, :], writes=[MB])

        def t8():
            return k.sb([128, 8, NW], F32)
        rt, kk, bb, kdt, lw, cum, ec = t8(), t8(), t8(), t8(), t8(), t8(), t8()
        INB = Buf()
        ECB = Buf()
        vst = k.sb([128, 4, 512], F32)
        VB_ = Buf()
        yt = k.sb([128, 8, NW], F32)
        YB = Buf()
        XR = [k.sb([128, 8, 192], F32) for _ in range(2)]
        BE = [k.sb([128, 8, 128], F32) for _ in range(2)]
        KT = [k.sb([128, 8, 128], F32) for _ in range(2)]
        CHB = [Buf(), Buf()]
        AM2 = [k.sb([128, 8, 512], F32) for _ in range(2)]
        AMB2 = [[Buf() for _ in range(8)] for _ in range(2)]
        X2 = [k.sb([128, 8, 128], F32) for _ in range(2)]
        XB2 = [[Buf(), Buf()], [Buf(), Buf()]]
        Pn = k.sb([128, 8, 128], F32)
        PTn = k.sb([128, 8, 128], F32)
        PNB = [Buf(), Buf()]
        PTB = [Buf(), Buf()]
        BEt2 = [k.sb([128, 8, 128], F32) for _ in range(2)]
        KTt2 = [k.sb([128, 8, 128], F32) for _ in range(2)]
        BTB2, KTTB2 = [Buf(), Buf()], [Buf(), Buf()]
        Vbd2 = [k.sb([128, 8, 128], F32) for _ in range(2)]
        VBB2 = [Buf(), Buf()]
        Wsb = k.sb([128, 8, 64], F32)
        Ust = k.sb([128, 8, 64], F32)
        Ubd = k.sb([128, 8, 128], F32)
        Sst = k.sb([128, 8, 64], F32)
        Sbd = k.sb([128, 8, 128], F32)
        WSB, USB, UBB, SSB, SBB = [Buf() for _ in range(5)]
        for t_, B_ in ((XR[0], CHB[0]), (XR[1], CHB[1]), (BE[0], CHB[0]), (BE[1], CHB[1]), (KT[0], CHB[0]),
                       (KT[1], CHB[1]), (Ubd, UBB), (Vbd2[0], VBB2[0]), (Vbd2[1], VBB2[1]), (Sbd, SBB), (Sst, SSB)):
            k.op("pool", lambda e: e.memset(t_, 0.0), writes=[B_])
        r3 = lambda t_: t_.rearrange("p (q c) -> p q c", c=64)
        r4 = lambda t_: t_.rearrange("p (q c) -> p q c", c=128)

        def pre_gen(ch, jb):
            c0 = ch * 64
            cs_ = slice(c0, c0 + 64)
            xr_, be_, kt_, CB = XR[jb], BE[jb], KT[jb], CHB[jb]
            AM, AMB, X, XB = AM2[jb], AMB2[jb], X2[jb], XB2[jb]
            BEt, KTt, BTB, KTTB, Vbd, VBB = BEt2[jb], KTt2[jb], BTB2[jb], KTTB2[jb], Vbd2[jb], VBB2[jb]
            for hp in range(2):
                ps_ = slice(hp * 64, hp * 64 + 64)
                k.op("dve", lambda e: e.scalar_tensor_tensor(out=xr_[ps_, :, hp * 64:hp * 64 + 64], in0=kk[ps_, :, cs_],
                                                             scalar=-1.0, in1=lw[ps_, :, cs_], op0=ALU.mult,
                                                             op1=ALU.mult), reads=[INB], writes=[CB])
                k.op("pool", lambda e: e.tensor_tensor(out=be_[ps_, :, hp * 64:hp * 64 + 64], in0=bb[ps_, :, cs_],
                                                       in1=cum[ps_, :, cs_], op=ALU.mult), reads=[INB, ECB], writes=[CB])
                k.op("pool", lambda e: e.tensor_tensor(out=kt_[ps_, :, hp * 64:hp * 64 + 64], in0=kdt[ps_, :, cs_],
                                                       in1=cum[ps_, :, cs_], op=ALU.mult), reads=[INB, ECB], writes=[CB])
            k.op("dve", lambda e: e.tensor_tensor(out=xr_[:, :, 128:192], in0=rt[:, :, cs_], in1=ec[:, :, cs_],
                                                  op=ALU.mult), reads=[INB, ECB], writes=[CB])
            vs_ = vst[:, ch, :].rearrange("p (q v) -> p q v", v=64)
            for hp in range(2):
                ps_ = slice(hp * 64, hp * 64 + 64)
                k.op("act", lambda e: e.activation(out=Vbd[ps_, :, hp * 64:hp * 64 + 64], in_=vs_[ps_, :, :],
                                                   func=AF.Copy), reads=[VB_], writes=[VBB])
            yield
            for p_ in range(8):
                pa, PA = self.pb()
                k.op("pe", lambda e: e.matmul(pa[:, 0:192], lhsT=be_[:, p_, :], rhs=xr_[:, p_, :], start=True,
                                              stop=True), reads=[CB], writes=[PA], sig=False)
                k.op("pe", lambda e: e.matmul(pa[:, 192:384], lhsT=kt_[:, p_, :], rhs=xr_[:, p_, :], start=True,
                                              stop=True), reads=[CB], writes=[PA], sig=False)
                k.op("pe", lambda e: e.matmul(pa[:, 384:512], lhsT=xr_[:, p_, 0:128], rhs=be_[:, p_, :], start=True,
                                              stop=True), reads=[CB], writes=[PA])
                k.op("dve", lambda e: e.tensor_tensor(out=AM[:, p_, :], in0=pa, in1=msk, op=ALU.mult),
                     reads=[PA, MB], writes=[AMB[p_]])
                if p_ == 3:
                    yield
            yield
            for (src_, dst_, DB_) in ((be_, BEt, BTB), (kt_, KTt, KTTB)):
                for q_ in range(2):
                    pt_, PT_ = self.pb()
                    for pi in range(4):
                        p_ = q_ * 4 + pi
                        k.op("pe", lambda e: e.transpose(pt_[:, pi * 128:(pi + 1) * 128], src_[:, p_, :], self.ident),
                             reads=[CB, self.CS], writes=[PT_], sig=(pi == 3))
                    k.op("act", lambda e: e.activation(out=dst_[:, q_ * 4:q_ * 4 + 4, :], in_=r4(pt_), func=AF.Copy),
                         reads=[PT_], writes=[DB_])
            for q_ in range(2):
                k.op("dve", lambda e: e.tensor_tensor(out=X[:, q_ * 4:q_ * 4 + 4, :], in0=AM[:, q_ * 4:q_ * 4 + 4, 0:128],
                                                      in1=self.ident.unsqueeze(1).broadcast_to([128, 4, 128]), op=ALU.add),
                     reads=AMB[q_ * 4:q_ * 4 + 4] + [self.CS], writes=[XB[q_]])
            yield
            for kk_ in range(1, 6):
                Pq = (lambda p_: AM[:, p_, 0:128]) if kk_ == 1 else (lambda p_: Pn[:, p_, :])
                PTq = (lambda p_: AM[:, p_, 384:512]) if kk_ == 1 else (lambda p_: PTn[:, p_, :])
                bk = {}
                for q_ in range(2):
                    RD = (AMB[q_ * 4:q_ * 4 + 4]) if kk_ == 1 else [PNB[q_], PTB[q_]]
                    pb_, PB_ = self.pb()
                    for pi in range(4):
                        p_ = q_ * 4 + pi
                        k.op("pe", lambda e: e.matmul(pb_[:, pi * 128:(pi + 1) * 128], lhsT=Pq(p_), rhs=PTq(p_),
                                                      start=True, stop=True), reads=RD, writes=[PB_], sig=(pi == 3))
                    bk[("b", q_)] = (pb_, PB_)
                    if kk_ < 5:
                        pa_, PA_ = self.pb()
                        for pi in range(4):
                            p_ = q_ * 4 + pi
                            k.op("pe", lambda e: e.matmul(pa_[:, pi * 128:(pi + 1) * 128], lhsT=PTq(p_), rhs=Pq(p_),
                                                          start=True, stop=True), reads=RD, writes=[PA_], sig=(pi == 3))
                        bk[("a", q_)] = (pa_, PA_)
                yield
                for q_ in range(2):
                    pb_, PB_ = bk[("b", q_)]
                    k.op("act", lambda e: e.activation(out=PTn[:, q_ * 4:q_ * 4 + 4, :], in_=r4(pb_), func=AF.Copy),
                         reads=[PB_], writes=[PTB[q_]])
                    if kk_ < 5:
                        pa_, PA_ = bk[("a", q_)]
                        k.op("dve", lambda e: e.tensor_copy(out=Pn[:, q_ * 4:q_ * 4 + 4, :], in_=r4(pa_)),
                             reads=[PA_], writes=[PNB[q_]])
                for q_ in range(2):
                    pc_, PC_ = self.pb()
                    for pi in range(4):
                        p_ = q_ * 4 + pi
                        k.op("pe", lambda e: e.matmul(pc_[:, pi * 128:(pi + 1) * 128], lhsT=PTn[:, p_, :], rhs=X[:, p_, :],
                                                      start=True, stop=True), reads=[PTB[q_], XB[q_]], writes=[PC_],
                             sig=(pi == 3))
                    bk[("c", q_)] = (pc_, PC_)
                yield
                for q_ in range(2):
                    pc_, PC_ = bk[("c", q_)]
                    k.op("dve", lambda e: e.tensor_tensor(out=X[:, q_ * 4:q_ * 4 + 4, :], in0=X[:, q_ * 4:q_ * 4 + 4, :],
                                                          in1=r4(pc_), op=ALU.add), reads=[PC_, XB[q_]], writes=[XB[q_]])

        def state_gen(ch, jb):
            c0 = ch * 64
            cs_ = slice(c0, c0 + 64)
            xr_, CB = XR[jb], CHB[jb]
            AM, AMB, X, XB = AM2[jb], AMB2[jb], X2[jb], XB2[jb]
            BEt, KTt, BTB, KTTB, Vbd, VBB = BEt2[jb], KTt2[jb], BTB2[jb], KTTB2[jb], Vbd2[jb], VBB2[jb]
            vs_ = vst[:, ch, :].rearrange("p (q v) -> p q v", v=64)
            pw, PW = self.pb()
            pw2, PW2 = self.pb()
            for p_ in range(8):
                k.op("pe", lambda e: e.matmul(pw[:, p_ * 64:(p_ + 1) * 64], lhsT=xr_[:, p_, 0:128], rhs=Sst[:, p_, :],
                                              start=True, stop=True), reads=[CB, SSB], writes=[PW], sig=(p_ == 7))
            for p_ in range(8):
                k.op("pe", lambda e: e.matmul(pw2[:, p_ * 64:(p_ + 1) * 64], lhsT=AM[:, p_, 192:320], rhs=vs_[:, p_, :],
                                              start=True, stop=True), reads=[AMB[p_], VB_], writes=[PW2], sig=(p_ == 7))
            py1, PY1 = self.pb()
            py3, PY3 = self.pb()
            for p_ in range(8):
                k.op("pe", lambda e: e.matmul(py1[:, p_ * 64:(p_ + 1) * 64], lhsT=Sbd[:, p_, :], rhs=xr_[:, p_, 128:192],
                                              start=True, stop=True), reads=[SBB, CB], writes=[PY1], sig=(p_ == 7))
            for p_ in range(8):
                k.op("pe", lambda e: e.matmul(py3[:, p_ * 64:(p_ + 1) * 64], lhsT=Vbd[:, p_, :], rhs=AM[:, p_, 320:384],
                                              start=True, stop=True), reads=[VBB, AMB[p_]], writes=[PY3], sig=(p_ == 7))
            k.op("act", lambda e: e.activation(out=Wsb, in_=r3(pw), func=AF.Copy), reads=[PW], writes=[WSB])
            k.op("dve", lambda e: e.tensor_tensor(out=Wsb, in0=Wsb, in1=r3(pw2), op=ALU.add), reads=[WSB, PW2],
                 writes=[WSB])
            k.op("act", lambda e: e.activation(out=yt[:, :, cs_], in_=r3(py1), func=AF.Copy), reads=[PY1], writes=[YB])
            k.op("dve", lambda e: e.tensor_tensor(out=yt[:, :, cs_], in0=yt[:, :, cs_], in1=r3(py3), op=ALU.add),
                 reads=[YB, PY3], writes=[YB])
            yield
            pu, PU = self.pb()
            for p_ in range(8):
                k.op("pe", lambda e: e.matmul(pu[:, p_ * 64:(p_ + 1) * 64], lhsT=X[:, p_, :], rhs=Wsb[:, p_, :], start=True,
                                              stop=True), reads=[XB[p_ // 4], WSB], writes=[PU], sig=(p_ == 7))
            pu3 = r3(pu)
            k.op("act", lambda e: e.activation(out=Ust, in_=pu3, func=AF.Copy), reads=[PU], writes=[USB])
            for hp in range(2):
                ps_ = slice(hp * 64, hp * 64 + 64)
                k.op("act", lambda e: e.activation(out=Ubd[ps_, :, hp * 64:hp * 64 + 64], in_=pu3[ps_, :, :],
                                                   func=AF.Copy), reads=[PU], writes=[UBB])
            yield
            py2, PY2 = self.pb()
            pS1, PS1 = self.pb()
            pS2, PS2 = self.pb()
            for p_ in range(8):
                k.op("pe", lambda e: e.matmul(pS2[:, p_ * 64:(p_ + 1) * 64], lhsT=KTt[:, p_, :], rhs=vs_[:, p_, :],
                                              start=True, stop=True), reads=[KTTB, VB_], writes=[PS2], sig=(p_ == 7))
            for p_ in range(8):
                k.op("pe", lambda e: e.matmul(pS1[:, p_ * 64:(p_ + 1) * 64], lhsT=BEt[:, p_, :], rhs=Ust[:, p_, :],
                                              start=True, stop=True), reads=[BTB, USB], writes=[PS1], sig=(p_ == 7))
            for p_ in range(8):
                k.op("pe", lambda e: e.matmul(py2[:, p_ * 64:(p_ + 1) * 64], lhsT=Ubd[:, p_, :], rhs=AM[:, p_, 128:192],
                                              start=True, stop=True), reads=[UBB, AMB[p_]], writes=[PY2], sig=(p_ == 7))
            k.op("act", lambda e: e.activation(out=Wsb, in_=r3(pS1), func=AF.Copy), reads=[PS1], writes=[WSB])
            k.op("dve", lambda e: e.tensor_tensor(out=Wsb, in0=Wsb, in1=r3(pS2), op=ALU.add), reads=[WSB, PS2], writes=[WSB])
            k.op("dve", lambda e: e.tensor_tensor(out=Wsb, in0=Wsb, in1=Sst, op=ALU.add), reads=[WSB, SSB], writes=[WSB])
            gcol = (c0 + 63) if not rev else c0
            k.op("dve", lambda e: e.tensor_tensor(out=Sst, in0=Wsb, in1=ec[:, :, gcol:gcol + 1].broadcast_to([128, 8, 64]),
                                                  op=ALU.mult), reads=[WSB, ECB], writes=[SSB])
            for hp in range(2):
                ps_ = slice(hp * 64, hp * 64 + 64)
                k.op("act", lambda e: e.activation(out=Sbd[ps_, :, hp * 64:hp * 64 + 64], in_=Sst[ps_, :, :],
                                                   func=AF.Copy), reads=[SSB], writes=[SBB])
            k.op("dve", lambda e: e.tensor_tensor(out=yt[:, :, cs_], in0=yt[:, :, cs_], in1=r3(py2), op=ALU.add),
                 reads=[YB, PY2], writes=[YB])

        def drain(g):
            for _ in g:
                pass

        def interleave(pre, st):
            pre_done = pre is None
            st_done = False
            while not (pre_done and st_done):
                for _ in range(self.wkv_ratio):
                    if not pre_done:
                        try:
                            next(pre)
                        except StopIteration:
                            pre_done = True
                if not st_done:
                    try:
                        next(st)
                    except StopIteration:
                        st_done = True

        wt = [(0, 0)] + [(1 + w, CTX + NW * w) for w in range(16)]
        order = wt if not rev else [wt[0]] + wt[:0:-1]
        nchunk = 0
        for (wi, t0) in order[:self.wkv_ntiles]:
            ti = wi
            for (dst, src, key) in ((rt, self.rT, ("rT", ti)), (kk, self.kkT, ("kkT", ti)), (bb, self.bT[d], ("bT", d, ti)),
                                    (kdt, self.kdT[d], ("kdT", d, ti)), (lw, self.lwT[d], ("lwT", d, ti))):
                k.dma("sp", dst, self.fm(src, t0, NW), reads=[self.DB(*key)], writes=[INB])
            k.dma("sp", vst, self.Vst[t0 // 64:t0 // 64 + 4].rearrange("c p f -> p c f"), reads=[self.DB("Vst", ti)],
                  writes=[VB_])
            fl = lambda a: a.rearrange("p f t -> p (f t)")
            rv = (lambda a: a[:, ::-1]) if rev else (lambda a: a)
            k.op("dve", lambda e: e.tensor_tensor_scan(out=rv(fl(cum)), data0=rv(rmk), data1=rv(fl(lw)), initial=0.0,
                                                       op0=ALU.mult, op1=ALU.add), reads=[INB, MB, ECB], writes=[ECB])
            k.op("dve", lambda e: e.tensor_tensor(out=lw, in0=cum, in1=lw, op=ALU.subtract), reads=[ECB, INB], writes=[INB])
            k.op("act", lambda e: e.activation(out=lw, in_=lw, func=AF.Exp), reads=[INB], writes=[INB])
            k.op("act", lambda e: e.activation(out=ec, in_=cum, func=AF.Exp), reads=[ECB], writes=[ECB])
            k.op("act", lambda e: e.activation(out=cum, in_=cum, func=AF.Exp, scale=-1.0), reads=[ECB], writes=[ECB])
            chs = list(range(4)) if not rev else list(range(3, -1, -1))
            jbs = [(nchunk + i_) % 2 for i_ in range(4)]
            nchunk += 4
            drain(pre_gen(chs[0], jbs[0]))
            for i_ in range(4):
                nxt = pre_gen(chs[i_ + 1], jbs[i_ + 1]) if i_ < 3 else None
                interleave(nxt, state_gen(chs[i_], jbs[i_]))
            k.dma("sp", self.fm(self.yT[d], t0, NW), yt, reads=[YB], writes=[self.DB("yT", d, wi)])
        k.phase_reset()

    def rw_out(self, l, b, last):
        k = self.k
        i = l // 2
        wo = k.sb([128, 8, D], BF16)
        WOB = Buf()
        self.load_w(wo, WOB, self.rw_wo[i], 8)
        bones_f = self.cs[:, 128:256]
        y0 = k.sb([128, 8, 512], F32)
        y1 = k.sb([128, 8, 512], F32)
        bon = k.sb([128, 8, 512], F32)
        gt = k.sb([128, 8, 512], F32)
        xt = k.sb([128, 8, 512], F32)
        yc = k.sb([128, 8, 512], F32)
        rs = k.sb([128, 8, 512], F32)
        ob = k.sb([128, 8, 512], BF16)
        Y0B, Y1B, BNB, GTB, XB, OBB, YCB, RSB = [Buf() for _ in range(8)]

        def bc(name, n):
            return self.V(name, 0, 8).unsqueeze(2).broadcast_to([128, 8, n])
        for ti, (t0, n) in enumerate(TILES):
            if last and ti == 0:
                continue
            mi = 2 if ti == 0 else b
            wis = [0] if ti == 0 else [2 * ti - 1, 2 * ti]
            k.dma("sp", y0[:, :, 0:n], self.fm(self.yT[0], t0, n), reads=[self.DB("yT", 0, w) for w in wis], writes=[Y0B])
            k.dma("sp", y1[:, :, 0:n], self.fm(self.yT[1], t0, n), reads=[self.DB("yT", 1, w) for w in wis], writes=[Y1B])
            k.dma("sp", bon[:, :, 0:n], self.fm(self.bonT, t0, n), reads=[self.DB("bonT", w) for w in wis], writes=[BNB])
            k.dma("sp", gt[:, :, 0:n], self.fm(self.gT, t0, n), reads=[self.DB("gT", w) for w in wis], writes=[GTB])
            k.dma("sp", xt[:, :, 0:n], self.fm(self.xT[b], t0, n), reads=[self.DB("xT", b, ti)], writes=[XB])
            k.op("pool", lambda e: e.tensor_tensor(out=y0[:, :, 0:n], in0=y0[:, :, 0:n], in1=y1[:, :, 0:n], op=ALU.add),
                 reads=[Y0B, Y1B], writes=[Y0B])
            for fc in range(8):
                p, P = self.pb()
                k.op("pe", lambda e: e.matmul(p[:, 0:n], lhsT=bones_f, rhs=y0[:, fc, 0:n], start=True, stop=True),
                     reads=[Y0B, self.CS], writes=[P])
                k.op("dve", lambda e: e.scalar_tensor_tensor(out=yc[:, fc, 0:n], in0=p[:, 0:n], scalar=-1.0 / 64,
                                                             in1=y0[:, fc, 0:n], op0=ALU.mult, op1=ALU.add),
                     reads=[P, Y0B], writes=[YCB])
            k.op("act", lambda e: e.activation(out=y1[:, :, 0:n], in_=yc[:, :, 0:n], func=AF.Square), reads=[YCB, Y1B],
                 writes=[Y1B])
            for fc in range(8):
                p2, P2 = self.pb()
                k.op("pe", lambda e: e.matmul(p2[:, 0:n], lhsT=bones_f, rhs=y1[:, fc, 0:n], start=True, stop=True),
                     reads=[Y1B, self.CS], writes=[P2])
                k.op("act", lambda e: e.activation(out=rs[:, fc, 0:n], in_=p2[:, 0:n], func=AF.Sqrt, bias=GN_EPS,
                                                   scale=1.0 / 64), reads=[P2], writes=[RSB])
            k.op("dve", lambda e: e.reciprocal(out=rs[:, :, 0:n], in_=rs[:, :, 0:n]), reads=[RSB], writes=[RSB])
            k.op("dve", lambda e: e.tensor_tensor(out=yc[:, :, 0:n], in0=yc[:, :, 0:n], in1=rs[:, :, 0:n], op=ALU.mult),
                 reads=[YCB, RSB], writes=[YCB])
            k.op("pool", lambda e: e.tensor_tensor(out=yc[:, :, 0:n], in0=yc[:, :, 0:n], in1=bc("gng_%d" % i, n), op=ALU.mult),
                 reads=[YCB, self.VB], writes=[YCB])
            k.op("pool", lambda e: e.tensor_tensor(out=bon[:, :, 0:n], in0=bon[:, :, 0:n], in1=bc("gnb_%d" % i, n), op=ALU.add),
                 reads=[BNB, self.VB], writes=[BNB])
            k.op("dve", lambda e: e.tensor_tensor(out=yc[:, :, 0:n], in0=yc[:, :, 0:n], in1=bon[:, :, 0:n], op=ALU.add),
                 reads=[YCB, BNB], writes=[YCB])
            k.op("dve", lambda e: e.tensor_tensor(out=ob[:, :, 0:n], in0=yc[:, :, 0:n], in1=gt[:, :, 0:n], op=ALU.mult),
                 reads=[YCB, GTB], writes=[OBB])
            for oc in range(8):
                p, P = self.pb()
                for kc in range(8):
                    k.op("pe", lambda e: e.matmul(p[:, 0:n], lhsT=wo[:, kc, oc * 128:(oc + 1) * 128], rhs=ob[:, kc, 0:n],
                                                  start=(kc == 0), stop=(kc == 7)), reads=[WOB, OBB], writes=[P],
                         sig=(kc == 7))
                k.op("dve", lambda e: e.scalar_tensor_tensor(out=xt[:, oc, 0:n], in0=p[:, 0:n],
                                                             scalar=self.mod[:, 16 + oc, mi:mi + 1], in1=xt[:, oc, 0:n],
                                                             op0=ALU.mult, op1=ALU.add), reads=[P, self.MOD, XB], writes=[XB])
            k.dma("sp", self.fm(self.xT[b], t0, n), xt[:, :, 0:n], reads=[XB], writes=[self.DB("xT", b, ti)])
        k.phase_reset()

    def build(self, nphases=None):
        ph = [lambda: self.setup()]
        for l in self.layers:
            last = (l == DEPTH - 1)
            ph.append(lambda l=l: self.phase_mod(l))
            for b in range(NB):
                if l % 2 == 0:
                    ph.append(lambda l=l, b=b: self.phase_hy_inproj(l, b))
                    ph.append(lambda l=l, b=b: self.phase_rglru(l, b))
                    ph.append(lambda l=l, b=b: self.phase_attn(l, b))
                else:
                    ph.append(lambda l=l, b=b: self.phase_rwkv(l, b))
            ph.append(lambda l=l, last=last: self.phase_mlp(l, last))
        for f in (ph if nphases is None else ph[:nphases]):
            f()
        self.k.finish()
        return self.nc


def host_consts():
    cs = np.zeros((128, 512), np.float32)
    cs[:, 0:128] = np.eye(128, dtype=np.float32)
    bo = np.zeros((128, 128), np.float32)
    bo[0:64, 0:64] = 1.0
    bo[64:128, 64:128] = 1.0
    cs[:, 128:256] = bo
    pw = np.zeros((128, 128), np.float32)
    for blk in range(2):
        for n in range(64):
            pw[blk * 64 + n, blk * 64 + (n + 32) % 64] = 1.0
    cs[:, 256:384] = pw
    rows = SEQ // 64
    row = np.repeat(np.arange(rows, dtype=np.float32), 64)
    col = np.tile(np.arange(64, dtype=np.float32), rows)
    inv = (np.float32(10000.0) ** (-np.arange(0, 32, 2, dtype=np.float32) / np.float32(32))).astype(np.float32)
    ang = np.concatenate([row[:, None] * inv, col[:, None] * inv], axis=-1).astype(np.float32)
    c = np.cos(ang).astype(np.float32).T
    s_ = np.sin(ang).astype(np.float32).T
    cos64 = np.concatenate([c, c], 0)
    sin64 = np.concatenate([-s_, s_], 0)
    cosT = np.ascontiguousarray(np.concatenate([cos64, cos64], 0))
    sinT = np.ascontiguousarray(np.concatenate([sin64, sin64], 0))
    return cs, cosT, sinT


def fmaj(v):
    return np.ascontiguousarray(np.asarray(v, np.float32).reshape(-1, 128).T)


def host_vb(inp):
    vb = np.zeros((128, NVB), np.float32)

    def put(name, arr):
        arr = np.asarray(arr, np.float32)
        vb[:, VBM[name]:VBM[name] + arr.shape[1]] = arr
    for l in range(DEPTH):
        put("ng0_%d" % l, fmaj(inp["norm_g"][l, 0]))
        put("ng1_%d" % l, fmaj(inp["norm_g"][l, 1]))
        put("adab_%d" % l, fmaj(inp["ada_b"][l]))
    for i in range(2):
        gq = inp["hy_q_norm"][i][PERM]
        gk = inp["hy_k_norm"][i][PERM]
        put("gq_%d" % i, np.concatenate([gq, gq])[:, None])
        put("gk_%d" % i, np.concatenate([gk, gk])[:, None])
        put("convw_%d" % i, np.concatenate([fmaj(inp["hy_conv_w"][i][j]) for j in range(4)], 1))
        put("convb_%d" % i, fmaj(inp["hy_conv_b"][i]))
        put("gateb_%d" % i, np.concatenate([fmaj(inp["hy_gate_b"][i][d][g]) for d in range(2) for g in range(2)], 1))
        put("lam_%d" % i, np.concatenate([fmaj(inp["hy_lam"][i][d]) for d in range(2)], 1))
    for i in range(2):
        put("mu_%d" % i, np.concatenate([fmaj(inp["rw_mu"][i][j]) for j in range(6)], 1))
        put("kk_%d" % i, fmaj(inp["rw_k_k"][i]))
        put("ka_%d" % i, fmaj(inp["rw_k_a"][i]))
        put("rk_%d" % i, fmaj(inp["rw_r_k"][i].reshape(-1)))
        put("gng_%d" % i, fmaj(inp["rw_gn_g"][i]))
        put("gnb_%d" % i, fmaj(inp["rw_gn_b"][i]))
        put("lb_%d" % i, np.concatenate([fmaj(inp["rw_lora_bias"][i][d][j]) for d in range(2) for j in range(2)], 1))
    return vb


def host_shared(inp):
    sh = {}
    cs, cosT, sinT = host_consts()
    sh["consts"], sh["cosT"], sh["sinT"] = cs, cosT, sinT
    sh["vb"] = host_vb(inp)
    sh["ada_w"] = np.ascontiguousarray(inp["ada_w"], np.float32)
    sh["mlp_w1"] = np.ascontiguousarray(inp["mlp_w1"], np.float32)
    sh["mlp_w2"] = np.ascontiguousarray(inp["mlp_w2"], np.float32)
    win = inp["hy_w_in"]
    cols = []
    for h in range(8):
        cols.append(h * 64 + PERM)
    for kv in range(2):
        cols.append(512 + kv * 64 + PERM)
        cols.append(512 + kv * 64 + PERM)
    cols.append(np.arange(640, 768))
    cols.append(np.arange(768, 1792))
    cols = np.concatenate(cols)
    sh["hy_win"] = np.ascontiguousarray(win[:, :, cols], np.float32)
    sh["hy_wout"] = np.ascontiguousarray(inp["hy_w_out"], np.float32)
    gw = inp["hy_gate_w"]
    bd = np.zeros((2, 2, 2, 4, 128, 128), np.float32)
    for c in range(4):
        bd[:, :, :, c, 0:64, 0:64] = gw[:, :, :, 2 * c]
        bd[:, :, :, c, 64:128, 64:128] = gw[:, :, :, 2 * c + 1]
    sh["hy_gw"] = bd
    sh["rw_wrkv"] = np.ascontiguousarray(inp["rw_w_rkv"], np.float32)
    st = np.concatenate([np.arange((2 * p + hp) * 64, (2 * p + hp) * 64 + 64) for hp in range(2) for p in range(8)])
    sh["rw_wvst"] = np.ascontiguousarray(inp["rw_w_rkv"][:, 2][:, :, st], np.float32)
    sh["rw_wo"] = np.ascontiguousarray(inp["rw_w_o"], np.float32)
    ldn = inp["rw_lora_down"]
    sh["rw_ld"] = np.ascontiguousarray(np.concatenate([ldn[:, d, j] for d in range(2) for j in range(2)], axis=-1), np.float32)
    lup = inp["rw_lora_up"]
    sh["rw_lu"] = np.ascontiguousarray(np.stack([lup[:, d, j] for d in range(2) for j in range(2)], axis=1), np.float32)
    sh["rw_gd"] = np.ascontiguousarray(inp["rw_gate_down"], np.float32)
    sh["rw_gu"] = np.ascontiguousarray(inp["rw_gate_up"], np.float32)
    ii = np.arange(64)
    up = (ii[None, :] > ii[:, None]).astype(np.float32)
    le = (ii[:, None] <= ii[None, :]).astype(np.float32)
    lo_ = (ii[None, :] < ii[:, None]).astype(np.float32)

    def bdm(m):
        o = np.zeros((128, 128), np.float32)
        o[0:64, 0:64] = m
        o[64:128, 64:128] = m
        return o

    def stk(m):
        return np.concatenate([m, m], 0)
    wm = np.zeros((128, 2, 512), np.float32)
    for d, (u_, l_, s_) in enumerate(((up, lo_, le), (up.T, lo_.T, le.T))):
        wm[:, d, 0:128] = bdm(u_)
        wm[:, d, 128:192] = stk(s_)
        wm[:, d, 192:320] = bdm(u_)
        wm[:, d, 320:384] = stk(s_)
        wm[:, d, 384:512] = bdm(l_)
    sh["wmask"] = wm
    rm = np.ones((128, 2, 2048), np.float32)
    tt = np.arange(2048)
    rm[:, 0, tt % 64 == 0] = 0.0
    rm[:, 1, tt % 64 == 63] = 0.0
    sh["rmask"] = rm
    return sh


_CACHE = {}


def kernel(**inp):
    inp = {k_: np.asarray(v) for k_, v in inp.items()}
    sh = host_shared(inp)
    if "nc" not in _CACHE:
        _CACHE["nc"] = Prog().build()
    nc = _CACHE["nc"]
    in_maps = []
    for core in range(8):
        bs = [2 * core, 2 * core + 1]
        m = dict(sh)
        m["xT_in"] = np.ascontiguousarray(np.stack([inp["x"][b].T for b in bs]), np.float32)
        m["ctxT_in"] = np.ascontiguousarray(np.stack([inp["ctx"][b].T for b in bs]), np.float32)
        cv = np.stack([inp["c"][bs[0]], inp["c"][bs[1]], inp["c_ctx"]], 0)
        m["cT"] = np.ascontiguousarray(cv.reshape(3, 8, 128).transpose(2, 1, 0), np.float32)
        in_maps.append(m)
    res = run_bass_kernel_spmd(nc, in_maps, core_ids=list(range(8)))
    out = np.empty((16, SEQ, D), np.float32)
    for core in range(8):
        o = res.results[core]["outT"]
        for j in range(NB):
            out[2 * core + j] = o[j].T
    return out
```

```python
import math
import numpy as np
import concourse.bass as bass
import concourse.mybir as mybir
from concourse.bass_utils import run_bass_kernel_spmd

F32 = mybir.dt.float32
BF16 = mybir.dt.bfloat16
AF = mybir.ActivationFunctionType
ALU = mybir.AluOpType

D = 1024
SEQ = 4096
CTX = 256
T = SEQ + CTX
NB = 2
DEPTH = 4
EPS = 1e-6
DFF = 4096
TILES = [(0, CTX)] + [(CTX + 512 * i, 512) for i in range(8)]
WTILES = [(0, CTX)] + [(CTX + 256 * i, 256) for i in range(16)]
GELU_C = 2.0 * math.sqrt(2.0 / math.pi)
DECAY_SCALE = math.exp(-0.5)
GN_EPS = 64e-5


class Buf:
    __slots__ = ("w", "r")

    def __init__(self):
        self.w = None
        self.r = {}


class KB:
    NDMA = 40

    def __init__(self, nc):
        self.nc = nc
        self.eng = dict(pe=nc.tensor, act=nc.scalar, dve=nc.vector, pool=nc.gpsimd, sp=nc.sync)
        self.sems = {}
        self.cnt = {}
        for e in ("pe", "act", "dve", "pool"):
            self.sems[e] = nc.alloc_semaphore("s_" + e)
            self.cnt[e] = 0
        self.dsem = [nc.alloc_semaphore("d%d" % i) for i in range(self.NDMA)]
        self.dval = [0] * self.NDMA
        self.drr = 0
        self.waited = {e: {} for e in self.eng}
        self.sb_off = 16512
        self.sb_base = 16512
        self.nalloc = 0
        self.ninst = 0
        self.pending = []
        self.rec = None

    def sb(self, shape, dt=F32, name=None):
        self.nalloc += 1
        nm = "%s_%d" % (name or "t", self.nalloc)
        n = 1
        for s_ in shape[1:]:
            n *= s_
        nbytes = n * (4 if dt == F32 else 2)
        nbytes = (nbytes + 63) // 64 * 64
        off = self.sb_off
        self.sb_off += nbytes
        assert self.sb_off <= 229376, ("SBUF overflow", nm, self.sb_off)
        return self.nc.alloc_sbuf_tensor_at(nm, list(shape), dt, offset=off).ap()

    def phase_reset(self):
        self.barrier()
        self.sb_off = self.sb_base

    def persist_mark(self):
        self.sb_base = self.sb_off

    def _semh(self, key):
        return self.sems[key] if isinstance(key, str) else self.dsem[key[1]]

    def _wait(self, e, key, val, raw=False):
        if key == e and (not raw or e == "pe"):
            return
        w = self.waited[e]
        if w.get(key, 0) >= val:
            return
        w[key] = val
        self.pending.append((key, val))

    def _take(self):
        p = self.pending
        self.pending = []
        d = {}
        for k_, v_ in p:
            if d.get(k_, 0) < v_:
                d[k_] = v_
        return list(d.items())

    def _deps(self, e, reads, writes):
        for b in reads:
            if b.w is not None:
                self._wait(e, b.w[0], b.w[1], raw=True)
        for b in writes:
            if b.w is not None:
                self._wait(e, b.w[0], b.w[1])
            for k_, v_ in b.r.items():
                self._wait(e, k_, v_)

    def _mark(self, tok, reads, writes):
        k_, v_ = tok
        for b in reads:
            if b.r.get(k_, 0) < v_:
                b.r[k_] = v_
        for b in writes:
            b.w = tok
            b.r = {}

    def op(self, e, ins_fn, reads=(), writes=(), sig=True):
        self._deps(e, reads, writes)
        items = self._take()
        if self.rec is not None:
            self.rec.append((e, list(items), e if sig else None, 1))
        last = items.pop() if items else None
        for k_, v_ in items:
            self.eng[e].wait_ge(self._semh(k_), v_)
            self.ninst += 1
        ins = ins_fn(self.eng[e])
        if last is not None:
            ins._wait_ge(self._semh(last[0]), last[1])
        self.ninst += 1
        if sig:
            self.cnt[e] += 1
            ins.then_inc(self.sems[e], 1)
            tok = (e, self.cnt[e])
        else:
            tok = (e, self.cnt[e] + 1)
        self._mark(tok, reads, writes)
        return tok

    def dma(self, q, out, in_, reads=(), writes=(), **kw):
        i = self.drr
        self.drr = (self.drr + 1) % self.NDMA
        key = ("d", i)
        if self.dval[i] > 0:
            self._wait(q, key, self.dval[i])
        self._deps(q, reads, writes)
        its_ = self._take()
        if self.rec is not None:
            self.rec.append((q, list(its_), key, 16))
        for k_, v_ in its_:
            self.eng[q].wait_ge(self._semh(k_), v_)
            self.ninst += 1
        self.eng[q].dma_start(out=out, in_=in_, **kw).then_inc(self.dsem[i], 16)
        self.ninst += 1
        self.dval[i] += 16
        tok = (key, self.dval[i])
        self._mark(tok, reads, writes)
        return tok

    def barrier(self):
        for e in self.eng:
            for o in ("pe", "act", "dve", "pool"):
                if self.cnt[o] > 0:
                    self._wait(e, o, self.cnt[o])
            for i in range(self.NDMA):
                if self.dval[i] > 0:
                    self._wait(e, ("d", i), self.dval[i])
            its_ = self._take()
            if self.rec is not None:
                self.rec.append((e, list(its_), None, 0))
            for k_, v_ in its_:
                self.eng[e].wait_ge(self._semh(k_), v_)
                self.ninst += 1

    def finish(self):
        self.barrier()


def vb_layout():
    m = {}
    off = 0

    def add(name, n):
        nonlocal off
        m[name] = off
        off += n
    for l in range(DEPTH):
        add("ng0_%d" % l, 8)
        add("ng1_%d" % l, 8)
        add("adab_%d" % l, 48)
    for i in range(2):
        add("gq_%d" % i, 1)
        add("gk_%d" % i, 1)
        add("convw_%d" % i, 16)
        add("convb_%d" % i, 4)
        add("gateb_%d" % i, 16)
        add("lam_%d" % i, 8)
    for i in range(2):
        add("mu_%d" % i, 48)
        add("kk_%d" % i, 8)
        add("ka_%d" % i, 8)
        add("rk_%d" % i, 8)
        add("gng_%d" % i, 8)
        add("gnb_%d" % i, 8)
        add("lb_%d" % i, 32)
    return m, off


VBM, NVB = vb_layout()
PERM = np.concatenate([np.arange(0, 64, 2), np.arange(1, 64, 2)])


class Prog:
    def __init__(self, debug=(), nlayers=DEPTH, ext_in=()):
        self.debug = set(debug)
        self.ext_in = set(ext_in)
        self.nlayers = nlayers
        self.layers = list(range(nlayers))
        self.rw_stop = 99
        self.wkv_stage = 99
        self.wkv_ntiles = 99
        self.wkv_ratio = 1
        nc = self.nc = bass.Bass("TRN2", target_bir_lowering=False)
        k = self.k = KB(nc)
        self.dbuf = {}
        di = self.din
        self.xin = di("xT_in", [NB, D, SEQ])
        self.cin = di("ctxT_in", [NB, D, CTX])
        self.cT = di("cT", [128, 8, 3])
        self.vbd = di("vb", [128, NVB])
        self.ada_w = di("ada_w", [DEPTH, D, 6 * D])
        self.w1 = di("mlp_w1", [DEPTH, D, DFF])
        self.w2 = di("mlp_w2", [DEPTH, DFF, D])
        self.hy_win = di("hy_win", [2, D, 1920])
        self.hy_wout = di("hy_wout", [2, D, D])
        self.hy_gw = di("hy_gw", [2, 2, 2, 4, 128, 128])
        self.cst = di("consts", [128, 512])
        self.cosd = di("cosT", [128, SEQ])
        self.sind = di("sinT", [128, SEQ])
        self.rw_wrkv = di("rw_wrkv", [2, 3, D, D])
        self.rw_wvst = di("rw_wvst", [2, D, D])
        self.rw_wo = di("rw_wo", [2, D, D])
        self.rw_ld = di("rw_ld", [2, D, 256])
        self.rw_lu = di("rw_lu", [2, 4, 64, D])
        self.rw_gd = di("rw_gd", [2, D, 128])
        self.rw_gu = di("rw_gu", [2, 128, D])
        self.wmask = di("wmask", [128, 2, 512])
        self.rmask = di("rmask", [128, 2, 2048])
        self.out = nc.dram_tensor("outT", [NB, D, SEQ], F32, kind="ExternalOutput").ap()
        self.hnT = self.dscr("hnT", [D, T])
        self.rT = self.dscr("rT", [D, T])
        self.kkT = self.dscr("kkT", [D, T])
        self.kdT = self.dscr("kdT", [2, D, T])
        self.bT = self.dscr("bT", [2, D, T])
        self.lwT = self.dscr("lwT", [2, D, T])
        self.gT = self.dscr("gT", [D, T])
        self.bonT = self.dscr("bonT", [D, T])
        self.Vst = self.dscr("Vst", [T // 64, 128, 512])
        self.yT = self.dscr("yT", [2, D, T])
        self.xT = self.dscr("xT", [NB, D, T])
        self.qT = self.dscr("qT", [512, T], BF16)
        self.kT2 = self.dscr("kT2", [256, T], BF16)
        self.vtok = self.dscr("vtok", [T // 128, 128, 384], BF16)
        self.xr = self.dscr("xr", [512, T])
        self.gg = self.dscr("gg", [512, T], BF16)
        self.rec = self.dscr("rec", [512, T], BF16)
        self.ps = [nc.alloc_psum_tensor("ps%d" % i, [128, 512], F32).ap() for i in range(8)]
        self.PS = [Buf() for _ in range(8)]
        self.prr = 0
        self.vb = k.sb([128, NVB], F32, "vb")
        self.VB = Buf()
        self.cs = k.sb([128, 512], F32, "cs")
        self.CS = Buf()
        self.ident = self.cs[:, 0:128]
        self.ones_bf = k.sb([128, 128], BF16, "ones")
        self.bones_bf = k.sb([128, 128], BF16, "bones")
        self.pswap_bf = k.sb([128, 128], BF16, "pswap")
        self.scT = k.sb([128, 8, 3], BF16, "scT")
        self.mod = k.sb([128, 48, 3], F32, "mod")
        self.A1 = k.sb([128, 8, 3], F32, "A1")
        self.A2 = k.sb([128, 8, 3], F32, "A2")
        self.MOD = Buf()
        self.CONST = Buf()
        k.persist_mark()

    def din(self, name, shape, dt=F32):
        return self.nc.dram_tensor(name, list(shape), dt, kind="ExternalInput").ap()

    def dscr(self, name, shape, dt=F32):
        kind = "ExternalOutput" if name in self.debug else "Internal"
        if name in getattr(self, "ext_in", ()):
            kind = "ExternalInput"
        return self.nc.dram_tensor(name, list(shape), dt, kind=kind).ap()

    def DB(self, *key):
        b = self.dbuf.get(key)
        if b is None:
            b = self.dbuf[key] = Buf()
        return b

    def pb(self):
        i = self.prr
        self.prr = (self.prr + 1) % 8
        return self.ps[i], self.PS[i]

    @staticmethod
    def fm(ap2d, t0, n):
        return ap2d.rearrange("(fc p) t -> p fc t", p=128)[:, :, t0:t0 + n]

    def V(self, name, j=0, n=1):
        o = VBM[name] + j
        return self.vb[:, o:o + n]

    def setup(self):
        k = self.k
        k.dma("sp", self.vb, self.vbd, writes=[self.VB])
        k.dma("sp", self.cs, self.cst, writes=[self.CS])
        k.op("dve", lambda e: e.memset(self.ones_bf, 1.0), writes=[self.CONST])
        k.op("act", lambda e: e.activation(out=self.bones_bf, in_=self.cs[:, 128:256], func=AF.Copy),
             reads=[self.CS], writes=[self.CONST])
        k.op("act", lambda e: e.activation(out=self.pswap_bf, in_=self.cs[:, 256:384], func=AF.Copy),
             reads=[self.CS], writes=[self.CONST])
        ct = k.sb([128, 8, 3], F32)
        sg = k.sb([128, 8, 3], F32)
        CTB = Buf()
        k.dma("sp", ct, self.cT, writes=[CTB])
        k.op("act", lambda e: e.activation(out=sg, in_=ct, func=AF.Sigmoid), reads=[CTB], writes=[CTB])
        k.op("dve", lambda e: e.tensor_tensor(out=self.scT, in0=ct, in1=sg, op=ALU.mult), reads=[CTB],
             writes=[self.CONST])
        for b in range(NB):
            k.dma("sp", self.xT[b, :, 0:CTX], self.cin[b], writes=[self.DB("xT", b, 0)])
            for ti in range(1, 9):
                t0, n = TILES[ti]
                k.dma("sp", self.xT[b, :, t0:t0 + n], self.xin[b, :, t0 - CTX:t0 - CTX + n],
                      writes=[self.DB("xT", b, ti)])
        k.phase_reset()

    def phase_mod(self, l):
        k = self.k
        wa = k.sb([128, 8, 3072], BF16)
        WA = Buf()
        pm, PM = self.pb()
        for half in range(2):
            src = self.ada_w[l].rearrange("(kc p) n -> p kc n", p=128)[:, :, half * 3072:(half + 1) * 3072]
            for kc in range(8):
                k.dma("pool", wa[:, kc, :], src[:, kc, :], writes=[WA])
            for j in range(24):
                jj = half * 24 + j
                for kc in range(8):
                    k.op("pe", lambda e: e.matmul(pm[:, jj * 4:jj * 4 + 3], lhsT=wa[:, kc, j * 128:(j + 1) * 128],
                                                  rhs=self.scT[:, kc, :], start=(kc == 0), stop=(kc == 7)),
                         reads=[WA, self.CONST], writes=[PM], sig=(kc == 7))
        pmv = pm[:, 0:192].rearrange("p (j f) -> p j f", f=4)[:, :, 0:3]
        bias = self.V("adab_%d" % l, 0, 48).unsqueeze(2).broadcast_to([128, 48, 3])
        k.op("dve", lambda e: e.tensor_tensor(out=self.mod, in0=pmv, in1=bias, op=ALU.add),
             reads=[PM, self.VB], writes=[self.MOD])
        for (A, gname, sc0) in ((self.A1, "ng0_%d" % l, 8), (self.A2, "ng1_%d" % l, 32)):
            g = self.V(gname, 0, 8).unsqueeze(2).broadcast_to([128, 8, 3])
            k.op("dve", lambda e: e.scalar_tensor_tensor(out=A, in0=self.mod[:, sc0:sc0 + 8, :], scalar=1.0, in1=g,
                                                         op0=ALU.add, op1=ALU.mult),
                 reads=[self.MOD, self.VB], writes=[self.MOD])
        k.phase_reset()

    def norm_tile(self, xt, XTB, out, OUTB, A, Bsh, mi, n, W):
        k = self.k
        sq, rstd, tmp = W["sq"], W["rstd"], W["tmp"]
        k.op("act", lambda e: e.activation(out=sq[:, :, 0:n], in_=xt[:, :, 0:n], func=AF.Square),
             reads=[XTB], writes=[W["SQ"]])
        pn, PN = self.pb()
        for fc in range(8):
            k.op("pe", lambda e: e.matmul(pn[:, 0:n], lhsT=self.ones_bf, rhs=sq[:, fc, 0:n], start=(fc == 0),
                                          stop=(fc == 7)), reads=[W["SQ"], self.CONST], writes=[PN], sig=(fc == 7))
        k.op("act", lambda e: e.activation(out=rstd[:, 0:n], in_=pn[:, 0:n], func=AF.Sqrt, bias=EPS, scale=1.0 / D),
             reads=[PN], writes=[W["RSTD"]])
        k.op("dve", lambda e: e.reciprocal(out=rstd[:, 0:n], in_=rstd[:, 0:n]), reads=[W["RSTD"]], writes=[W["RSTD"]])
        for fc in range(8):
            j = fc % 2
            k.op("dve", lambda e: e.tensor_tensor(out=tmp[:, j, 0:n], in0=xt[:, fc, 0:n], in1=rstd[:, 0:n],
                                                  op=ALU.mult), reads=[XTB, W["RSTD"]], writes=[W["TMP"][j]])
            k.op("act", lambda e: e.activation(out=out[:, fc, 0:n], in_=tmp[:, j, 0:n], func=AF.Identity,
                                               bias=Bsh[:, fc, mi:mi + 1], scale=A[:, fc, mi:mi + 1]),
                 reads=[W["TMP"][j], self.MOD], writes=[OUTB])

    def norm_work(self):
        k = self.k
        return dict(sq=k.sb([128, 8, 512], BF16), rstd=k.sb([128, 512], F32), tmp=k.sb([128, 2, 512], F32),
                    SQ=Buf(), RSTD=Buf(), TMP=[Buf(), Buf()])

    def load_w(self, dst, DSTB, src2d, nkc, split=1):
        v = src2d.rearrange("(kc p) n -> p kc n", p=128)
        for kc in range(nkc):
            self.k.dma("pool", dst[:, kc, :], v[:, kc, :], writes=[DSTB])

    def phase_mlp(self, l, last):
        k = self.k
        w1 = k.sb([128, 8, DFF], BF16)
        w2 = k.sb([128, 32, D], BF16)
        W1B, W2B = Buf(), Buf()
        self.load_w(w1, W1B, self.w1[l], 8)
        self.load_w(w2, W2B, self.w2[l], 32)
        W = self.norm_work()
        xt = k.sb([128, 8, 512], F32)
        XTB = Buf()
        hn = k.sb([128, 8, 512], BF16)
        HNB = Buf()
        h1 = k.sb([128, 32, 512], BF16)
        H1B = [Buf() for _ in range(32)]
        rl = k.sb([128, 2, 512], BF16)
        RLB = [Buf(), Buf()]
        for b, (ti, (t0, n)) in [(b_, t_) for b_ in range(NB) for t_ in enumerate(TILES)]:
            if last and ti == 0:
                continue
            mi = 2 if ti == 0 else b
            k.dma("sp", xt[:, :, 0:n], self.fm(self.xT[b], t0, n), reads=[self.DB("xT", b, ti)], writes=[XTB])
            self.norm_tile(xt, XTB, hn, HNB, self.A2, self.mod[:, 24:32, :], mi, n, W)
            for oc in range(32):
                p, P = self.pb()
                for kc in range(8):
                    k.op("pe", lambda e: e.matmul(p[:, 0:n], lhsT=w1[:, kc, oc * 128:(oc + 1) * 128], rhs=hn[:, kc, 0:n],
                                                  start=(kc == 0), stop=(kc == 7)),
                         reads=[W1B, HNB], writes=[P], sig=(kc == 7))
                j = oc % 2
                k.op("act", lambda e: e.activation(out=rl[:, j, 0:n], in_=p[:, 0:n], func=AF.Relu),
                     reads=[P], writes=[RLB[j]])
                k.op("pool", lambda e: e.tensor_tensor(out=h1[:, oc, 0:n], in0=rl[:, j, 0:n], in1=rl[:, j, 0:n],
                                                       op=ALU.mult), reads=[RLB[j]], writes=[H1B[oc]])
            for oc in range(8):
                p, P = self.pb()
                for kc in range(32):
                    k.op("pe", lambda e: e.matmul(p[:, 0:n], lhsT=w2[:, kc, oc * 128:(oc + 1) * 128], rhs=h1[:, kc, 0:n],
                                                  start=(kc == 0), stop=(kc == 31)),
                         reads=[W2B, H1B[kc]], writes=[P], sig=(kc == 31))
                k.op("dve", lambda e: e.scalar_tensor_tensor(out=xt[:, oc, 0:n], in0=p[:, 0:n],
                                                             scalar=self.mod[:, 40 + oc, mi:mi + 1],
                                                             in1=xt[:, oc, 0:n], op0=ALU.mult, op1=ALU.add),
                     reads=[P, self.MOD, XTB], writes=[XTB])
            if last:
                k.dma("sp", self.out[b].rearrange("(fc p) t -> p fc t", p=128)[:, :, t0 - CTX:t0 - CTX + n],
                      xt[:, :, 0:n], reads=[XTB], writes=[self.DB("out", b, ti)])
            else:
                k.dma("sp", self.fm(self.xT[b], t0, n), xt[:, :, 0:n], reads=[XTB], writes=[self.DB("xT", b, ti)])
        k.phase_reset()

    def phase_hy_inproj(self, l, b):
        k = self.k
        i = l // 2
        win = k.sb([128, 8, 1920], BF16)
        WB = Buf()
        self.load_w(win, WB, self.hy_win[i], 8)
        W = self.norm_work()
        xt = [k.sb([128, 8, 512], F32) for _ in range(2)]
        XTB = [Buf(), Buf()]
        hn = k.sb([128, 8, 512], BF16)
        HNB = Buf()
        cs_t = k.sb([128, 2, 512], F32)
        CSB = Buf()
        sq = k.sb([128, 512], BF16)
        SQB = Buf()
        rs = k.sb([128, 512], F32)
        RSB = Buf()
        xn = k.sb([128, 512], BF16)
        XNB = Buf()
        t1 = k.sb([128, 512], F32)
        t2 = k.sb([128, 512], F32)
        T1B, T2B = Buf(), Buf()
        qk = k.sb([128, 6, 512], BF16)
        QKB = Buf()
        vt = k.sb([128, 4, 384], BF16)
        VTB = Buf()
        xro = k.sb([128, 4, 512], F32)
        XRB = Buf()
        ggo = k.sb([128, 4, 512], BF16)
        GGB = Buf()
        z2 = k.sb([128, 512], F32)
        Z2B = Buf()
        k.op("dve", lambda e: e.memset(vt, 1.0), writes=[VTB])
        for ti, (t0, n) in enumerate(TILES):
            mi = 2 if ti == 0 else b
            x_ = xt[ti % 2]
            XB = XTB[ti % 2]
            k.dma("sp", x_[:, :, 0:n], self.fm(self.xT[b], t0, n), reads=[self.DB("xT", b, ti)], writes=[XB])
            if ti > 0:
                k.dma("sp", cs_t[:, 0, 0:n], self.cosd[:, t0 - CTX:t0 - CTX + n], writes=[CSB])
                k.dma("sp", cs_t[:, 1, 0:n], self.sind[:, t0 - CTX:t0 - CTX + n], writes=[CSB])
            self.norm_tile(x_, XB, hn, HNB, self.A1, self.mod[:, 0:8, :], mi, n, W)
            for oc in range(6):
                p, P = self.pb()
                for kc in range(8):
                    k.op("pe", lambda e: e.matmul(p[:, 0:n], lhsT=win[:, kc, oc * 128:(oc + 1) * 128], rhs=hn[:, kc, 0:n],
                                                  start=(kc == 0), stop=(kc == 7)), reads=[WB, HNB], writes=[P],
                         sig=(kc == 7))
                k.op("act", lambda e: e.activation(out=sq[:, 0:n], in_=p[:, 0:n], func=AF.Square), reads=[P], writes=[SQB])
                p2, P2 = self.pb()
                k.op("pe", lambda e: e.matmul(p2[:, 0:n], lhsT=self.bones_bf, rhs=sq[:, 0:n], start=True, stop=True),
                     reads=[SQB, self.CONST], writes=[P2])
                k.op("act", lambda e: e.activation(out=rs[:, 0:n], in_=p2[:, 0:n], func=AF.Sqrt, bias=EPS,
                                                   scale=1.0 / 64), reads=[P2], writes=[RSB])
                k.op("dve", lambda e: e.reciprocal(out=rs[:, 0:n], in_=rs[:, 0:n]), reads=[RSB], writes=[RSB])
                g = self.V("gq_%d" % i) if oc < 4 else self.V("gk_%d" % i)
                dst = qk[:, oc, 0:n] if ti == 0 else xn[:, 0:n]
                DSTB = QKB if ti == 0 else XNB
                k.op("dve", lambda e: e.scalar_tensor_tensor(out=dst, in0=p[:, 0:n], scalar=g, in1=rs[:, 0:n],
                                                             op0=ALU.mult, op1=ALU.mult),
                     reads=[P, RSB, self.VB], writes=[DSTB])
                if ti > 0:
                    p3, P3 = self.pb()
                    k.op("pe", lambda e: e.matmul(p3[:, 0:n], lhsT=self.pswap_bf, rhs=xn[:, 0:n], start=True, stop=True),
                         reads=[XNB, self.CONST], writes=[P3])
                    k.op("pool", lambda e: e.tensor_tensor(out=t1[:, 0:n], in0=xn[:, 0:n], in1=cs_t[:, 0, 0:n],
                                                           op=ALU.mult), reads=[XNB, CSB], writes=[T1B])
                    k.op("dve", lambda e: e.tensor_tensor(out=t2[:, 0:n], in0=p3[:, 0:n], in1=cs_t[:, 1, 0:n],
                                                          op=ALU.mult), reads=[P3, CSB], writes=[T2B])
                    k.op("dve", lambda e: e.tensor_tensor(out=qk[:, oc, 0:n], in0=t1[:, 0:n], in1=t2[:, 0:n],
                                                          op=ALU.add), reads=[T1B, T2B], writes=[QKB])
            k.dma("sp", self.fm(self.qT, t0, n), qk[:, 0:4, 0:n], reads=[QKB], writes=[self.DB("qT", ti)])
            k.dma("sp", self.fm(self.kT2, t0, n), qk[:, 4:6, 0:n], reads=[QKB], writes=[self.DB("kT2", ti)])
            nst = n // 128
            for st in range(nst):
                p, P = self.pb()
                for kc in range(8):
                    k.op("pe", lambda e: e.matmul(p[:, 0:128], lhsT=hn[:, kc, st * 128:(st + 1) * 128],
                                                  rhs=win[:, kc, 768:896], start=(kc == 0), stop=(kc == 7)),
                         reads=[WB, HNB], writes=[P], sig=(kc == 7))
                vv = vt[:, st, :].rearrange("p (h c) -> p h c", c=192)[:, :, 64:128]
                k.op("act", lambda e: e.activation(out=vv, in_=p[:, 0:128].rearrange("p (h c) -> p h c", c=64),
                                                   func=AF.Copy), reads=[P], writes=[VTB])
            c0 = t0 // 128
            k.dma("sp", self.vtok[c0:c0 + nst].rearrange("c p f -> p c f"), vt[:, 0:nst, :], reads=[VTB],
                  writes=[self.DB("vtok", ti)])
            for oc in range(4):
                p, P = self.pb()
                for kc in range(8):
                    k.op("pe", lambda e: e.matmul(p[:, 0:n], lhsT=win[:, kc, 896 + oc * 128:896 + (oc + 1) * 128],
                                                  rhs=hn[:, kc, 0:n], start=(kc == 0), stop=(kc == 7)),
                         reads=[WB, HNB], writes=[P], sig=(kc == 7))
                k.op("act", lambda e: e.activation(out=xro[:, oc, 0:n], in_=p[:, 0:n], func=AF.Copy), reads=[P],
                     writes=[XRB])
            k.dma("sp", self.fm(self.xr, t0, n), xro[:, :, 0:n], reads=[XRB], writes=[self.DB("xr", ti)])
            for oc in range(4):
                p, P = self.pb()
                for kc in range(8):
                    k.op("pe", lambda e: e.matmul(p[:, 0:n], lhsT=win[:, kc, 1408 + oc * 128:1408 + (oc + 1) * 128],
                                                  rhs=hn[:, kc, 0:n], start=(kc == 0), stop=(kc == 7)),
                         reads=[WB, HNB], writes=[P], sig=(kc == 7))
                self.gelu_from_psum(p, P, ggo[:, oc, 0:n], GGB, n, z2, Z2B, t1, T1B)
            k.dma("sp", self.fm(self.gg, t0, n), ggo[:, :, 0:n], reads=[GGB], writes=[self.DB("gg", ti)])
        k.phase_reset()

    def gelu_from_psum(self, p, P, dst, DSTB, n, z2, Z2B, t1, T1B):
        k = self.k
        k.op("act", lambda e: e.activation(out=z2[:, 0:n], in_=p[:, 0:n], func=AF.Square), reads=[P], writes=[Z2B])
        k.op("dve", lambda e: e.tensor_scalar(out=z2[:, 0:n], in0=z2[:, 0:n], scalar1=0.044715, scalar2=1.0,
                                              op0=ALU.mult, op1=ALU.add), reads=[Z2B], writes=[Z2B])
        k.op("dve", lambda e: e.tensor_tensor(out=z2[:, 0:n], in0=z2[:, 0:n], in1=p[:, 0:n], op=ALU.mult),
             reads=[Z2B, P], writes=[Z2B])
        k.op("act", lambda e: e.activation(out=t1[:, 0:n], in_=z2[:, 0:n], func=AF.Sigmoid, scale=GELU_C),
             reads=[Z2B], writes=[T1B])
        k.op("dve", lambda e: e.tensor_tensor(out=dst, in0=t1[:, 0:n], in1=p[:, 0:n], op=ALU.mult),
             reads=[T1B, P], writes=[DSTB])

    def phase_rglru(self, l, b):
        k = self.k
        i = l // 2
        gw = k.sb([128, 16, 128], BF16)
        GWB = Buf()
        k.dma("pool", gw, self.hy_gw[i].rearrange("d g c p m -> p (d g c) m"), writes=[GWB])
        c1 = k.sb([128, 8], F32)
        c2 = k.sb([128, 8], F32)
        C1B = Buf()
        k.op("act", lambda e: e.activation(out=c1, in_=self.V("lam_%d" % i, 0, 8), func=AF.Exp, scale=-1.0),
             reads=[self.VB], writes=[C1B])
        k.op("act", lambda e: e.activation(out=c1, in_=c1, func=AF.Ln, bias=1.0), reads=[C1B], writes=[C1B])
        k.op("dve", lambda e: e.tensor_scalar(out=c2, in0=c1, scalar1=-16.0, scalar2=None, op0=ALU.mult),
             reads=[C1B], writes=[C1B])
        k.op("dve", lambda e: e.tensor_scalar(out=c1, in0=c1, scalar1=-8.0, scalar2=None, op0=ALU.mult),
             reads=[C1B], writes=[C1B])
        x = k.sb([128, T], F32)
        xc = k.sb([128, T], F32)
        xcb = k.sb([128, T], BF16)
        r = k.sb([128, T], F32)
        ig = k.sb([128, T], F32)
        a = k.sb([128, T], F32)
        u = k.sb([128, T], F32)
        h = [k.sb([128, T], F32) for _ in range(2)]
        ggt = k.sb([128, T], BF16)
        rec = k.sb([128, T], BF16)
        XB, XCB, XCBB, RB, IB, AB, UB, GB, RECB = [Buf() for _ in range(9)]
        HB = [Buf(), Buf()]
        segs = [(0, CTX), (CTX, T)]
        for c in range(4):
            k.dma("sp", x, self.xr[c * 128:(c + 1) * 128, :], reads=[self.DB("xr", ti) for ti in range(9)], writes=[XB])
            k.dma("sp", ggt, self.gg[c * 128:(c + 1) * 128, :], reads=[self.DB("gg", ti) for ti in range(9)],
                  writes=[GB])
            k.op("act", lambda e: e.activation(out=xc, in_=x, func=AF.Identity, bias=self.V("convb_%d" % i, c),
                                               scale=self.V("convw_%d" % i, 2 * 4 + c)),
                 reads=[XB, self.VB], writes=[XCB])
            for (s0, s1) in segs:
                for j in (0, 1, 3):
                    sh = j - 2
                    a0 = max(s0, s0 - sh)
                    a1 = min(s1, s1 - sh)
                    k.op("dve", lambda e: e.scalar_tensor_tensor(out=xc[:, a0:a1], in0=x[:, a0 + sh:a1 + sh],
                                                                 scalar=self.V("convw_%d" % i, j * 4 + c),
                                                                 in1=xc[:, a0:a1], op0=ALU.mult, op1=ALU.add),
                         reads=[XB, XCB, self.VB], writes=[XCB])
            k.op("act", lambda e: e.activation(out=xcb, in_=xc, func=AF.Copy), reads=[XCB], writes=[XCBB])
            for d in range(2):
                for (t0, n) in TILES:
                    for g, (dst, DB_) in enumerate(((r, RB), (ig, IB))):
                        p, P = self.pb()
                        k.op("pe", lambda e: e.matmul(p[:, 0:n], lhsT=gw[:, (d * 2 + g) * 4 + c, :], rhs=xcb[:, t0:t0 + n],
                                                      start=True, stop=True), reads=[GWB, XCBB], writes=[P])
                        k.op("act", lambda e: e.activation(out=dst[:, t0:t0 + n], in_=p[:, 0:n], func=AF.Sigmoid,
                                                           bias=self.V("gateb_%d" % i, (d * 2 + g) * 4 + c)),
                             reads=[P, self.VB], writes=[DB_])
                k.op("act", lambda e: e.activation(out=a, in_=r, func=AF.Exp, scale=c1[:, d * 4 + c:d * 4 + c + 1]),
                     reads=[RB, C1B], writes=[AB])
                k.op("act", lambda e: e.activation(out=u, in_=r, func=AF.Exp, scale=c2[:, d * 4 + c:d * 4 + c + 1]),
                     reads=[RB, C1B], writes=[UB])
                k.op("dve", lambda e: e.tensor_scalar(out=u, in0=u, scalar1=1.0, scalar2=None, op0=ALU.min),
                     reads=[UB], writes=[UB])
                k.op("act", lambda e: e.activation(out=u, in_=u, func=AF.Sqrt, bias=1.0, scale=-1.0),
                     reads=[UB], writes=[UB])
                k.op("dve", lambda e: e.tensor_tensor(out=ig, in0=ig, in1=xc, op=ALU.mult), reads=[IB, XCB], writes=[IB])
                k.op("dve", lambda e: e.tensor_tensor(out=u, in0=u, in1=ig, op=ALU.mult), reads=[UB, IB], writes=[UB])
                hd = h[d]
                if d == 0:
                    k.op("dve", lambda e: e.tensor_tensor_scan(out=hd, data0=a, data1=u, initial=0.0, op0=ALU.mult,
                                                               op1=ALU.add), reads=[AB, UB], writes=[HB[d]])
                else:
                    k.op("dve", lambda e: e.tensor_tensor_scan(out=hd[:, 0:CTX][:, ::-1], data0=a[:, 0:CTX][:, ::-1],
                                                               data1=u[:, 0:CTX][:, ::-1], initial=0.0, op0=ALU.mult,
                                                               op1=ALU.add), reads=[AB, UB], writes=[HB[d]])
                    k.op("dve", lambda e: e.tensor_tensor_scan(out=hd[:, CTX:T][:, ::-1], data0=a[:, CTX:T][:, ::-1],
                                                               data1=u[:, CTX:T][:, ::-1], initial=hd[:, 0:1],
                                                               op0=ALU.mult, op1=ALU.add), reads=[AB, UB, HB[d]],
                         writes=[HB[d]])
            if "rgd" in self.debug and c == 0 and b == 0:
                rgd = self.nc.dram_tensor("rgd", [8, 128, T], F32, kind="ExternalOutput").ap()
                for j_, (t_, B_) in enumerate(((x, XB), (xc, XCB), (r, RB), (ig, IB), (a, AB), (u, UB), (h[0], HB[0]),
                                               (h[1], HB[1]))):
                    k.dma("sp", rgd[j_], t_, reads=[B_], writes=[Buf()])
                c1d = self.nc.dram_tensor("c1d", [2, 128, 8], F32, kind="ExternalOutput").ap()
                k.dma("sp", c1d[0], c1, reads=[C1B], writes=[Buf()])
                k.dma("sp", c1d[1], c2, reads=[C1B], writes=[Buf()])
            k.op("dve", lambda e: e.tensor_tensor(out=h[0], in0=h[0], in1=h[1], op=ALU.add), reads=[HB[0], HB[1]],
                 writes=[HB[0]])
            k.op("dve", lambda e: e.tensor_tensor(out=rec, in0=h[0], in1=ggt, op=ALU.mult), reads=[HB[0], GB],
                 writes=[RECB])
            k.dma("sp", self.rec[c * 128:(c + 1) * 128, :], rec, reads=[RECB], writes=[self.DB("rec", c)])
        k.phase_reset()

    def phase_attn(self, l, b):
        k = self.k
        i = l // 2
        kt = k.sb([128, 2, T], BF16)
        KTB = Buf()
        k.dma("sp", kt, self.kT2.rearrange("(c p) t -> p c t", p=128), reads=[self.DB("kT2", ti) for ti in range(9)],
              writes=[KTB])
        vt = k.sb([128, T // 128, 384], BF16)
        VTB = Buf()
        k.dma("sp", vt, self.vtok.rearrange("c p f -> p c f"), reads=[self.DB("vtok", ti) for ti in range(9)],
              writes=[VTB])
        wo = k.sb([128, 8, D], BF16)
        WOB = Buf()
        self.load_w(wo, WOB, self.hy_wout[i], 8)
        xt = k.sb([128, 8, 512], F32)
        XB = Buf()
        q = k.sb([128, 4, 512], BF16)
        QB = Buf()
        rc = k.sb([128, 4, 512], BF16)
        RCB = Buf()
        att = k.sb([128, 4, 512], BF16)
        ATB = Buf()
        NPT = 8
        pt = [k.sb([128, 512], BF16) for _ in range(NPT)]
        PTB = [Buf() for _ in range(NPT)]
        den = k.sb([128, 512], F32)
        DNB = Buf()
        ptr = 0
        RECALL = [self.DB("rec", c) for c in range(4)]
        for ti, (t0, n) in enumerate(TILES):
            mi = 2 if ti == 0 else b
            nkc = 2 if ti == 0 else T // 128
            k.dma("sp", xt[:, :, 0:n], self.fm(self.xT[b], t0, n), reads=[self.DB("xT", b, ti)], writes=[XB])
            k.dma("sp", q[:, :, 0:n], self.fm(self.qT, t0, n), reads=[self.DB("qT", ti)], writes=[QB])
            k.dma("sp", rc[:, :, 0:n], self.fm(self.rec, t0, n), reads=RECALL, writes=[RCB])
            for hh in range(8):
                kv, hp, fc = hh // 4, hh % 2, hh // 2
                lo, hi = hp * 64, hp * 64 + 64
                olo, ohi = (1 - hp) * 64, (1 - hp) * 64 + 64
                po, PO = self.ps[6 + hh % 2], self.PS[6 + hh % 2]
                voff = kv * 192 + (64 if hp == 0 else 0)
                LA = 5
                slots = []

                def pv(kc, pj):
                    k.op("pe", lambda e: e.matmul(po[:, 0:n], lhsT=vt[:, kc, voff:voff + 128], rhs=pt[pj][:, 0:n],
                                                  start=(kc == 0), stop=(kc == nkc - 1)),
                         reads=[VTB, PTB[pj]], writes=[PO], sig=(kc == nkc - 1))
                for kc in range(nkc):
                    sbi = ptr % 6
                    ps_, PSB = self.ps[sbi], self.PS[sbi]
                    k.op("pe", lambda e: e.matmul(ps_[:, 0:n], lhsT=kt[lo:hi, kv, kc * 128:(kc + 1) * 128],
                                                  rhs=q[lo:hi, fc, 0:n], start=True, stop=True),
                         reads=[KTB, QB], writes=[PSB])
                    pj = ptr % NPT
                    ptr += 1
                    k.op("act", lambda e: e.activation(out=pt[pj][:, 0:n], in_=ps_[:, 0:n], func=AF.Exp, scale=0.125),
                         reads=[PSB], writes=[PTB[pj]])
                    slots.append((kc, pj))
                    if len(slots) > LA:
                        pv(*slots.pop(0))
                while slots:
                    pv(*slots.pop(0))
                k.op("act", lambda e: e.activation(out=den[lo:hi, 0:n], in_=po[olo:ohi, 0:n], func=AF.Copy),
                     reads=[PO], writes=[DNB])
                k.op("dve", lambda e: e.reciprocal(out=den[lo:hi, 0:n], in_=den[lo:hi, 0:n]), reads=[DNB], writes=[DNB])
                k.op("dve", lambda e: e.tensor_tensor(out=att[lo:hi, fc, 0:n], in0=po[lo:hi, 0:n], in1=den[lo:hi, 0:n],
                                                      op=ALU.mult), reads=[PO, DNB], writes=[ATB])
            for oc in range(8):
                p, P = self.pb()
                for kc in range(8):
                    src = att[:, kc, 0:n] if kc < 4 else rc[:, kc - 4, 0:n]
                    k.op("pe", lambda e: e.matmul(p[:, 0:n], lhsT=wo[:, kc, oc * 128:(oc + 1) * 128], rhs=src,
                                                  start=(kc == 0), stop=(kc == 7)),
                         reads=[WOB, ATB, RCB], writes=[P], sig=(kc == 7))
                k.op("dve", lambda e: e.scalar_tensor_tensor(out=xt[:, oc, 0:n], in0=p[:, 0:n],
                                                             scalar=self.mod[:, 16 + oc, mi:mi + 1], in1=xt[:, oc, 0:n],
                                                             op0=ALU.mult, op1=ALU.add),
                     reads=[P, self.MOD, XB], writes=[XB])
            k.dma("sp", self.fm(self.xT[b], t0, n), xt[:, :, 0:n], reads=[XB], writes=[self.DB("xT", b, ti)])
        k.phase_reset()

    def phase_rwkv(self, l, b):
        last = (l == DEPTH - 1)
        self.rw_norm(l, b)
        if self.rw_stop >= 1:
            self.rw_proj(l, b)
        for d in range(2):
            if self.rw_stop >= 2 + d:
                self.rw_wkv(l, b, d)
        if self.rw_stop >= 4:
            self.rw_out(l, b, last)

    def rw_norm(self, l, b):
        k = self.k
        W = self.norm_work()
        xt = [k.sb([128, 8, 512], F32) for _ in range(2)]
        XB = [Buf(), Buf()]
        hn = [k.sb([128, 8, 512], F32) for _ in range(2)]
        HB = [Buf(), Buf()]
        for ti, (t0, n) in enumerate(TILES):
            mi = 2 if ti == 0 else b
            j = ti % 2
            k.dma("sp", xt[j][:, :, 0:n], self.fm(self.xT[b], t0, n), reads=[self.DB("xT", b, ti)], writes=[XB[j]])
            self.norm_tile(xt[j], XB[j], hn[j], HB[j], self.A1, self.mod[:, 0:8, :], mi, n, W)
            k.dma("sp", self.fm(self.hnT, t0, n), hn[j][:, :, 0:n], reads=[HB[j]], writes=[self.DB("hnT", ti)])
        k.phase_reset()

    def rw_proj(self, l, b):
        k = self.k
        i = l // 2
        wr = k.sb([128, 8, D], BF16)
        wk = k.sb([128, 8, D], BF16)
        wvs = k.sb([128, 8, D], BF16)
        ld = k.sb([128, 8, 256], BF16)
        lu = k.sb([64, 4, D], BF16)
        gd = k.sb([128, 8, 128], BF16)
        gu = k.sb([128, D], BF16)
        WB = Buf()
        self.load_w(wr, WB, self.rw_wrkv[i, 0], 8)
        self.load_w(wk, WB, self.rw_wrkv[i, 1], 8)
        self.load_w(wvs, WB, self.rw_wvst[i], 8)
        self.load_w(ld, WB, self.rw_ld[i], 8)
        self.load_w(gd, WB, self.rw_gd[i], 8)
        k.dma("pool", lu, self.rw_lu[i].rearrange("q p n -> p q n"), writes=[WB])
        k.dma("pool", gu, self.rw_gu[i], writes=[WB])
        omka = k.sb([128, 8], F32)
        OMB = Buf()
        k.op("dve", lambda e: e.tensor_scalar(out=omka, in0=self.V("ka_%d" % i, 0, 8), scalar1=-1.0, scalar2=1.0,
                                              op0=ALU.mult, op1=ALU.add), reads=[self.VB], writes=[OMB])
        NT = 256
        hh = k.sb([128, 8, NT + 2], F32)
        HHB = Buf()
        xx = k.sb([128, 8, NT], F32)
        XXB = Buf()
        L = [k.sb([128, 8, NT], BF16) for _ in range(6)]
        LB = [Buf() for _ in range(6)]
        rt = k.sb([128, 8, NT], F32)
        kt = k.sb([128, 8, NT], F32)
        kkt = k.sb([128, 8, NT], F32)
        kd = [k.sb([128, 8, NT], F32) for _ in range(2)]
        o1 = k.sb([128, 8, NT], F32)
        RTB, KTB, KKB, O1B = Buf(), Buf(), Buf(), Buf()
        KDB = [Buf(), Buf()]
        vst = k.sb([128, 2, 512], F32)
        VSB = [Buf(), Buf()]
        sm = k.sb([128, NT], BF16)
        SMB = Buf()
        at8 = k.sb([128, 8, NT], F32)
        AT8 = Buf()
        tp = k.sb([128, 8, NT], F32)
        TPB = Buf()
        w18 = k.sb([128, 8, NT], F32)
        W18 = Buf()
        sq8 = k.sb([128, 8, NT], BF16)
        SQ8 = Buf()
        bones_f = self.cs[:, 128:256]

        def bc(name, j0, n):
            return self.V(name, j0, 8).unsqueeze(2).broadcast_to([128, 8, n])

        for ti, (t0, n) in enumerate(WTILES):
            seg0, seg1 = (0, CTX) if ti == 0 else (CTX, T)
            lo = max(t0 - 1, seg0)
            hi = min(t0 + n + 1, seg1)
            k.dma("sp", hh[:, :, lo - (t0 - 1):hi - (t0 - 1)], self.fm(self.hnT, lo, hi - lo),
                  reads=[self.DB("hnT", j) for j in range(9)], writes=[HHB])
            if lo != t0 - 1:
                k.op("dve", lambda e: e.memset(hh[:, :, 0:1], 0.0), writes=[HHB])
            if hi != t0 + n + 1:
                k.op("dve", lambda e: e.memset(hh[:, :, n + 1:n + 2], 0.0), writes=[HHB])
            h = hh[:, :, 1:n + 1]
            k.op("dve", lambda e: e.tensor_tensor(out=xx[:, :, 0:n], in0=hh[:, :, 0:n], in1=hh[:, :, 2:n + 2], op=ALU.add),
                 reads=[HHB], writes=[XXB])
            k.op("dve", lambda e: e.scalar_tensor_tensor(out=xx[:, :, 0:n], in0=xx[:, :, 0:n], scalar=0.5, in1=h,
                                                         op0=ALU.mult, op1=ALU.subtract), reads=[XXB, HHB], writes=[XXB])
            for j in (0, 2, 3, 1, 4, 5):
                if j in (0, 2, 3):
                    eng, tmp, TB = "dve", at8, AT8
                else:
                    eng, tmp, TB = "pool", tp, TPB
                k.op(eng, lambda e: e.tensor_tensor(out=tmp[:, :, 0:n], in0=xx[:, :, 0:n], in1=bc("mu_%d" % i, j * 8, n),
                                                    op=ALU.mult), reads=[XXB, self.VB], writes=[TB])
                k.op(eng, lambda e: e.tensor_tensor(out=L[j][:, :, 0:n], in0=tmp[:, :, 0:n], in1=h, op=ALU.add),
                     reads=[TB, HHB], writes=[LB[j]])

            def proj(w, Lj, LjB, dst, DSTB):
                for oc in range(8):
                    p, P = self.pb()
                    for kc in range(8):
                        k.op("pe", lambda e: e.matmul(p[:, 0:n], lhsT=w[:, kc, oc * 128:(oc + 1) * 128], rhs=Lj[:, kc, 0:n],
                                                      start=(kc == 0), stop=(kc == 7)), reads=[WB, LjB], writes=[P],
                             sig=(kc == 7))
                    k.op("act", lambda e: e.activation(out=dst[:, oc, 0:n], in_=p[:, 0:n], func=AF.Copy), reads=[P],
                         writes=[DSTB])
            proj(wr, L[0], LB[0], rt, RTB)
            k.dma("sp", self.fm(self.rT, t0, n), rt[:, :, 0:n], reads=[RTB], writes=[self.DB("rT", ti)])
            proj(wk, L[2], LB[2], kt, KTB)
            for ch in range(n // 64):
                p, P = self.pb()
                for hp in range(2):
                    for kc in range(8):
                        k.op("pe", lambda e: e.matmul(p[hp * 64:(hp + 1) * 64, :], lhsT=L[3][:, kc, ch * 64:(ch + 1) * 64],
                                                      rhs=wvs[:, kc, hp * 512:(hp + 1) * 512], start=(kc == 0),
                                                      stop=(kc == 7)), reads=[WB, LB[3]], writes=[P],
                             sig=(kc == 7 and hp == 1))
                j = ch % 2
                k.op("act", lambda e: e.activation(out=vst[:, j, :], in_=p, func=AF.Copy), reads=[P], writes=[VSB[j]])
                k.dma("sp", self.Vst[t0 // 64 + ch], vst[:, j, :], reads=[VSB[j]], writes=[self.DB("Vst", ti)])
            p, P = self.pb()
            for kc in range(8):
                k.op("pe", lambda e: e.matmul(p[:, 0:n], lhsT=gd[:, kc, :], rhs=L[5][:, kc, 0:n], start=(kc == 0),
                                              stop=(kc == 7)), reads=[WB, LB[5]], writes=[P], sig=(kc == 7))
            k.op("act", lambda e: e.activation(out=sm[:, 0:n], in_=p[:, 0:n], func=AF.Sigmoid), reads=[P], writes=[SMB])
            for oc in range(8):
                p, P = self.pb()
                k.op("pe", lambda e: e.matmul(p[:, 0:n], lhsT=gu[:, oc * 128:(oc + 1) * 128], rhs=sm[:, 0:n], start=True,
                                              stop=True), reads=[WB, SMB], writes=[P])
                k.op("act", lambda e: e.activation(out=o1[:, oc, 0:n], in_=p[:, 0:n], func=AF.Copy), reads=[P],
                     writes=[O1B])
            k.dma("sp", self.fm(self.gT, t0, n), o1[:, :, 0:n], reads=[O1B], writes=[self.DB("gT", ti)])
            k.op("dve", lambda e: e.tensor_tensor(out=kkt[:, :, 0:n], in0=kt[:, :, 0:n], in1=bc("kk_%d" % i, 0, n),
                                                  op=ALU.mult), reads=[KTB, self.VB], writes=[KKB])
            k.op("act", lambda e: e.activation(out=sq8[:, :, 0:n], in_=kkt[:, :, 0:n], func=AF.Square), reads=[KKB],
                 writes=[SQ8])
            for oc in range(8):
                p, P = self.pb()
                k.op("pe", lambda e: e.matmul(p[:, 0:n], lhsT=self.bones_bf, rhs=sq8[:, oc, 0:n], start=True, stop=True),
                     reads=[SQ8, self.CONST], writes=[P])
                k.op("act", lambda e: e.activation(out=w18[:, oc, 0:n], in_=p[:, 0:n], func=AF.Sqrt), reads=[P],
                     writes=[W18])
            k.op("dve", lambda e: e.tensor_scalar(out=w18[:, :, 0:n], in0=w18[:, :, 0:n], scalar1=1e-12, scalar2=None,
                                                  op0=ALU.max), reads=[W18], writes=[W18])
            k.op("dve", lambda e: e.reciprocal(out=w18[:, :, 0:n], in_=w18[:, :, 0:n]), reads=[W18], writes=[W18])
            k.op("dve", lambda e: e.tensor_tensor(out=kkt[:, :, 0:n], in0=kkt[:, :, 0:n], in1=w18[:, :, 0:n], op=ALU.mult),
                 reads=[KKB, W18], writes=[KKB])
            k.dma("sp", self.fm(self.kkT, t0, n), kkt[:, :, 0:n], reads=[KKB], writes=[self.DB("kkT", ti)])
            for d in range(2):
                p, P = self.pb()
                for kc in range(8):
                    k.op("pe", lambda e: e.matmul(p[0:64, 0:n], lhsT=ld[:, kc, (d * 2) * 64:(d * 2 + 1) * 64],
                                                  rhs=L[1][:, kc, 0:n], start=(kc == 0), stop=(kc == 7)),
                         reads=[WB, LB[1]], writes=[P], sig=(kc == 7))
                k.op("act", lambda e: e.activation(out=sm[0:64, 0:n], in_=p[0:64, 0:n], func=AF.Tanh), reads=[P],
                     writes=[SMB])
                for oc in range(8):
                    p, P = self.pb()
                    k.op("pe", lambda e: e.matmul(p[:, 0:n], lhsT=lu[:, d * 2, oc * 128:(oc + 1) * 128], rhs=sm[0:64, 0:n],
                                                  start=True, stop=True), reads=[WB, SMB], writes=[P])
                    k.op("act", lambda e: e.activation(out=o1[:, oc, 0:n], in_=p[:, 0:n], func=AF.Sigmoid,
                                                       bias=self.V("lb_%d" % i, (d * 2) * 8 + oc)),
                         reads=[P, self.VB], writes=[O1B])
                k.op("dve", lambda e: e.tensor_scalar(out=o1[:, :, 0:n], in0=o1[:, :, 0:n], scalar1=-DECAY_SCALE,
                                                      scalar2=None, op0=ALU.mult), reads=[O1B], writes=[O1B])
                k.dma("sp", self.fm(self.lwT[d], t0, n), o1[:, :, 0:n], reads=[O1B], writes=[self.DB("lwT", d, ti)])
                p, P = self.pb()
                for kc in range(8):
                    k.op("pe", lambda e: e.matmul(p[0:64, 0:n], lhsT=ld[:, kc, (d * 2 + 1) * 64:(d * 2 + 2) * 64],
                                                  rhs=L[4][:, kc, 0:n], start=(kc == 0), stop=(kc == 7)),
                         reads=[WB, LB[4]], writes=[P], sig=(kc == 7))
                k.op("act", lambda e: e.activation(out=sm[0:64, 0:n], in_=p[0:64, 0:n], func=AF.Copy), reads=[P],
                     writes=[SMB])
                for oc in range(8):
                    p, P = self.pb()
                    k.op("pe", lambda e: e.matmul(p[:, 0:n], lhsT=lu[:, d * 2 + 1, oc * 128:(oc + 1) * 128],
                                                  rhs=sm[0:64, 0:n], start=True, stop=True), reads=[WB, SMB], writes=[P])
                    k.op("act", lambda e: e.activation(out=at8[:, oc, 0:n], in_=p[:, 0:n], func=AF.Sigmoid,
                                                       bias=self.V("lb_%d" % i, (d * 2 + 1) * 8 + oc)),
                         reads=[P, self.VB], writes=[AT8])
                k.op("dve", lambda e: e.tensor_tensor(out=o1[:, :, 0:n], in0=kkt[:, :, 0:n], in1=at8[:, :, 0:n], op=ALU.mult),
                     reads=[KKB, AT8], writes=[O1B])
                k.dma("sp", self.fm(self.bT[d], t0, n), o1[:, :, 0:n], reads=[O1B], writes=[self.DB("bT", d, ti)])
                k.op("dve", lambda e: e.tensor_tensor(out=at8[:, :, 0:n], in0=at8[:, :, 0:n], in1=bc("ka_%d" % i, 0, n),
                                                      op=ALU.mult), reads=[AT8, self.VB], writes=[AT8])
                k.op("dve", lambda e: e.tensor_tensor(out=at8[:, :, 0:n], in0=at8[:, :, 0:n],
                                                      in1=omka.unsqueeze(2).broadcast_to([128, 8, n]), op=ALU.add),
                     reads=[AT8, OMB], writes=[AT8])
                k.op("dve", lambda e: e.tensor_tensor(out=kd[d][:, :, 0:n], in0=at8[:, :, 0:n], in1=kt[:, :, 0:n], op=ALU.mult),
                     reads=[AT8, KTB], writes=[KDB[d]])
                k.dma("sp", self.fm(self.kdT[d], t0, n), kd[d][:, :, 0:n], reads=[KDB[d]], writes=[self.DB("kdT", d, ti)])
            k.op("dve", lambda e: e.tensor_tensor(out=at8[:, :, 0:n], in0=kd[0][:, :, 0:n], in1=kd[1][:, :, 0:n], op=ALU.add),
                 reads=[KDB[0], KDB[1]], writes=[AT8])
            k.op("dve", lambda e: e.tensor_tensor(out=at8[:, :, 0:n], in0=at8[:, :, 0:n], in1=rt[:, :, 0:n], op=ALU.mult),
                 reads=[AT8, RTB], writes=[AT8])
            k.op("dve", lambda e: e.tensor_tensor(out=at8[:, :, 0:n], in0=at8[:, :, 0:n], in1=bc("rk_%d" % i, 0, n),
                                                  op=ALU.mult), reads=[AT8, self.VB], writes=[AT8])
            for oc in range(8):
                p, P = self.pb()
                k.op("pe", lambda e: e.matmul(p[:, 0:n], lhsT=bones_f, rhs=at8[:, oc, 0:n], start=True, stop=True),
                     reads=[AT8, self.CS], writes=[P])
                k.op("act", lambda e: e.activation(out=w18[:, oc, 0:n], in_=p[:, 0:n], func=AF.Copy), reads=[P],
                     writes=[W18])
            for oc in range(8):
                p2, P2 = self.pb()
                for half in range(2):
                    hd_ = 2 * oc + half
                    c0 = (hd_ % 2) * 512 + (hd_ // 2) * 64
                    for kc in range(8):
                        k.op("pe", lambda e: e.matmul(p2[half * 64:(half + 1) * 64, 0:n], lhsT=wvs[:, kc, c0:c0 + 64],
                                                      rhs=L[3][:, kc, 0:n], start=(kc == 0), stop=(kc == 7)),
                             reads=[WB, LB[3]], writes=[P2], sig=(kc == 7 and half == 1))
                k.op("dve", lambda e: e.tensor_tensor(out=o1[:, oc, 0:n], in0=p2[:, 0:n], in1=w18[:, oc, 0:n], op=ALU.mult),
                     reads=[P2, W18], writes=[O1B])
            k.dma("sp", self.fm(self.bonT, t0, n), o1[:, :, 0:n], reads=[O1B], writes=[self.DB("bonT", ti)])
        k.phase_reset()

    def rw_wkv(self, l, b, d):
        k = self.k
        NW = 256
        rev = (d == 1)
        msk = k.sb([128, 512], F32)
        rmk = k.sb([128, 2048], F32)
        MB = Buf()
        k.dma("sp", msk, self.wmask[:, d, :], writes=[MB])
        k.dma("sp", rmk, self.rmask[:, d, :], writes=[MB])

        def t8():
            return k.sb([128, 8, NW], F32)
        rt, kk, bb, kdt, lw, cum, ec = t8(), t8(), t8(), t8(), t8(), t8(), t8()
        INB = Buf()
        ECB = Buf()
        vst = k.sb([128, 4, 512], F32)
        VB_ = Buf()
        yt = k.sb([128, 8, NW], F32)
        YB = Buf()
        XR = [k.sb([128, 8, 192], F32) for _ in range(2)]
        BE = [k.sb([128, 8, 128], F32) for _ in range(2)]
        KT = [k.sb([128, 8, 128], F32) for _ in range(2)]
        CHB = [Buf(), Buf()]
        AM2 = [k.sb([128, 8, 512], F32) for _ in range(2)]
        AMB2 = [[Buf() for _ in range(8)] for _ in range(2)]
        X2 = [k.sb([128, 8, 128], F32) for _ in range(2)]
        XB2 = [[Buf(), Buf()], [Buf(), Buf()]]
        Pn = k.sb([128, 8, 128], F32)
        PTn = k.sb([128, 8, 128], F32)
        PNB = [Buf(), Buf()]
        PTB = [Buf(), Buf()]
        BEt2 = [k.sb([128, 8, 128], F32) for _ in range(2)]
        KTt2 = [k.sb([128, 8, 128], F32) for _ in range(2)]
        BTB2, KTTB2 = [Buf(), Buf()], [Buf(), Buf()]
        Vbd2 = [k.sb([128, 8, 128], F32) for _ in range(2)]
        VBB2 = [Buf(), Buf()]
        Wsb = k.sb([128, 8, 64], F32)
        Ust = k.sb([128, 8, 64], F32)
        Ubd = k.sb([128, 8, 128], F32)
        Sst = k.sb([128, 8, 64], F32)
        Sbd = k.sb([128, 8, 128], F32)
        WSB, USB, UBB, SSB, SBB = [Buf() for _ in range(5)]
        for t_, B_ in ((XR[0], CHB[0]), (XR[1], CHB[1]), (BE[0], CHB[0]), (BE[1], CHB[1]), (KT[0], CHB[0]),
                       (KT[1], CHB[1]), (Ubd, UBB), (Vbd2[0], VBB2[0]), (Vbd2[1], VBB2[1]), (Sbd, SBB), (Sst, SSB)):
            k.op("pool", lambda e: e.memset(t_, 0.0), writes=[B_])
        r3 = lambda t_: t_.rearrange("p (q c) -> p q c", c=64)
        r4 = lambda t_: t_.rearrange("p (q c) -> p q c", c=128)

        def pre_gen(ch, jb):
            c0 = ch * 64
            cs_ = slice(c0, c0 + 64)
            xr_, be_, kt_, CB = XR[jb], BE[jb], KT[jb], CHB[jb]
            AM, AMB, X, XB = AM2[jb], AMB2[jb], X2[jb], XB2[jb]
            BEt, KTt, BTB, KTTB, Vbd, VBB = BEt2[jb], KTt2[jb], BTB2[jb], KTTB2[jb], Vbd2[jb], VBB2[jb]
            for hp in range(2):
                ps_ = slice(hp * 64, hp * 64 + 64)
                k.op("dve", lambda e: e.scalar_tensor_tensor(out=xr_[ps_, :, hp * 64:hp * 64 + 64], in0=kk[ps_, :, cs_],
                                                             scalar=-1.0, in1=lw[ps_, :, cs_], op0=ALU.mult,
                                                             op1=ALU.mult), reads=[INB], writes=[CB])
                k.op("pool", lambda e: e.tensor_tensor(out=be_[ps_, :, hp * 64:hp * 64 + 64], in0=bb[ps_, :, cs_],
                                                       in1=cum[ps_, :, cs_], op=ALU.mult), reads=[INB, ECB], writes=[CB])
                k.op("pool", lambda e: e.tensor_tensor(out=kt_[ps_, :, hp * 64:hp * 64 + 64], in0=kdt[ps_, :, cs_],
                                                       in1=cum[ps_, :, cs_], op=ALU.mult), reads=[INB, ECB], writes=[CB])
            k.op("dve", lambda e: e.tensor_tensor(out=xr_[:, :, 128:192], in0=rt[:, :, cs_], in1=ec[:, :, cs_],
                                                  op=ALU.mult), reads=[INB, ECB], writes=[CB])
            vs_ = vst[:, ch, :].rearrange("p (q v) -> p q v", v=64)
            for hp in range(2):
                ps_ = slice(hp * 64, hp * 64 + 64)
                k.op("act", lambda e: e.activation(out=Vbd[ps_, :, hp * 64:hp * 64 + 64], in_=vs_[ps_, :, :],
                                                   func=AF.Copy), reads=[VB_], writes=[VBB])
            yield
            for p_ in range(8):
                pa, PA = self.pb()
                k.op("pe", lambda e: e.matmul(pa[:, 0:192], lhsT=be_[:, p_, :], rhs=xr_[:, p_, :], start=True,
                                              stop=True), reads=[CB], writes=[PA], sig=False)
                k.op("pe", lambda e: e.matmul(pa[:, 192:384], lhsT=kt_[:, p_, :], rhs=xr_[:, p_, :], start=True,
                                              stop=True), reads=[CB], writes=[PA], sig=False)
                k.op("pe", lambda e: e.matmul(pa[:, 384:512], lhsT=xr_[:, p_, 0:128], rhs=be_[:, p_, :], start=True,
                                              stop=True), reads=[CB], writes=[PA])
                k.op("dve", lambda e: e.tensor_tensor(out=AM[:, p_, :], in0=pa, in1=msk, op=ALU.mult),
                     reads=[PA, MB], writes=[AMB[p_]])
                if p_ == 3:
                    yield
            yield
            for (src_, dst_, DB_) in ((be_, BEt, BTB), (kt_, KTt, KTTB)):
                for q_ in range(2):
                    pt_, PT_ = self.pb()
                    for pi in range(4):
                        p_ = q_ * 4 + pi
                        k.op("pe", lambda e: e.transpose(pt_[:, pi * 128:(pi + 1) * 128], src_[:, p_, :], self.ident),
                             reads=[CB, self.CS], writes=[PT_], sig=(pi == 3))
                    k.op("act", lambda e: e.activation(out=dst_[:, q_ * 4:q_ * 4 + 4, :], in_=r4(pt_), func=AF.Copy),
                         reads=[PT_], writes=[DB_])
            for q_ in range(2):
                k.op("dve", lambda e: e.tensor_tensor(out=X[:, q_ * 4:q_ * 4 + 4, :], in0=AM[:, q_ * 4:q_ * 4 + 4, 0:128],
                                                      in1=self.ident.unsqueeze(1).broadcast_to([128, 4, 128]), op=ALU.add),
                     reads=AMB[q_ * 4:q_ * 4 + 4] + [self.CS], writes=[XB[q_]])
            yield
            for kk_ in range(1, 6):
                Pq = (lambda p_: AM[:, p_, 0:128]) if kk_ == 1 else (lambda p_: Pn[:, p_, :])
                PTq = (lambda p_: AM[:, p_, 384:512]) if kk_ == 1 else (lambda p_: PTn[:, p_, :])
                bk = {}
                for q_ in range(2):
                    RD = (AMB[q_ * 4:q_ * 4 + 4]) if kk_ == 1 else [PNB[q_], PTB[q_]]
                    pb_, PB_ = self.pb()
                    for pi in range(4):
                        p_ = q_ * 4 + pi
                        k.op("pe", lambda e: e.matmul(pb_[:, pi * 128:(pi + 1) * 128], lhsT=Pq(p_), rhs=PTq(p_),
                                                      start=True, stop=True), reads=RD, writes=[PB_], sig=(pi == 3))
                    bk[("b", q_)] = (pb_, PB_)
                    if kk_ < 5:
                        pa_, PA_ = self.pb()
                        for pi in range(4):
                            p_ = q_ * 4 + pi
                            k.op("pe", lambda e: e.matmul(pa_[:, pi * 128:(pi + 1) * 128], lhsT=PTq(p_), rhs=Pq(p_),
                                                          start=True, stop=True), reads=RD, writes=[PA_], sig=(pi == 3))
                        bk[("a", q_)] = (pa_, PA_)
                yield
                for q_ in range(2):
                    pb_, PB_ = bk[("b", q_)]
                    k.op("act", lambda e: e.activation(out=PTn[:, q_ * 4:q_ * 4 + 4, :], in_=r4(pb_), func=AF.Copy),
                         reads=[PB_], writes=[PTB[q_]])
                    if kk_ < 5:
                        pa_, PA_ = bk[("a", q_)]
                        k.op("dve", lambda e: e.tensor_copy(out=Pn[:, q_ * 4:q_ * 4 + 4, :], in_=r4(pa_)),
                             reads=[PA_], writes=[PNB[q_]])
                for q_ in range(2):
                    pc_, PC_ = self.pb()
                    for pi in range(4):
                        p_ = q_ * 4 + pi
                        k.op("pe", lambda e: e.matmul(pc_[:, pi * 128:(pi + 1) * 128], lhsT=PTn[:, p_, :], rhs=X[:, p_, :],
                                                      start=True, stop=True), reads=[PTB[q_], XB[q_]], writes=[PC_],
                             sig=(pi == 3))
                    bk[("c", q_)] = (pc_, PC_)
                yield
                for q_ in range(2):
                    pc_, PC_ = bk[("c", q_)]
                    k.op("dve", lambda e: e.tensor_tensor(out=X[:, q_ * 4:q_ * 4 + 4, :], in0=X[:, q_ * 4:q_ * 4 + 4, :],
                                                          in1=r4(pc_), op=ALU.add), reads=[PC_, XB[q_]], writes=[XB[q_]])

        def state_gen(ch, jb):
            c0 = ch * 64
            cs_ = slice(c0, c0 + 64)
            xr_, CB = XR[jb], CHB[jb]
            AM, AMB, X, XB = AM2[jb], AMB2[jb], X2[jb], XB2[jb]
            BEt, KTt, BTB, KTTB, Vbd, VBB = BEt2[jb], KTt2[jb], BTB2[jb], KTTB2[jb], Vbd2[jb], VBB2[jb]
            vs_ = vst[:, ch, :].rearrange("p (q v) -> p q v", v=64)
            pw, PW = self.pb()
            pw2, PW2 = self.pb()
            for p_ in range(8):
                k.op("pe", lambda e: e.matmul(pw[:, p_ * 64:(p_ + 1) * 64], lhsT=xr_[:, p_, 0:128], rhs=Sst[:, p_, :],
                                              start=True, stop=True), reads=[CB, SSB], writes=[PW], sig=(p_ == 7))
            for p_ in range(8):
                k.op("pe", lambda e: e.matmul(pw2[:, p_ * 64:(p_ + 1) * 64], lhsT=AM[:, p_, 192:320], rhs=vs_[:, p_, :],
                                              start=True, stop=True), reads=[AMB[p_], VB_], writes=[PW2], sig=(p_ == 7))
            py1, PY1 = self.pb()
            py3, PY3 = self.pb()
            for p_ in range(8):
                k.op("pe", lambda e: e.matmul(py1[:, p_ * 64:(p_ + 1) * 64], lhsT=Sbd[:, p_, :], rhs=xr_[:, p_, 128:192],
                                              start=True, stop=True), reads=[SBB, CB], writes=[PY1], sig=(p_ == 7))
            for p_ in range(8):
                k.op("pe", lambda e: e.matmul(py3[:, p_ * 64:(p_ + 1) * 64], lhsT=Vbd[:, p_, :], rhs=AM[:, p_, 320:384],
                                              start=True, stop=True), reads=[VBB, AMB[p_]], writes=[PY3], sig=(p_ == 7))
            k.op("act", lambda e: e.activation(out=Wsb, in_=r3(pw), func=AF.Copy), reads=[PW], writes=[WSB])
            k.op("dve", lambda e: e.tensor_tensor(out=Wsb, in0=Wsb, in1=r3(pw2), op=ALU.add), reads=[WSB, PW2],
                 writes=[WSB])
            k.op("act", lambda e: e.activation(out=yt[:, :, cs_], in_=r3(py1), func=AF.Copy), reads=[PY1], writes=[YB])
            k.op("dve", lambda e: e.tensor_tensor(out=yt[:, :, cs_], in0=yt[:, :, cs_], in1=r3(py3), op=ALU.add),
                 reads=[YB, PY3], writes=[YB])
            yield
            pu, PU = self.pb()
            for p_ in range(8):
                k.op("pe", lambda e: e.matmul(pu[:, p_ * 64:(p_ + 1) * 64], lhsT=X[:, p_, :], rhs=Wsb[:, p_, :], start=True,
                                              stop=True), reads=[XB[p_ // 4], WSB], writes=[PU], sig=(p_ == 7))
            pu3 = r3(pu)
            k.op("act", lambda e: e.activation(out=Ust, in_=pu3, func=AF.Copy), reads=[PU], writes=[USB])
            for hp in range(2):
                ps_ = slice(hp * 64, hp * 64 + 64)
                k.op("act", lambda e: e.activation(out=Ubd[ps_, :, hp * 64:hp * 64 + 64], in_=pu3[ps_, :, :],
                                                   func=AF.Copy), reads=[PU], writes=[UBB])
            yield
            py2, PY2 = self.pb()
            pS1, PS1 = self.pb()
            pS2, PS2 = self.pb()
            for p_ in range(8):
                k.op("pe", lambda e: e.matmul(pS2[:, p_ * 64:(p_ + 1) * 64], lhsT=KTt[:, p_, :], rhs=vs_[:, p_, :],
                                              start=True, stop=True), reads=[KTTB, VB_], writes=[PS2], sig=(p_ == 7))
            for p_ in range(8):
                k.op("pe", lambda e: e.matmul(pS1[:, p_ * 64:(p_ + 1) * 64], lhsT=BEt[:, p_, :], rhs=Ust[:, p_, :],
                                              start=True, stop=True), reads=[BTB, USB], writes=[PS1], sig=(p_ == 7))
            for p_ in range(8):
                k.op("pe", lambda e: e.matmul(py2[:, p_ * 64:(p_ + 1) * 64], lhsT=Ubd[:, p_, :], rhs=AM[:, p_, 128:192],
                                              start=True, stop=True), reads=[UBB, AMB[p_]], writes=[PY2], sig=(p_ == 7))
            k.op("act", lambda e: e.activation(out=Wsb, in_=r3(pS1), func=AF.Copy), reads=[PS1], writes=[WSB])
            k.op("dve", lambda e: e.tensor_tensor(out=Wsb, in0=Wsb, in1=r3(pS2), op=ALU.add), reads=[WSB, PS2], writes=[WSB])
            k.op("dve", lambda e: e.tensor_tensor(out=Wsb, in0=Wsb, in1=Sst, op=ALU.add), reads=[WSB, SSB], writes=[WSB])
            gcol = (c0 + 63) if not rev else c0
            k.op("dve", lambda e: e.tensor_tensor(out=Sst, in0=Wsb, in1=ec[:, :, gcol:gcol + 1].broadcast_to([128, 8, 64]),
                                                  op=ALU.mult), reads=[WSB, ECB], writes=[SSB])
            for hp in range(2):
                ps_ = slice(hp * 64, hp * 64 + 64)
                k.op("act", lambda e: e.activation(out=Sbd[ps_, :, hp * 64:hp * 64 + 64], in_=Sst[ps_, :, :],
                                                   func=AF.Copy), reads=[SSB], writes=[SBB])
            k.op("dve", lambda e: e.tensor_tensor(out=yt[:, :, cs_], in0=yt[:, :, cs_], in1=r3(py2), op=ALU.add),
                 reads=[YB, PY2], writes=[YB])

        def drain(g):
            for _ in g:
                pass

        def interleave(pre, st):
            pre_done = pre is None
            st_done = False
            while not (pre_done and st_done):
                for _ in range(self.wkv_ratio):
                    if not pre_done:
                        try:
                            next(pre)
                        except StopIteration:
                            pre_done = True
                if not st_done:
                    try:
                        next(st)
                    except StopIteration:
                        st_done = True

        wt = [(0, 0)] + [(1 + w, CTX + NW * w) for w in range(16)]
        order = wt if not rev else [wt[0]] + wt[:0:-1]
        nchunk = 0
        for (wi, t0) in order[:self.wkv_ntiles]:
            ti = wi
            for (dst, src, key) in ((rt, self.rT, ("rT", ti)), (kk, self.kkT, ("kkT", ti)), (bb, self.bT[d], ("bT", d, ti)),
                                    (kdt, self.kdT[d], ("kdT", d, ti)), (lw, self.lwT[d], ("lwT", d, ti))):
                k.dma("sp", dst, self.fm(src, t0, NW), reads=[self.DB(*key)], writes=[INB])
            k.dma("sp", vst, self.Vst[t0 // 64:t0 // 64 + 4].rearrange("c p f -> p c f"), reads=[self.DB("Vst", ti)],
                  writes=[VB_])
            fl = lambda a: a.rearrange("p f t -> p (f t)")
            rv = (lambda a: a[:, ::-1]) if rev else (lambda a: a)
            k.op("dve", lambda e: e.tensor_tensor_scan(out=rv(fl(cum)), data0=rv(rmk), data1=rv(fl(lw)), initial=0.0,
                                                       op0=ALU.mult, op1=ALU.add), reads=[INB, MB, ECB], writes=[ECB])
            k.op("dve", lambda e: e.tensor_tensor(out=lw, in0=cum, in1=lw, op=ALU.subtract), reads=[ECB, INB], writes=[INB])
            k.op("act", lambda e: e.activation(out=lw, in_=lw, func=AF.Exp), reads=[INB], writes=[INB])
            k.op("act", lambda e: e.activation(out=ec, in_=cum, func=AF.Exp), reads=[ECB], writes=[ECB])
            k.op("act", lambda e: e.activation(out=cum, in_=cum, func=AF.Exp, scale=-1.0), reads=[ECB], writes=[ECB])
            chs = list(range(4)) if not rev else list(range(3, -1, -1))
            jbs = [(nchunk + i_) % 2 for i_ in range(4)]
            nchunk += 4
            drain(pre_gen(chs[0], jbs[0]))
            for i_ in range(4):
                nxt = pre_gen(chs[i_ + 1], jbs[i_ + 1]) if i_ < 3 else None
                interleave(nxt, state_gen(chs[i_], jbs[i_]))
            k.dma("sp", self.fm(self.yT[d], t0, NW), yt, reads=[YB], writes=[self.DB("yT", d, wi)])
        k.phase_reset()

    def rw_out(self, l, b, last):
        k = self.k
        i = l // 2
        wo = k.sb([128, 8, D], BF16)
        WOB = Buf()
        self.load_w(wo, WOB, self.rw_wo[i], 8)
        bones_f = self.cs[:, 128:256]
        y0 = k.sb([128, 8, 512], F32)
        y1 = k.sb([128, 8, 512], F32)
        bon = k.sb([128, 8, 512], F32)
        gt = k.sb([128, 8, 512], F32)
        xt = k.sb([128, 8, 512], F32)
        yc = k.sb([128, 8, 512], F32)
        rs = k.sb([128, 8, 512], F32)
        ob = k.sb([128, 8, 512], BF16)
        Y0B, Y1B, BNB, GTB, XB, OBB, YCB, RSB = [Buf() for _ in range(8)]

        def bc(name, n):
            return self.V(name, 0, 8).unsqueeze(2).broadcast_to([128, 8, n])
        for ti, (t0, n) in enumerate(TILES):
            if last and ti == 0:
                continue
            mi = 2 if ti == 0 else b
            wis = [0] if ti == 0 else [2 * ti - 1, 2 * ti]
            k.dma("sp", y0[:, :, 0:n], self.fm(self.yT[0], t0, n), reads=[self.DB("yT", 0, w) for w in wis], writes=[Y0B])
            k.dma("sp", y1[:, :, 0:n], self.fm(self.yT[1], t0, n), reads=[self.DB("yT", 1, w) for w in wis], writes=[Y1B])
            k.dma("sp", bon[:, :, 0:n], self.fm(self.bonT, t0, n), reads=[self.DB("bonT", w) for w in wis], writes=[BNB])
            k.dma("sp", gt[:, :, 0:n], self.fm(self.gT, t0, n), reads=[self.DB("gT", w) for w in wis], writes=[GTB])
            k.dma("sp", xt[:, :, 0:n], self.fm(self.xT[b], t0, n), reads=[self.DB("xT", b, ti)], writes=[XB])
            k.op("pool", lambda e: e.tensor_tensor(out=y0[:, :, 0:n], in0=y0[:, :, 0:n], in1=y1[:, :, 0:n], op=ALU.add),
                 reads=[Y0B, Y1B], writes=[Y0B])
            for fc in range(8):
                p, P = self.pb()
                k.op("pe", lambda e: e.matmul(p[:, 0:n], lhsT=bones_f, rhs=y0[:, fc, 0:n], start=True, stop=True),
                     reads=[Y0B, self.CS], writes=[P])
                k.op("dve", lambda e: e.scalar_tensor_tensor(out=yc[:, fc, 0:n], in0=p[:, 0:n], scalar=-1.0 / 64,
                                                             in1=y0[:, fc, 0:n], op0=ALU.mult, op1=ALU.add),
                     reads=[P, Y0B], writes=[YCB])
            k.op("act", lambda e: e.activation(out=y1[:, :, 0:n], in_=yc[:, :, 0:n], func=AF.Square), reads=[YCB, Y1B],
                 writes=[Y1B])
            for fc in range(8):
                p2, P2 = self.pb()
                k.op("pe", lambda e: e.matmul(p2[:, 0:n], lhsT=bones_f, rhs=y1[:, fc, 0:n], start=True, stop=True),
                     reads=[Y1B, self.CS], writes=[P2])
                k.op("act", lambda e: e.activation(out=rs[:, fc, 0:n], in_=p2[:, 0:n], func=AF.Sqrt, bias=GN_EPS,
                                                   scale=1.0 / 64), reads=[P2], writes=[RSB])
            k.op("dve", lambda e: e.reciprocal(out=rs[:, :, 0:n], in_=rs[:, :, 0:n]), reads=[RSB], writes=[RSB])
            k.op("dve", lambda e: e.tensor_tensor(out=yc[:, :, 0:n], in0=yc[:, :, 0:n], in1=rs[:, :, 0:n], op=ALU.mult),
                 reads=[YCB, RSB], writes=[YCB])
            k.op("pool", lambda e: e.tensor_tensor(out=yc[:, :, 0:n], in0=yc[:, :, 0:n], in1=bc("gng_%d" % i, n), op=ALU.mult),
                 reads=[YCB, self.VB], writes=[YCB])
            k.op("pool", lambda e: e.tensor_tensor(out=bon[:, :, 0:n], in0=bon[:, :, 0:n], in1=bc("gnb_%d" % i, n), op=ALU.add),
                 reads=[BNB, self.VB], writes=[BNB])
            k.op("dve", lambda e: e.tensor_tensor(out=yc[:, :, 0:n], in0=yc[:, :, 0:n], in1=bon[:, :, 0:n], op=ALU.add),
                 reads=[YCB, BNB], writes=[YCB])
            k.op("dve", lambda e: e.tensor_tensor(out=ob[:, :, 0:n], in0=yc[:, :, 0:n], in1=gt[:, :, 0:n], op=ALU.mult),
                 reads=[YCB, GTB], writes=[OBB])
            for oc in range(8):
                p, P = self.pb()
                for kc in range(8):
                    k.op("pe", lambda e: e.matmul(p[:, 0:n], lhsT=wo[:, kc, oc * 128:(oc + 1) * 128], rhs=ob[:, kc, 0:n],
                                                  start=(kc == 0), stop=(kc == 7)), reads=[WOB, OBB], writes=[P],
                         sig=(kc == 7))
                k.op("dve", lambda e: e.scalar_tensor_tensor(out=xt[:, oc, 0:n], in0=p[:, 0:n],
                                                             scalar=self.mod[:, 16 + oc, mi:mi + 1], in1=xt[:, oc, 0:n],
                                                             op0=ALU.mult, op1=ALU.add), reads=[P, self.MOD, XB], writes=[XB])
            k.dma("sp", self.fm(self.xT[b], t0, n), xt[:, :, 0:n], reads=[XB], writes=[self.DB("xT", b, ti)])
        k.phase_reset()

    def build(self, nphases=None):
        ph = [lambda: self.setup()]
        for l in self.layers:
            last = (l == DEPTH - 1)
            ph.append(lambda l=l: self.phase_mod(l))
            for b in range(NB):
                if l % 2 == 0:
                    ph.append(lambda l=l, b=b: self.phase_hy_inproj(l, b))
                    ph.append(lambda l=l, b=b: self.phase_rglru(l, b))
                    ph.append(lambda l=l, b=b: self.phase_attn(l, b))
                else:
                    ph.append(lambda l=l, b=b: self.phase_rwkv(l, b))
            ph.append(lambda l=l, last=last: self.phase_mlp(l, last))
        for f in (ph if nphases is None else ph[:nphases]):
            f()
        self.k.finish()
        return self.nc


def host_consts():
    cs = np.zeros((128, 512), np.float32)
    cs[:, 0:128] = np.eye(128, dtype=np.float32)
    bo = np.zeros((128, 128), np.float32)
    bo[0:64, 0:64] = 1.0
    bo[64:128, 64:128] = 1.0
    cs[:, 128:256] = bo
    pw = np.zeros((128, 128), np.float32)
    for blk in range(2):
        for n in range(64):
            pw[blk * 64 + n, blk * 64 + (n + 32) % 64] = 1.0
    cs[:, 256:384] = pw
    rows = SEQ // 64
    row = np.repeat(np.arange(rows, dtype=np.float32), 64)
    col = np.tile(np.arange(64, dtype=np.float32), rows)
    inv = (np.float32(10000.0) ** (-np.arange(0, 32, 2, dtype=np.float32) / np.float32(32))).astype(np.float32)
    ang = np.concatenate([row[:, None] * inv, col[:, None] * inv], axis=-1).astype(np.float32)
    c = np.cos(ang).astype(np.float32).T
    s_ = np.sin(ang).astype(np.float32).T
    cos64 = np.concatenate([c, c], 0)
    sin64 = np.concatenate([-s_, s_], 0)
    cosT = np.ascontiguousarray(np.concatenate([cos64, cos64], 0))
    sinT = np.ascontiguousarray(np.concatenate([sin64, sin64], 0))
    return cs, cosT, sinT


def fmaj(v):
    return np.ascontiguousarray(np.asarray(v, np.float32).reshape(-1, 128).T)


def host_vb(inp):
    vb = np.zeros((128, NVB), np.float32)

    def put(name, arr):
        arr = np.asarray(arr, np.float32)
        vb[:, VBM[name]:VBM[name] + arr.shape[1]] = arr
    for l in range(DEPTH):
        put("ng0_%d" % l, fmaj(inp["norm_g"][l, 0]))
        put("ng1_%d" % l, fmaj(inp["norm_g"][l, 1]))
        put("adab_%d" % l, fmaj(inp["ada_b"][l]))
    for i in range(2):
        gq = inp["hy_q_norm"][i][PERM]
        gk = inp["hy_k_norm"][i][PERM]
        put("gq_%d" % i, np.concatenate([gq, gq])[:, None])
        put("gk_%d" % i, np.concatenate([gk, gk])[:, None])
        put("convw_%d" % i, np.concatenate([fmaj(inp["hy_conv_w"][i][j]) for j in range(4)], 1))
        put("convb_%d" % i, fmaj(inp["hy_conv_b"][i]))
        put("gateb_%d" % i, np.concatenate([fmaj(inp["hy_gate_b"][i][d][g]) for d in range(2) for g in range(2)], 1))
        put("lam_%d" % i, np.concatenate([fmaj(inp["hy_lam"][i][d]) for d in range(2)], 1))
    for i in range(2):
        put("mu_%d" % i, np.concatenate([fmaj(inp["rw_mu"][i][j]) for j in range(6)], 1))
        put("kk_%d" % i, fmaj(inp["rw_k_k"][i]))
        put("ka_%d" % i, fmaj(inp["rw_k_a"][i]))
        put("rk_%d" % i, fmaj(inp["rw_r_k"][i].reshape(-1)))
        put("gng_%d" % i, fmaj(inp["rw_gn_g"][i]))
        put("gnb_%d" % i, fmaj(inp["rw_gn_b"][i]))
        put("lb_%d" % i, np.concatenate([fmaj(inp["rw_lora_bias"][i][d][j]) for d in range(2) for j in range(2)], 1))
    return vb


def host_shared(inp):
    sh = {}
    cs, cosT, sinT = host_consts()
    sh["consts"], sh["cosT"], sh["sinT"] = cs, cosT, sinT
    sh["vb"] = host_vb(inp)
    sh["ada_w"] = np.ascontiguousarray(inp["ada_w"], np.float32)
    sh["mlp_w1"] = np.ascontiguousarray(inp["mlp_w1"], np.float32)
    sh["mlp_w2"] = np.ascontiguousarray(inp["mlp_w2"], np.float32)
    win = inp["hy_w_in"]
    cols = []
    for h in range(8):
        cols.append(h * 64 + PERM)
    for kv in range(2):
        cols.append(512 + kv * 64 + PERM)
        cols.append(512 + kv * 64 + PERM)
    cols.append(np.arange(640, 768))
    cols.append(np.arange(768, 1792))
    cols = np.concatenate(cols)
    sh["hy_win"] = np.ascontiguousarray(win[:, :, cols], np.float32)
    sh["hy_wout"] = np.ascontiguousarray(inp["hy_w_out"], np.float32)
    gw = inp["hy_gate_w"]
    bd = np.zeros((2, 2, 2, 4, 128, 128), np.float32)
    for c in range(4):
        bd[:, :, :, c, 0:64, 0:64] = gw[:, :, :, 2 * c]
        bd[:, :, :, c, 64:128, 64:128] = gw[:, :, :, 2 * c + 1]
    sh["hy_gw"] = bd
    sh["rw_wrkv"] = np.ascontiguousarray(inp["rw_w_rkv"], np.float32)
    st = np.concatenate([np.arange((2 * p + hp) * 64, (2 * p + hp) * 64 + 64) for hp in range(2) for p in range(8)])
    sh["rw_wvst"] = np.ascontiguousarray(inp["rw_w_rkv"][:, 2][:, :, st], np.float32)
    sh["rw_wo"] = np.ascontiguousarray(inp["rw_w_o"], np.float32)
    ldn = inp["rw_lora_down"]
    sh["rw_ld"] = np.ascontiguousarray(np.concatenate([ldn[:, d, j] for d in range(2) for j in range(2)], axis=-1), np.float32)
    lup = inp["rw_lora_up"]
    sh["rw_lu"] = np.ascontiguousarray(np.stack([lup[:, d, j] for d in range(2) for j in range(2)], axis=1), np.float32)
    sh["rw_gd"] = np.ascontiguousarray(inp["rw_gate_down"], np.float32)
    sh["rw_gu"] = np.ascontiguousarray(inp["rw_gate_up"], np.float32)
    ii = np.arange(64)
    up = (ii[None, :] > ii[:, None]).astype(np.float32)
    le = (ii[:, None] <= ii[None, :]).astype(np.float32)
    lo_ = (ii[None, :] < ii[:, None]).astype(np.float32)

    def bdm(m):
        o = np.zeros((128, 128), np.float32)
        o[0:64, 0:64] = m
        o[64:128, 64:128] = m
        return o

    def stk(m):
        return np.concatenate([m, m], 0)
    wm = np.zeros((128, 2, 512), np.float32)
    for d, (u_, l_, s_) in enumerate(((up, lo_, le), (up.T, lo_.T, le.T))):
        wm[:, d, 0:128] = bdm(u_)
        wm[:, d, 128:192] = stk(s_)
        wm[:, d, 192:320] = bdm(u_)
        wm[:, d, 320:384] = stk(s_)
        wm[:, d, 384:512] = bdm(l_)
    sh["wmask"] = wm
    rm = np.ones((128, 2, 2048), np.float32)
    tt = np.arange(2048)
    rm[:, 0, tt % 64 == 0] = 0.0
    rm[:, 1, tt % 64 == 63] = 0.0
    sh["rmask"] = rm
    return sh


_CACHE = {}


def kernel(**inp):
    inp = {k_: np.asarray(v) for k_, v in inp.items()}
    sh = host_shared(inp)
    if "nc" not in _CACHE:
        _CACHE["nc"] = Prog().build()
    nc = _CACHE["nc"]
    in_maps = []
    for core in range(8):
        bs = [2 * core, 2 * core + 1]
        m = dict(sh)
        m["xT_in"] = np.ascontiguousarray(np.stack([inp["x"][b].T for b in bs]), np.float32)
        m["ctxT_in"] = np.ascontiguousarray(np.stack([inp["ctx"][b].T for b in bs]), np.float32)
        cv = np.stack([inp["c"][bs[0]], inp["c"][bs[1]], inp["c_ctx"]], 0)
        m["cT"] = np.ascontiguousarray(cv.reshape(3, 8, 128).transpose(2, 1, 0), np.float32)
        in_maps.append(m)
    res = run_bass_kernel_spmd(nc, in_maps, core_ids=list(range(8)))
    out = np.empty((16, SEQ, D), np.float32)
    for core in range(8):
        o = res.results[core]["outT"]
        for j in range(NB):
            out[2 * core + j] = o[j].T
    return out
```

```python
import math
import numpy as np
import concourse.bass as bass
import concourse.mybir as mybir
from concourse.bass_utils import run_bass_kernel_spmd

F32 = mybir.dt.float32
BF16 = mybir.dt.bfloat16
AF = mybir.ActivationFunctionType
ALU = mybir.AluOpType

D = 1024
SEQ = 4096
CTX = 256
T = SEQ + CTX
NB = 2
DEPTH = 4
EPS = 1e-6
DFF = 4096
TILES = [(0, CTX)] + [(CTX + 512 * i, 512) for i in range(8)]
WTILES = [(0, CTX)] + [(CTX + 256 * i, 256) for i in range(16)]
GELU_C = 2.0 * math.sqrt(2.0 / math.pi)
DECAY_SCALE = math.exp(-0.5)
GN_EPS = 64e-5


class Buf:
    __slots__ = ("w", "r")

    def __init__(self):
        self.w = None
        self.r = {}


class KB:
    NDMA = 40

    def __init__(self, nc):
        self.nc = nc
        self.eng = dict(pe=nc.tensor, act=nc.scalar, dve=nc.vector, pool=nc.gpsimd, sp=nc.sync)
        self.sems = {}
        self.cnt = {}
        for e in ("pe", "act", "dve", "pool"):
            self.sems[e] = nc.alloc_semaphore("s_" + e)
            self.cnt[e] = 0
        self.dsem = [nc.alloc_semaphore("d%d" % i) for i in range(self.NDMA)]
        self.dval = [0] * self.NDMA
        self.drr = 0
        self.waited = {e: {} for e in self.eng}
        self.sb_off = 16512
        self.sb_base = 16512
        self.nalloc = 0
        self.ninst = 0
        self.pending = []
        self.rec = None

    def sb(self, shape, dt=F32, name=None):
        self.nalloc += 1
        nm = "%s_%d" % (name or "t", self.nalloc)
        n = 1
        for s_ in shape[1:]:
            n *= s_
        nbytes = n * (4 if dt == F32 else 2)
        nbytes = (nbytes + 63) // 64 * 64
        off = self.sb_off
        self.sb_off += nbytes
        assert self.sb_off <= 229376, ("SBUF overflow", nm, self.sb_off)
        return self.nc.alloc_sbuf_tensor_at(nm, list(shape), dt, offset=off).ap()

    def phase_reset(self):
        self.barrier()
        self.sb_off = self.sb_base

    def persist_mark(self):
        self.sb_base = self.sb_off

    def _semh(self, key):
        return self.sems[key] if isinstance(key, str) else self.dsem[key[1]]

    def _wait(self, e, key, val, raw=False):
        if key == e and (not raw or e == "pe"):
            return
        w = self.waited[e]
        if w.get(key, 0) >= val:
            return
        w[key] = val
        self.pending.append((key, val))

    def _take(self):
        p = self.pending
        self.pending = []
        d = {}
        for k_, v_ in p:
            if d.get(k_, 0) < v_:
                d[k_] = v_
        return list(d.items())

    def _deps(self, e, reads, writes):
        for b in reads:
            if b.w is not None:
                self._wait(e, b.w[0], b.w[1], raw=True)
        for b in writes:
            if b.w is not None:
                self._wait(e, b.w[0], b.w[1])
            for k_, v_ in b.r.items():
                self._wait(e, k_, v_)

    def _mark(self, tok, reads, writes):
        k_, v_ = tok
        for b in reads:
            if b.r.get(k_, 0) < v_:
                b.r[k_] = v_
        for b in writes:
            b.w = tok
            b.r = {}

    def op(self, e, ins_fn, reads=(), writes=(), sig=True):
        self._deps(e, reads, writes)
        items = self._take()
        if self.rec is not None:
            self.rec.append((e, list(items), e if sig else None, 1))
        last = items.pop() if items else None
        for k_, v_ in items:
            self.eng[e].wait_ge(self._semh(k_), v_)
            self.ninst += 1
        ins = ins_fn(self.eng[e])
        if last is not None:
            ins._wait_ge(self._semh(last[0]), last[1])
        self.ninst += 1
        if sig:
            self.cnt[e] += 1
            ins.then_inc(self.sems[e], 1)
            tok = (e, self.cnt[e])
        else:
            tok = (e, self.cnt[e] + 1)
        self._mark(tok, reads, writes)
        return tok

    def dma(self, q, out, in_, reads=(), writes=(), **kw):
        i = self.drr
        self.drr = (self.drr + 1) % self.NDMA
        key = ("d", i)
        if self.dval[i] > 0:
            self._wait(q, key, self.dval[i])
        self._deps(q, reads, writes)
        its_ = self._take()
        if self.rec is not None:
            self.rec.append((q, list(its_), key, 16))
        for k_, v_ in its_:
            self.eng[q].wait_ge(self._semh(k_), v_)
            self.ninst += 1
        self.eng[q].dma_start(out=out, in_=in_, **kw).then_inc(self.dsem[i], 16)
        self.ninst += 1
        self.dval[i] += 16
        tok = (key, self.dval[i])
        self._mark(tok, reads, writes)
        return tok

    def barrier(self):
        for e in self.eng:
            for o in ("pe", "act", "dve", "pool"):
                if self.cnt[o] > 0:
                    self._wait(e, o, self.cnt[o])
            for i in range(self.NDMA):
                if self.dval[i] > 0:
                    self._wait(e, ("d", i), self.dval[i])
            its_ = self._take()
            if self.rec is not None:
                self.rec.append((e, list(its_), None, 0))
            for k_, v_ in its_:
                self.eng[e].wait_ge(self._semh(k_), v_)
                self.ninst += 1

    def finish(self):
        self.barrier()


def vb_layout():
    m = {}
    off = 0

    def add(name, n):
        nonlocal off
        m[name] = off
        off += n
    for l in range(DEPTH):
        add("ng0_%d" % l, 8)
        add("ng1_%d" % l, 8)
        add("adab_%d" % l, 48)
    for i in range(2):
        add("gq_%d" % i, 1)
        add("gk_%d" % i, 1)
        add("convw_%d" % i, 16)
        add("convb_%d" % i, 4)
        add("gateb_%d" % i, 16)
        add("lam_%d" % i, 8)
    for i in range(2):
        add("mu_%d" % i, 48)
        add("kk_%d" % i, 8)
        add("ka_%d" % i, 8)
        add("rk_%d" % i, 8)
        add("gng_%d" % i, 8)
        add("gnb_%d" % i, 8)
        add("lb_%d" % i, 32)
    return m, off


VBM, NVB = vb_layout()
PERM = np.concatenate([np.arange(0, 64, 2), np.arange(1, 64, 2)])


class Prog:
    def __init__(self, debug=(), nlayers=DEPTH, ext_in=()):
        self.debug = set(debug)
        self.ext_in = set(ext_in)
        self.nlayers = nlayers
        self.layers = list(range(nlayers))
        self.rw_stop = 99
        self.wkv_stage = 99
        self.wkv_ntiles = 99
        self.wkv_ratio = 1
        nc = self.nc = bass.Bass("TRN2", target_bir_lowering=False)
        k = self.k = KB(nc)
        self.dbuf = {}
        di = self.din
        self.xin = di("xT_in", [NB, D, SEQ])
        self.cin = di("ctxT_in", [NB, D, CTX])
        self.cT = di("cT", [128, 8, 3])
        self.vbd = di("vb", [128, NVB])
        self.ada_w = di("ada_w", [DEPTH, D, 6 * D])
        self.w1 = di("mlp_w1", [DEPTH, D, DFF])
        self.w2 = di("mlp_w2", [DEPTH, DFF, D])
        self.hy_win = di("hy_win", [2, D, 1920])
        self.hy_wout = di("hy_wout", [2, D, D])
        self.hy_gw = di("hy_gw", [2, 2, 2, 4, 128, 128])
        self.cst = di("consts", [128, 512])
        self.cosd = di("cosT", [128, SEQ])
        self.sind = di("sinT", [128, SEQ])
        self.rw_wrkv = di("rw_wrkv", [2, 3, D, D])
        self.rw_wvst = di("rw_wvst", [2, D, D])
        self.rw_wo = di("rw_wo", [2, D, D])
        self.rw_ld = di("rw_ld", [2, D, 256])
        self.rw_lu = di("rw_lu", [2, 4, 64, D])
        self.rw_gd = di("rw_gd", [2, D, 128])
        self.rw_gu = di("rw_gu", [2, 128, D])
        self.wmask = di("wmask", [128, 2, 512])
        self.rmask = di("rmask", [128, 2, 2048])
        self.out = nc.dram_tensor("outT", [NB, D, SEQ], F32, kind="ExternalOutput").ap()
        self.hnT = self.dscr("hnT", [D, T])
        self.rT = self.dscr("rT", [D, T])
        self.kkT = self.dscr("kkT", [D, T])
        self.kdT = self.dscr("kdT", [2, D, T])
        self.bT = self.dscr("bT", [2, D, T])
        self.lwT = self.dscr("lwT", [2, D, T])
        self.gT = self.dscr("gT", [D, T])
        self.bonT = self.dscr("bonT", [D, T])
        self.Vst = self.dscr("Vst", [T // 64, 128, 512])
        self.yT = self.dscr("yT", [2, D, T])
        self.xT = self.dscr("xT", [NB, D, T])
        self.qT = self.dscr("qT", [512, T], BF16)
        self.kT2 = self.dscr("kT2", [256, T], BF16)
        self.vtok = self.dscr("vtok", [T // 128, 128, 384], BF16)
        self.xr = self.dscr("xr", [512, T])
        self.gg = self.dscr("gg", [512, T], BF16)
        self.rec = self.dscr("rec", [512, T], BF16)
        self.ps = [nc.alloc_psum_tensor("ps%d" % i, [128, 512], F32).ap() for i in range(8)]
        self.PS = [Buf() for _ in range(8)]
        self.prr = 0
        self.vb = k.sb([128, NVB], F32, "vb")
        self.VB = Buf()
        self.cs = k.sb([128, 512], F32, "cs")
        self.CS = Buf()
        self.ident = self.cs[:, 0:128]
        self.ones_bf = k.sb([128, 128], BF16, "ones")
        self.bones_bf = k.sb([128, 128], BF16, "bones")
        self.pswap_bf = k.sb([128, 128], BF16, "pswap")
        self.scT = k.sb([128, 8, 3], BF16, "scT")
        self.mod = k.sb([128, 48, 3], F32, "mod")
        self.A1 = k.sb([128, 8, 3], F32, "A1")
        self.A2 = k.sb([128, 8, 3], F32, "A2")
        self.MOD = Buf()
        self.CONST = Buf()
        k.persist_mark()

    def din(self, name, shape, dt=F32):
        return self.nc.dram_tensor(name, list(shape), dt, kind="ExternalInput").ap()

    def dscr(self, name, shape, dt=F32):
        kind = "ExternalOutput" if name in self.debug else "Internal"
        if name in getattr(self, "ext_in", ()):
            kind = "ExternalInput"
        return self.nc.dram_tensor(name, list(shape), dt, kind=kind).ap()

    def DB(self, *key):
        b = self.dbuf.get(key)
        if b is None:
            b = self.dbuf[key] = Buf()
        return b

    def pb(self):
        i = self.prr
        self.prr = (self.prr + 1) % 8
        return self.ps[i], self.PS[i]

    @staticmethod
    def fm(ap2d, t0, n):
        return ap2d.rearrange("(fc p) t -> p fc t", p=128)[:, :, t0:t0 + n]

    def V(self, name, j=0, n=1):
        o = VBM[name] + j
        return self.vb[:, o:o + n]

    def setup(self):
        k = self.k
        k.dma("sp", self.vb, self.vbd, writes=[self.VB])
        k.dma("sp", self.cs, self.cst, writes=[self.CS])
        k.op("dve", lambda e: e.memset(self.ones_bf, 1.0), writes=[self.CONST])
        k.op("act", lambda e: e.activation(out=self.bones_bf, in_=self.cs[:, 128:256], func=AF.Copy),
             reads=[self.CS], writes=[self.CONST])
        k.op("act", lambda e: e.activation(out=self.pswap_bf, in_=self.cs[:, 256:384], func=AF.Copy),
             reads=[self.CS], writes=[self.CONST])
        ct = k.sb([128, 8, 3], F32)
        sg = k.sb([128, 8, 3], F32)
        CTB = Buf()
        k.dma("sp", ct, self.cT, writes=[CTB])
        k.op("act", lambda e: e.activation(out=sg, in_=ct, func=AF.Sigmoid), reads=[CTB], writes=[CTB])
        k.op("dve", lambda e: e.tensor_tensor(out=self.scT, in0=ct, in1=sg, op=ALU.mult), reads=[CTB],
             writes=[self.CONST])
        for b in range(NB):
            k.dma("sp", self.xT[b, :, 0:CTX], self.cin[b], writes=[self.DB("xT", b, 0)])
            for ti in range(1, 9):
                t0, n = TILES[ti]
                k.dma("sp", self.xT[b, :, t0:t0 + n], self.xin[b, :, t0 - CTX:t0 - CTX + n],
                      writes=[self.DB("xT", b, ti)])
        k.phase_reset()

    def phase_mod(self, l):
        k = self.k
        wa = k.sb([128, 8, 3072], BF16)
        WA = Buf()
        pm, PM = self.pb()
        for half in range(2):
            src = self.ada_w[l].rearrange("(kc p) n -> p kc n", p=128)[:, :, half * 3072:(half + 1) * 3072]
            for kc in range(8):
                k.dma("pool", wa[:, kc, :], src[:, kc, :], writes=[WA])
            for j in range(24):
                jj = half * 24 + j
                for kc in range(8):
                    k.op("pe", lambda e: e.matmul(pm[:, jj * 4:jj * 4 + 3], lhsT=wa[:, kc, j * 128:(j + 1) * 128],
                                                  rhs=self.scT[:, kc, :], start=(kc == 0), stop=(kc == 7)),
                         reads=[WA, self.CONST], writes=[PM], sig=(kc == 7))
        pmv = pm[:, 0:192].rearrange("p (j f) -> p j f", f=4)[:, :, 0:3]
        bias = self.V("adab_%d" % l, 0, 48).unsqueeze(2).broadcast_to([128, 48, 3])
        k.op("dve", lambda e: e.tensor_tensor(out=self.mod, in0=pmv, in1=bias, op=ALU.add),
             reads=[PM, self.VB], writes=[self.MOD])
        for (A, gname, sc0) in ((self.A1, "ng0_%d" % l, 8), (self.A2, "ng1_%d" % l, 32)):
            g = self.V(gname, 0, 8).unsqueeze(2).broadcast_to([128, 8, 3])
            k.op("dve", lambda e: e.scalar_tensor_tensor(out=A, in0=self.mod[:, sc0:sc0 + 8, :], scalar=1.0, in1=g,
                                                         op0=ALU.add, op1=ALU.mult),
                 reads=[self.MOD, self.VB], writes=[self.MOD])
        k.phase_reset()

    def norm_tile(self, xt, XTB, out, OUTB, A, Bsh, mi, n, W):
        k = self.k
        sq, rstd, tmp = W["sq"], W["rstd"], W["tmp"]
        k.op("act", lambda e: e.activation(out=sq[:, :, 0:n], in_=xt[:, :, 0:n], func=AF.Square),
             reads=[XTB], writes=[W["SQ"]])
        pn, PN = self.pb()
        for fc in range(8):
            k.op("pe", lambda e: e.matmul(pn[:, 0:n], lhsT=self.ones_bf, rhs=sq[:, fc, 0:n], start=(fc == 0),
                                          stop=(fc == 7)), reads=[W["SQ"], self.CONST], writes=[PN], sig=(fc == 7))
        k.op("act", lambda e: e.activation(out=rstd[:, 0:n], in_=pn[:, 0:n], func=AF.Sqrt, bias=EPS, scale=1.0 / D),
             reads=[PN], writes=[W["RSTD"]])
        k.op("dve", lambda e: e.reciprocal(out=rstd[:, 0:n], in_=rstd[:, 0:n]), reads=[W["RSTD"]], writes=[W["RSTD"]])
        for fc in range(8):
            j = fc % 2
            k.op("dve", lambda e: e.tensor_tensor(out=tmp[:, j, 0:n], in0=xt[:, fc, 0:n], in1=rstd[:, 0:n],
                                                  op=ALU.mult), reads=[XTB, W["RSTD"]], writes=[W["TMP"][j]])
            k.op("act", lambda e: e.activation(out=out[:, fc, 0:n], in_=tmp[:, j, 0:n], func=AF.Identity,
                                               bias=Bsh[:, fc, mi:mi + 1], scale=A[:, fc, mi:mi + 1]),
                 reads=[W["TMP"][j], self.MOD], writes=[OUTB])

    def norm_work(self):
        k = self.k
        return dict(sq=k.sb([128, 8, 512], BF16), rstd=k.sb([128, 512], F32), tmp=k.sb([128, 2, 512], F32),
                    SQ=Buf(), RSTD=Buf(), TMP=[Buf(), Buf()])

    def load_w(self, dst, DSTB, src2d, nkc, split=1):
        v = src2d.rearrange("(kc p) n -> p kc n", p=128)
        for kc in range(nkc):
            self.k.dma("pool", dst[:, kc, :], v[:, kc, :], writes=[DSTB])

    def phase_mlp(self, l, last):
        k = self.k
        w1 = k.sb([128, 8, DFF], BF16)
        w2 = k.sb([128, 32, D], BF16)
        W1B, W2B = Buf(), Buf()
        self.load_w(w1, W1B, self.w1[l], 8)
        self.load_w(w2, W2B, self.w2[l], 32)
        W = self.norm_work()
        xt = k.sb([128, 8, 512], F32)
        XTB = Buf()
        hn = k.sb([128, 8, 512], BF16)
        HNB = Buf()
        h1 = k.sb([128, 32, 512], BF16)
        H1B = [Buf() for _ in range(32)]
        rl = k.sb([128, 2, 512], BF16)
        RLB = [Buf(), Buf()]
        for b, (ti, (t0, n)) in [(b_, t_) for b_ in range(NB) for t_ in enumerate(TILES)]:
            if last and ti == 0:
                continue
            mi = 2 if ti == 0 else b
            k.dma("sp", xt[:, :, 0:n], self.fm(self.xT[b], t0, n), reads=[self.DB("xT", b, ti)], writes=[XTB])
            self.norm_tile(xt, XTB, hn, HNB, self.A2, self.mod[:, 24:32, :], mi, n, W)
            for oc in range(32):
                p, P = self.pb()
                for kc in range(8):
                    k.op("pe", lambda e: e.matmul(p[:, 0:n], lhsT=w1[:, kc, oc * 128:(oc + 1) * 128], rhs=hn[:, kc, 0:n],
                                                  start=(kc == 0), stop=(kc == 7)),
                         reads=[W1B, HNB], writes=[P], sig=(kc == 7))
                j = oc % 2
                k.op("act", lambda e: e.activation(out=rl[:, j, 0:n], in_=p[:, 0:n], func=AF.Relu),
                     reads=[P], writes=[RLB[j]])
                k.op("pool", lambda e: e.tensor_tensor(out=h1[:, oc, 0:n], in0=rl[:, j, 0:n], in1=rl[:, j, 0:n],
                                                       op=ALU.mult), reads=[RLB[j]], writes=[H1B[oc]])
            for oc in range(8):
                p, P = self.pb()
                for kc in range(32):
                    k.op("pe", lambda e: e.matmul(p[:, 0:n], lhsT=w2[:, kc, oc * 128:(oc + 1) * 128], rhs=h1[:, kc, 0:n],
                                                  start=(kc == 0), stop=(kc == 31)),
                         reads=[W2B, H1B[kc]], writes=[P], sig=(kc == 31))
                k.op("dve", lambda e: e.scalar_tensor_tensor(out=xt[:, oc, 0:n], in0=p[:, 0:n],
                                                             scalar=self.mod[:, 40 + oc, mi:mi + 1],
                                                             in1=xt[:, oc, 0:n], op0=ALU.mult, op1=ALU.add),
                     reads=[P, self.MOD, XTB], writes=[XTB])
            if last:
                k.dma("sp", self.out[b].rearrange("(fc p) t -> p fc t", p=128)[:, :, t0 - CTX:t0 - CTX + n],
                      xt[:, :, 0:n], reads=[XTB], writes=[self.DB("out", b, ti)])
            else:
                k.dma("sp", self.fm(self.xT[b], t0, n), xt[:, :, 0:n], reads=[XTB], writes=[self.DB("xT", b, ti)])
        k.phase_reset()

    def phase_hy_inproj(self, l, b):
        k = self.k
        i = l // 2
        win = k.sb([128, 8, 1920], BF16)
        WB = Buf()
        self.load_w(win, WB, self.hy_win[i], 8)
        W = self.norm_work()
        xt = [k.sb([128, 8, 512], F32) for _ in range(2)]
        XTB = [Buf(), Buf()]
        hn = k.sb([128, 8, 512], BF16)
        HNB = Buf()
        cs_t = k.sb([128, 2, 512], F32)
        CSB = Buf()
        sq = k.sb([128, 512], BF16)
        SQB = Buf()
        rs = k.sb([128, 512], F32)
        RSB = Buf()
        xn = k.sb([128, 512], BF16)
        XNB = Buf()
        t1 = k.sb([128, 512], F32)
        t2 = k.sb([128, 512], F32)
        T1B, T2B = Buf(), Buf()
        qk = k.sb([128, 6, 512], BF16)
        QKB = Buf()
        vt = k.sb([128, 4, 384], BF16)
        VTB = Buf()
        xro = k.sb([128, 4, 512], F32)
        XRB = Buf()
        ggo = k.sb([128, 4, 512], BF16)
        GGB = Buf()
        z2 = k.sb([128, 512], F32)
        Z2B = Buf()
        k.op("dve", lambda e: e.memset(vt, 1.0), writes=[VTB])
        for ti, (t0, n) in enumerate(TILES):
            mi = 2 if ti == 0 else b
            x_ = xt[ti % 2]
            XB = XTB[ti % 2]
            k.dma("sp", x_[:, :, 0:n], self.fm(self.xT[b], t0, n), reads=[self.DB("xT", b, ti)], writes=[XB])
            if ti > 0:
                k.dma("sp", cs_t[:, 0, 0:n], self.cosd[:, t0 - CTX:t0 - CTX + n], writes=[CSB])
                k.dma("sp", cs_t[:, 1, 0:n], self.sind[:, t0 - CTX:t0 - CTX + n], writes=[CSB])
            self.norm_tile(x_, XB, hn, HNB, self.A1, self.mod[:, 0:8, :], mi, n, W)
            for oc in range(6):
                p, P = self.pb()
                for kc in range(8):
                    k.op("pe", lambda e: e.matmul(p[:, 0:n], lhsT=win[:, kc, oc * 128:(oc + 1) * 128], rhs=hn[:, kc, 0:n],
                                                  start=(kc == 0), stop=(kc == 7)), reads=[WB, HNB], writes=[P],
                         sig=(kc == 7))
                k.op("act", lambda e: e.activation(out=sq[:, 0:n], in_=p[:, 0:n], func=AF.Square), reads=[P], writes=[SQB])
                p2, P2 = self.pb()
                k.op("pe", lambda e: e.matmul(p2[:, 0:n], lhsT=self.bones_bf, rhs=sq[:, 0:n], start=True, stop=True),
                     reads=[SQB, self.CONST], writes=[P2])
                k.op("act", lambda e: e.activation(out=rs[:, 0:n], in_=p2[:, 0:n], func=AF.Sqrt, bias=EPS,
                                                   scale=1.0 / 64), reads=[P2], writes=[RSB])
                k.op("dve", lambda e: e.reciprocal(out=rs[:, 0:n], in_=rs[:, 0:n]), reads=[RSB], writes=[RSB])
                g = self.V("gq_%d" % i) if oc < 4 else self.V("gk_%d" % i)
                dst = qk[:, oc, 0:n] if ti == 0 else xn[:, 0:n]
                DSTB = QKB if ti == 0 else XNB
                k.op("dve", lambda e: e.scalar_tensor_tensor(out=dst, in0=p[:, 0:n], scalar=g, in1=rs[:, 0:n],
                                                             op0=ALU.mult, op1=ALU.mult),
                     reads=[P, RSB, self.VB], writes=[DSTB])
                if ti > 0:
                    p3, P3 = self.pb()
                    k.op("pe", lambda e: e.matmul(p3[:, 0:n], lhsT=self.pswap_bf, rhs=xn[:, 0:n], start=True, stop=True),
                         reads=[XNB, self.CONST], writes=[P3])
                    k.op("pool", lambda e: e.tensor_tensor(out=t1[:, 0:n], in0=xn[:, 0:n], in1=cs_t[:, 0, 0:n],
                                                           op=ALU.mult), reads=[XNB, CSB], writes=[T1B])
                    k.op("dve", lambda e: e.tensor_tensor(out=t2[:, 0:n], in0=p3[:, 0:n], in1=cs_t[:, 1, 0:n],
                                                          op=ALU.mult), reads=[P3, CSB], writes=[T2B])
                    k.op("dve", lambda e: e.tensor_tensor(out=qk[:, oc, 0:n], in0=t1[:, 0:n], in1=t2[:, 0:n],
                                                          op=ALU.add), reads=[T1B, T2B], writes=[QKB])
            k.dma("sp", self.fm(self.qT, t0, n), qk[:, 0:4, 0:n], reads=[QKB], writes=[self.DB("qT", ti)])
            k.dma("sp", self.fm(self.kT2, t0, n), qk[:, 4:6, 0:n], reads=[QKB], writes=[self.DB("kT2", ti)])
            nst = n // 128
            for st in range(nst):
                p, P = self.pb()
                for kc in range(8):
                    k.op("pe", lambda e: e.matmul(p[:, 0:128], lhsT=hn[:, kc, st * 128:(st + 1) * 128],
                                                  rhs=win[:, kc, 768:896], start=(kc == 0), stop=(kc == 7)),
                         reads=[WB, HNB], writes=[P], sig=(kc == 7))
                vv = vt[:, st, :].rearrange("p (h c) -> p h c", c=192)[:, :, 64:128]
                k.op("act", lambda e: e.activation(out=vv, in_=p[:, 0:128].rearrange("p (h c) -> p h c", c=64),
                                                   func=AF.Copy), reads=[P], writes=[VTB])
            c0 = t0 // 128
            k.dma("sp", self.vtok[c0:c0 + nst].rearrange("c p f -> p c f"), vt[:, 0:nst, :], reads=[VTB],
                  writes=[self.DB("vtok", ti)])
            for oc in range(4):
                p, P = self.pb()
                for kc in range(8):
                    k.op("pe", lambda e: e.matmul(p[:, 0:n], lhsT=win[:, kc, 896 + oc * 128:896 + (oc + 1) * 128],
                                                  rhs=hn[:, kc, 0:n], start=(kc == 0), stop=(kc == 7)),
                         reads=[WB, HNB], writes=[P], sig=(kc == 7))
                k.op("act", lambda e: e.activation(out=xro[:, oc, 0:n], in_=p[:, 0:n], func=AF.Copy), reads=[P],
                     writes=[XRB])
            k.dma("sp", self.fm(self.xr, t0, n), xro[:, :, 0:n], reads=[XRB], writes=[self.DB("xr", ti)])
            for oc in range(4):
                p, P = self.pb()
                for kc in range(8):
                    k.op("pe", lambda e: e.matmul(p[:, 0:n], lhsT=win[:, kc, 1408 + oc * 128:1408 + (oc + 1) * 128],
                                                  rhs=hn[:, kc, 0:n], start=(kc == 0), stop=(kc == 7)),
                         reads=[WB, HNB], writes=[P], sig=(kc == 7))
                self.gelu_from_psum(p, P, ggo[:, oc, 0:n], GGB, n, z2, Z2B, t1, T1B)
            k.dma("sp", self.fm(self.gg, t0, n), ggo[:, :, 0:n], reads=[GGB], writes=[self.DB("gg", ti)])
        k.phase_reset()

    def gelu_from_psum(self, p, P, dst, DSTB, n, z2, Z2B, t1, T1B):
        k = self.k
        k.op("act", lambda e: e.activation(out=z2[:, 0:n], in_=p[:, 0:n], func=AF.Square), reads=[P], writes=[Z2B])
        k.op("dve", lambda e: e.tensor_scalar(out=z2[:, 0:n], in0=z2[:, 0:n], scalar1=0.044715, scalar2=1.0,
                                              op0=ALU.mult, op1=ALU.add), reads=[Z2B], writes=[Z2B])
        k.op("dve", lambda e: e.tensor_tensor(out=z2[:, 0:n], in0=z2[:, 0:n], in1=p[:, 0:n], op=ALU.mult),
             reads=[Z2B, P], writes=[Z2B])
        k.op("act", lambda e: e.activation(out=t1[:, 0:n], in_=z2[:, 0:n], func=AF.Sigmoid, scale=GELU_C),
             reads=[Z2B], writes=[T1B])
        k.op("dve", lambda e: e.tensor_tensor(out=dst, in0=t1[:, 0:n], in1=p[:, 0:n], op=ALU.mult),
             reads=[T1B, P], writes=[DSTB])

    def phase_rglru(self, l, b):
        k = self.k
        i = l // 2
        gw = k.sb([128, 16, 128], BF16)
        GWB = Buf()
        k.dma("pool", gw, self.hy_gw[i].rearrange("d g c p m -> p (d g c) m"), writes=[GWB])
        c1 = k.sb([128, 8], F32)
        c2 = k.sb([128, 8], F32)
        C1B = Buf()
        k.op("act", lambda e: e.activation(out=c1, in_=self.V("lam_%d" % i, 0, 8), func=AF.Exp, scale=-1.0),
             reads=[self.VB], writes=[C1B])
        k.op("act", lambda e: e.activation(out=c1, in_=c1, func=AF.Ln, bias=1.0), reads=[C1B], writes=[C1B])
        k.op("dve", lambda e: e.tensor_scalar(out=c2, in0=c1, scalar1=-16.0, scalar2=None, op0=ALU.mult),
             reads=[C1B], writes=[C1B])
        k.op("dve", lambda e: e.tensor_scalar(out=c1, in0=c1, scalar1=-8.0, scalar2=None, op0=ALU.mult),
             reads=[C1B], writes=[C1B])
        x = k.sb([128, T], F32)
        xc = k.sb([128, T], F32)
        xcb = k.sb([128, T], BF16)
        r = k.sb([128, T], F32)
        ig = k.sb([128, T], F32)
        a = k.sb([128, T], F32)
        u = k.sb([128, T], F32)
        h = [k.sb([128, T], F32) for _ in range(2)]
        ggt = k.sb([128, T], BF16)
        rec = k.sb([128, T], BF16)
        XB, XCB, XCBB, RB, IB, AB, UB, GB, RECB = [Buf() for _ in range(9)]
        HB = [Buf(), Buf()]
        segs = [(0, CTX), (CTX, T)]
        for c in range(4):
            k.dma("sp", x, self.xr[c * 128:(c + 1) * 128, :], reads=[self.DB("xr", ti) for ti in range(9)], writes=[XB])
            k.dma("sp", ggt, self.gg[c * 128:(c + 1) * 128, :], reads=[self.DB("gg", ti) for ti in range(9)],
                  writes=[GB])
            k.op("act", lambda e: e.activation(out=xc, in_=x, func=AF.Identity, bias=self.V("convb_%d" % i, c),
                                               scale=self.V("convw_%d" % i, 2 * 4 + c)),
                 reads=[XB, self.VB], writes=[XCB])
            for (s0, s1) in segs:
                for j in (0, 1, 3):
                    sh = j - 2
                    a0 = max(s0, s0 - sh)
                    a1 = min(s1, s1 - sh)
                    k.op("dve", lambda e: e.scalar_tensor_tensor(out=xc[:, a0:a1], in0=x[:, a0 + sh:a1 + sh],
                                                                 scalar=self.V("convw_%d" % i, j * 4 + c),
                                                                 in1=xc[:, a0:a1], op0=ALU.mult, op1=ALU.add),
                         reads=[XB, XCB, self.VB], writes=[XCB])
            k.op("act", lambda e: e.activation(out=xcb, in_=xc, func=AF.Copy), reads=[XCB], writes=[XCBB])
            for d in range(2):
                for (t0, n) in TILES:
                    for g, (dst, DB_) in enumerate(((r, RB), (ig, IB))):
                        p, P = self.pb()
                        k.op("pe", lambda e: e.matmul(p[:, 0:n], lhsT=gw[:, (d * 2 + g) * 4 + c, :], rhs=xcb[:, t0:t0 + n],
                                                      start=True, stop=True), reads=[GWB, XCBB], writes=[P])
                        k.op("act", lambda e: e.activation(out=dst[:, t0:t0 + n], in_=p[:, 0:n], func=AF.Sigmoid,
                                                           bias=self.V("gateb_%d" % i, (d * 2 + g) * 4 + c)),
                             reads=[P, self.VB], writes=[DB_])
                k.op("act", lambda e: e.activation(out=a, in_=r, func=AF.Exp, scale=c1[:, d * 4 + c:d * 4 + c + 1]),
                     reads=[RB, C1B], writes=[AB])
                k.op("act", lambda e: e.activation(out=u, in_=r, func=AF.Exp, scale=c2[:, d * 4 + c:d * 4 + c + 1]),
                     reads=[RB, C1B], writes=[UB])
                k.op("dve", lambda e: e.tensor_scalar(out=u, in0=u, scalar1=1.0, scalar2=None, op0=ALU.min),
                     reads=[UB], writes=[UB])
                k.op("act", lambda e: e.activation(out=u, in_=u, func=AF.Sqrt, bias=1.0, scale=-1.0),
                     reads=[UB], writes=[UB])
                k.op("dve", lambda e: e.tensor_tensor(out=ig, in0=ig, in1=xc, op=ALU.mult), reads=[IB, XCB], writes=[IB])
                k.op("dve", lambda e: e.tensor_tensor(out=u, in0=u, in1=ig, op=ALU.mult), reads=[UB, IB], writes=[UB])
                hd = h[d]
                if d == 0:
                    k.op("dve", lambda e: e.tensor_tensor_scan(out=hd, data0=a, data1=u, initial=0.0, op0=ALU.mult,
                                                               op1=ALU.add), reads=[AB, UB], writes=[HB[d]])
                else:
                    k.op("dve", lambda e: e.tensor_tensor_scan(out=hd[:, 0:CTX][:, ::-1], data0=a[:, 0:CTX][:, ::-1],
                                                               data1=u[:, 0:CTX][:, ::-1], initial=0.0, op0=ALU.mult,
                                                               op1=ALU.add), reads=[AB, UB], writes=[HB[d]])
                    k.op("dve", lambda e: e.tensor_tensor_scan(out=hd[:, CTX:T][:, ::-1], data0=a[:, CTX:T][:, ::-1],
                                                               data1=u[:, CTX:T][:, ::-1], initial=hd[:, 0:1],
                                                               op0=ALU.mult, op1=ALU.add), reads=[AB, UB, HB[d]],
                         writes=[HB[d]])
            if "rgd" in self.debug and c == 0 and b == 0:
                rgd = self.nc.dram_tensor("rgd", [8, 128, T], F32, kind="ExternalOutput").ap()
                for j_, (t_, B_) in enumerate(((x, XB), (xc, XCB), (r, RB), (ig, IB), (a, AB), (u, UB), (h[0], HB[0]),
                                               (h[1], HB[1]))):
                    k.dma("sp", rgd[j_], t_, reads=[B_], writes=[Buf()])
                c1d = self.nc.dram_tensor("c1d", [2, 128, 8], F32, kind="ExternalOutput").ap()
                k.dma("sp", c1d[0], c1, reads=[C1B], writes=[Buf()])
                k.dma("sp", c1d[1], c2, reads=[C1B], writes=[Buf()])
            k.op("dve", lambda e: e.tensor_tensor(out=h[0], in0=h[0], in1=h[1], op=ALU.add), reads=[HB[0], HB[1]],
                 writes=[HB[0]])
            k.op("dve", lambda e: e.tensor_tensor(out=rec, in0=h[0], in1=ggt, op=ALU.mult), reads=[HB[0], GB],
                 writes=[RECB])
            k.dma("sp", self.rec[c * 128:(c + 1) * 128, :], rec, reads=[RECB], writes=[self.DB("rec", c)])
        k.phase_reset()

    def phase_attn(self, l, b):
        k = self.k
        i = l // 2
        kt = k.sb([128, 2, T], BF16)
        KTB = Buf()
        k.dma("sp", kt, self.kT2.rearrange("(c p) t -> p c t", p=128), reads=[self.DB("kT2", ti) for ti in range(9)],
              writes=[KTB])
        vt = k.sb([128, T // 128, 384], BF16)
        VTB = Buf()
        k.dma("sp", vt, self.vtok.rearrange("c p f -> p c f"), reads=[self.DB("vtok", ti) for ti in range(9)],
              writes=[VTB])
        wo = k.sb([128, 8, D], BF16)
        WOB = Buf()
        self.load_w(wo, WOB, self.hy_wout[i], 8)
        xt = k.sb([128, 8, 512], F32)
        XB = Buf()
        q = k.sb([128, 4, 512], BF16)
        QB = Buf()
        rc = k.sb([128, 4, 512], BF16)
        RCB = Buf()
        att = k.sb([128, 4, 512], BF16)
        ATB = Buf()
        NPT = 8
        pt = [k.sb([128, 512], BF16) for _ in range(NPT)]
        PTB = [Buf() for _ in range(NPT)]
        den = k.sb([128, 512], F32)
        DNB = Buf()
        ptr = 0
        RECALL = [self.DB("rec", c) for c in range(4)]
        for ti, (t0, n) in enumerate(TILES):
            mi = 2 if ti == 0 else b
            nkc = 2 if ti == 0 else T // 128
            k.dma("sp", xt[:, :, 0:n], self.fm(self.xT[b], t0, n), reads=[self.DB("xT", b, ti)], writes=[XB])
            k.dma("sp", q[:, :, 0:n], self.fm(self.qT, t0, n), reads=[self.DB("qT", ti)], writes=[QB])
            k.dma("sp", rc[:, :, 0:n], self.fm(self.rec, t0, n), reads=RECALL, writes=[RCB])
            for hh in range(8):
                kv, hp, fc = hh // 4, hh % 2, hh // 2
                lo, hi = hp * 64, hp * 64 + 64
                olo, ohi = (1 - hp) * 64, (1 - hp) * 64 + 64
                po, PO = self.ps[6 + hh % 2], self.PS[6 + hh % 2]
                voff = kv * 192 + (64 if hp == 0 else 0)
                LA = 5
                slots = []

                def pv(kc, pj):
                    k.op("pe", lambda e: e.matmul(po[:, 0:n], lhsT=vt[:, kc, voff:voff + 128], rhs=pt[pj][:, 0:n],
                                                  start=(kc == 0), stop=(kc == nkc - 1)),
                         reads=[VTB, PTB[pj]], writes=[PO], sig=(kc == nkc - 1))
                for kc in range(nkc):
                    sbi = ptr % 6
                    ps_, PSB = self.ps[sbi], self.PS[sbi]
                    k.op("pe", lambda e: e.matmul(ps_[:, 0:n], lhsT=kt[lo:hi, kv, kc * 128:(kc + 1) * 128],
                                                  rhs=q[lo:hi, fc, 0:n], start=True, stop=True),
                         reads=[KTB, QB], writes=[PSB])
                    pj = ptr % NPT
                    ptr += 1
                    k.op("act", lambda e: e.activation(out=pt[pj][:, 0:n], in_=ps_[:, 0:n], func=AF.Exp, scale=0.125),
                         reads=[PSB], writes=[PTB[pj]])
                    slots.append((kc, pj))
                    if len(slots) > LA:
                        pv(*slots.pop(0))
                while slots:
                    pv(*slots.pop(0))
                k.op("act", lambda e: e.activation(out=den[lo:hi, 0:n], in_=po[olo:ohi, 0:n], func=AF.Copy),
                     reads=[PO], writes=[DNB])
                k.op("dve", lambda e: e.reciprocal(out=den[lo:hi, 0:n], in_=den[lo:hi, 0:n]), reads=[DNB], writes=[DNB])
                k.op("dve", lambda e: e.tensor_tensor(out=att[lo:hi, fc, 0:n], in0=po[lo:hi, 0:n], in1=den[lo:hi, 0:n],
                                                      op=ALU.mult), reads=[PO, DNB], writes=[ATB])
            for oc in range(8):
                p, P = self.pb()
                for kc in range(8):
                    src = att[:, kc, 0:n] if kc < 4 else rc[:, kc - 4, 0:n]
                    k.op("pe", lambda e: e.matmul(p[:, 0:n], lhsT=wo[:, kc, oc * 128:(oc + 1) * 128], rhs=src,
                                                  start=(kc == 0), stop=(kc == 7)),
                         reads=[WOB, ATB, RCB], writes=[P], sig=(kc == 7))
                k.op("dve", lambda e: e.scalar_tensor_tensor(out=xt[:, oc, 0:n], in0=p[:, 0:n],
                                                             scalar=self.mod[:, 16 + oc, mi:mi + 1], in1=xt[:, oc, 0:n],
                                                             op0=ALU.mult, op1=ALU.add),
                     reads=[P, self.MOD, XB], writes=[XB])
            k.dma("sp", self.fm(self.xT[b], t0, n), xt[:, :, 0:n], reads=[XB], writes=[self.DB("xT", b, ti)])
        k.phase_reset()

    def phase_rwkv(self, l, b):
        last = (l == DEPTH - 1)
        self.rw_norm(l, b)
        if self.rw_stop >= 1:
            self.rw_proj(l, b)
        for d in range(2):
            if self.rw_stop >= 2 + d:
                self.rw_wkv(l, b, d)
        if self.rw_stop >= 4:
            self.rw_out(l, b, last)

    def rw_norm(self, l, b):
        k = self.k
        W = self.norm_work()
        xt = [k.sb([128, 8, 512], F32) for _ in range(2)]
        XB = [Buf(), Buf()]
        hn = [k.sb([128, 8, 512], F32) for _ in range(2)]
        HB = [Buf(), Buf()]
        for ti, (t0, n) in enumerate(TILES):
            mi = 2 if ti == 0 else b
            j = ti % 2
            k.dma("sp", xt[j][:, :, 0:n], self.fm(self.xT[b], t0, n), reads=[self.DB("xT", b, ti)], writes=[XB[j]])
            self.norm_tile(xt[j], XB[j], hn[j], HB[j], self.A1, self.mod[:, 0:8, :], mi, n, W)
            k.dma("sp", self.fm(self.hnT, t0, n), hn[j][:, :, 0:n], reads=[HB[j]], writes=[self.DB("hnT", ti)])
        k.phase_reset()

    def rw_proj(self, l, b):
        k = self.k
        i = l // 2
        wr = k.sb([128, 8, D], BF16)
        wk = k.sb([128, 8, D], BF16)
        wvs = k.sb([128, 8, D], BF16)
        ld = k.sb([128, 8, 256], BF16)
        lu = k.sb([64, 4, D], BF16)
        gd = k.sb([128, 8, 128], BF16)
        gu = k.sb([128, D], BF16)
        WB = Buf()
        self.load_w(wr, WB, self.rw_wrkv[i, 0], 8)
        self.load_w(wk, WB, self.rw_wrkv[i, 1], 8)
        self.load_w(wvs, WB, self.rw_wvst[i], 8)
        self.load_w(ld, WB, self.rw_ld[i], 8)
        self.load_w(gd, WB, self.rw_gd[i], 8)
        k.dma("pool", lu, self.rw_lu[i].rearrange("q p n -> p q n"), writes=[WB])
        k.dma("pool", gu, self.rw_gu[i], writes=[WB])
        omka = k.sb([128, 8], F32)
        OMB = Buf()
        k.op("dve", lambda e: e.tensor_scalar(out=omka, in0=self.V("ka_%d" % i, 0, 8), scalar1=-1.0, scalar2=1.0,
                                              op0=ALU.mult, op1=ALU.add), reads=[self.VB], writes=[OMB])
        NT = 256
        hh = k.sb([128, 8, NT + 2], F32)
        HHB = Buf()
        xx = k.sb([128, 8, NT], F32)
        XXB = Buf()
        L = [k.sb([128, 8, NT], BF16) for _ in range(6)]
        LB = [Buf() for _ in range(6)]
        rt = k.sb([128, 8, NT], F32)
        kt = k.sb([128, 8, NT], F32)
        kkt = k.sb([128, 8, NT], F32)
        kd = [k.sb([128, 8, NT], F32) for _ in range(2)]
        o1 = k.sb([128, 8, NT], F32)
        RTB, KTB, KKB, O1B = Buf(), Buf(), Buf(), Buf()
        KDB = [Buf(), Buf()]
        vst = k.sb([128, 2, 512], F32)
        VSB = [Buf(), Buf()]
        sm = k.sb([128, NT], BF16)
        SMB = Buf()
        at8 = k.sb([128, 8, NT], F32)
        AT8 = Buf()
        tp = k.sb([128, 8, NT], F32)
        TPB = Buf()
        w18 = k.sb([128, 8, NT], F32)
        W18 = Buf()
        sq8 = k.sb([128, 8, NT], BF16)
        SQ8 = Buf()
        bones_f = self.cs[:, 128:256]

        def bc(name, j0, n):
            return self.V(name, j0, 8).unsqueeze(2).broadcast_to([128, 8, n])

        for ti, (t0, n) in enumerate(WTILES):
            seg0, seg1 = (0, CTX) if ti == 0 else (CTX, T)
            lo = max(t0 - 1, seg0)
            hi = min(t0 + n + 1, seg1)
            k.dma("sp", hh[:, :, lo - (t0 - 1):hi - (t0 - 1)], self.fm(self.hnT, lo, hi - lo),
                  reads=[self.DB("hnT", j) for j in range(9)], writes=[HHB])
            if lo != t0 - 1:
                k.op("dve", lambda e: e.memset(hh[:, :, 0:1], 0.0), writes=[HHB])
            if hi != t0 + n + 1:
                k.op("dve", lambda e: e.memset(hh[:, :, n + 1:n + 2], 0.0), writes=[HHB])
            h = hh[:, :, 1:n + 1]
            k.op("dve", lambda e: e.tensor_tensor(out=xx[:, :, 0:n], in0=hh[:, :, 0:n], in1=hh[:, :, 2:n + 2], op=ALU.add),
                 reads=[HHB], writes=[XXB])
            k.op("dve", lambda e: e.scalar_tensor_tensor(out=xx[:, :, 0:n], in0=xx[:, :, 0:n], scalar=0.5, in1=h,
                                                         op0=ALU.mult, op1=ALU.subtract), reads=[XXB, HHB], writes=[XXB])
            for j in (0, 2, 3, 1, 4, 5):
                if j in (0, 2, 3):
                    eng, tmp, TB = "dve", at8, AT8
                else:
                    eng, tmp, TB = "pool", tp, TPB
                k.op(eng, lambda e: e.tensor_tensor(out=tmp[:, :, 0:n], in0=xx[:, :, 0:n], in1=bc("mu_%d" % i, j * 8, n),
                                                    op=ALU.mult), reads=[XXB, self.VB], writes=[TB])
                k.op(eng, lambda e: e.tensor_tensor(out=L[j][:, :, 0:n], in0=tmp[:, :, 0:n], in1=h, op=ALU.add),
                     reads=[TB, HHB], writes=[LB[j]])

            def proj(w, Lj, LjB, dst, DSTB):
                for oc in range(8):
                    p, P = self.pb()
                    for kc in range(8):
                        k.op("pe", lambda e: e.matmul(p[:, 0:n], lhsT=w[:, kc, oc * 128:(oc + 1) * 128], rhs=Lj[:, kc, 0:n],
                                                      start=(kc == 0), stop=(kc == 7)), reads=[WB, LjB], writes=[P],
                             sig=(kc == 7))
                    k.op("act", lambda e: e.activation(out=dst[:, oc, 0:n], in_=p[:, 0:n], func=AF.Copy), reads=[P],
                         writes=[DSTB])
            proj(wr, L[0], LB[0], rt, RTB)
            k.dma("sp", self.fm(self.rT, t0, n), rt[:, :, 0:n], reads=[RTB], writes=[self.DB("rT", ti)])
            proj(wk, L[2], LB[2], kt, KTB)
            for ch in range(n // 64):
                p, P = self.pb()
                for hp in range(2):
                    for kc in range(8):
                        k.op("pe", lambda e: e.matmul(p[hp * 64:(hp + 1) * 64, :], lhsT=L[3][:, kc, ch * 64:(ch + 1) * 64],
                                                      rhs=wvs[:, kc, hp * 512:(hp + 1) * 512], start=(kc == 0),
                                                      stop=(kc == 7)), reads=[WB, LB[3]], writes=[P],
                             sig=(kc == 7 and hp == 1))
                j = ch % 2
                k.op("act", lambda e: e.activation(out=vst[:, j, :], in_=p, func=AF.Copy), reads=[P], writes=[VSB[j]])
                k.dma("sp", self.Vst[t0 // 64 + ch], vst[:, j, :], reads=[VSB[j]], writes=[self.DB("Vst", ti)])
            p, P = self.pb()
            for kc in range(8):
                k.op("pe", lambda e: e.matmul(p[:, 0:n], lhsT=gd[:, kc, :], rhs=L[5][:, kc, 0:n], start=(kc == 0),
                                              stop=(kc == 7)), reads=[WB, LB[5]], writes=[P], sig=(kc == 7))
            k.op("act", lambda e: e.activation(out=sm[:, 0:n], in_=p[:, 0:n], func=AF.Sigmoid), reads=[P], writes=[SMB])
            for oc in range(8):
                p, P = self.pb()
                k.op("pe", lambda e: e.matmul(p[:, 0:n], lhsT=gu[:, oc * 128:(oc + 1) * 128], rhs=sm[:, 0:n], start=True,
                                              stop=True), reads=[WB, SMB], writes=[P])
                k.op("act", lambda e: e.activation(out=o1[:, oc, 0:n], in_=p[:, 0:n], func=AF.Copy), reads=[P],
                     writes=[O1B])
            k.dma("sp", self.fm(self.gT, t0, n), o1[:, :, 0:n], reads=[O1B], writes=[self.DB("gT", ti)])
            k.op("dve", lambda e: e.tensor_tensor(out=kkt[:, :, 0:n], in0=kt[:, :, 0:n], in1=bc("kk_%d" % i, 0, n),
                                                  op=ALU.mult), reads=[KTB, self.VB], writes=[KKB])
            k.op("act", lambda e: e.activation(out=sq8[:, :, 0:n], in_=kkt[:, :, 0:n], func=AF.Square), reads=[KKB],
                 writes=[SQ8])
            for oc in range(8):
                p, P = self.pb()
                k.op("pe", lambda e: e.matmul(p[:, 0:n], lhsT=self.bones_bf, rhs=sq8[:, oc, 0:n], start=True, stop=True),
                     reads=[SQ8, self.CONST], writes=[P])
                k.op("act", lambda e: e.activation(out=w18[:, oc, 0:n], in_=p[:, 0:n], func=AF.Sqrt), reads=[P],
                     writes=[W18])
            k.op("dve", lambda e: e.tensor_scalar(out=w18[:, :, 0:n], in0=w18[:, :, 0:n], scalar1=1e-12, scalar2=None,
                                                  op0=ALU.max), reads=[W18], writes=[W18])
            k.op("dve", lambda e: e.reciprocal(out=w18[:, :, 0:n], in_=w18[:, :, 0:n]), reads=[W18], writes=[W18])
            k.op("dve", lambda e: e.tensor_tensor(out=kkt[:, :, 0:n], in0=kkt[:, :, 0:n], in1=w18[:, :, 0:n], op=ALU.mult),
                 reads=[KKB, W18], writes=[KKB])
            k.dma("sp", self.fm(self.kkT, t0, n), kkt[:, :, 0:n], reads=[KKB], writes=[self.DB("kkT", ti)])
            for d in range(2):
                p, P = self.pb()
                for kc in range(8):
                    k.op("pe", lambda e: e.matmul(p[0:64, 0:n], lhsT=ld[:, kc, (d * 2) * 64:(d * 2 + 1) * 64],
                                                  rhs=L[1][:, kc, 0:n], start=(kc == 0), stop=(kc == 7)),
                         reads=[WB, LB[1]], writes=[P], sig=(kc == 7))
                k.op("act", lambda e: e.activation(out=sm[0:64, 0:n], in_=p[0:64, 0:n], func=AF.Tanh), reads=[P],
                     writes=[SMB])
                for oc in range(8):
                    p, P = self.pb()
                    k.op("pe", lambda e: e.matmul(p[:, 0:n], lhsT=lu[:, d * 2, oc * 128:(oc + 1) * 128], rhs=sm[0:64, 0:n],
                                                  start=True, stop=True), reads=[WB, SMB], writes=[P])
                    k.op("act", lambda e: e.activation(out=o1[:, oc, 0:n], in_=p[:, 0:n], func=AF.Sigmoid,
                                                       bias=self.V("lb_%d" % i, (d * 2) * 8 + oc)),
                         reads=[P, self.VB], writes=[O1B])
                k.op("dve", lambda e: e.tensor_scalar(out=o1[:, :, 0:n], in0=o1[:, :, 0:n], scalar1=-DECAY_SCALE,
                                                      scalar2=None, op0=ALU.mult), reads=[O1B], writes=[O1B])
                k.dma("sp", self.fm(self.lwT[d], t0, n), o1[:, :, 0:n], reads=[O1B], writes=[self.DB("lwT", d, ti)])
                p, P = self.pb()
                for kc in range(8):
                    k.op("pe", lambda e: e.matmul(p[0:64, 0:n], lhsT=ld[:, kc, (d * 2 + 1) * 64:(d * 2 + 2) * 64],
                                                  rhs=L[4][:, kc, 0:n], start=(kc == 0), stop=(kc == 7)),
                         reads=[WB, LB[4]], writes=[P], sig=(kc == 7))
                k.op("act", lambda e: e.activation(out=sm[0:64, 0:n], in_=p[0:64, 0:n], func=AF.Copy), reads=[P],
                     writes=[SMB])
                for oc in range(8):
                    p, P = self.pb()
                    k.op("pe", lambda e: e.matmul(p[:, 0:n], lhsT=lu[:, d * 2 + 1, oc * 128:(oc + 1) * 128],
                                                  rhs=sm[0:64, 0:n], start=True, stop=True), reads=[WB, SMB], writes=[P])
                    k.op("act", lambda e: e.activation(out=at8[:, oc, 0:n], in_=p[:, 0:n], func=AF.Sigmoid,
                                                       bias=self.V("lb_%d" % i, (d * 2 + 1) * 8 + oc)),
                         reads=[P, self.VB], writes=[AT8])
                k.op("dve", lambda e: e.tensor_tensor(out=o1[:, :, 0:n], in0=kkt[:, :, 0:n], in1=at8[:, :, 0:n], op=ALU.mult),
                     reads=[KKB, AT8], writes=[O1B])
                k.dma("sp", self.fm(self.bT[d], t0, n), o1[:, :, 0:n], reads=[O1B], writes=[self.DB("bT", d, ti)])
                k.op("dve", lambda e: e.tensor_tensor(out=at8[:, :, 0:n], in0=at8[:, :, 0:n], in1=bc("ka_%d" % i, 0, n),
                                                      op=ALU.mult), reads=[AT8, self.VB], writes=[AT8])
                k.op("dve", lambda e: e.tensor_tensor(out=at8[:, :, 0:n], in0=at8[:, :, 0:n],
                                                      in1=omka.unsqueeze(2).broadcast_to([128, 8, n]), op=ALU.add),
                     reads=[AT8, OMB], writes=[AT8])
                k.op("dve", lambda e: e.tensor_tensor(out=kd[d][:, :, 0:n], in0=at8[:, :, 0:n], in1=kt[:, :, 0:n], op=ALU.mult),
                     reads=[AT8, KTB], writes=[KDB[d]])
                k.dma("sp", self.fm(self.kdT[d], t0, n), kd[d][:, :, 0:n], reads=[KDB[d]], writes=[self.DB("kdT", d, ti)])
            k.op("dve", lambda e: e.tensor_tensor(out=at8[:, :, 0:n], in0=kd[0][:, :, 0:n], in1=kd[1][:, :, 0:n], op=ALU.add),
                 reads=[KDB[0], KDB[1]], writes=[AT8])
            k.op("dve", lambda e: e.tensor_tensor(out=at8[:, :, 0:n], in0=at8[:, :, 0:n], in1=rt[:, :, 0:n], op=ALU.mult),
                 reads=[AT8, RTB], writes=[AT8])
            k.op("dve", lambda e: e.tensor_tensor(out=at8[:, :, 0:n], in0=at8[:, :, 0:n], in1=bc("rk_%d" % i, 0, n),
                                                  op=ALU.mult), reads=[AT8, self.VB], writes=[AT8])
            for oc in range(8):
                p, P = self.pb()
                k.op("pe", lambda e: e.matmul(p[:, 0:n], lhsT=bones_f, rhs=at8[:, oc, 0:n], start=True, stop=True),
                     reads=[AT8, self.CS], writes=[P])
                k.op("act", lambda e: e.activation(out=w18[:, oc, 0:n], in_=p[:, 0:n], func=AF.Copy), reads=[P],
                     writes=[W18])
            for oc in range(8):
                p2, P2 = self.pb()
                for half in range(2):
                    hd_ = 2 * oc + half
                    c0 = (hd_ % 2) * 512 + (hd_ // 2) * 64
                    for kc in range(8):
                        k.op("pe", lambda e: e.matmul(p2[half * 64:(half + 1) * 64, 0:n], lhsT=wvs[:, kc, c0:c0 + 64],
                                                      rhs=L[3][:, kc, 0:n], start=(kc == 0), stop=(kc == 7)),
                             reads=[WB, LB[3]], writes=[P2], sig=(kc == 7 and half == 1))
                k.op("dve", lambda e: e.tensor_tensor(out=o1[:, oc, 0:n], in0=p2[:, 0:n], in1=w18[:, oc, 0:n], op=ALU.mult),
                     reads=[P2, W18], writes=[O1B])
            k.dma("sp", self.fm(self.bonT, t0, n), o1[:, :, 0:n], reads=[O1B], writes=[self.DB("bonT", ti)])
        k.phase_reset()

    def rw_wkv(self, l, b, d):
        k = self.k
        NW = 256
        rev = (d == 1)
        msk = k.sb([128, 512], F32)
        rmk = k.sb([128, 2048], F32)
        MB = Buf()
        k.dma("sp", msk, self.wmask[:, d, :], writes=[MB])
        k.dma("sp", rmk, self.rmask[:, d, :], writes=[MB])

        def t8():
            return k.sb([128, 8, NW], F32)
        rt, kk, bb, kdt, lw, cum, ec = t8(), t8(), t8(), t8(), t8(), t8(), t8()
        INB = Buf()
        ECB = Buf()
        vst = k.sb([128, 4, 512], F32)
        VB_ = Buf()
        yt = k.sb([128, 8, NW], F32)
        YB = Buf()
        XR = [k.sb([128, 8, 192], F32) for _ in range(2)]
        BE = [k.sb([128, 8, 128], F32) for _ in range(2)]
        KT = [k.sb([128, 8, 128], F32) for _ in range(2)]
        CHB = [Buf(), Buf()]
        AM2 = [k.sb([128, 8, 512], F32) for _ in range(2)]
        AMB2 = [[Buf() for _ in range(8)] for _ in range(2)]
        X2 = [k.sb([128, 8, 128], F32) for _ in range(2)]
        XB2 = [[Buf(), Buf()], [Buf(), Buf()]]
        Pn = k.sb([128, 8, 128], F32)
        PTn = k.sb([128, 8, 128], F32)
        PNB = [Buf(), Buf()]
        PTB = [Buf(), Buf()]
        BEt2 = [k.sb([128, 8, 128], F32) for _ in range(2)]
        KTt2 = [k.sb([128, 8, 128], F32) for _ in range(2)]
        BTB2, KTTB2 = [Buf(), Buf()], [Buf(), Buf()]
        Vbd2 = [k.sb([128, 8, 128], F32) for _ in range(2)]
        VBB2 = [Buf(), Buf()]
        Wsb = k.sb([128, 8, 64], F32)
        Ust = k.sb([128, 8, 64], F32)
        Ubd = k.sb([128, 8, 128], F32)
        Vs2 = [k.sb([128, 8, 64], F32) for _ in range(2)]
        Gc2 = [k.sb([128, 8, 1], F32) for _ in range(2)]
        VS2B = [Buf(), Buf()]
        GCB = [Buf(), Buf()]
        Sst = k.sb([128, 8, 64], F32)
        Sbd = k.sb([128, 8, 128], F32)
        WSB, USB, UBB, SSB, SBB = [Buf() for _ in range(5)]
        for t_, B_ in ((XR[0], CHB[0]), (XR[1], CHB[1]), (BE[0], CHB[0]), (BE[1], CHB[1]), (KT[0], CHB[0]),
                       (KT[1], CHB[1]), (Ubd, UBB), (Vbd2[0], VBB2[0]), (Vbd2[1], VBB2[1]), (Sbd, SBB), (Sst, SSB)):
            k.op("pool", lambda e: e.memset(t_, 0.0), writes=[B_])
        r3 = lambda t_: t_.rearrange("p (q c) -> p q c", c=64)
        r4 = lambda t_: t_.rearrange("p (q c) -> p q c", c=128)

        def pre_gen(ch, jb):
            c0 = ch * 64
            cs_ = slice(c0, c0 + 64)
            xr_, be_, kt_, CB = XR[jb], BE[jb], KT[jb], CHB[jb]
            AM, AMB, X, XB = AM2[jb], AMB2[jb], X2[jb], XB2[jb]
            BEt, KTt, BTB, KTTB, Vbd, VBB = BEt2[jb], KTt2[jb], BTB2[jb], KTTB2[jb], Vbd2[jb], VBB2[jb]
            for hp in range(2):
                ps_ = slice(hp * 64, hp * 64 + 64)
                k.op("dve", lambda e: e.scalar_tensor_tensor(out=xr_[ps_, :, hp * 64:hp * 64 + 64], in0=kk[ps_, :, cs_],
                                                             scalar=-1.0, in1=lw[ps_, :, cs_], op0=ALU.mult,
                                                             op1=ALU.mult), reads=[INB], writes=[CB])
                k.op("pool", lambda e: e.tensor_tensor(out=be_[ps_, :, hp * 64:hp * 64 + 64], in0=bb[ps_, :, cs_],
                                                       in1=cum[ps_, :, cs_], op=ALU.mult), reads=[INB, ECB], writes=[CB])
                k.op("pool", lambda e: e.tensor_tensor(out=kt_[ps_, :, hp * 64:hp * 64 + 64], in0=kdt[ps_, :, cs_],
                                                       in1=cum[ps_, :, cs_], op=ALU.mult), reads=[INB, ECB], writes=[CB])
            k.op("dve", lambda e: e.tensor_tensor(out=xr_[:, :, 128:192], in0=rt[:, :, cs_], in1=ec[:, :, cs_],
                                                  op=ALU.mult), reads=[INB, ECB], writes=[CB])
            vs_ = vst[:, ch, :].rearrange("p (q v) -> p q v", v=64)
            for hp in range(2):
                ps_ = slice(hp * 64, hp * 64 + 64)
                k.op("act", lambda e: e.activation(out=Vbd[ps_, :, hp * 64:hp * 64 + 64], in_=vs_[ps_, :, :],
                                                   func=AF.Copy), reads=[VB_], writes=[VBB])
            k.op("act", lambda e: e.activation(out=Vs2[jb], in_=vs_, func=AF.Copy), reads=[VB_], writes=[VS2B[jb]])
            gcol_ = (c0 + 63) if not rev else c0
            k.op("act", lambda e: e.activation(out=Gc2[jb], in_=ec[:, :, gcol_:gcol_ + 1], func=AF.Copy), reads=[ECB],
                 writes=[GCB[jb]])
            yield
            for p_ in range(8):
                pa, PA = self.pb()
                k.op("pe", lambda e: e.matmul(pa[:, 0:192], lhsT=be_[:, p_, :], rhs=xr_[:, p_, :], start=True,
                                              stop=True), reads=[CB], writes=[PA], sig=False)
                k.op("pe", lambda e: e.matmul(pa[:, 192:384], lhsT=kt_[:, p_, :], rhs=xr_[:, p_, :], start=True,
                                              stop=True), reads=[CB], writes=[PA], sig=False)
                k.op("pe", lambda e: e.matmul(pa[:, 384:512], lhsT=xr_[:, p_, 0:128], rhs=be_[:, p_, :], start=True,
                                              stop=True), reads=[CB], writes=[PA])
                k.op("dve", lambda e: e.tensor_tensor(out=AM[:, p_, :], in0=pa, in1=msk, op=ALU.mult),
                     reads=[PA, MB], writes=[AMB[p_]])
                if p_ == 3:
                    yield
            yield
            for (src_, dst_, DB_) in ((be_, BEt, BTB), (kt_, KTt, KTTB)):
                for q_ in range(2):
                    pt_, PT_ = self.pb()
                    for pi in range(4):
                        p_ = q_ * 4 + pi
                        k.op("pe", lambda e: e.transpose(pt_[:, pi * 128:(pi + 1) * 128], src_[:, p_, :], self.ident),
                             reads=[CB, self.CS], writes=[PT_], sig=(pi == 3))
                    k.op("act", lambda e: e.activation(out=dst_[:, q_ * 4:q_ * 4 + 4, :], in_=r4(pt_), func=AF.Copy),
                         reads=[PT_], writes=[DB_])
            for q_ in range(2):
                k.op("dve", lambda e: e.tensor_tensor(out=X[:, q_ * 4:q_ * 4 + 4, :], in0=AM[:, q_ * 4:q_ * 4 + 4, 0:128],
                                                      in1=self.ident.unsqueeze(1).broadcast_to([128, 4, 128]), op=ALU.add),
                     reads=AMB[q_ * 4:q_ * 4 + 4] + [self.CS], writes=[XB[q_]])
            yield
            for kk_ in range(1, 6):
                Pq = (lambda p_: AM[:, p_, 0:128]) if kk_ == 1 else (lambda p_: Pn[:, p_, :])
                PTq = (lambda p_: AM[:, p_, 384:512]) if kk_ == 1 else (lambda p_: PTn[:, p_, :])
                bk = {}
                for q_ in range(2):
                    RD = (AMB[q_ * 4:q_ * 4 + 4]) if kk_ == 1 else [PNB[q_], PTB[q_]]
                    pb_, PB_ = self.pb()
                    for pi in range(4):
                        p_ = q_ * 4 + pi
                        k.op("pe", lambda e: e.matmul(pb_[:, pi * 128:(pi + 1) * 128], lhsT=Pq(p_), rhs=PTq(p_),
                                                      start=True, stop=True), reads=RD, writes=[PB_], sig=(pi == 3))
                    bk[("b", q_)] = (pb_, PB_)
                    if kk_ < 5:
                        pa_, PA_ = self.pb()
                        for pi in range(4):
                            p_ = q_ * 4 + pi
                            k.op("pe", lambda e: e.matmul(pa_[:, pi * 128:(pi + 1) * 128], lhsT=PTq(p_), rhs=Pq(p_),
                                                          start=True, stop=True), reads=RD, writes=[PA_], sig=(pi == 3))
                        bk[("a", q_)] = (pa_, PA_)
                yield
                for q_ in range(2):
                    pb_, PB_ = bk[("b", q_)]
                    k.op("act", lambda e: e.activation(out=PTn[:, q_ * 4:q_ * 4 + 4, :], in_=r4(pb_), func=AF.Copy),
                         reads=[PB_], writes=[PTB[q_]])
                    if kk_ < 5:
                        pa_, PA_ = bk[("a", q_)]
                        k.op("dve", lambda e: e.tensor_copy(out=Pn[:, q_ * 4:q_ * 4 + 4, :], in_=r4(pa_)),
                             reads=[PA_], writes=[PNB[q_]])
                for q_ in range(2):
                    pc_, PC_ = self.pb()
                    for pi in range(4):
                        p_ = q_ * 4 + pi
                        k.op("pe", lambda e: e.matmul(pc_[:, pi * 128:(pi + 1) * 128], lhsT=PTn[:, p_, :], rhs=X[:, p_, :],
                                                      start=True, stop=True), reads=[PTB[q_], XB[q_]], writes=[PC_],
                             sig=(pi == 3))
                    bk[("c", q_)] = (pc_, PC_)
                yield
                for q_ in range(2):
                    pc_, PC_ = bk[("c", q_)]
                    k.op("dve", lambda e: e.tensor_tensor(out=X[:, q_ * 4:q_ * 4 + 4, :], in0=X[:, q_ * 4:q_ * 4 + 4, :],
                                                          in1=r4(pc_), op=ALU.add), reads=[PC_, XB[q_]], writes=[XB[q_]])

        def state_gen(ch, jb):
            c0 = ch * 64
            cs_ = slice(c0, c0 + 64)
            xr_, CB = XR[jb], CHB[jb]
            AM, AMB, X, XB = AM2[jb], AMB2[jb], X2[jb], XB2[jb]
            BEt, KTt, BTB, KTTB, Vbd, VBB = BEt2[jb], KTt2[jb], BTB2[jb], KTTB2[jb], Vbd2[jb], VBB2[jb]
            vs_ = Vs2[jb]
            VB_ = VS2B[jb]
            pw, PW = self.pb()
            pw2, PW2 = self.pb()
            for p_ in range(8):
                k.op("pe", lambda e: e.matmul(pw[:, p_ * 64:(p_ + 1) * 64], lhsT=xr_[:, p_, 0:128], rhs=Sst[:, p_, :],
                                              start=True, stop=True), reads=[CB, SSB], writes=[PW], sig=(p_ == 7))
            for p_ in range(8):
                k.op("pe", lambda e: e.matmul(pw2[:, p_ * 64:(p_ + 1) * 64], lhsT=AM[:, p_, 192:320], rhs=vs_[:, p_, :],
                                              start=True, stop=True), reads=[AMB[p_], VB_], writes=[PW2], sig=(p_ == 7))
            py1, PY1 = self.pb()
            py3, PY3 = self.pb()
            for p_ in range(8):
                k.op("pe", lambda e: e.matmul(py1[:, p_ * 64:(p_ + 1) * 64], lhsT=Sbd[:, p_, :], rhs=xr_[:, p_, 128:192],
                                              start=True, stop=True), reads=[SBB, CB], writes=[PY1], sig=(p_ == 7))
            for p_ in range(8):
                k.op("pe", lambda e: e.matmul(py3[:, p_ * 64:(p_ + 1) * 64], lhsT=Vbd[:, p_, :], rhs=AM[:, p_, 320:384],
                                              start=True, stop=True), reads=[VBB, AMB[p_]], writes=[PY3], sig=(p_ == 7))
            k.op("act", lambda e: e.activation(out=Wsb, in_=r3(pw), func=AF.Copy), reads=[PW], writes=[WSB])
            k.op("dve", lambda e: e.tensor_tensor(out=Wsb, in0=Wsb, in1=r3(pw2), op=ALU.add), reads=[WSB, PW2],
                 writes=[WSB])
            k.op("act", lambda e: e.activation(out=yt[:, :, cs_], in_=r3(py1), func=AF.Copy), reads=[PY1], writes=[YB])
            k.op("dve", lambda e: e.tensor_tensor(out=yt[:, :, cs_], in0=yt[:, :, cs_], in1=r3(py3), op=ALU.add),
                 reads=[YB, PY3], writes=[YB])
            yield
            pu, PU = self.pb()
            for p_ in range(8):
                k.op("pe", lambda e: e.matmul(pu[:, p_ * 64:(p_ + 1) * 64], lhsT=X[:, p_, :], rhs=Wsb[:, p_, :], start=True,
                                              stop=True), reads=[XB[p_ // 4], WSB], writes=[PU], sig=(p_ == 7))
            pu3 = r3(pu)
            k.op("act", lambda e: e.activation(out=Ust, in_=pu3, func=AF.Copy), reads=[PU], writes=[USB])
            for hp in range(2):
                ps_ = slice(hp * 64, hp * 64 + 64)
                k.op("act", lambda e: e.activation(out=Ubd[ps_, :, hp * 64:hp * 64 + 64], in_=pu3[ps_, :, :],
                                                   func=AF.Copy), reads=[PU], writes=[UBB])
            yield
            py2, PY2 = self.pb()
            pS1, PS1 = self.pb()
            pS2, PS2 = self.pb()
            for p_ in range(8):
                k.op("pe", lambda e: e.matmul(pS2[:, p_ * 64:(p_ + 1) * 64], lhsT=KTt[:, p_, :], rhs=vs_[:, p_, :],
                                              start=True, stop=True), reads=[KTTB, VB_], writes=[PS2], sig=(p_ == 7))
            for p_ in range(8):
                k.op("pe", lambda e: e.matmul(pS1[:, p_ * 64:(p_ + 1) * 64], lhsT=BEt[:, p_, :], rhs=Ust[:, p_, :],
                                              start=True, stop=True), reads=[BTB, USB], writes=[PS1], sig=(p_ == 7))
            for p_ in range(8):
                k.op("pe", lambda e: e.matmul(py2[:, p_ * 64:(p_ + 1) * 64], lhsT=Ubd[:, p_, :], rhs=AM[:, p_, 128:192],
                                              start=True, stop=True), reads=[UBB, AMB[p_]], writes=[PY2], sig=(p_ == 7))
            k.op("act", lambda e: e.activation(out=Wsb, in_=r3(pS1), func=AF.Copy), reads=[PS1], writes=[WSB])
            k.op("dve", lambda e: e.tensor_tensor(out=Wsb, in0=Wsb, in1=r3(pS2), op=ALU.add), reads=[WSB, PS2], writes=[WSB])
            k.op("dve", lambda e: e.tensor_tensor(out=Wsb, in0=Wsb, in1=Sst, op=ALU.add), reads=[WSB, SSB], writes=[WSB])
            k.op("dve", lambda e: e.tensor_tensor(out=Sst, in0=Wsb, in1=Gc2[jb].broadcast_to([128, 8, 64]),
                                                  op=ALU.mult), reads=[WSB, GCB[jb]], writes=[SSB])
            for hp in range(2):
                ps_ = slice(hp * 64, hp * 64 + 64)
                k.op("act", lambda e: e.activation(out=Sbd[ps_, :, hp * 64:hp * 64 + 64], in_=Sst[ps_, :, :],
                                                   func=AF.Copy), reads=[SSB], writes=[SBB])
            k.op("dve", lambda e: e.tensor_tensor(out=yt[:, :, cs_], in0=yt[:, :, cs_], in1=r3(py2), op=ALU.add),
                 reads=[YB, PY2], writes=[YB])

        def drain(g):
            for _ in g:
                pass

        def interleave(pre, st):
            pre_done = pre is None
            st_done = False
            while not (pre_done and st_done):
                for _ in range(self.wkv_ratio):
                    if not pre_done:
                        try:
                            next(pre)
                        except StopIteration:
                            pre_done = True
                if not st_done:
                    try:
                        next(st)
                    except StopIteration:
                        st_done = True

        wt = [(0, 0)] + [(1 + w, CTX + NW * w) for w in range(16)]
        order = (wt if not rev else [wt[0]] + wt[:0:-1])[:self.wkv_ntiles]
        fl = lambda a_: a_.rearrange("p f t -> p (f t)")
        rv = (lambda a_: a_[:, ::-1]) if rev else (lambda a_: a_)

        def prologue(wi, t0):
            for (dst, src, key) in ((rt, self.rT, ("rT", wi)), (kk, self.kkT, ("kkT", wi)), (bb, self.bT[d], ("bT", d, wi)),
                                    (kdt, self.kdT[d], ("kdT", d, wi)), (lw, self.lwT[d], ("lwT", d, wi))):
                k.dma("sp", dst, self.fm(src, t0, NW), reads=[self.DB(*key)], writes=[INB])
            k.dma("sp", vst, self.Vst[t0 // 64:t0 // 64 + 4].rearrange("c p f -> p c f"), reads=[self.DB("Vst", wi)],
                  writes=[VB_])
            k.op("dve", lambda e: e.tensor_tensor_scan(out=rv(fl(cum)), data0=rv(rmk), data1=rv(fl(lw)), initial=0.0,
                                                       op0=ALU.mult, op1=ALU.add), reads=[INB, MB, ECB], writes=[ECB])
            k.op("dve", lambda e: e.tensor_tensor(out=lw, in0=cum, in1=lw, op=ALU.subtract), reads=[ECB, INB], writes=[INB])
            k.op("act", lambda e: e.activation(out=lw, in_=lw, func=AF.Exp), reads=[INB], writes=[INB])
            k.op("act", lambda e: e.activation(out=ec, in_=cum, func=AF.Exp), reads=[ECB], writes=[ECB])
            k.op("act", lambda e: e.activation(out=cum, in_=cum, func=AF.Exp, scale=-1.0), reads=[ECB], writes=[ECB])

        chs = list(range(4)) if not rev else list(range(3, -1, -1))
        nchunk = 0
        prologue(*order[0])
        drain(pre_gen(chs[0], 0))
        for oi, (wi, t0) in enumerate(order):
            jbs = [(nchunk + i_) % 2 for i_ in range(5)]
            nchunk += 4
            for i_ in range(4):
                if i_ < 3:
                    nxt = pre_gen(chs[i_ + 1], jbs[i_ + 1])
                elif oi + 1 < len(order):
                    prologue(*order[oi + 1])
                    nxt = pre_gen(chs[0], jbs[4])
                else:
                    nxt = None
                interleave(nxt, state_gen(chs[i_], jbs[i_]))
            k.dma("sp", self.fm(self.yT[d], t0, NW), yt, reads=[YB], writes=[self.DB("yT", d, wi)])
        k.phase_reset()

    def rw_out(self, l, b, last):
        k = self.k
        i = l // 2
        wo = k.sb([128, 8, D], BF16)
        WOB = Buf()
        self.load_w(wo, WOB, self.rw_wo[i], 8)
        bones_f = self.cs[:, 128:256]
        y0 = k.sb([128, 8, 512], F32)
        y1 = k.sb([128, 8, 512], F32)
        bon = k.sb([128, 8, 512], F32)
        gt = k.sb([128, 8, 512], F32)
        xt = k.sb([128, 8, 512], F32)
        yc = k.sb([128, 8, 512], F32)
        rs = k.sb([128, 8, 512], F32)
        ob = k.sb([128, 8, 512], BF16)
        Y0B, Y1B, BNB, GTB, XB, OBB, YCB, RSB = [Buf() for _ in range(8)]

        def bc(name, n):
            return self.V(name, 0, 8).unsqueeze(2).broadcast_to([128, 8, n])
        for ti, (t0, n) in enumerate(TILES):
            if last and ti == 0:
                continue
            mi = 2 if ti == 0 else b
            wis = [0] if ti == 0 else [2 * ti - 1, 2 * ti]
            k.dma("sp", y0[:, :, 0:n], self.fm(self.yT[0], t0, n), reads=[self.DB("yT", 0, w) for w in wis], writes=[Y0B])
            k.dma("sp", y1[:, :, 0:n], self.fm(self.yT[1], t0, n), reads=[self.DB("yT", 1, w) for w in wis], writes=[Y1B])
            k.dma("sp", bon[:, :, 0:n], self.fm(self.bonT, t0, n), reads=[self.DB("bonT", w) for w in wis], writes=[BNB])
            k.dma("sp", gt[:, :, 0:n], self.fm(self.gT, t0, n), reads=[self.DB("gT", w) for w in wis], writes=[GTB])
            k.dma("sp", xt[:, :, 0:n], self.fm(self.xT[b], t0, n), reads=[self.DB("xT", b, ti)], writes=[XB])
            k.op("pool", lambda e: e.tensor_tensor(out=y0[:, :, 0:n], in0=y0[:, :, 0:n], in1=y1[:, :, 0:n], op=ALU.add),
                 reads=[Y0B, Y1B], writes=[Y0B])
            for fc in range(8):
                p, P = self.pb()
                k.op("pe", lambda e: e.matmul(p[:, 0:n], lhsT=bones_f, rhs=y0[:, fc, 0:n], start=True, stop=True),
                     reads=[Y0B, self.CS], writes=[P])
                k.op("dve", lambda e: e.scalar_tensor_tensor(out=yc[:, fc, 0:n], in0=p[:, 0:n], scalar=-1.0 / 64,
                                                             in1=y0[:, fc, 0:n], op0=ALU.mult, op1=ALU.add),
                     reads=[P, Y0B], writes=[YCB])
            k.op("act", lambda e: e.activation(out=y1[:, :, 0:n], in_=yc[:, :, 0:n], func=AF.Square), reads=[YCB, Y1B],
                 writes=[Y1B])
            for fc in range(8):
                p2, P2 = self.pb()
                k.op("pe", lambda e: e.matmul(p2[:, 0:n], lhsT=bones_f, rhs=y1[:, fc, 0:n], start=True, stop=True),
                     reads=[Y1B, self.CS], writes=[P2])
                k.op("act", lambda e: e.activation(out=rs[:, fc, 0:n], in_=p2[:, 0:n], func=AF.Sqrt, bias=GN_EPS,
                                                   scale=1.0 / 64), reads=[P2], writes=[RSB])
            k.op("dve", lambda e: e.reciprocal(out=rs[:, :, 0:n], in_=rs[:, :, 0:n]), reads=[RSB], writes=[RSB])
            k.op("dve", lambda e: e.tensor_tensor(out=yc[:, :, 0:n], in0=yc[:, :, 0:n], in1=rs[:, :, 0:n], op=ALU.mult),
                 reads=[YCB, RSB], writes=[YCB])
            k.op("pool", lambda e: e.tensor_tensor(out=yc[:, :, 0:n], in0=yc[:, :, 0:n], in1=bc("gng_%d" % i, n), op=ALU.mult),
                 reads=[YCB, self.VB], writes=[YCB])
            k.op("pool", lambda e: e.tensor_tensor(out=bon[:, :, 0:n], in0=bon[:, :, 0:n], in1=bc("gnb_%d" % i, n), op=ALU.add),
                 reads=[BNB, self.VB], writes=[BNB])
            k.op("dve", lambda e: e.tensor_tensor(out=yc[:, :, 0:n], in0=yc[:, :, 0:n], in1=bon[:, :, 0:n], op=ALU.add),
                 reads=[YCB, BNB], writes=[YCB])
            k.op("dve", lambda e: e.tensor_tensor(out=ob[:, :, 0:n], in0=yc[:, :, 0:n], in1=gt[:, :, 0:n], op=ALU.mult),
                 reads=[YCB, GTB], writes=[OBB])
            for oc in range(8):
                p, P = self.pb()
                for kc in range(8):
                    k.op("pe", lambda e: e.matmul(p[:, 0:n], lhsT=wo[:, kc, oc * 128:(oc + 1) * 128], rhs=ob[:, kc, 0:n],
                                                  start=(kc == 0), stop=(kc == 7)), reads=[WOB, OBB], writes=[P],
                         sig=(kc == 7))
                k.op("dve", lambda e: e.scalar_tensor_tensor(out=xt[:, oc, 0:n], in0=p[:, 0:n],
                                                             scalar=self.mod[:, 16 + oc, mi:mi + 1], in1=xt[:, oc, 0:n],
                                                             op0=ALU.mult, op1=ALU.add), reads=[P, self.MOD, XB], writes=[XB])
            k.dma("sp", self.fm(self.xT[b], t0, n), xt[:, :, 0:n], reads=[XB], writes=[self.DB("xT", b, ti)])
        k.phase_reset()

    def build(self, nphases=None):
        ph = [lambda: self.setup()]
        for l in self.layers:
            last = (l == DEPTH - 1)
            ph.append(lambda l=l: self.phase_mod(l))
            for b in range(NB):
                if l % 2 == 0:
                    ph.append(lambda l=l, b=b: self.phase_hy_inproj(l, b))
                    ph.append(lambda l=l, b=b: self.phase_rglru(l, b))
                    ph.append(lambda l=l, b=b: self.phase_attn(l, b))
                else:
                    ph.append(lambda l=l, b=b: self.phase_rwkv(l, b))
            ph.append(lambda l=l, last=last: self.phase_mlp(l, last))
        for f in (ph if nphases is None else ph[:nphases]):
            f()
        self.k.finish()
        return self.nc


def host_consts():
    cs = np.zeros((128, 512), np.float32)
    cs[:, 0:128] = np.eye(128, dtype=np.float32)
    bo = np.zeros((128, 128), np.float32)
    bo[0:64, 0:64] = 1.0
    bo[64:128, 64:128] = 1.0
    cs[:, 128:256] = bo
    pw = np.zeros((128, 128), np.float32)
    for blk in range(2):
        for n in range(64):
            pw[blk * 64 + n, blk * 64 + (n + 32) % 64] = 1.0
    cs[:, 256:384] = pw
    rows = SEQ // 64
    row = np.repeat(np.arange(rows, dtype=np.float32), 64)
    col = np.tile(np.arange(64, dtype=np.float32), rows)
    inv = (np.float32(10000.0) ** (-np.arange(0, 32, 2, dtype=np.float32) / np.float32(32))).astype(np.float32)
    ang = np.concatenate([row[:, None] * inv, col[:, None] * inv], axis=-1).astype(np.float32)
    c = np.cos(ang).astype(np.float32).T
    s_ = np.sin(ang).astype(np.float32).T
    cos64 = np.concatenate([c, c], 0)
    sin64 = np.concatenate([-s_, s_], 0)
    cosT = np.ascontiguousarray(np.concatenate([cos64, cos64], 0))
    sinT = np.ascontiguousarray(np.concatenate([sin64, sin64], 0))
    return cs, cosT, sinT


def fmaj(v):
    return np.ascontiguousarray(np.asarray(v, np.float32).reshape(-1, 128).T)


def host_vb(inp):
    vb = np.zeros((128, NVB), np.float32)

    def put(name, arr):
        arr = np.asarray(arr, np.float32)
        vb[:, VBM[name]:VBM[name] + arr.shape[1]] = arr
    for l in range(DEPTH):
        put("ng0_%d" % l, fmaj(inp["norm_g"][l, 0]))
        put("ng1_%d" % l, fmaj(inp["norm_g"][l, 1]))
        put("adab_%d" % l, fmaj(inp["ada_b"][l]))
    for i in range(2):
        gq = inp["hy_q_norm"][i][PERM]
        gk = inp["hy_k_norm"][i][PERM]
        put("gq_%d" % i, np.concatenate([gq, gq])[:, None])
        put("gk_%d" % i, np.concatenate([gk, gk])[:, None])
        put("convw_%d" % i, np.concatenate([fmaj(inp["hy_conv_w"][i][j]) for j in range(4)], 1))
        put("convb_%d" % i, fmaj(inp["hy_conv_b"][i]))
        put("gateb_%d" % i, np.concatenate([fmaj(inp["hy_gate_b"][i][d][g]) for d in range(2) for g in range(2)], 1))
        put("lam_%d" % i, np.concatenate([fmaj(inp["hy_lam"][i][d]) for d in range(2)], 1))
    for i in range(2):
        put("mu_%d" % i, np.concatenate([fmaj(inp["rw_mu"][i][j]) for j in range(6)], 1))
        put("kk_%d" % i, fmaj(inp["rw_k_k"][i]))
        put("ka_%d" % i, fmaj(inp["rw_k_a"][i]))
        put("rk_%d" % i, fmaj(inp["rw_r_k"][i].reshape(-1)))
        put("gng_%d" % i, fmaj(inp["rw_gn_g"][i]))
        put("gnb_%d" % i, fmaj(inp["rw_gn_b"][i]))
        put("lb_%d" % i, np.concatenate([fmaj(inp["rw_lora_bias"][i][d][j]) for d in range(2) for j in range(2)], 1))
    return vb


def host_shared(inp):
    sh = {}
    cs, cosT, sinT = host_consts()
    sh["consts"], sh["cosT"], sh["sinT"] = cs, cosT, sinT
    sh["vb"] = host_vb(inp)
    sh["ada_w"] = np.ascontiguousarray(inp["ada_w"], np.float32)
    sh["mlp_w1"] = np.ascontiguousarray(inp["mlp_w1"], np.float32)
    sh["mlp_w2"] = np.ascontiguousarray(inp["mlp_w2"], np.float32)
    win = inp["hy_w_in"]
    cols = []
    for h in range(8):
        cols.append(h * 64 + PERM)
    for kv in range(2):
        cols.append(512 + kv * 64 + PERM)
        cols.append(512 + kv * 64 + PERM)
    cols.append(np.arange(640, 768))
    cols.append(np.arange(768, 1792))
    cols = np.concatenate(cols)
    sh["hy_win"] = np.ascontiguousarray(win[:, :, cols], np.float32)
    sh["hy_wout"] = np.ascontiguousarray(inp["hy_w_out"], np.float32)
    gw = inp["hy_gate_w"]
    bd = np.zeros((2, 2, 2, 4, 128, 128), np.float32)
    for c in range(4):
        bd[:, :, :, c, 0:64, 0:64] = gw[:, :, :, 2 * c]
        bd[:, :, :, c, 64:128, 64:128] = gw[:, :, :, 2 * c + 1]
    sh["hy_gw"] = bd
    sh["rw_wrkv"] = np.ascontiguousarray(inp["rw_w_rkv"], np.float32)
    st = np.concatenate([np.arange((2 * p + hp) * 64, (2 * p + hp) * 64 + 64) for hp in range(2) for p in range(8)])
    sh["rw_wvst"] = np.ascontiguousarray(inp["rw_w_rkv"][:, 2][:, :, st], np.float32)
    sh["rw_wo"] = np.ascontiguousarray(inp["rw_w_o"], np.float32)
    ldn = inp["rw_lora_down"]
    sh["rw_ld"] = np.ascontiguousarray(np.concatenate([ldn[:, d, j] for d in range(2) for j in range(2)], axis=-1), np.float32)
    lup = inp["rw_lora_up"]
    sh["rw_lu"] = np.ascontiguousarray(np.stack([lup[:, d, j] for d in range(2) for j in range(2)], axis=1), np.float32)
    sh["rw_gd"] = np.ascontiguousarray(inp["rw_gate_down"], np.float32)
    sh["rw_gu"] = np.ascontiguousarray(inp["rw_gate_up"], np.float32)
    ii = np.arange(64)
    up = (ii[None, :] > ii[:, None]).astype(np.float32)
    le = (ii[:, None] <= ii[None, :]).astype(np.float32)
    lo_ = (ii[None, :] < ii[:, None]).astype(np.float32)

    def bdm(m):
        o = np.zeros((128, 128), np.float32)
        o[0:64, 0:64] = m
        o[64:128, 64:128] = m
        return o

    def stk(m):
        return np.concatenate([m, m], 0)
    wm = np.zeros((128, 2, 512), np.float32)
    for d, (u_, l_, s_) in enumerate(((up, lo_, le), (up.T, lo_.T, le.T))):
        wm[:, d, 0:128] = bdm(u_)
        wm[:, d, 128:192] = stk(s_)
        wm[:, d, 192:320] = bdm(u_)
        wm[:, d, 320:384] = stk(s_)
        wm[:, d, 384:512] = bdm(l_)
    sh["wmask"] = wm
    rm = np.ones((128, 2, 2048), np.float32)
    tt = np.arange(2048)
    rm[:, 0, tt % 64 == 0] = 0.0
    rm[:, 1, tt % 64 == 63] = 0.0
    sh["rmask"] = rm
    return sh


_CACHE = {}


def kernel(**inp):
    inp = {k_: np.asarray(v) for k_, v in inp.items()}
    sh = host_shared(inp)
    if "nc" not in _CACHE:
        _CACHE["nc"] = Prog().build()
    nc = _CACHE["nc"]
    in_maps = []
    for core in range(8):
        bs = [2 * core, 2 * core + 1]
        m = dict(sh)
        m["xT_in"] = np.ascontiguousarray(np.stack([inp["x"][b].T for b in bs]), np.float32)
        m["ctxT_in"] = np.ascontiguousarray(np.stack([inp["ctx"][b].T for b in bs]), np.float32)
        cv = np.stack([inp["c"][bs[0]], inp["c"][bs[1]], inp["c_ctx"]], 0)
        m["cT"] = np.ascontiguousarray(cv.reshape(3, 8, 128).transpose(2, 1, 0), np.float32)
        in_maps.append(m)
    res = run_bass_kernel_spmd(nc, in_maps, core_ids=list(range(8)))
    out = np.empty((16, SEQ, D), np.float32)
    for core in range(8):
        o = res.results[core]["outT"]
        for j in range(NB):
            out[2 * core + j] = o[j].T
    return out
```

```python
import math
import numpy as np
import concourse.bass as bass
import concourse.mybir as mybir
from concourse.bass_utils import run_bass_kernel_spmd

F32 = mybir.dt.float32
BF16 = mybir.dt.bfloat16
AF = mybir.ActivationFunctionType
ALU = mybir.AluOpType

D = 1024
SEQ = 4096
CTX = 256
T = SEQ + CTX
NB = 2
DEPTH = 4
EPS = 1e-6
DFF = 4096
TILES = [(0, CTX)] + [(CTX + 512 * i, 512) for i in range(8)]
WTILES = [(0, CTX)] + [(CTX + 256 * i, 256) for i in range(16)]
GELU_C = 2.0 * math.sqrt(2.0 / math.pi)
DECAY_SCALE = math.exp(-0.5)
GN_EPS = 64e-5


class Buf:
    __slots__ = ("w", "r")

    def __init__(self):
        self.w = None
        self.r = {}


class KB:
    NDMA = 40

    def __init__(self, nc):
        self.nc = nc
        self.eng = dict(pe=nc.tensor, act=nc.scalar, dve=nc.vector, pool=nc.gpsimd, sp=nc.sync)
        self.sems = {}
        self.cnt = {}
        for e in ("pe", "act", "dve", "pool"):
            self.sems[e] = nc.alloc_semaphore("s_" + e)
            self.cnt[e] = 0
        self.dsem = [nc.alloc_semaphore("d%d" % i) for i in range(self.NDMA)]
        self.dval = [0] * self.NDMA
        self.drr = 0
        self.waited = {e: {} for e in self.eng}
        self.sb_off = 16512
        self.sb_base = 16512
        self.nalloc = 0
        self.ninst = 0
        self.pending = []
        self.rec = None

    def sb(self, shape, dt=F32, name=None):
        self.nalloc += 1
        nm = "%s_%d" % (name or "t", self.nalloc)
        n = 1
        for s_ in shape[1:]:
            n *= s_
        nbytes = n * (4 if dt == F32 else 2)
        nbytes = (nbytes + 63) // 64 * 64
        off = self.sb_off
        self.sb_off += nbytes
        assert self.sb_off <= 229376, ("SBUF overflow", nm, self.sb_off)
        return self.nc.alloc_sbuf_tensor_at(nm, list(shape), dt, offset=off).ap()

    def phase_reset(self):
        self.barrier()
        self.sb_off = self.sb_base

    def persist_mark(self):
        self.sb_base = self.sb_off

    def _semh(self, key):
        return self.sems[key] if isinstance(key, str) else self.dsem[key[1]]

    def _wait(self, e, key, val, raw=False):
        if key == e and (not raw or e == "pe"):
            return
        w = self.waited[e]
        if w.get(key, 0) >= val:
            return
        w[key] = val
        self.pending.append((key, val))

    def _take(self):
        p = self.pending
        self.pending = []
        d = {}
        for k_, v_ in p:
            if d.get(k_, 0) < v_:
                d[k_] = v_
        return list(d.items())

    def _deps(self, e, reads, writes):
        for b in reads:
            if b.w is not None:
                self._wait(e, b.w[0], b.w[1], raw=True)
        for b in writes:
            if b.w is not None:
                self._wait(e, b.w[0], b.w[1])
            for k_, v_ in b.r.items():
                self._wait(e, k_, v_)

    def _mark(self, tok, reads, writes):
        k_, v_ = tok
        for b in reads:
            if b.r.get(k_, 0) < v_:
                b.r[k_] = v_
        for b in writes:
            b.w = tok
            b.r = {}

    def op(self, e, ins_fn, reads=(), writes=(), sig=True):
        self._deps(e, reads, writes)
        items = self._take()
        if self.rec is not None:
            self.rec.append((e, list(items), e if sig else None, 1))
        last = items.pop() if items else None
        for k_, v_ in items:
            self.eng[e].wait_ge(self._semh(k_), v_)
            self.ninst += 1
        ins = ins_fn(self.eng[e])
        if last is not None:
            ins._wait_ge(self._semh(last[0]), last[1])
        self.ninst += 1
        if sig:
            self.cnt[e] += 1
            ins.then_inc(self.sems[e], 1)
            tok = (e, self.cnt[e])
        else:
            tok = (e, self.cnt[e] + 1)
        self._mark(tok, reads, writes)
        return tok

    def dma(self, q, out, in_, reads=(), writes=(), **kw):
        i = self.drr
        self.drr = (self.drr + 1) % self.NDMA
        key = ("d", i)
        if self.dval[i] > 0:
            self._wait(q, key, self.dval[i])
        self._deps(q, reads, writes)
        its_ = self._take()
        if self.rec is not None:
            self.rec.append((q, list(its_), key, 16))
        for k_, v_ in its_:
            self.eng[q].wait_ge(self._semh(k_), v_)
            self.ninst += 1
        self.eng[q].dma_start(out=out, in_=in_, **kw).then_inc(self.dsem[i], 16)
        self.ninst += 1
        self.dval[i] += 16
        tok = (key, self.dval[i])
        self._mark(tok, reads, writes)
        return tok

    def barrier(self):
        for e in self.eng:
            for o in ("pe", "act", "dve", "pool"):
                if self.cnt[o] > 0:
                    self._wait(e, o, self.cnt[o])
            for i in range(self.NDMA):
                if self.dval[i] > 0:
                    self._wait(e, ("d", i), self.dval[i])
            its_ = self._take()
            if self.rec is not None:
                self.rec.append((e, list(its_), None, 0))
            for k_, v_ in its_:
                self.eng[e].wait_ge(self._semh(k_), v_)
                self.ninst += 1

    def finish(self):
        self.barrier()


def vb_layout():
    m = {}
    off = 0

    def add(name, n):
        nonlocal off
        m[name] = off
        off += n
    for l in range(DEPTH):
        add("ng0_%d" % l, 8)
        add("ng1_%d" % l, 8)
        add("adab_%d" % l, 48)
    for i in range(2):
        add("gq_%d" % i, 1)
        add("gk_%d" % i, 1)
        add("convw_%d" % i, 16)
        add("convb_%d" % i, 4)
        add("gateb_%d" % i, 16)
        add("lam_%d" % i, 8)
    for i in range(2):
        add("mu_%d" % i, 48)
        add("kk_%d" % i, 8)
        add("ka_%d" % i, 8)
        add("rk_%d" % i, 8)
        add("gng_%d" % i, 8)
        add("gnb_%d" % i, 8)
        add("lb_%d" % i, 32)
    return m, off


VBM, NVB = vb_layout()
PERM = np.concatenate([np.arange(0, 64, 2), np.arange(1, 64, 2)])


class Prog:
    def __init__(self, debug=(), nlayers=DEPTH, ext_in=()):
        self.debug = set(debug)
        self.ext_in = set(ext_in)
        self.nlayers = nlayers
        self.layers = list(range(nlayers))
        self.rw_stop = 99
        self.wkv_stage = 99
        self.wkv_ntiles = 99
        self.wkv_ratio = 1
        nc = self.nc = bass.Bass("TRN2", target_bir_lowering=False)
        k = self.k = KB(nc)
        self.dbuf = {}
        di = self.din
        self.xin = di("xT_in", [NB, D, SEQ])
        self.cin = di("ctxT_in", [NB, D, CTX])
        self.cT = di("cT", [128, 8, 3])
        self.vbd = di("vb", [128, NVB])
        self.ada_w = di("ada_w", [DEPTH, D, 6 * D])
        self.w1 = di("mlp_w1", [DEPTH, D, DFF])
        self.w2 = di("mlp_w2", [DEPTH, DFF, D])
        self.hy_win = di("hy_win", [2, D, 1920])
        self.hy_wout = di("hy_wout", [2, D, D])
        self.hy_gw = di("hy_gw", [2, 2, 2, 4, 128, 128])
        self.cst = di("consts", [128, 512])
        self.cosd = di("cosT", [128, SEQ])
        self.sind = di("sinT", [128, SEQ])
        self.rw_wrkv = di("rw_wrkv", [2, 3, D, D])
        self.rw_wvst = di("rw_wvst", [2, D, D])
        self.rw_wo = di("rw_wo", [2, D, D])
        self.rw_ld = di("rw_ld", [2, D, 256])
        self.rw_lu = di("rw_lu", [2, 4, 64, D])
        self.rw_gd = di("rw_gd", [2, D, 128])
        self.rw_gu = di("rw_gu", [2, 128, D])
        self.wmask = di("wmask", [128, 2, 512])
        self.rmask = di("rmask", [128, 2, 2048])
        self.out = nc.dram_tensor("outT", [NB, D, SEQ], F32, kind="ExternalOutput").ap()
        self.hnT = self.dscr("hnT", [D, T])
        self.rT = self.dscr("rT", [D, T])
        self.kkT = self.dscr("kkT", [D, T])
        self.kdT = self.dscr("kdT", [2, D, T])
        self.bT = self.dscr("bT", [2, D, T])
        self.lwT = self.dscr("lwT", [2, D, T])
        self.gT = self.dscr("gT", [D, T])
        self.bonT = self.dscr("bonT", [D, T])
        self.Vst = self.dscr("Vst", [T // 64, 128, 512])
        self.yT = self.dscr("yT", [2, D, T])
        self.xT = self.dscr("xT", [NB, D, T])
        self.qT = self.dscr("qT", [512, T], BF16)
        self.kT2 = self.dscr("kT2", [256, T], BF16)
        self.vtok = self.dscr("vtok", [T // 128, 128, 384], BF16)
        self.xr = self.dscr("xr", [512, T])
        self.gg = self.dscr("gg", [512, T], BF16)
        self.rec = self.dscr("rec", [512, T], BF16)
        self.ps = [nc.alloc_psum_tensor("ps%d" % i, [128, 512], F32).ap() for i in range(8)]
        self.PS = [Buf() for _ in range(8)]
        self.prr = 0
        self.vb = k.sb([128, NVB], F32, "vb")
        self.VB = Buf()
        self.cs = k.sb([128, 512], F32, "cs")
        self.CS = Buf()
        self.ident = self.cs[:, 0:128]
        self.ones_bf = k.sb([128, 128], BF16, "ones")
        self.bones_bf = k.sb([128, 128], BF16, "bones")
        self.pswap_bf = k.sb([128, 128], BF16, "pswap")
        self.scT = k.sb([128, 8, 3], BF16, "scT")
        self.mod = k.sb([128, 48, 3], F32, "mod")
        self.A1 = k.sb([128, 8, 3], F32, "A1")
        self.A2 = k.sb([128, 8, 3], F32, "A2")
        self.MOD = Buf()
        self.CONST = Buf()
        k.persist_mark()

    def din(self, name, shape, dt=F32):
        return self.nc.dram_tensor(name, list(shape), dt, kind="ExternalInput").ap()

    def dscr(self, name, shape, dt=F32):
        kind = "ExternalOutput" if name in self.debug else "Internal"
        if name in getattr(self, "ext_in", ()):
            kind = "ExternalInput"
        return self.nc.dram_tensor(name, list(shape), dt, kind=kind).ap()

    def DB(self, *key):
        b = self.dbuf.get(key)
        if b is None:
            b = self.dbuf[key] = Buf()
        return b

    def pb(self):
        i = self.prr
        self.prr = (self.prr + 1) % 8
        return self.ps[i], self.PS[i]

    @staticmethod
    def fm(ap2d, t0, n):
        return ap2d.rearrange("(fc p) t -> p fc t", p=128)[:, :, t0:t0 + n]

    def V(self, name, j=0, n=1):
        o = VBM[name] + j
        return self.vb[:, o:o + n]

    def setup(self):
        k = self.k
        k.dma("sp", self.vb, self.vbd, writes=[self.VB])
        k.dma("sp", self.cs, self.cst, writes=[self.CS])
        k.op("dve", lambda e: e.memset(self.ones_bf, 1.0), writes=[self.CONST])
        k.op("act", lambda e: e.activation(out=self.bones_bf, in_=self.cs[:, 128:256], func=AF.Copy),
             reads=[self.CS], writes=[self.CONST])
        k.op("act", lambda e: e.activation(out=self.pswap_bf, in_=self.cs[:, 256:384], func=AF.Copy),
             reads=[self.CS], writes=[self.CONST])
        ct = k.sb([128, 8, 3], F32)
        sg = k.sb([128, 8, 3], F32)
        CTB = Buf()
        k.dma("sp", ct, self.cT, writes=[CTB])
        k.op("act", lambda e: e.activation(out=sg, in_=ct, func=AF.Sigmoid), reads=[CTB], writes=[CTB])
        k.op("dve", lambda e: e.tensor_tensor(out=self.scT, in0=ct, in1=sg, op=ALU.mult), reads=[CTB],
             writes=[self.CONST])
        for b in range(NB):
            k.dma("sp", self.xT[b, :, 0:CTX], self.cin[b], writes=[self.DB("xT", b, 0)])
            for ti in range(1, 9):
                t0, n = TILES[ti]
                k.dma("sp", self.xT[b, :, t0:t0 + n], self.xin[b, :, t0 - CTX:t0 - CTX + n],
                      writes=[self.DB("xT", b, ti)])
        k.phase_reset()

    def phase_mod(self, l):
        k = self.k
        wa = k.sb([128, 8, 3072], BF16)
        WA = Buf()
        pm, PM = self.pb()
        for half in range(2):
            src = self.ada_w[l].rearrange("(kc p) n -> p kc n", p=128)[:, :, half * 3072:(half + 1) * 3072]
            for kc in range(8):
                k.dma("pool", wa[:, kc, :], src[:, kc, :], writes=[WA])
            for j in range(24):
                jj = half * 24 + j
                for kc in range(8):
                    k.op("pe", lambda e: e.matmul(pm[:, jj * 4:jj * 4 + 3], lhsT=wa[:, kc, j * 128:(j + 1) * 128],
                                                  rhs=self.scT[:, kc, :], start=(kc == 0), stop=(kc == 7)),
                         reads=[WA, self.CONST], writes=[PM], sig=(kc == 7))
        pmv = pm[:, 0:192].rearrange("p (j f) -> p j f", f=4)[:, :, 0:3]
        bias = self.V("adab_%d" % l, 0, 48).unsqueeze(2).broadcast_to([128, 48, 3])
        k.op("dve", lambda e: e.tensor_tensor(out=self.mod, in0=pmv, in1=bias, op=ALU.add),
             reads=[PM, self.VB], writes=[self.MOD])
        for (A, gname, sc0) in ((self.A1, "ng0_%d" % l, 8), (self.A2, "ng1_%d" % l, 32)):
            g = self.V(gname, 0, 8).unsqueeze(2).broadcast_to([128, 8, 3])
            k.op("dve", lambda e: e.scalar_tensor_tensor(out=A, in0=self.mod[:, sc0:sc0 + 8, :], scalar=1.0, in1=g,
                                                         op0=ALU.add, op1=ALU.mult),
                 reads=[self.MOD, self.VB], writes=[self.MOD])
        k.phase_reset()

    def norm_tile(self, xt, XTB, out, OUTB, A, Bsh, mi, n, W):
        k = self.k
        sq, rstd, tmp = W["sq"], W["rstd"], W["tmp"]
        k.op("act", lambda e: e.activation(out=sq[:, :, 0:n], in_=xt[:, :, 0:n], func=AF.Square),
             reads=[XTB], writes=[W["SQ"]])
        pn, PN = self.pb()
        for fc in range(8):
            k.op("pe", lambda e: e.matmul(pn[:, 0:n], lhsT=self.ones_bf, rhs=sq[:, fc, 0:n], start=(fc == 0),
                                          stop=(fc == 7)), reads=[W["SQ"], self.CONST], writes=[PN], sig=(fc == 7))
        k.op("act", lambda e: e.activation(out=rstd[:, 0:n], in_=pn[:, 0:n], func=AF.Sqrt, bias=EPS, scale=1.0 / D),
             reads=[PN], writes=[W["RSTD"]])
        k.op("dve", lambda e: e.reciprocal(out=rstd[:, 0:n], in_=rstd[:, 0:n]), reads=[W["RSTD"]], writes=[W["RSTD"]])
        for fc in range(8):
            j = fc % 2
            k.op("dve", lambda e: e.tensor_tensor(out=tmp[:, j, 0:n], in0=xt[:, fc, 0:n], in1=rstd[:, 0:n],
                                                  op=ALU.mult), reads=[XTB, W["RSTD"]], writes=[W["TMP"][j]])
            k.op("act", lambda e: e.activation(out=out[:, fc, 0:n], in_=tmp[:, j, 0:n], func=AF.Identity,
                                               bias=Bsh[:, fc, mi:mi + 1], scale=A[:, fc, mi:mi + 1]),
                 reads=[W["TMP"][j], self.MOD], writes=[OUTB])

    def norm_work(self):
        k = self.k
        return dict(sq=k.sb([128, 8, 512], BF16), rstd=k.sb([128, 512], F32), tmp=k.sb([128, 2, 512], F32),
                    SQ=Buf(), RSTD=Buf(), TMP=[Buf(), Buf()])

    def load_w(self, dst, DSTB, src2d, nkc, split=1):
        v = src2d.rearrange("(kc p) n -> p kc n", p=128)
        for kc in range(nkc):
            self.k.dma("pool", dst[:, kc, :], v[:, kc, :], writes=[DSTB])

    def phase_mlp(self, l, last):
        k = self.k
        w1 = k.sb([128, 8, DFF], BF16)
        w2 = k.sb([128, 32, D], BF16)
        W1B, W2B = Buf(), Buf()
        self.load_w(w1, W1B, self.w1[l], 8)
        self.load_w(w2, W2B, self.w2[l], 32)
        W = self.norm_work()
        xt = k.sb([128, 8, 512], F32)
        XTB = Buf()
        hn = k.sb([128, 8, 512], BF16)
        HNB = Buf()
        h1 = k.sb([128, 32, 512], BF16)
        H1B = [Buf() for _ in range(32)]
        rl = k.sb([128, 2, 512], BF16)
        RLB = [Buf(), Buf()]
        for b, (ti, (t0, n)) in [(b_, t_) for b_ in range(NB) for t_ in enumerate(TILES)]:
            if last and ti == 0:
                continue
            mi = 2 if ti == 0 else b
            k.dma("sp", xt[:, :, 0:n], self.fm(self.xT[b], t0, n), reads=[self.DB("xT", b, ti)], writes=[XTB])
            self.norm_tile(xt, XTB, hn, HNB, self.A2, self.mod[:, 24:32, :], mi, n, W)
            for oc in range(32):
                p, P = self.pb()
                for kc in range(8):
                    k.op("pe", lambda e: e.matmul(p[:, 0:n], lhsT=w1[:, kc, oc * 128:(oc + 1) * 128], rhs=hn[:, kc, 0:n],
                                                  start=(kc == 0), stop=(kc == 7)),
                         reads=[W1B, HNB], writes=[P], sig=(kc == 7))
                j = oc % 2
                k.op("act", lambda e: e.activation(out=rl[:, j, 0:n], in_=p[:, 0:n], func=AF.Relu),
                     reads=[P], writes=[RLB[j]])
                k.op("pool", lambda e: e.tensor_tensor(out=h1[:, oc, 0:n], in0=rl[:, j, 0:n], in1=rl[:, j, 0:n],
                                                       op=ALU.mult), reads=[RLB[j]], writes=[H1B[oc]])
            for oc in range(8):
                p, P = self.pb()
                for kc in range(32):
                    k.op("pe", lambda e: e.matmul(p[:, 0:n], lhsT=w2[:, kc, oc * 128:(oc + 1) * 128], rhs=h1[:, kc, 0:n],
                                                  start=(kc == 0), stop=(kc == 31)),
                         reads=[W2B, H1B[kc]], writes=[P], sig=(kc == 31))
                k.op("dve", lambda e: e.scalar_tensor_tensor(out=xt[:, oc, 0:n], in0=p[:, 0:n],
                                                             scalar=self.mod[:, 40 + oc, mi:mi + 1],
                                                             in1=xt[:, oc, 0:n], op0=ALU.mult, op1=ALU.add),
                     reads=[P, self.MOD, XTB], writes=[XTB])
            if last:
                k.dma("sp", self.out[b].rearrange("(fc p) t -> p fc t", p=128)[:, :, t0 - CTX:t0 - CTX + n],
                      xt[:, :, 0:n], reads=[XTB], writes=[self.DB("out", b, ti)])
            else:
                k.dma("sp", self.fm(self.xT[b], t0, n), xt[:, :, 0:n], reads=[XTB], writes=[self.DB("xT", b, ti)])
        k.phase_reset()

    def phase_hy_inproj(self, l, b):
        k = self.k
        i = l // 2
        win = k.sb([128, 8, 1920], BF16)
        WB = Buf()
        self.load_w(win, WB, self.hy_win[i], 8)
        W = self.norm_work()
        xt = [k.sb([128, 8, 512], F32) for _ in range(2)]
        XTB = [Buf(), Buf()]
        hn = k.sb([128, 8, 512], BF16)
        HNB = Buf()
        cs_t = k.sb([128, 2, 512], F32)
        CSB = Buf()
        sq = k.sb([128, 512], BF16)
        SQB = Buf()
        rs = k.sb([128, 512], F32)
        RSB = Buf()
        xn = k.sb([128, 512], BF16)
        XNB = Buf()
        t1 = k.sb([128, 512], F32)
        t2 = k.sb([128, 512], F32)
        T1B, T2B = Buf(), Buf()
        qk = k.sb([128, 6, 512], BF16)
        QKB = Buf()
        vt = k.sb([128, 4, 384], BF16)
        VTB = Buf()
        xro = k.sb([128, 4, 512], F32)
        XRB = Buf()
        ggo = k.sb([128, 4, 512], BF16)
        GGB = Buf()
        z2 = k.sb([128, 512], F32)
        Z2B = Buf()
        k.op("dve", lambda e: e.memset(vt, 1.0), writes=[VTB])
        for ti, (t0, n) in enumerate(TILES):
            mi = 2 if ti == 0 else b
            x_ = xt[ti % 2]
            XB = XTB[ti % 2]
            k.dma("sp", x_[:, :, 0:n], self.fm(self.xT[b], t0, n), reads=[self.DB("xT", b, ti)], writes=[XB])
            if ti > 0:
                k.dma("sp", cs_t[:, 0, 0:n], self.cosd[:, t0 - CTX:t0 - CTX + n], writes=[CSB])
                k.dma("sp", cs_t[:, 1, 0:n], self.sind[:, t0 - CTX:t0 - CTX + n], writes=[CSB])
            self.norm_tile(x_, XB, hn, HNB, self.A1, self.mod[:, 0:8, :], mi, n, W)
            for oc in range(6):
                p, P = self.pb()
                for kc in range(8):
                    k.op("pe", lambda e: e.matmul(p[:, 0:n], lhsT=win[:, kc, oc * 128:(oc + 1) * 128], rhs=hn[:, kc, 0:n],
                                                  start=(kc == 0), stop=(kc == 7)), reads=[WB, HNB], writes=[P],
                         sig=(kc == 7))
                k.op("act", lambda e: e.activation(out=sq[:, 0:n], in_=p[:, 0:n], func=AF.Square), reads=[P], writes=[SQB])
                p2, P2 = self.pb()
                k.op("pe", lambda e: e.matmul(p2[:, 0:n], lhsT=self.bones_bf, rhs=sq[:, 0:n], start=True, stop=True),
                     reads=[SQB, self.CONST], writes=[P2])
                k.op("act", lambda e: e.activation(out=rs[:, 0:n], in_=p2[:, 0:n], func=AF.Sqrt, bias=EPS,
                                                   scale=1.0 / 64), reads=[P2], writes=[RSB])
                k.op("dve", lambda e: e.reciprocal(out=rs[:, 0:n], in_=rs[:, 0:n]), reads=[RSB], writes=[RSB])
                g = self.V("gq_%d" % i) if oc < 4 else self.V("gk_%d" % i)
                dst = qk[:, oc, 0:n] if ti == 0 else xn[:, 0:n]
                DSTB = QKB if ti == 0 else XNB
                k.op("dve", lambda e: e.scalar_tensor_tensor(out=dst, in0=p[:, 0:n], scalar=g, in1=rs[:, 0:n],
                                                             op0=ALU.mult, op1=ALU.mult),
                     reads=[P, RSB, self.VB], writes=[DSTB])
                if ti > 0:
                    p3, P3 = self.pb()
                    k.op("pe", lambda e: e.matmul(p3[:, 0:n], lhsT=self.pswap_bf, rhs=xn[:, 0:n], start=True, stop=True),
                         reads=[XNB, self.CONST], writes=[P3])
                    k.op("pool", lambda e: e.tensor_tensor(out=t1[:, 0:n], in0=xn[:, 0:n], in1=cs_t[:, 0, 0:n],
                                                           op=ALU.mult), reads=[XNB, CSB], writes=[T1B])
                    k.op("dve", lambda e: e.tensor_tensor(out=t2[:, 0:n], in0=p3[:, 0:n], in1=cs_t[:, 1, 0:n],
                                                          op=ALU.mult), reads=[P3, CSB], writes=[T2B])
                    k.op("dve", lambda e: e.tensor_tensor(out=qk[:, oc, 0:n], in0=t1[:, 0:n], in1=t2[:, 0:n],
                                                          op=ALU.add), reads=[T1B, T2B], writes=[QKB])
            k.dma("sp", self.fm(self.qT, t0, n), qk[:, 0:4, 0:n], reads=[QKB], writes=[self.DB("qT", ti)])
            k.dma("sp", self.fm(self.kT2, t0, n), qk[:, 4:6, 0:n], reads=[QKB], writes=[self.DB("kT2", ti)])
            nst = n // 128
            for st in range(nst):
                p, P = self.pb()
                for kc in range(8):
                    k.op("pe", lambda e: e.matmul(p[:, 0:128], lhsT=hn[:, kc, st * 128:(st + 1) * 128],
                                                  rhs=win[:, kc, 768:896], start=(kc == 0), stop=(kc == 7)),
                         reads=[WB, HNB], writes=[P], sig=(kc == 7))
                vv = vt[:, st, :].rearrange("p (h c) -> p h c", c=192)[:, :, 64:128]
                k.op("act", lambda e: e.activation(out=vv, in_=p[:, 0:128].rearrange("p (h c) -> p h c", c=64),
                                                   func=AF.Copy), reads=[P], writes=[VTB])
            c0 = t0 // 128
            k.dma("sp", self.vtok[c0:c0 + nst].rearrange("c p f -> p c f"), vt[:, 0:nst, :], reads=[VTB],
                  writes=[self.DB("vtok", ti)])
            for oc in range(4):
                p, P = self.pb()
                for kc in range(8):
                    k.op("pe", lambda e: e.matmul(p[:, 0:n], lhsT=win[:, kc, 896 + oc * 128:896 + (oc + 1) * 128],
                                                  rhs=hn[:, kc, 0:n], start=(kc == 0), stop=(kc == 7)),
                         reads=[WB, HNB], writes=[P], sig=(kc == 7))
                k.op("act", lambda e: e.activation(out=xro[:, oc, 0:n], in_=p[:, 0:n], func=AF.Copy), reads=[P],
                     writes=[XRB])
            k.dma("sp", self.fm(self.xr, t0, n), xro[:, :, 0:n], reads=[XRB], writes=[self.DB("xr", ti)])
            for oc in range(4):
                p, P = self.pb()
                for kc in range(8):
                    k.op("pe", lambda e: e.matmul(p[:, 0:n], lhsT=win[:, kc, 1408 + oc * 128:1408 + (oc + 1) * 128],
                                                  rhs=hn[:, kc, 0:n], start=(kc == 0), stop=(kc == 7)),
                         reads=[WB, HNB], writes=[P], sig=(kc == 7))
                self.gelu_from_psum(p, P, ggo[:, oc, 0:n], GGB, n, z2, Z2B, t1, T1B)
            k.dma("sp", self.fm(self.gg, t0, n), ggo[:, :, 0:n], reads=[GGB], writes=[self.DB("gg", ti)])
        k.phase_reset()

    def gelu_from_psum(self, p, P, dst, DSTB, n, z2, Z2B, t1, T1B):
        k = self.k
        k.op("act", lambda e: e.activation(out=z2[:, 0:n], in_=p[:, 0:n], func=AF.Square), reads=[P], writes=[Z2B])
        k.op("dve", lambda e: e.tensor_scalar(out=z2[:, 0:n], in0=z2[:, 0:n], scalar1=0.044715, scalar2=1.0,
                                              op0=ALU.mult, op1=ALU.add), reads=[Z2B], writes=[Z2B])
        k.op("dve", lambda e: e.tensor_tensor(out=z2[:, 0:n], in0=z2[:, 0:n], in1=p[:, 0:n], op=ALU.mult),
             reads=[Z2B, P], writes=[Z2B])
        k.op("act", lambda e: e.activation(out=t1[:, 0:n], in_=z2[:, 0:n], func=AF.Sigmoid, scale=GELU_C),
             reads=[Z2B], writes=[T1B])
        k.op("dve", lambda e: e.tensor_tensor(out=dst, in0=t1[:, 0:n], in1=p[:, 0:n], op=ALU.mult),
             reads=[T1B, P], writes=[DSTB])

    def phase_rglru(self, l, b):
        k = self.k
        i = l // 2
        gw = k.sb([128, 16, 128], BF16)
        GWB = Buf()
        k.dma("pool", gw, self.hy_gw[i].rearrange("d g c p m -> p (d g c) m"), writes=[GWB])
        c1 = k.sb([128, 8], F32)
        c2 = k.sb([128, 8], F32)
        C1B = Buf()
        k.op("act", lambda e: e.activation(out=c1, in_=self.V("lam_%d" % i, 0, 8), func=AF.Exp, scale=-1.0),
             reads=[self.VB], writes=[C1B])
        k.op("act", lambda e: e.activation(out=c1, in_=c1, func=AF.Ln, bias=1.0), reads=[C1B], writes=[C1B])
        k.op("dve", lambda e: e.tensor_scalar(out=c2, in0=c1, scalar1=-16.0, scalar2=None, op0=ALU.mult),
             reads=[C1B], writes=[C1B])
        k.op("dve", lambda e: e.tensor_scalar(out=c1, in0=c1, scalar1=-8.0, scalar2=None, op0=ALU.mult),
             reads=[C1B], writes=[C1B])
        x = k.sb([128, T], F32)
        xc = k.sb([128, T], F32)
        xcb = k.sb([128, T], BF16)
        r = k.sb([128, T], F32)
        ig = k.sb([128, T], F32)
        a = k.sb([128, T], F32)
        u = k.sb([128, T], F32)
        h = [k.sb([128, T], F32) for _ in range(2)]
        ggt = k.sb([128, T], BF16)
        rec = k.sb([128, T], BF16)
        XB, XCB, XCBB, RB, IB, AB, UB, GB, RECB = [Buf() for _ in range(9)]
        HB = [Buf(), Buf()]
        segs = [(0, CTX), (CTX, T)]
        for c in range(4):
            k.dma("sp", x, self.xr[c * 128:(c + 1) * 128, :], reads=[self.DB("xr", ti) for ti in range(9)], writes=[XB])
            k.dma("sp", ggt, self.gg[c * 128:(c + 1) * 128, :], reads=[self.DB("gg", ti) for ti in range(9)],
                  writes=[GB])
            k.op("act", lambda e: e.activation(out=xc, in_=x, func=AF.Identity, bias=self.V("convb_%d" % i, c),
                                               scale=self.V("convw_%d" % i, 2 * 4 + c)),
                 reads=[XB, self.VB], writes=[XCB])
            for (s0, s1) in segs:
                for j in (0, 1, 3):
                    sh = j - 2
                    a0 = max(s0, s0 - sh)
                    a1 = min(s1, s1 - sh)
                    k.op("dve", lambda e: e.scalar_tensor_tensor(out=xc[:, a0:a1], in0=x[:, a0 + sh:a1 + sh],
                                                                 scalar=self.V("convw_%d" % i, j * 4 + c),
                                                                 in1=xc[:, a0:a1], op0=ALU.mult, op1=ALU.add),
                         reads=[XB, XCB, self.VB], writes=[XCB])
            k.op("act", lambda e: e.activation(out=xcb, in_=xc, func=AF.Copy), reads=[XCB], writes=[XCBB])
            for d in range(2):
                for (t0, n) in TILES:
                    for g, (dst, DB_) in enumerate(((r, RB), (ig, IB))):
                        p, P = self.pb()
                        k.op("pe", lambda e: e.matmul(p[:, 0:n], lhsT=gw[:, (d * 2 + g) * 4 + c, :], rhs=xcb[:, t0:t0 + n],
                                                      start=True, stop=True), reads=[GWB, XCBB], writes=[P])
                        k.op("act", lambda e: e.activation(out=dst[:, t0:t0 + n], in_=p[:, 0:n], func=AF.Sigmoid,
                                                           bias=self.V("gateb_%d" % i, (d * 2 + g) * 4 + c)),
                             reads=[P, self.VB], writes=[DB_])
                k.op("act", lambda e: e.activation(out=a, in_=r, func=AF.Exp, scale=c1[:, d * 4 + c:d * 4 + c + 1]),
                     reads=[RB, C1B], writes=[AB])
                k.op("act", lambda e: e.activation(out=u, in_=r, func=AF.Exp, scale=c2[:, d * 4 + c:d * 4 + c + 1]),
                     reads=[RB, C1B], writes=[UB])
                k.op("dve", lambda e: e.tensor_scalar(out=u, in0=u, scalar1=1.0, scalar2=None, op0=ALU.min),
                     reads=[UB], writes=[UB])
                k.op("act", lambda e: e.activation(out=u, in_=u, func=AF.Sqrt, bias=1.0, scale=-1.0),
                     reads=[UB], writes=[UB])
                k.op("dve", lambda e: e.tensor_tensor(out=ig, in0=ig, in1=xc, op=ALU.mult), reads=[IB, XCB], writes=[IB])
                k.op("dve", lambda e: e.tensor_tensor(out=u, in0=u, in1=ig, op=ALU.mult), reads=[UB, IB], writes=[UB])
                hd = h[d]
                if d == 0:
                    k.op("dve", lambda e: e.tensor_tensor_scan(out=hd, data0=a, data1=u, initial=0.0, op0=ALU.mult,
                                                               op1=ALU.add), reads=[AB, UB], writes=[HB[d]])
                else:
                    k.op("dve", lambda e: e.tensor_tensor_scan(out=hd[:, 0:CTX][:, ::-1], data0=a[:, 0:CTX][:, ::-1],
                                                               data1=u[:, 0:CTX][:, ::-1], initial=0.0, op0=ALU.mult,
                                                               op1=ALU.add), reads=[AB, UB], writes=[HB[d]])
                    k.op("dve", lambda e: e.tensor_tensor_scan(out=hd[:, CTX:T][:, ::-1], data0=a[:, CTX:T][:, ::-1],
                                                               data1=u[:, CTX:T][:, ::-1], initial=hd[:, 0:1],
                                                               op0=ALU.mult, op1=ALU.add), reads=[AB, UB, HB[d]],
                         writes=[HB[d]])
            if "rgd" in self.debug and c == 0 and b == 0:
                rgd = self.nc.dram_tensor("rgd", [8, 128, T], F32, kind="ExternalOutput").ap()
                for j_, (t_, B_) in enumerate(((x, XB), (xc, XCB), (r, RB), (ig, IB), (a, AB), (u, UB), (h[0], HB[0]),
                                               (h[1], HB[1]))):
                    k.dma("sp", rgd[j_], t_, reads=[B_], writes=[Buf()])
                c1d = self.nc.dram_tensor("c1d", [2, 128, 8], F32, kind="ExternalOutput").ap()
                k.dma("sp", c1d[0], c1, reads=[C1B], writes=[Buf()])
                k.dma("sp", c1d[1], c2, reads=[C1B], writes=[Buf()])
            k.op("dve", lambda e: e.tensor_tensor(out=h[0], in0=h[0], in1=h[1], op=ALU.add), reads=[HB[0], HB[1]],
                 writes=[HB[0]])
            k.op("dve", lambda e: e.tensor_tensor(out=rec, in0=h[0], in1=ggt, op=ALU.mult), reads=[HB[0], GB],
                 writes=[RECB])
            k.dma("sp", self.rec[c * 128:(c + 1) * 128, :], rec, reads=[RECB], writes=[self.DB("rec", c)])
        k.phase_reset()

    def phase_attn(self, l, b):
        k = self.k
        i = l // 2
        kt = k.sb([128, 2, T], BF16)
        KTB = Buf()
        k.dma("sp", kt, self.kT2.rearrange("(c p) t -> p c t", p=128), reads=[self.DB("kT2", ti) for ti in range(9)],
              writes=[KTB])
        vt = k.sb([128, T // 128, 384], BF16)
        VTB = Buf()
        k.dma("sp", vt, self.vtok.rearrange("c p f -> p c f"), reads=[self.DB("vtok", ti) for ti in range(9)],
              writes=[VTB])
        wo = k.sb([128, 8, D], BF16)
        WOB = Buf()
        self.load_w(wo, WOB, self.hy_wout[i], 8)
        xt = k.sb([128, 8, 512], F32)
        XB = Buf()
        q = k.sb([128, 4, 512], BF16)
        QB = Buf()
        rc = k.sb([128, 4, 512], BF16)
        RCB = Buf()
        att = k.sb([128, 4, 512], BF16)
        ATB = Buf()
        NPT = 8
        pt = [k.sb([128, 512], BF16) for _ in range(NPT)]
        PTB = [Buf() for _ in range(NPT)]
        den = k.sb([128, 512], F32)
        DNB = Buf()
        ptr = 0
        RECALL = [self.DB("rec", c) for c in range(4)]
        for ti, (t0, n) in enumerate(TILES):
            mi = 2 if ti == 0 else b
            nkc = 2 if ti == 0 else T // 128
            k.dma("sp", xt[:, :, 0:n], self.fm(self.xT[b], t0, n), reads=[self.DB("xT", b, ti)], writes=[XB])
            k.dma("sp", q[:, :, 0:n], self.fm(self.qT, t0, n), reads=[self.DB("qT", ti)], writes=[QB])
            k.dma("sp", rc[:, :, 0:n], self.fm(self.rec, t0, n), reads=RECALL, writes=[RCB])
            for hh in range(8):
                kv, hp, fc = hh // 4, hh % 2, hh // 2
                lo, hi = hp * 64, hp * 64 + 64
                olo, ohi = (1 - hp) * 64, (1 - hp) * 64 + 64
                po, PO = self.ps[6 + hh % 2], self.PS[6 + hh % 2]
                voff = kv * 192 + (64 if hp == 0 else 0)
                LA = 5
                slots = []

                def pv(kc, pj):
                    k.op("pe", lambda e: e.matmul(po[:, 0:n], lhsT=vt[:, kc, voff:voff + 128], rhs=pt[pj][:, 0:n],
                                                  start=(kc == 0), stop=(kc == nkc - 1)),
                         reads=[VTB, PTB[pj]], writes=[PO], sig=(kc == nkc - 1))
                for kc in range(nkc):
                    sbi = ptr % 6
                    ps_, PSB = self.ps[sbi], self.PS[sbi]
                    k.op("pe", lambda e: e.matmul(ps_[:, 0:n], lhsT=kt[lo:hi, kv, kc * 128:(kc + 1) * 128],
                                                  rhs=q[lo:hi, fc, 0:n], start=True, stop=True),
                         reads=[KTB, QB], writes=[PSB])
                    pj = ptr % NPT
                    ptr += 1
                    k.op("act", lambda e: e.activation(out=pt[pj][:, 0:n], in_=ps_[:, 0:n], func=AF.Exp, scale=0.125),
                         reads=[PSB], writes=[PTB[pj]])
                    slots.append((kc, pj))
                    if len(slots) > LA:
                        pv(*slots.pop(0))
                while slots:
                    pv(*slots.pop(0))
                k.op("act", lambda e: e.activation(out=den[lo:hi, 0:n], in_=po[olo:ohi, 0:n], func=AF.Copy),
                     reads=[PO], writes=[DNB])
                k.op("dve", lambda e: e.reciprocal(out=den[lo:hi, 0:n], in_=den[lo:hi, 0:n]), reads=[DNB], writes=[DNB])
                k.op("dve", lambda e: e.tensor_tensor(out=att[lo:hi, fc, 0:n], in0=po[lo:hi, 0:n], in1=den[lo:hi, 0:n],
                                                      op=ALU.mult), reads=[PO, DNB], writes=[ATB])
            for oc in range(8):
                p, P = self.pb()
                for kc in range(8):
                    src = att[:, kc, 0:n] if kc < 4 else rc[:, kc - 4, 0:n]
                    k.op("pe", lambda e: e.matmul(p[:, 0:n], lhsT=wo[:, kc, oc * 128:(oc + 1) * 128], rhs=src,
                                                  start=(kc == 0), stop=(kc == 7)),
                         reads=[WOB, ATB, RCB], writes=[P], sig=(kc == 7))
                k.op("dve", lambda e: e.scalar_tensor_tensor(out=xt[:, oc, 0:n], in0=p[:, 0:n],
                                                             scalar=self.mod[:, 16 + oc, mi:mi + 1], in1=xt[:, oc, 0:n],
                                                             op0=ALU.mult, op1=ALU.add),
                     reads=[P, self.MOD, XB], writes=[XB])
            k.dma("sp", self.fm(self.xT[b], t0, n), xt[:, :, 0:n], reads=[XB], writes=[self.DB("xT", b, ti)])
        k.phase_reset()

    def phase_rwkv(self, l, b):
        last = (l == DEPTH - 1)
        self.rw_norm(l, b)
        if self.rw_stop >= 1:
            self.rw_proj(l, b)
        for d in range(2):
            if self.rw_stop >= 2 + d:
                self.rw_wkv(l, b, d)
        if self.rw_stop >= 4:
            self.rw_out(l, b, last)

    def rw_norm(self, l, b):
        k = self.k
        W = self.norm_work()
        xt = [k.sb([128, 8, 512], F32) for _ in range(2)]
        XB = [Buf(), Buf()]
        hn = [k.sb([128, 8, 512], F32) for _ in range(2)]
        HB = [Buf(), Buf()]
        for ti, (t0, n) in enumerate(TILES):
            mi = 2 if ti == 0 else b
            j = ti % 2
            k.dma("sp", xt[j][:, :, 0:n], self.fm(self.xT[b], t0, n), reads=[self.DB("xT", b, ti)], writes=[XB[j]])
            self.norm_tile(xt[j], XB[j], hn[j], HB[j], self.A1, self.mod[:, 0:8, :], mi, n, W)
            k.dma("sp", self.fm(self.hnT, t0, n), hn[j][:, :, 0:n], reads=[HB[j]], writes=[self.DB("hnT", ti)])
        k.phase_reset()

    def rw_proj(self, l, b):
        k = self.k
        i = l // 2
        wr = k.sb([128, 8, D], BF16)
        wk = k.sb([128, 8, D], BF16)
        wvs = k.sb([128, 8, D], BF16)
        ld = k.sb([128, 8, 256], BF16)
        lu = k.sb([64, 4, D], BF16)
        gd = k.sb([128, 8, 128], BF16)
        gu = k.sb([128, D], BF16)
        WBr, WBk, WBv, WBld, WBgd, WBlu, WBgu = [Buf() for _ in range(7)]
        self.load_w(wr, WBr, self.rw_wrkv[i, 0], 8)
        self.load_w(wk, WBk, self.rw_wrkv[i, 1], 8)
        self.load_w(wvs, WBv, self.rw_wvst[i], 8)
        self.load_w(gd, WBgd, self.rw_gd[i], 8)
        k.dma("pool", gu, self.rw_gu[i], writes=[WBgu])
        self.load_w(ld, WBld, self.rw_ld[i], 8)
        k.dma("pool", lu, self.rw_lu[i].rearrange("q p n -> p q n"), writes=[WBlu])
        omka = k.sb([128, 8], F32)
        OMB = Buf()
        k.op("dve", lambda e: e.tensor_scalar(out=omka, in0=self.V("ka_%d" % i, 0, 8), scalar1=-1.0, scalar2=1.0,
                                              op0=ALU.mult, op1=ALU.add), reads=[self.VB], writes=[OMB])
        NT = 256
        hh = k.sb([128, 8, NT + 2], F32)
        HHB = Buf()
        xx = k.sb([128, 8, NT], F32)
        XXB = Buf()
        L = [k.sb([128, 8, NT], BF16) for _ in range(6)]
        LB = [Buf() for _ in range(6)]
        rt = k.sb([128, 8, NT], F32)
        kt = k.sb([128, 8, NT], F32)
        kkt = k.sb([128, 8, NT], F32)
        kd = [k.sb([128, 8, NT], F32) for _ in range(2)]
        o1 = k.sb([128, 8, NT], F32)
        RTB, KTB, KKB, O1B = Buf(), Buf(), Buf(), Buf()
        KDB = [Buf(), Buf()]
        vst = k.sb([128, 2, 512], F32)
        VSB = [Buf(), Buf()]
        sm = k.sb([128, NT], BF16)
        SMB = Buf()
        at8 = k.sb([128, 8, NT], F32)
        AT8 = Buf()
        tp = k.sb([128, 8, NT], F32)
        TPB = Buf()
        w18 = k.sb([128, 8, NT], F32)
        W18 = Buf()
        sq8 = k.sb([128, 8, NT], BF16)
        SQ8 = Buf()
        bones_f = self.cs[:, 128:256]

        def bc(name, j0, n):
            return self.V(name, j0, 8).unsqueeze(2).broadcast_to([128, 8, n])

        for ti, (t0, n) in enumerate(WTILES):
            seg0, seg1 = (0, CTX) if ti == 0 else (CTX, T)
            lo = max(t0 - 1, seg0)
            hi = min(t0 + n + 1, seg1)
            k.dma("sp", hh[:, :, lo - (t0 - 1):hi - (t0 - 1)], self.fm(self.hnT, lo, hi - lo),
                  reads=[self.DB("hnT", j) for j in range(9)], writes=[HHB])
            if lo != t0 - 1:
                k.op("dve", lambda e: e.memset(hh[:, :, 0:1], 0.0), writes=[HHB])
            if hi != t0 + n + 1:
                k.op("dve", lambda e: e.memset(hh[:, :, n + 1:n + 2], 0.0), writes=[HHB])
            h = hh[:, :, 1:n + 1]
            k.op("dve", lambda e: e.tensor_tensor(out=xx[:, :, 0:n], in0=hh[:, :, 0:n], in1=hh[:, :, 2:n + 2], op=ALU.add),
                 reads=[HHB], writes=[XXB])
            k.op("dve", lambda e: e.scalar_tensor_tensor(out=xx[:, :, 0:n], in0=xx[:, :, 0:n], scalar=0.5, in1=h,
                                                         op0=ALU.mult, op1=ALU.subtract), reads=[XXB, HHB], writes=[XXB])
            for j in (0, 2, 3, 1, 4, 5):
                if j in (0, 2, 3):
                    eng, tmp, TB = "dve", at8, AT8
                else:
                    eng, tmp, TB = "pool", tp, TPB
                k.op(eng, lambda e: e.tensor_tensor(out=tmp[:, :, 0:n], in0=xx[:, :, 0:n], in1=bc("mu_%d" % i, j * 8, n),
                                                    op=ALU.mult), reads=[XXB, self.VB], writes=[TB])
                k.op(eng, lambda e: e.tensor_tensor(out=L[j][:, :, 0:n], in0=tmp[:, :, 0:n], in1=h, op=ALU.add),
                     reads=[TB, HHB], writes=[LB[j]])

            def proj(w, WB, Lj, LjB, dst, DSTB):
                for oc in range(8):
                    p, P = self.pb()
                    for kc in range(8):
                        k.op("pe", lambda e: e.matmul(p[:, 0:n], lhsT=w[:, kc, oc * 128:(oc + 1) * 128], rhs=Lj[:, kc, 0:n],
                                                      start=(kc == 0), stop=(kc == 7)), reads=[WB, LjB], writes=[P],
                             sig=(kc == 7))
                    k.op("act", lambda e: e.activation(out=dst[:, oc, 0:n], in_=p[:, 0:n], func=AF.Copy), reads=[P],
                         writes=[DSTB])
            proj(wr, WBr, L[0], LB[0], rt, RTB)
            k.dma("sp", self.fm(self.rT, t0, n), rt[:, :, 0:n], reads=[RTB], writes=[self.DB("rT", ti)])
            proj(wk, WBk, L[2], LB[2], kt, KTB)
            for ch in range(n // 64):
                p, P = self.pb()
                for hp in range(2):
                    for kc in range(8):
                        k.op("pe", lambda e: e.matmul(p[hp * 64:(hp + 1) * 64, :], lhsT=L[3][:, kc, ch * 64:(ch + 1) * 64],
                                                      rhs=wvs[:, kc, hp * 512:(hp + 1) * 512], start=(kc == 0),
                                                      stop=(kc == 7)), reads=[WBv, LB[3]], writes=[P],
                             sig=(kc == 7 and hp == 1))
                j = ch % 2
                k.op("act", lambda e: e.activation(out=vst[:, j, :], in_=p, func=AF.Copy), reads=[P], writes=[VSB[j]])
                k.dma("sp", self.Vst[t0 // 64 + ch], vst[:, j, :], reads=[VSB[j]], writes=[self.DB("Vst", ti)])
            p, P = self.pb()
            for kc in range(8):
                k.op("pe", lambda e: e.matmul(p[:, 0:n], lhsT=gd[:, kc, :], rhs=L[5][:, kc, 0:n], start=(kc == 0),
                                              stop=(kc == 7)), reads=[WBgd, LB[5]], writes=[P], sig=(kc == 7))
            k.op("act", lambda e: e.activation(out=sm[:, 0:n], in_=p[:, 0:n], func=AF.Sigmoid), reads=[P], writes=[SMB])
            for oc in range(8):
                p, P = self.pb()
                k.op("pe", lambda e: e.matmul(p[:, 0:n], lhsT=gu[:, oc * 128:(oc + 1) * 128], rhs=sm[:, 0:n], start=True,
                                              stop=True), reads=[WBgu, SMB], writes=[P])
                k.op("act", lambda e: e.activation(out=o1[:, oc, 0:n], in_=p[:, 0:n], func=AF.Copy), reads=[P],
                     writes=[O1B])
            k.dma("sp", self.fm(self.gT, t0, n), o1[:, :, 0:n], reads=[O1B], writes=[self.DB("gT", ti)])
            k.op("dve", lambda e: e.tensor_tensor(out=kkt[:, :, 0:n], in0=kt[:, :, 0:n], in1=bc("kk_%d" % i, 0, n),
                                                  op=ALU.mult), reads=[KTB, self.VB], writes=[KKB])
            k.op("act", lambda e: e.activation(out=sq8[:, :, 0:n], in_=kkt[:, :, 0:n], func=AF.Square), reads=[KKB],
                 writes=[SQ8])
            for oc in range(8):
                p, P = self.pb()
                k.op("pe", lambda e: e.matmul(p[:, 0:n], lhsT=self.bones_bf, rhs=sq8[:, oc, 0:n], start=True, stop=True),
                     reads=[SQ8, self.CONST], writes=[P])
                k.op("act", lambda e: e.activation(out=w18[:, oc, 0:n], in_=p[:, 0:n], func=AF.Sqrt), reads=[P],
                     writes=[W18])
            k.op("dve", lambda e: e.tensor_scalar(out=w18[:, :, 0:n], in0=w18[:, :, 0:n], scalar1=1e-12, scalar2=None,
                                                  op0=ALU.max), reads=[W18], writes=[W18])
            k.op("dve", lambda e: e.reciprocal(out=w18[:, :, 0:n], in_=w18[:, :, 0:n]), reads=[W18], writes=[W18])
            k.op("dve", lambda e: e.tensor_tensor(out=kkt[:, :, 0:n], in0=kkt[:, :, 0:n], in1=w18[:, :, 0:n], op=ALU.mult),
                 reads=[KKB, W18], writes=[KKB])
            k.dma("sp", self.fm(self.kkT, t0, n), kkt[:, :, 0:n], reads=[KKB], writes=[self.DB("kkT", ti)])
            for d in range(2):
                p, P = self.pb()
                for kc in range(8):
                    k.op("pe", lambda e: e.matmul(p[0:64, 0:n], lhsT=ld[:, kc, (d * 2) * 64:(d * 2 + 1) * 64],
                                                  rhs=L[1][:, kc, 0:n], start=(kc == 0), stop=(kc == 7)),
                         reads=[WBld, LB[1]], writes=[P], sig=(kc == 7))
                k.op("act", lambda e: e.activation(out=sm[0:64, 0:n], in_=p[0:64, 0:n], func=AF.Tanh), reads=[P],
                     writes=[SMB])
                for oc in range(8):
                    p, P = self.pb()
                    k.op("pe", lambda e: e.matmul(p[:, 0:n], lhsT=lu[:, d * 2, oc * 128:(oc + 1) * 128], rhs=sm[0:64, 0:n],
                                                  start=True, stop=True), reads=[WBlu, SMB], writes=[P])
                    k.op("act", lambda e: e.activation(out=o1[:, oc, 0:n], in_=p[:, 0:n], func=AF.Sigmoid,
                                                       bias=self.V("lb_%d" % i, (d * 2) * 8 + oc)),
                         reads=[P, self.VB], writes=[O1B])
                k.op("dve", lambda e: e.tensor_scalar(out=o1[:, :, 0:n], in0=o1[:, :, 0:n], scalar1=-DECAY_SCALE,
                                                      scalar2=None, op0=ALU.mult), reads=[O1B], writes=[O1B])
                k.dma("sp", self.fm(self.lwT[d], t0, n), o1[:, :, 0:n], reads=[O1B], writes=[self.DB("lwT", d, ti)])
                p, P = self.pb()
                for kc in range(8):
                    k.op("pe", lambda e: e.matmul(p[0:64, 0:n], lhsT=ld[:, kc, (d * 2 + 1) * 64:(d * 2 + 2) * 64],
                                                  rhs=L[4][:, kc, 0:n], start=(kc == 0), stop=(kc == 7)),
                         reads=[WBld, LB[4]], writes=[P], sig=(kc == 7))
                k.op("act", lambda e: e.activation(out=sm[0:64, 0:n], in_=p[0:64, 0:n], func=AF.Copy), reads=[P],
                     writes=[SMB])
                for oc in range(8):
                    p, P = self.pb()
                    k.op("pe", lambda e: e.matmul(p[:, 0:n], lhsT=lu[:, d * 2 + 1, oc * 128:(oc + 1) * 128],
                                                  rhs=sm[0:64, 0:n], start=True, stop=True), reads=[WBlu, SMB], writes=[P])
                    k.op("act", lambda e: e.activation(out=at8[:, oc, 0:n], in_=p[:, 0:n], func=AF.Sigmoid,
                                                       bias=self.V("lb_%d" % i, (d * 2 + 1) * 8 + oc)),
                         reads=[P, self.VB], writes=[AT8])
                k.op("dve", lambda e: e.tensor_tensor(out=o1[:, :, 0:n], in0=kkt[:, :, 0:n], in1=at8[:, :, 0:n], op=ALU.mult),
                     reads=[KKB, AT8], writes=[O1B])
                k.dma("sp", self.fm(self.bT[d], t0, n), o1[:, :, 0:n], reads=[O1B], writes=[self.DB("bT", d, ti)])
                k.op("dve", lambda e: e.tensor_tensor(out=at8[:, :, 0:n], in0=at8[:, :, 0:n], in1=bc("ka_%d" % i, 0, n),
                                                      op=ALU.mult), reads=[AT8, self.VB], writes=[AT8])
                k.op("dve", lambda e: e.tensor_tensor(out=at8[:, :, 0:n], in0=at8[:, :, 0:n],
                                                      in1=omka.unsqueeze(2).broadcast_to([128, 8, n]), op=ALU.add),
                     reads=[AT8, OMB], writes=[AT8])
                k.op("dve", lambda e: e.tensor_tensor(out=kd[d][:, :, 0:n], in0=at8[:, :, 0:n], in1=kt[:, :, 0:n], op=ALU.mult),
                     reads=[AT8, KTB], writes=[KDB[d]])
                k.dma("sp", self.fm(self.kdT[d], t0, n), kd[d][:, :, 0:n], reads=[KDB[d]], writes=[self.DB("kdT", d, ti)])
            k.op("dve", lambda e: e.tensor_tensor(out=at8[:, :, 0:n], in0=kd[0][:, :, 0:n], in1=kd[1][:, :, 0:n], op=ALU.add),
                 reads=[KDB[0], KDB[1]], writes=[AT8])
            k.op("dve", lambda e: e.tensor_tensor(out=at8[:, :, 0:n], in0=at8[:, :, 0:n], in1=rt[:, :, 0:n], op=ALU.mult),
                 reads=[AT8, RTB], writes=[AT8])
            k.op("dve", lambda e: e.tensor_tensor(out=at8[:, :, 0:n], in0=at8[:, :, 0:n], in1=bc("rk_%d" % i, 0, n),
                                                  op=ALU.mult), reads=[AT8, self.VB], writes=[AT8])
            for oc in range(8):
                p, P = self.pb()
                k.op("pe", lambda e: e.matmul(p[:, 0:n], lhsT=bones_f, rhs=at8[:, oc, 0:n], start=True, stop=True),
                     reads=[AT8, self.CS], writes=[P])
                k.op("act", lambda e: e.activation(out=w18[:, oc, 0:n], in_=p[:, 0:n], func=AF.Copy), reads=[P],
                     writes=[W18])
            for oc in range(8):
                p2, P2 = self.pb()
                for half in range(2):
                    hd_ = 2 * oc + half
                    c0 = (hd_ % 2) * 512 + (hd_ // 2) * 64
                    for kc in range(8):
                        k.op("pe", lambda e: e.matmul(p2[half * 64:(half + 1) * 64, 0:n], lhsT=wvs[:, kc, c0:c0 + 64],
                                                      rhs=L[3][:, kc, 0:n], start=(kc == 0), stop=(kc == 7)),
                             reads=[WBv, LB[3]], writes=[P2], sig=(kc == 7 and half == 1))
                k.op("dve", lambda e: e.tensor_tensor(out=o1[:, oc, 0:n], in0=p2[:, 0:n], in1=w18[:, oc, 0:n], op=ALU.mult),
                     reads=[P2, W18], writes=[O1B])
            k.dma("sp", self.fm(self.bonT, t0, n), o1[:, :, 0:n], reads=[O1B], writes=[self.DB("bonT", ti)])
        k.phase_reset()

    def rw_wkv(self, l, b, d):
        k = self.k
        NW = 256
        rev = (d == 1)
        msk = k.sb([128, 512], F32)
        rmk = k.sb([128, 2048], F32)
        MB = Buf()
        k.dma("sp", msk, self.wmask[:, d, :], writes=[MB])
        k.dma("sp", rmk, self.rmask[:, d, :], writes=[MB])

        def t8():
            return k.sb([128, 8, NW], F32)
        rt, kk, bb, kdt, lw, cum, ec = t8(), t8(), t8(), t8(), t8(), t8(), t8()
        INB = Buf()
        ECB = Buf()
        vst = k.sb([128, 4, 512], F32)
        VB_ = Buf()
        yt = k.sb([128, 8, NW], F32)
        YB = Buf()
        XR = [k.sb([128, 8, 192], F32) for _ in range(2)]
        BE = [k.sb([128, 8, 128], F32) for _ in range(2)]
        KT = [k.sb([128, 8, 128], F32) for _ in range(2)]
        CHB = [Buf(), Buf()]
        AM2 = [k.sb([128, 8, 512], F32) for _ in range(2)]
        AMB2 = [[Buf() for _ in range(8)] for _ in range(2)]
        X2 = [k.sb([128, 8, 128], F32) for _ in range(2)]
        XB2 = [[Buf(), Buf()], [Buf(), Buf()]]
        Pn = k.sb([128, 8, 128], F32)
        PTn = k.sb([128, 8, 128], F32)
        PNB = [Buf(), Buf()]
        PTB = [Buf(), Buf()]
        BEt2 = [k.sb([128, 8, 128], F32) for _ in range(2)]
        KTt2 = [k.sb([128, 8, 128], F32) for _ in range(2)]
        BTB2, KTTB2 = [Buf(), Buf()], [Buf(), Buf()]
        Vbd2 = [k.sb([128, 8, 128], F32) for _ in range(2)]
        VBB2 = [Buf(), Buf()]
        Wsb = k.sb([128, 8, 64], F32)
        Ust = k.sb([128, 8, 64], F32)
        Ubd = k.sb([128, 8, 128], F32)
        Vs2 = [k.sb([128, 8, 64], F32) for _ in range(2)]
        Gc2 = [k.sb([128, 8, 1], F32) for _ in range(2)]
        VS2B = [Buf(), Buf()]
        GCB = [Buf(), Buf()]
        Sst = k.sb([128, 8, 64], F32)
        Sbd = k.sb([128, 8, 128], F32)
        WSB, USB, UBB, SSB, SBB = [Buf() for _ in range(5)]
        for t_, B_ in ((XR[0], CHB[0]), (XR[1], CHB[1]), (BE[0], CHB[0]), (BE[1], CHB[1]), (KT[0], CHB[0]),
                       (KT[1], CHB[1]), (Ubd, UBB), (Vbd2[0], VBB2[0]), (Vbd2[1], VBB2[1]), (Sbd, SBB), (Sst, SSB)):
            k.op("pool", lambda e: e.memset(t_, 0.0), writes=[B_])
        r3 = lambda t_: t_.rearrange("p (q c) -> p q c", c=64)
        r4 = lambda t_: t_.rearrange("p (q c) -> p q c", c=128)

        def pre_gen(ch, jb):
            c0 = ch * 64
            cs_ = slice(c0, c0 + 64)
            xr_, be_, kt_, CB = XR[jb], BE[jb], KT[jb], CHB[jb]
            AM, AMB, X, XB = AM2[jb], AMB2[jb], X2[jb], XB2[jb]
            BEt, KTt, BTB, KTTB, Vbd, VBB = BEt2[jb], KTt2[jb], BTB2[jb], KTTB2[jb], Vbd2[jb], VBB2[jb]
            for hp in range(2):
                ps_ = slice(hp * 64, hp * 64 + 64)
                k.op("dve", lambda e: e.scalar_tensor_tensor(out=xr_[ps_, :, hp * 64:hp * 64 + 64], in0=kk[ps_, :, cs_],
                                                             scalar=-1.0, in1=lw[ps_, :, cs_], op0=ALU.mult,
                                                             op1=ALU.mult), reads=[INB], writes=[CB])
                k.op("pool", lambda e: e.tensor_tensor(out=be_[ps_, :, hp * 64:hp * 64 + 64], in0=bb[ps_, :, cs_],
                                                       in1=cum[ps_, :, cs_], op=ALU.mult), reads=[INB, ECB], writes=[CB])
                k.op("pool", lambda e: e.tensor_tensor(out=kt_[ps_, :, hp * 64:hp * 64 + 64], in0=kdt[ps_, :, cs_],
                                                       in1=cum[ps_, :, cs_], op=ALU.mult), reads=[INB, ECB], writes=[CB])
            k.op("dve", lambda e: e.tensor_tensor(out=xr_[:, :, 128:192], in0=rt[:, :, cs_], in1=ec[:, :, cs_],
                                                  op=ALU.mult), reads=[INB, ECB], writes=[CB])
            vs_ = vst[:, ch, :].rearrange("p (q v) -> p q v", v=64)
            for hp in range(2):
                ps_ = slice(hp * 64, hp * 64 + 64)
                k.op("act", lambda e: e.activation(out=Vbd[ps_, :, hp * 64:hp * 64 + 64], in_=vs_[ps_, :, :],
                                                   func=AF.Copy), reads=[VB_], writes=[VBB])
            k.op("act", lambda e: e.activation(out=Vs2[jb], in_=vs_, func=AF.Copy), reads=[VB_], writes=[VS2B[jb]])
            gcol_ = (c0 + 63) if not rev else c0
            k.op("act", lambda e: e.activation(out=Gc2[jb], in_=ec[:, :, gcol_:gcol_ + 1], func=AF.Copy), reads=[ECB],
                 writes=[GCB[jb]])
            yield
            for p_ in range(8):
                pa, PA = self.pb()
                k.op("pe", lambda e: e.matmul(pa[:, 0:192], lhsT=be_[:, p_, :], rhs=xr_[:, p_, :], start=True,
                                              stop=True), reads=[CB], writes=[PA], sig=False)
                k.op("pe", lambda e: e.matmul(pa[:, 192:384], lhsT=kt_[:, p_, :], rhs=xr_[:, p_, :], start=True,
                                              stop=True), reads=[CB], writes=[PA], sig=False)
                k.op("pe", lambda e: e.matmul(pa[:, 384:512], lhsT=xr_[:, p_, 0:128], rhs=be_[:, p_, :], start=True,
                                              stop=True), reads=[CB], writes=[PA])
                k.op("dve", lambda e: e.tensor_tensor(out=AM[:, p_, :], in0=pa, in1=msk, op=ALU.mult),
                     reads=[PA, MB], writes=[AMB[p_]])
                if p_ == 3:
                    yield
            yield
            for (src_, dst_, DB_) in ((be_, BEt, BTB), (kt_, KTt, KTTB)):
                for q_ in range(2):
                    pt_, PT_ = self.pb()
                    for pi in range(4):
                        p_ = q_ * 4 + pi
                        k.op("pe", lambda e: e.transpose(pt_[:, pi * 128:(pi + 1) * 128], src_[:, p_, :], self.ident),
                             reads=[CB, self.CS], writes=[PT_], sig=(pi == 3))
                    k.op("act", lambda e: e.activation(out=dst_[:, q_ * 4:q_ * 4 + 4, :], in_=r4(pt_), func=AF.Copy),
                         reads=[PT_], writes=[DB_])
            for q_ in range(2):
                k.op("dve", lambda e: e.tensor_tensor(out=X[:, q_ * 4:q_ * 4 + 4, :], in0=AM[:, q_ * 4:q_ * 4 + 4, 0:128],
                                                      in1=self.ident.unsqueeze(1).broadcast_to([128, 4, 128]), op=ALU.add),
                     reads=AMB[q_ * 4:q_ * 4 + 4] + [self.CS], writes=[XB[q_]])
            yield
            for kk_ in range(1, 6):
                Pq = (lambda p_: AM[:, p_, 0:128]) if kk_ == 1 else (lambda p_: Pn[:, p_, :])
                PTq = (lambda p_: AM[:, p_, 384:512]) if kk_ == 1 else (lambda p_: PTn[:, p_, :])
                bk = {}
                for q_ in range(2):
                    RD = (AMB[q_ * 4:q_ * 4 + 4]) if kk_ == 1 else [PNB[q_], PTB[q_]]
                    pb_, PB_ = self.pb()
                    for pi in range(4):
                        p_ = q_ * 4 + pi
                        k.op("pe", lambda e: e.matmul(pb_[:, pi * 128:(pi + 1) * 128], lhsT=Pq(p_), rhs=PTq(p_),
                                                      start=True, stop=True), reads=RD, writes=[PB_], sig=(pi == 3))
                    bk[("b", q_)] = (pb_, PB_)
                    if kk_ < 5:
                        pa_, PA_ = self.pb()
                        for pi in range(4):
                            p_ = q_ * 4 + pi
                            k.op("pe", lambda e: e.matmul(pa_[:, pi * 128:(pi + 1) * 128], lhsT=PTq(p_), rhs=Pq(p_),
                                                          start=True, stop=True), reads=RD, writes=[PA_], sig=(pi == 3))
                        bk[("a", q_)] = (pa_, PA_)
                yield
                for q_ in range(2):
                    pb_, PB_ = bk[("b", q_)]
                    k.op("act", lambda e: e.activation(out=PTn[:, q_ * 4:q_ * 4 + 4, :], in_=r4(pb_), func=AF.Copy),
                         reads=[PB_], writes=[PTB[q_]])
                    if kk_ < 5:
                        pa_, PA_ = bk[("a", q_)]
                        k.op("dve", lambda e: e.tensor_copy(out=Pn[:, q_ * 4:q_ * 4 + 4, :], in_=r4(pa_)),
                             reads=[PA_], writes=[PNB[q_]])
                for q_ in range(2):
                    pc_, PC_ = self.pb()
                    for pi in range(4):
                        p_ = q_ * 4 + pi
                        k.op("pe", lambda e: e.matmul(pc_[:, pi * 128:(pi + 1) * 128], lhsT=PTn[:, p_, :], rhs=X[:, p_, :],
                                                      start=True, stop=True), reads=[PTB[q_], XB[q_]], writes=[PC_],
                             sig=(pi == 3))
                    bk[("c", q_)] = (pc_, PC_)
                yield
                for q_ in range(2):
                    pc_, PC_ = bk[("c", q_)]
                    k.op("dve", lambda e: e.tensor_tensor(out=X[:, q_ * 4:q_ * 4 + 4, :], in0=X[:, q_ * 4:q_ * 4 + 4, :],
                                                          in1=r4(pc_), op=ALU.add), reads=[PC_, XB[q_]], writes=[XB[q_]])

        def state_gen(ch, jb):
            c0 = ch * 64
            cs_ = slice(c0, c0 + 64)
            xr_, CB = XR[jb], CHB[jb]
            AM, AMB, X, XB = AM2[jb], AMB2[jb], X2[jb], XB2[jb]
            BEt, KTt, BTB, KTTB, Vbd, VBB = BEt2[jb], KTt2[jb], BTB2[jb], KTTB2[jb], Vbd2[jb], VBB2[jb]
            vs_ = Vs2[jb]
            VB_ = VS2B[jb]
            pw, PW = self.pb()
            pw2, PW2 = self.pb()
            for p_ in range(8):
                k.op("pe", lambda e: e.matmul(pw[:, p_ * 64:(p_ + 1) * 64], lhsT=xr_[:, p_, 0:128], rhs=Sst[:, p_, :],
                                              start=True, stop=True), reads=[CB, SSB], writes=[PW], sig=(p_ == 7))
            for p_ in range(8):
                k.op("pe", lambda e: e.matmul(pw2[:, p_ * 64:(p_ + 1) * 64], lhsT=AM[:, p_, 192:320], rhs=vs_[:, p_, :],
                                              start=True, stop=True), reads=[AMB[p_], VB_], writes=[PW2], sig=(p_ == 7))
            py1, PY1 = self.pb()
            py3, PY3 = self.pb()
            for p_ in range(8):
                k.op("pe", lambda e: e.matmul(py1[:, p_ * 64:(p_ + 1) * 64], lhsT=Sbd[:, p_, :], rhs=xr_[:, p_, 128:192],
                                              start=True, stop=True), reads=[SBB, CB], writes=[PY1], sig=(p_ == 7))
            for p_ in range(8):
                k.op("pe", lambda e: e.matmul(py3[:, p_ * 64:(p_ + 1) * 64], lhsT=Vbd[:, p_, :], rhs=AM[:, p_, 320:384],
                                              start=True, stop=True), reads=[VBB, AMB[p_]], writes=[PY3], sig=(p_ == 7))
            k.op("act", lambda e: e.activation(out=Wsb, in_=r3(pw), func=AF.Copy), reads=[PW], writes=[WSB])
            k.op("dve", lambda e: e.tensor_tensor(out=Wsb, in0=Wsb, in1=r3(pw2), op=ALU.add), reads=[WSB, PW2],
                 writes=[WSB])
            k.op("act", lambda e: e.activation(out=yt[:, :, cs_], in_=r3(py1), func=AF.Copy), reads=[PY1], writes=[YB])
            k.op("dve", lambda e: e.tensor_tensor(out=yt[:, :, cs_], in0=yt[:, :, cs_], in1=r3(py3), op=ALU.add),
                 reads=[YB, PY3], writes=[YB])
            yield
            pu, PU = self.pb()
            for p_ in range(8):
                k.op("pe", lambda e: e.matmul(pu[:, p_ * 64:(p_ + 1) * 64], lhsT=X[:, p_, :], rhs=Wsb[:, p_, :], start=True,
                                              stop=True), reads=[XB[p_ // 4], WSB], writes=[PU], sig=(p_ == 7))
            pu3 = r3(pu)
            k.op("act", lambda e: e.activation(out=Ust, in_=pu3, func=AF.Copy), reads=[PU], writes=[USB])
            for hp in range(2):
                ps_ = slice(hp * 64, hp * 64 + 64)
                k.op("act", lambda e: e.activation(out=Ubd[ps_, :, hp * 64:hp * 64 + 64], in_=pu3[ps_, :, :],
                                                   func=AF.Copy), reads=[PU], writes=[UBB])
            yield
            py2, PY2 = self.pb()
            pS1, PS1 = self.pb()
            pS2, PS2 = self.pb()
            for p_ in range(8):
                k.op("pe", lambda e: e.matmul(pS2[:, p_ * 64:(p_ + 1) * 64], lhsT=KTt[:, p_, :], rhs=vs_[:, p_, :],
                                              start=True, stop=True), reads=[KTTB, VB_], writes=[PS2], sig=(p_ == 7))
            for p_ in range(8):
                k.op("pe", lambda e: e.matmul(pS1[:, p_ * 64:(p_ + 1) * 64], lhsT=BEt[:, p_, :], rhs=Ust[:, p_, :],
                                              start=True, stop=True), reads=[BTB, USB], writes=[PS1], sig=(p_ == 7))
            for p_ in range(8):
                k.op("pe", lambda e: e.matmul(py2[:, p_ * 64:(p_ + 1) * 64], lhsT=Ubd[:, p_, :], rhs=AM[:, p_, 128:192],
                                              start=True, stop=True), reads=[UBB, AMB[p_]], writes=[PY2], sig=(p_ == 7))
            k.op("act", lambda e: e.activation(out=Wsb, in_=r3(pS1), func=AF.Copy), reads=[PS1], writes=[WSB])
            k.op("dve", lambda e: e.tensor_tensor(out=Wsb, in0=Wsb, in1=r3(pS2), op=ALU.add), reads=[WSB, PS2], writes=[WSB])
            k.op("dve", lambda e: e.tensor_tensor(out=Wsb, in0=Wsb, in1=Sst, op=ALU.add), reads=[WSB, SSB], writes=[WSB])
            k.op("dve", lambda e: e.tensor_tensor(out=Sst, in0=Wsb, in1=Gc2[jb].broadcast_to([128, 8, 64]),
                                                  op=ALU.mult), reads=[WSB, GCB[jb]], writes=[SSB])
            for hp in range(2):
                ps_ = slice(hp * 64, hp * 64 + 64)
                k.op("act", lambda e: e.activation(out=Sbd[ps_, :, hp * 64:hp * 64 + 64], in_=Sst[ps_, :, :],
                                                   func=AF.Copy), reads=[SSB], writes=[SBB])
            k.op("dve", lambda e: e.tensor_tensor(out=yt[:, :, cs_], in0=yt[:, :, cs_], in1=r3(py2), op=ALU.add),
                 reads=[YB, PY2], writes=[YB])

        def drain(g):
            for _ in g:
                pass

        def interleave(pre, st):
            pre_done = pre is None
            st_done = False
            while not (pre_done and st_done):
                for _ in range(self.wkv_ratio):
                    if not pre_done:
                        try:
                            next(pre)
                        except StopIteration:
                            pre_done = True
                if not st_done:
                    try:
                        next(st)
                    except StopIteration:
                        st_done = True

        wt = [(0, 0)] + [(1 + w, CTX + NW * w) for w in range(16)]
        order = (wt if not rev else [wt[0]] + wt[:0:-1])[:self.wkv_ntiles]
        fl = lambda a_: a_.rearrange("p f t -> p (f t)")
        rv = (lambda a_: a_[:, ::-1]) if rev else (lambda a_: a_)

        def prologue(wi, t0):
            for (dst, src, key) in ((rt, self.rT, ("rT", wi)), (kk, self.kkT, ("kkT", wi)), (bb, self.bT[d], ("bT", d, wi)),
                                    (kdt, self.kdT[d], ("kdT", d, wi)), (lw, self.lwT[d], ("lwT", d, wi))):
                k.dma("sp", dst, self.fm(src, t0, NW), reads=[self.DB(*key)], writes=[INB])
            k.dma("sp", vst, self.Vst[t0 // 64:t0 // 64 + 4].rearrange("c p f -> p c f"), reads=[self.DB("Vst", wi)],
                  writes=[VB_])
            k.op("dve", lambda e: e.tensor_tensor_scan(out=rv(fl(cum)), data0=rv(rmk), data1=rv(fl(lw)), initial=0.0,
                                                       op0=ALU.mult, op1=ALU.add), reads=[INB, MB, ECB], writes=[ECB])
            k.op("dve", lambda e: e.tensor_tensor(out=lw, in0=cum, in1=lw, op=ALU.subtract), reads=[ECB, INB], writes=[INB])
            k.op("act", lambda e: e.activation(out=lw, in_=lw, func=AF.Exp), reads=[INB], writes=[INB])
            k.op("act", lambda e: e.activation(out=ec, in_=cum, func=AF.Exp), reads=[ECB], writes=[ECB])
            k.op("act", lambda e: e.activation(out=cum, in_=cum, func=AF.Exp, scale=-1.0), reads=[ECB], writes=[ECB])

        chs = list(range(4)) if not rev else list(range(3, -1, -1))
        nchunk = 0
        prologue(*order[0])
        drain(pre_gen(chs[0], 0))
        for oi, (wi, t0) in enumerate(order):
            jbs = [(nchunk + i_) % 2 for i_ in range(5)]
            nchunk += 4
            for i_ in range(4):
                if i_ < 3:
                    nxt = pre_gen(chs[i_ + 1], jbs[i_ + 1])
                elif oi + 1 < len(order):
                    prologue(*order[oi + 1])
                    nxt = pre_gen(chs[0], jbs[4])
                else:
                    nxt = None
                interleave(nxt, state_gen(chs[i_], jbs[i_]))
            k.dma("sp", self.fm(self.yT[d], t0, NW), yt, reads=[YB], writes=[self.DB("yT", d, wi)])
        k.phase_reset()

    def rw_out(self, l, b, last):
        k = self.k
        i = l // 2
        wo = k.sb([128, 8, D], BF16)
        WOB = Buf()
        self.load_w(wo, WOB, self.rw_wo[i], 8)
        bones_f = self.cs[:, 128:256]
        y0 = k.sb([128, 8, 512], F32)
        y1 = k.sb([128, 8, 512], F32)
        bon = k.sb([128, 8, 512], F32)
        gt = k.sb([128, 8, 512], F32)
        xt = k.sb([128, 8, 512], F32)
        yc = k.sb([128, 8, 512], F32)
        rs = k.sb([128, 8, 512], F32)
        ob = k.sb([128, 8, 512], BF16)
        Y0B, Y1B, BNB, GTB, XB, OBB, YCB, RSB = [Buf() for _ in range(8)]

        def bc(name, n):
            return self.V(name, 0, 8).unsqueeze(2).broadcast_to([128, 8, n])
        for ti, (t0, n) in enumerate(TILES):
            if last and ti == 0:
                continue
            mi = 2 if ti == 0 else b
            wis = [0] if ti == 0 else [2 * ti - 1, 2 * ti]
            k.dma("sp", y0[:, :, 0:n], self.fm(self.yT[0], t0, n), reads=[self.DB("yT", 0, w) for w in wis], writes=[Y0B])
            k.dma("sp", y1[:, :, 0:n], self.fm(self.yT[1], t0, n), reads=[self.DB("yT", 1, w) for w in wis], writes=[Y1B])
            k.dma("sp", bon[:, :, 0:n], self.fm(self.bonT, t0, n), reads=[self.DB("bonT", w) for w in wis], writes=[BNB])
            k.dma("sp", gt[:, :, 0:n], self.fm(self.gT, t0, n), reads=[self.DB("gT", w) for w in wis], writes=[GTB])
            k.dma("sp", xt[:, :, 0:n], self.fm(self.xT[b], t0, n), reads=[self.DB("xT", b, ti)], writes=[XB])
            k.op("pool", lambda e: e.tensor_tensor(out=y0[:, :, 0:n], in0=y0[:, :, 0:n], in1=y1[:, :, 0:n], op=ALU.add),
                 reads=[Y0B, Y1B], writes=[Y0B])
            for fc in range(8):
                p, P = self.pb()
                k.op("pe", lambda e: e.matmul(p[:, 0:n], lhsT=bones_f, rhs=y0[:, fc, 0:n], start=True, stop=True),
                     reads=[Y0B, self.CS], writes=[P])
                k.op("dve", lambda e: e.scalar_tensor_tensor(out=yc[:, fc, 0:n], in0=p[:, 0:n], scalar=-1.0 / 64,
                                                             in1=y0[:, fc, 0:n], op0=ALU.mult, op1=ALU.add),
                     reads=[P, Y0B], writes=[YCB])
            k.op("act", lambda e: e.activation(out=y1[:, :, 0:n], in_=yc[:, :, 0:n], func=AF.Square), reads=[YCB, Y1B],
                 writes=[Y1B])
            for fc in range(8):
                p2, P2 = self.pb()
                k.op("pe", lambda e: e.matmul(p2[:, 0:n], lhsT=bones_f, rhs=y1[:, fc, 0:n], start=True, stop=True),
                     reads=[Y1B, self.CS], writes=[P2])
                k.op("act", lambda e: e.activation(out=rs[:, fc, 0:n], in_=p2[:, 0:n], func=AF.Sqrt, bias=GN_EPS,
                                                   scale=1.0 / 64), reads=[P2], writes=[RSB])
            k.op("dve", lambda e: e.reciprocal(out=rs[:, :, 0:n], in_=rs[:, :, 0:n]), reads=[RSB], writes=[RSB])
            k.op("dve", lambda e: e.tensor_tensor(out=yc[:, :, 0:n], in0=yc[:, :, 0:n], in1=rs[:, :, 0:n], op=ALU.mult),
                 reads=[YCB, RSB], writes=[YCB])
            k.op("pool", lambda e: e.tensor_tensor(out=yc[:, :, 0:n], in0=yc[:, :, 0:n], in1=bc("gng_%d" % i, n), op=ALU.mult),
                 reads=[YCB, self.VB], writes=[YCB])
            k.op("pool", lambda e: e.tensor_tensor(out=bon[:, :, 0:n], in0=bon[:, :, 0:n], in1=bc("gnb_%d" % i, n), op=ALU.add),
                 reads=[BNB, self.VB], writes=[BNB])
            k.op("dve", lambda e: e.tensor_tensor(out=yc[:, :, 0:n], in0=yc[:, :, 0:n], in1=bon[:, :, 0:n], op=ALU.add),
                 reads=[YCB, BNB], writes=[YCB])
            k.op("dve", lambda e: e.tensor_tensor(out=ob[:, :, 0:n], in0=yc[:, :, 0:n], in1=gt[:, :, 0:n], op=ALU.mult),
                 reads=[YCB, GTB], writes=[OBB])
            for oc in range(8):
                p, P = self.pb()
                for kc in range(8):
                    k.op("pe", lambda e: e.matmul(p[:, 0:n], lhsT=wo[:, kc, oc * 128:(oc + 1) * 128], rhs=ob[:, kc, 0:n],
                                                  start=(kc == 0), stop=(kc == 7)), reads=[WOB, OBB], writes=[P],
                         sig=(kc == 7))
                k.op("dve", lambda e: e.scalar_tensor_tensor(out=xt[:, oc, 0:n], in0=p[:, 0:n],
                                                             scalar=self.mod[:, 16 + oc, mi:mi + 1], in1=xt[:, oc, 0:n],
                                                             op0=ALU.mult, op1=ALU.add), reads=[P, self.MOD, XB], writes=[XB])
            k.dma("sp", self.fm(self.xT[b], t0, n), xt[:, :, 0:n], reads=[XB], writes=[self.DB("xT", b, ti)])
        k.phase_reset()

    def build(self, nphases=None):
        ph = [lambda: self.setup()]
        for l in self.layers:
            last = (l == DEPTH - 1)
            ph.append(lambda l=l: self.phase_mod(l))
            for b in range(NB):
                if l % 2 == 0:
                    ph.append(lambda l=l, b=b: self.phase_hy_inproj(l, b))
                    ph.append(lambda l=l, b=b: self.phase_rglru(l, b))
                    ph.append(lambda l=l, b=b: self.phase_attn(l, b))
                else:
                    ph.append(lambda l=l, b=b: self.phase_rwkv(l, b))
            ph.append(lambda l=l, last=last: self.phase_mlp(l, last))
        for f in (ph if nphases is None else ph[:nphases]):
            f()
        self.k.finish()
        return self.nc


def host_consts():
    cs = np.zeros((128, 512), np.float32)
    cs[:, 0:128] = np.eye(128, dtype=np.float32)
    bo = np.zeros((128, 128), np.float32)
    bo[0:64, 0:64] = 1.0
    bo[64:128, 64:128] = 1.0
    cs[:, 128:256] = bo
    pw = np.zeros((128, 128), np.float32)
    for blk in range(2):
        for n in range(64):
            pw[blk * 64 + n, blk * 64 + (n + 32) % 64] = 1.0
    cs[:, 256:384] = pw
    rows = SEQ // 64
    row = np.repeat(np.arange(rows, dtype=np.float32), 64)
    col = np.tile(np.arange(64, dtype=np.float32), rows)
    inv = (np.float32(10000.0) ** (-np.arange(0, 32, 2, dtype=np.float32) / np.float32(32))).astype(np.float32)
    ang = np.concatenate([row[:, None] * inv, col[:, None] * inv], axis=-1).astype(np.float32)
    c = np.cos(ang).astype(np.float32).T
    s_ = np.sin(ang).astype(np.float32).T
    cos64 = np.concatenate([c, c], 0)
    sin64 = np.concatenate([-s_, s_], 0)
    cosT = np.ascontiguousarray(np.concatenate([cos64, cos64], 0))
    sinT = np.ascontiguousarray(np.concatenate([sin64, sin64], 0))
    return cs, cosT, sinT


def fmaj(v):
    return np.ascontiguousarray(np.asarray(v, np.float32).reshape(-1, 128).T)


def host_vb(inp):
    vb = np.zeros((128, NVB), np.float32)

    def put(name, arr):
        arr = np.asarray(arr, np.float32)
        vb[:, VBM[name]:VBM[name] + arr.shape[1]] = arr
    for l in range(DEPTH):
        put("ng0_%d" % l, fmaj(inp["norm_g"][l, 0]))
        put("ng1_%d" % l, fmaj(inp["norm_g"][l, 1]))
        put("adab_%d" % l, fmaj(inp["ada_b"][l]))
    for i in range(2):
        gq = inp["hy_q_norm"][i][PERM]
        gk = inp["hy_k_norm"][i][PERM]
        put("gq_%d" % i, np.concatenate([gq, gq])[:, None])
        put("gk_%d" % i, np.concatenate([gk, gk])[:, None])
        put("convw_%d" % i, np.concatenate([fmaj(inp["hy_conv_w"][i][j]) for j in range(4)], 1))
        put("convb_%d" % i, fmaj(inp["hy_conv_b"][i]))
        put("gateb_%d" % i, np.concatenate([fmaj(inp["hy_gate_b"][i][d][g]) for d in range(2) for g in range(2)], 1))
        put("lam_%d" % i, np.concatenate([fmaj(inp["hy_lam"][i][d]) for d in range(2)], 1))
    for i in range(2):
        put("mu_%d" % i, np.concatenate([fmaj(inp["rw_mu"][i][j]) for j in range(6)], 1))
        put("kk_%d" % i, fmaj(inp["rw_k_k"][i]))
        put("ka_%d" % i, fmaj(inp["rw_k_a"][i]))
        put("rk_%d" % i, fmaj(inp["rw_r_k"][i].reshape(-1)))
        put("gng_%d" % i, fmaj(inp["rw_gn_g"][i]))
        put("gnb_%d" % i, fmaj(inp["rw_gn_b"][i]))
        put("lb_%d" % i, np.concatenate([fmaj(inp["rw_lora_bias"][i][d][j]) for d in range(2) for j in range(2)], 1))
    return vb


def host_shared(inp):
    sh = {}
    cs, cosT, sinT = host_consts()
    sh["consts"], sh["cosT"], sh["sinT"] = cs, cosT, sinT
    sh["vb"] = host_vb(inp)
    sh["ada_w"] = np.ascontiguousarray(inp["ada_w"], np.float32)
    sh["mlp_w1"] = np.ascontiguousarray(inp["mlp_w1"], np.float32)
    sh["mlp_w2"] = np.ascontiguousarray(inp["mlp_w2"], np.float32)
    win = inp["hy_w_in"]
    cols = []
    for h in range(8):
        cols.append(h * 64 + PERM)
    for kv in range(2):
        cols.append(512 + kv * 64 + PERM)
        cols.append(512 + kv * 64 + PERM)
    cols.append(np.arange(640, 768))
    cols.append(np.arange(768, 1792))
    cols = np.concatenate(cols)
    sh["hy_win"] = np.ascontiguousarray(win[:, :, cols], np.float32)
    sh["hy_wout"] = np.ascontiguousarray(inp["hy_w_out"], np.float32)
    gw = inp["hy_gate_w"]
    bd = np.zeros((2, 2, 2, 4, 128, 128), np.float32)
    for c in range(4):
        bd[:, :, :, c, 0:64, 0:64] = gw[:, :, :, 2 * c]
        bd[:, :, :, c, 64:128, 64:128] = gw[:, :, :, 2 * c + 1]
    sh["hy_gw"] = bd
    sh["rw_wrkv"] = np.ascontiguousarray(inp["rw_w_rkv"], np.float32)
    st = np.concatenate([np.arange((2 * p + hp) * 64, (2 * p + hp) * 64 + 64) for hp in range(2) for p in range(8)])
    sh["rw_wvst"] = np.ascontiguousarray(inp["rw_w_rkv"][:, 2][:, :, st], np.float32)
    sh["rw_wo"] = np.ascontiguousarray(inp["rw_w_o"], np.float32)
    ldn = inp["rw_lora_down"]
    sh["rw_ld"] = np.ascontiguousarray(np.concatenate([ldn[:, d, j] for d in range(2) for j in range(2)], axis=-1), np.float32)
    lup = inp["rw_lora_up"]
    sh["rw_lu"] = np.ascontiguousarray(np.stack([lup[:, d, j] for d in range(2) for j in range(2)], axis=1), np.float32)
    sh["rw_gd"] = np.ascontiguousarray(inp["rw_gate_down"], np.float32)
    sh["rw_gu"] = np.ascontiguousarray(inp["rw_gate_up"], np.float32)
    ii = np.arange(64)
    up = (ii[None, :] > ii[:, None]).astype(np.float32)
    le = (ii[:, None] <= ii[None, :]).astype(np.float32)
    lo_ = (ii[None, :] < ii[:, None]).astype(np.float32)

    def bdm(m):
        o = np.zeros((128, 128), np.float32)
        o[0:64, 0:64] = m
        o[64:128, 64:128] = m
        return o

    def stk(m):
        return np.concatenate([m, m], 0)
    wm = np.zeros((128, 2, 512), np.float32)
    for d, (u_, l_, s_) in enumerate(((up, lo_, le), (up.T, lo_.T, le.T))):
        wm[:, d, 0:128] = bdm(u_)
        wm[:, d, 128:192] = stk(s_)
        wm[:, d, 192:320] = bdm(u_)
        wm[:, d, 320:384] = stk(s_)
        wm[:, d, 384:512] = bdm(l_)
    sh["wmask"] = wm
    rm = np.ones((128, 2, 2048), np.float32)
    tt = np.arange(2048)
    rm[:, 0, tt % 64 == 0] = 0.0
    rm[:, 1, tt % 64 == 63] = 0.0
    sh["rmask"] = rm
    return sh


_CACHE = {}


def kernel(**inp):
    inp = {k_: np.asarray(v) for k_, v in inp.items()}
    sh = host_shared(inp)
    if "nc" not in _CACHE:
        _CACHE["nc"] = Prog().build()
    nc = _CACHE["nc"]
    in_maps = []
    for core in range(8):
        bs = [2 * core, 2 * core + 1]
        m = dict(sh)
        m["xT_in"] = np.ascontiguousarray(np.stack([inp["x"][b].T for b in bs]), np.float32)
        m["ctxT_in"] = np.ascontiguousarray(np.stack([inp["ctx"][b].T for b in bs]), np.float32)
        cv = np.stack([inp["c"][bs[0]], inp["c"][bs[1]], inp["c_ctx"]], 0)
        m["cT"] = np.ascontiguousarray(cv.reshape(3, 8, 128).transpose(2, 1, 0), np.float32)
        in_maps.append(m)
    res = run_bass_kernel_spmd(nc, in_maps, core_ids=list(range(8)))
    out = np.empty((16, SEQ, D), np.float32)
    for core in range(8):
        o = res.results[core]["outT"]
        for j in range(NB):
            out[2 * core + j] = o[j].T
    return out
```
